# Optimizing a Trainium2 kernel written in Bass

```python
import math
import jax, jax.numpy as jnp
from jax import lax
import numpy as np

D_MODEL = 1024
BATCH = 2
SEQ = 16384
DEPTH = 1

MEM_LEN = 256
EPS = 1e-6

SSD_HEADS = 16
SSD_HEAD_DIM = 64
SSD_DIM = SSD_HEADS * SSD_HEAD_DIM
SSD_GROUPS = 2
SSD_HEADS_PER_GROUP = SSD_HEADS // SSD_GROUPS
SSD_STATE = 128
SSD_CONV = 4
SSD_CHUNK = 128
SSD_CONV_CH = SSD_DIM + 2 * SSD_GROUPS * SSD_STATE

HG_HEADS = 8
HG_K = 128
HG_V = 128
HG_KDIM = HG_HEADS * HG_K
HG_VDIM = HG_HEADS * HG_V
HG_CHUNK = 64

D_MIX = SSD_DIM + HG_VDIM
N_IN = SSD_DIM + SSD_CONV_CH + SSD_HEADS + 2 * HG_KDIM + 2 * HG_VDIM

XA_HEADS = 4
XA_HEAD_DIM = D_MODEL // XA_HEADS

FFN_DIM = -(-8 * D_MODEL // (3 * 256)) * 256

kernel_name = "hybrid_ssd_hgrn2_xattn_block"


def rmsnorm(x, w):
    xf = x.astype(jnp.float32)
    y = xf * lax.rsqrt(jnp.mean(xf * xf, axis=-1, keepdims=True) + EPS)
    return (y * w.astype(jnp.float32)).astype(x.dtype)


def causal_depthwise_conv(u, w, b):
    ch = u.shape[-1]
    out = lax.conv_general_dilated(
        u, w[:, None, :].astype(u.dtype), window_strides=(1,),
        padding=[(w.shape[0] - 1, 0)], dimension_numbers=("NWC", "WIO", "NWC"),
        feature_group_count=ch)
    return out + b.astype(u.dtype)


def ssd_mixer(xs, bm, cm, dt, a_log, d_skip):
    bsz, seqlen = xs.shape[0], xs.shape[1]
    nc, q = seqlen // SSD_CHUNK, SSD_CHUNK
    g, e, p, n = SSD_GROUPS, SSD_HEADS_PER_GROUP, SSD_HEAD_DIM, SSD_STATE
    a = -jnp.exp(a_log.astype(jnp.float32)).reshape(g, e)
    x_c = xs.reshape(bsz, nc, q, g, e, p)
    b_c = bm.reshape(bsz, nc, q, g, n)
    c_c = cm.reshape(bsz, nc, q, g, n)
    dt_c = dt.reshape(bsz, nc, q, g, e)
    acum = jnp.cumsum(jnp.moveaxis(dt_c * a, 2, -1), axis=-1)
    causal = jnp.tril(jnp.ones((q, q), dtype=bool))
    seg = acum[..., :, None] - acum[..., None, :]
    l_dec = jnp.exp(jnp.where(causal, seg, -jnp.inf))
    xdt = x_c * dt_c[..., None]
    cb = jnp.einsum("bclgn,bcsgn->bcgls", c_c, b_c)
    y_diag = jnp.einsum("bcgels,bcsgep->bclgep", cb[:, :, :, None] * l_dec, xdt)
    dec_to_end = jnp.moveaxis(jnp.exp(acum[..., -1:] - acum), -1, 2)
    states = jnp.einsum("bcsgn,bcsgep->bcgepn", b_c, xdt * dec_to_end[..., None])
    chunk_decay = jnp.exp(acum[..., -1])

    def step(s, inp):
        st, dec = inp
        return s * dec[..., None, None] + st, s

    s0 = jnp.zeros((bsz, g, e, p, n), jnp.float32)
    _, s_in = lax.scan(step, s0, (jnp.moveaxis(states, 1, 0).astype(jnp.float32),
                                  jnp.moveaxis(chunk_decay, 1, 0)))
    s_in = jnp.moveaxis(s_in, 0, 1)
    dec_from_start = jnp.moveaxis(jnp.exp(acum), -1, 2)
    y_off = jnp.einsum("bclgn,bcgepn->bclgep", c_c, s_in) * dec_from_start[..., None]
    y = (y_diag + y_off).reshape(bsz, seqlen, SSD_HEADS, p)
    return y + d_skip.astype(jnp.float32)[:, None] * xs


def hgrn2_mixer(q_raw, f_raw, i_val, lb):
    bsz, seqlen = q_raw.shape[0], q_raw.shape[1]
    nc, c = seqlen // HG_CHUNK, HG_CHUNK
    qf = jax.nn.silu(q_raw)
    fg = lb + (1.0 - lb) * jax.nn.sigmoid(f_raw.astype(jnp.float32))
    kf = 1.0 - fg
    gl = jnp.log(fg)

    def to_chunks(t):
        return t.reshape(bsz, nc, c, t.shape[2], t.shape[3]).transpose(1, 0, 3, 2, 4)

    causal = jnp.tril(jnp.ones((c, c), dtype=bool))[:, :, None]

    def step(s, inp):
        qc, kc, vc, gc = inp
        bcum = jnp.cumsum(gc, axis=2)
        o_inter = jnp.einsum("bhqk,bhkv->bhqv", qc * jnp.exp(bcum), s)
        seg = bcum[:, :, :, None, :] - bcum[:, :, None, :, :]
        dec = jnp.exp(jnp.where(causal, seg, -jnp.inf))
        att = jnp.einsum("bhik,bhijk->bhij", qc, dec * kc[:, :, None, :, :])
        o_intra = jnp.einsum("bhij,bhjv->bhiv", att, vc)
        b_last = bcum[:, :, -1:, :]
        s_new = s * jnp.exp(b_last[:, :, 0, :])[..., None] + jnp.einsum(
            "bhjk,bhjv->bhkv", kc * jnp.exp(b_last - bcum), vc)
        return s_new, (o_inter + o_intra).astype(jnp.float32)

    s0 = jnp.zeros((bsz, HG_HEADS, HG_K, HG_V), jnp.float32)
    _, o = lax.scan(step, s0, (to_chunks(qf), to_chunks(kf), to_chunks(i_val), to_chunks(gl)))
    return o.transpose(1, 0, 3, 2, 4).reshape(bsz, seqlen, HG_HEADS, HG_V)


def setup_inputs(seed: int = 0) -> dict:
    key = jax.random.key(seed)
    ks = jax.random.split(key, 24)
    f32 = jnp.float32

    def nrm(k, shape, scale):
        return jax.random.normal(k, shape, f32) * scale

    def gain(k, shape):
        return 1.0 + 0.02 * jax.random.normal(k, shape, f32)

    dt = jnp.exp(jax.random.uniform(ks[5], (DEPTH, SSD_HEADS), f32)
                 * (math.log(0.1) - math.log(0.001)) + math.log(0.001))
    return {
        "x": nrm(ks[0], (BATCH, SEQ, D_MODEL), 1.0),
        "mem": nrm(ks[1], (BATCH, MEM_LEN, D_MODEL), 1.0),
        "norm_mix_w": gain(ks[2], (DEPTH, D_MODEL)),
        "w_in": nrm(ks[3], (DEPTH, D_MODEL, N_IN), D_MODEL ** -0.5),
        "conv_w": nrm(ks[4], (DEPTH, SSD_CONV, SSD_CONV_CH), SSD_CONV ** -0.5),
        "conv_b": nrm(ks[6], (DEPTH, SSD_CONV_CH), 0.02),
        "dt_bias": dt + jnp.log(-jnp.expm1(-dt)),
        "a_log": jnp.log(jax.random.uniform(ks[7], (DEPTH, SSD_HEADS), f32, 1.0, 16.0)),
        "d_skip": 1.0 + 0.1 * jax.random.normal(ks[8], (DEPTH, SSD_HEADS), f32),
        "ssd_norm_w": gain(ks[9], (DEPTH, SSD_DIM)),
        "hg_lower_bounds": nrm(ks[10], (DEPTH + 1, HG_KDIM), 0.5),
        "hg_norm_w": gain(ks[11], (DEPTH, HG_V)),
        "w_out": nrm(ks[12], (DEPTH, D_MIX, D_MODEL), D_MIX ** -0.5),
        "norm_xa_w": gain(ks[13], (DEPTH, D_MODEL)),
        "norm_mem_w": gain(ks[14], (DEPTH, D_MODEL)),
        "xa_wq": nrm(ks[15], (DEPTH, D_MODEL, D_MODEL), D_MODEL ** -0.5),
        "xa_wkv": nrm(ks[16], (DEPTH, D_MODEL, 2 * D_MODEL), D_MODEL ** -0.5),
        "xa_wo": nrm(ks[17], (DEPTH, D_MODEL, D_MODEL), D_MODEL ** -0.5),
        "norm_ffn_w": gain(ks[18], (DEPTH, D_MODEL)),
        "ffn_w_gate": nrm(ks[19], (DEPTH, D_MODEL, FFN_DIM), D_MODEL ** -0.5),
        "ffn_w_up": nrm(ks[20], (DEPTH, D_MODEL, FFN_DIM), D_MODEL ** -0.5),
        "ffn_w_down": nrm(ks[21], (DEPTH, FFN_DIM, D_MODEL), FFN_DIM ** -0.5),
        "norm_final_w": gain(ks[22], (D_MODEL,)),
    }


def reference(x, mem, norm_mix_w, w_in, conv_w, conv_b, dt_bias, a_log, d_skip, ssd_norm_w,
              hg_lower_bounds, hg_norm_w, w_out, norm_xa_w, norm_mem_w, xa_wq, xa_wkv, xa_wo,
              norm_ffn_w, ffn_w_gate, ffn_w_up, ffn_w_down, norm_final_w):
    bsz, seqlen, _ = x.shape
    lb_all = jnp.cumsum(jax.nn.softmax(hg_lower_bounds.astype(jnp.float32), axis=0), axis=0)
    s1 = SSD_DIM
    s2 = s1 + SSD_CONV_CH
    s3 = s2 + SSD_HEADS
    s4 = s3 + HG_KDIM
    s5 = s4 + HG_KDIM
    s6 = s5 + HG_VDIM
    for l in range(DEPTH):
        h = rmsnorm(x, norm_mix_w[l])
        proj = h @ w_in[l]
        z, xbc, dt_raw, hq, hf, hi, hgate = jnp.split(proj, [s1, s2, s3, s4, s5, s6], axis=-1)
        xbc = jax.nn.silu(causal_depthwise_conv(xbc, conv_w[l], conv_b[l]))
        xs, bm, cm = jnp.split(xbc, [SSD_DIM, SSD_DIM + SSD_GROUPS * SSD_STATE], axis=-1)
        dt = jax.nn.softplus((dt_raw + dt_bias[l]).astype(jnp.float32))
        y_a = ssd_mixer(xs.reshape(bsz, seqlen, SSD_HEADS, SSD_HEAD_DIM),
                        bm.reshape(bsz, seqlen, SSD_GROUPS, SSD_STATE),
                        cm.reshape(bsz, seqlen, SSD_GROUPS, SSD_STATE),
                        dt, a_log[l], d_skip[l])
        yz = (y_a.reshape(bsz, seqlen, SSD_DIM) * jax.nn.silu(z)).reshape(
            bsz, seqlen, SSD_GROUPS, SSD_DIM // SSD_GROUPS)
        y_a = rmsnorm(yz, ssd_norm_w[l].reshape(SSD_GROUPS, -1)).reshape(bsz, seqlen, SSD_DIM)
        o_b = hgrn2_mixer(hq.reshape(bsz, seqlen, HG_HEADS, HG_K),
                          hf.reshape(bsz, seqlen, HG_HEADS, HG_K),
                          hi.reshape(bsz, seqlen, HG_HEADS, HG_V),
                          lb_all[l].reshape(HG_HEADS, HG_K))
        o_b = rmsnorm(o_b, hg_norm_w[l]) * jax.nn.silu(hgate.reshape(bsz, seqlen, HG_HEADS, HG_V))
        mixed = jnp.concatenate([y_a, o_b.reshape(bsz, seqlen, HG_VDIM)], axis=-1).astype(x.dtype)
        x = x + mixed @ w_out[l]
        h = rmsnorm(x, norm_xa_w[l])
        m = rmsnorm(mem, norm_mem_w[l])
        qx = (h @ xa_wq[l]).reshape(bsz, seqlen, XA_HEADS, XA_HEAD_DIM)
        km, vm = jnp.split(m @ xa_wkv[l], 2, axis=-1)
        km = km.reshape(bsz, MEM_LEN, XA_HEADS, XA_HEAD_DIM)
        vm = vm.reshape(bsz, MEM_LEN, XA_HEADS, XA_HEAD_DIM)
        sc = jnp.einsum("bqhd,bkhd->bhqk", qx, km, preferred_element_type=jnp.float32)
        pr = jax.nn.softmax(sc * (XA_HEAD_DIM ** -0.5), axis=-1).astype(vm.dtype)
        ox = jnp.einsum("bhqk,bkhd->bqhd", pr, vm).reshape(bsz, seqlen, D_MODEL)
        x = x + ox @ xa_wo[l]
        h = rmsnorm(x, norm_ffn_w[l])
        x = x + (jax.nn.silu(h @ ffn_w_gate[l]) * (h @ ffn_w_up[l])) @ ffn_w_down[l]
    return rmsnorm(x, norm_final_w)
```

```python
import contextlib
import numpy as np
import concourse.bass as bass
import concourse.mybir as mybir
from concourse.bass_utils import run_bass_kernel_spmd

F32 = mybir.dt.float32
BF16 = mybir.dt.bfloat16
AF = mybir.ActivationFunctionType
ALU = mybir.AluOpType
AX = mybir.AxisListType

D = 1024
FF = 2816
EPS = 1e-6
S1, S2, S3, S4, S5, S6 = 1024, 2560, 2576, 3600, 4624, 5648
NWS = 3


class Res:
    __slots__ = ("name", "last_w", "readers")

    def __init__(self, name=""):
        self.name = name
        self.last_w = None
        self.readers = {}


class Op:
    __slots__ = ("eng", "fn", "idx", "deps", "dma_key", "dma_cnt", "needs_inc", "inc_val")

    def __init__(self, eng, fn, idx, dma_key=None):
        self.eng = eng
        self.fn = fn
        self.idx = idx
        self.deps = {}
        self.dma_key = dma_key
        self.dma_cnt = 0
        self.needs_inc = False
        self.inc_val = 0


COMPUTE = ("pe", "act", "dve", "pool")


class Prog:
    def __init__(self, nc):
        self.nc = nc
        self.ops = []
        self.dma_counts = {}
        self.last_real = {}

    def add(self, eng, fn, reads=(), writes=(), dma_key=None):
        op = Op(eng, fn, len(self.ops), dma_key)
        if dma_key is not None:
            c = self.dma_counts.get(dma_key, 0) + 1
            self.dma_counts[dma_key] = c
            op.dma_cnt = c
        deps = op.deps
        for r in reads:
            if r.last_w is not None:
                deps.setdefault(r.last_w, set()).add("raw")
        for w in writes:
            lw = w.last_w
            if lw is not None:
                if not (dma_key is not None and lw.dma_key == dma_key):
                    deps.setdefault(lw, set()).add("waw")
            for rd in w.readers.values():
                deps.setdefault(rd, set()).add("war")
        k = ("dma", op.idx) if dma_key is not None else eng
        for r in reads:
            r.readers[k] = op
        for w in writes:
            w.last_w = op
            w.readers = {}
        self.ops.append(op)
        if dma_key is None:
            self.last_real[eng] = op
        return op

    def pe(self, fn, reads=(), writes=()):
        return self.add("pe", fn, reads, writes)

    def act(self, fn, reads=(), writes=()):
        return self.add("act", fn, reads, writes)

    def dve(self, fn, reads=(), writes=()):
        return self.add("dve", fn, reads, writes)

    def pool(self, fn, reads=(), writes=()):
        return self.add("pool", fn, reads, writes)

    def dma(self, queue, key, out, in_, reads=(), writes=()):
        return self.add(queue, lambda e: e.dma_start(out=out, in_=in_), reads, writes, dma_key=key)

    def barrier(self, extra=()):
        lasts = dict(self.last_real)
        for e in COMPUTE:
            op = Op(e, None, len(self.ops))
            for e2, lo in lasts.items():
                if e2 != e:
                    op.deps[lo] = {"raw"}
            for x in extra:
                op.deps[x] = {"raw"}
            self.ops.append(op)

    def emit(self, final_wait_ops=()):
        nc = self.nc
        fin = Op("sp", None, len(self.ops))
        for o in final_wait_ops:
            fin.deps[o] = {"raw"}
        ops = self.ops + [fin]
        for op in ops:
            real = {}
            for d, kinds in op.deps.items():
                if d.dma_key is None and d.eng == op.eng and op.dma_key is None:
                    if op.eng == "pe" or kinds == {"war"}:
                        continue
                real[d] = kinds
            op.deps = real
            for d in real:
                if d.dma_key is None:
                    d.needs_inc = True
        cnt = {e: 0 for e in COMPUTE + ("sp",)}
        for op in ops:
            if op.dma_key is None and op.needs_inc:
                cnt[op.eng] += 1
                op.inc_val = cnt[op.eng]
        dma_keys = sorted(self.dma_counts.keys())
        with contextlib.ExitStack() as st:
            esem = {e: st.enter_context(nc.semaphore("s_" + e)) for e in cnt}
            dsem = {k: st.enter_context(nc.semaphore("d_%d" % i)) for i, k in enumerate(dma_keys)}
            block = st.enter_context(nc.Block())
            engs = {"pe": block.tensor, "act": block.scalar, "dve": block.vector,
                    "pool": block.gpsimd, "sp": block.sync}
            for ename, deco in engs.items():
                my = [o for o in ops if o.eng == ename]
                if not my:
                    continue

                def body(e, my=my, ename=ename):
                    waited = {}
                    for op in my:
                        need = {}
                        for d in op.deps:
                            if d.dma_key is not None:
                                s, v = ("d", d.dma_key), 16 * d.dma_cnt
                            else:
                                s, v = ("e", d.eng), d.inc_val
                            if need.get(s, 0) < v:
                                need[s] = v
                        for s, v in need.items():
                            if waited.get(s, 0) >= v:
                                continue
                            waited[s] = v
                            e.wait_ge(dsem[s[1]] if s[0] == "d" else esem[s[1]], v)
                        if op.fn is None:
                            continue
                        ins = op.fn(e)
                        if op.dma_key is not None:
                            ins.then_inc(dsem[op.dma_key], 16)
                        elif op.needs_inc:
                            ins.then_inc(esem[ename], 1)

                deco(body)


def chunk_catalog():
    cat = []
    for j in range(2):
        cat.append(("Z%d" % j, "w_in", 0, 8, [(0, 512, 512 * j)], "mix"))
    for j in range(3):
        cat.append(("X%d" % j, "w_in", 0, 8, [(0, 512, S1 + 512 * j)], "mix"))
    for a in range(4):
        cat.append(("H%d" % a, "w_in", 0, 8, [(0, 256, S3 + 256 * a), (256, 256, S4 + 256 * a)], "mix"))
    for j in range(2):
        cat.append(("V%d" % j, "w_in", 0, 8, [(0, 512, S5 + 512 * j)], "mix"))
    for j in range(2):
        cat.append(("G%d" % j, "w_in", 0, 8, [(0, 512, S6 + 512 * j)], "mix"))
    for j in range(2):
        for i in range(2):
            cat.append(("OUT%d%d" % (j, i), "w_out", 8 * i, 8, [(0, 512, 512 * j)], "wout%d" % i))
    for j in range(2):
        cat.append(("Q%d" % j, "wq", 0, 8, [(0, 512, 512 * j)], "xa"))
    for j in range(4):
        cat.append(("KV%d" % j, "wkv", 0, 8, [(0, 512, 512 * j)], "mem"))
    for j in range(2):
        cat.append(("O%d" % j, "wo", 0, 8, [(0, 512, 512 * j)], None))
    for j in range(11):
        cat.append(("GU%d" % j, "wgu", 0, 8, [(0, 256, 256 * j), (256, 256, 256 * j)], "ffn"))
    for j in range(2):
        for i in range(3):
            cat.append(("D%d%d" % (j, i), "wd", 8 * i, 8 if i < 2 else 6, [(0, 512, 512 * j)], None))
    return cat


PC_CW, PC_CB, PC_DTB, PC_ALOG, PC_DSK, PC_HLB0, PC_HLB1 = 0, 48, 60, 76, 92, 108, 116
PC_MIX, PC_XA, PC_MEM, PC_FFN, PC_WOUT, PC_MASK = 124, 132, 140, 148, 156, 172


def build(T, NPRE):
    NT = T // 512
    NPAR = PC_MASK + max(NPRE, 1)
    nc = bass.Bass("TRN2", target_bir_lowering=False)
    xm = nc.dram_tensor("xm", [T, D], F32, kind="ExternalInput").ap()
    xp = nc.dram_tensor("xp", [max(NPRE, 1) * 512, D], F32, kind="ExternalInput").ap()
    memd = nc.dram_tensor("mem", [256, D], F32, kind="ExternalInput").ap()
    pard = nc.dram_tensor("par", [128, NPAR], F32, kind="ExternalInput").ap()
    nfwd = nc.dram_tensor("nfw", [128, D], F32, kind="ExternalInput").ap()
    wd_ = {
        "w_in": nc.dram_tensor("w_in", [D, 6672], F32, kind="ExternalInput").ap(),
        "w_out": nc.dram_tensor("w_out", [2048, D], F32, kind="ExternalInput").ap(),
        "wq": nc.dram_tensor("wq", [D, D], F32, kind="ExternalInput").ap(),
        "wkv": nc.dram_tensor("wkv", [D, 2048], F32, kind="ExternalInput").ap(),
        "wo": nc.dram_tensor("wo", [D, D], F32, kind="ExternalInput").ap(),
        "wg": nc.dram_tensor("wg", [D, FF], F32, kind="ExternalInput").ap(),
        "wu": nc.dram_tensor("wu", [D, FF], F32, kind="ExternalInput").ap(),
        "wd": nc.dram_tensor("wd", [FF, D], F32, kind="ExternalInput").ap(),
    }
    outd = nc.dram_tensor("out", [T, D], F32, kind="ExternalOutput").ap()
    cat = chunk_catalog()
    cid = {c[0]: i for i, c in enumerate(cat)}
    wsc = nc.dram_tensor("wsc", [len(cat), 128, 4096], BF16, kind="Internal").ap()
    R_wsc = [Res("wsc%d" % i) for i in range(len(cat))]

    P = Prog(nc)
    with contextlib.ExitStack() as st:
        def sb(name, shape, dt):
            return st.enter_context(nc.sbuf_tensor("sb_" + name, shape, dt))

        par = sb("par", [128, NPAR], F32); R_par = Res()
        ident = sb("ident", [128, 128], BF16); R_ident = Res()
        U = sb("U", [128, 128], F32); R_U = Res()
        ones = sb("ones", [128, 512], F32); R_ones = Res()
        cst = sb("cst", [128, 64], F32); R_cst = Res()
        wdt = sb("wdt", [128, 8, 16], BF16); R_wdt = Res()
        x_tm = sb("x_tm", [128, 4, D], F32); R_x = [Res() for _ in range(4)]
        hT = sb("hT", [128, 8, 512], BF16); R_hT = Res()
        hn = [sb("hn%d" % i, [128, D], BF16) for i in range(2)]; R_hn = [Res(), Res()]
        junk = sb("junk", [128, D], BF16); R_junk = Res()
        wbuf = [sb("wbuf%d" % i, [128, 8, 512], BF16) for i in range(NWS)]; R_wbuf = [Res() for _ in range(NWS)]
        Ssd = sb("Ssd", [128, D], F32); R_Ssd = Res()
        Ssdb = sb("Ssdb", [128, D], BF16); R_Ssdb = Res()
        Shg = sb("Shg", [128, 8, 128], F32); R_Shg = [Res() for _ in range(8)]
        Shgb = sb("Shgb", [128, 8, 128], BF16); R_Shgb = [Res() for _ in range(8)]
        halo = sb("halo", [128, 12, 3], F32); R_halo = Res()
        mixedT = sb("mixedT", [128, 16, 512], BF16); R_mixT = [Res() for _ in range(4)]
        kmT = sb("kmT", [128, 8, 256], BF16); R_kmT = Res()
        vm = sb("vm", [128, 2, D], BF16); R_vm = Res()
        ost = [sb("ost0", [128, D], F32)]; R_ost = [Res()]
        stat = sb("stat", [128, 64], F32)
        ARENA = 27500
        arena = sb("arena", [128, ARENA], F32)
        psb = [st.enter_context(nc.psum_tensor("ps%d" % i, [128, 512], F32)) for i in range(8)]
        R_ps = [Res() for _ in range(8)]
        pctr = [0]

        def psum():
            i = pctr[0] % 8
            pctr[0] += 1
            return psb[i], R_ps[i]

        def bfv(pt):
            return pt[:, 0:512].bitcast(BF16)

        class Arena:
            def __init__(self):
                self.off = 0

            def f32(self, n):
                a = arena[:, self.off:self.off + n]
                self.off += n
                assert self.off <= ARENA, self.off
                return a

            def bf(self, n):
                n32 = (n + 1) // 2
                a = arena[:, self.off:self.off + n32].bitcast(BF16)
                self.off += n32
                assert self.off <= ARENA, self.off
                return a

        d_par = P.dma("sp", "par", par[:], pard, writes=[R_par])
        P.pool(lambda e: e.memset(ones[:], 1.0), writes=[R_ones])
        P.pool(lambda e: e.memset(U[:], 1.0), writes=[R_U])
        P.pool(lambda e: e.affine_select(out=U[:], in_=U[:], pattern=[[1, 128]], compare_op=ALU.is_ge,
                                         fill=0.0, base=0, channel_multiplier=-1), reads=[R_U], writes=[R_U])
        idf = arena[:, 0:128]
        R_idf = Res()
        P.pool(lambda e: e.memset(idf, 0.0), writes=[R_idf])
        P.pool(lambda e: e.affine_select(out=idf, in_=ones[:, 0:128], pattern=[[1, 128]], compare_op=ALU.is_equal,
                                         fill=0.0, base=0, channel_multiplier=-1), reads=[R_ones, R_idf], writes=[R_idf])
        P.dve(lambda e: e.tensor_copy(out=ident[:], in_=idf), reads=[R_idf], writes=[R_ident])
        P.pool(lambda e: e.memset(Ssd[:], 0.0), writes=[R_Ssd])
        P.pool(lambda e: e.memset(Ssdb[:], 0.0), writes=[R_Ssdb])
        P.pool(lambda e: e.memset(Shg[:], 0.0), writes=R_Shg)
        P.pool(lambda e: e.memset(Shgb[:], 0.0), writes=R_Shgb)
        P.pool(lambda e: e.memset(halo[:], 0.0), writes=[R_halo])
        P.pool(lambda e: e.memset(cst[:, 32:40], 1.0), writes=[R_cst])
        P.dve(lambda e: e.tensor_tensor(out=cst[:, 40:48], in0=par[:, PC_HLB0:PC_HLB0 + 8],
                                        in1=par[:, PC_HLB1:PC_HLB1 + 8], op=ALU.subtract), reads=[R_par, R_cst], writes=[R_cst])
        P.act(lambda e: e.activation(out=cst[:, 0:8], in_=cst[:, 40:48], func=AF.Sigmoid), reads=[R_cst], writes=[R_cst])
        P.act(lambda e: e.activation(out=cst[:, 8:16], in_=cst[:, 40:48], func=AF.Sigmoid, scale=-1.0), reads=[R_cst], writes=[R_cst])
        P.act(lambda e: e.activation(out=cst[:, 48:64], in_=par[:, PC_ALOG:PC_ALOG + 16], func=AF.Exp), reads=[R_par, R_cst], writes=[R_cst])
        P.dve(lambda e: e.tensor_scalar(out=cst[:, 16:32], in0=cst[:, 48:64], scalar1=-1.0, scalar2=None, op0=ALU.mult),
              reads=[R_cst], writes=[R_cst])
        lb, oml, aneg, onesb = cst[:, 0:8], cst[:, 8:16], cst[:, 16:32], cst[:, 32:40]

        scale_ap = {"mix": par[:, PC_MIX:PC_MIX + 8], "xa": par[:, PC_XA:PC_XA + 8], "mem": par[:, PC_MEM:PC_MEM + 8],
                    "ffn": par[:, PC_FFN:PC_FFN + 8], "wout0": par[:, PC_WOUT:PC_WOUT + 8],
                    "wout1": par[:, PC_WOUT + 8:PC_WOUT + 16], None: onesb}
        ar = Arena(); ar.off = 128
        stg = [ar.f32(4096).rearrange("p (k n) -> p k n", k=8) for _ in range(2)]
        stgb = [ar.bf(4096).rearrange("p (k n) -> p k n", k=8) for _ in range(2)]
        wdt32 = ar.f32(128).rearrange("p (k n) -> p k n", k=8)
        R_stg = [Res(), Res()]; R_stgb = [Res(), Res()]; R_wdt32 = Res()
        pro_dmas = []

        def wsrc(key, kt0, nkt, c0, cw):
            if key == "wgu":
                raise AssertionError
            return wd_[key].rearrange("(kt p) n -> p kt n", p=128)[:, kt0:kt0 + nkt, c0:c0 + cw]

        for ci, (name, key, kt0, nkt, pieces, sk) in enumerate(cat):
            sl = ci % 2
            for pi, (dc, cw, sc) in enumerate(pieces):
                k2 = key
                if key == "wgu":
                    k2 = "wg" if pi == 0 else "wu"
                P.dma("sp", "stg%d" % sl, stg[sl][:, 0:nkt, dc:dc + cw], wsrc(k2, kt0, nkt, sc, cw), writes=[R_stg[sl]])
            sap = scale_ap[sk]
            f = (lambda e, sl=sl, nkt=nkt, sap=sap: e.tensor_tensor(
                out=stgb[sl][:, 0:nkt, :], in0=stg[sl][:, 0:nkt, :],
                in1=sap[:, 0:nkt].unsqueeze(2).broadcast_to([128, nkt, 512]), op=ALU.mult))
            (P.dve if ci % 2 == 0 else P.pool)(f, reads=[R_stg[sl], R_par, R_cst], writes=[R_stgb[sl]])
            pro_dmas.append(P.dma("sp", "wscw%d" % sl, wsc[ci].rearrange("p (k n) -> p k n", k=8)[:, 0:nkt, :],
                                  stgb[sl][:, 0:nkt, :], reads=[R_stgb[sl]], writes=[R_wsc[ci]]))
        P.dma("sp", "wdt32", wdt32, wsrc("w_in", 0, 8, S2, 16), writes=[R_wdt32])
        P.dve(lambda e: e.tensor_tensor(out=wdt[:], in0=wdt32, in1=par[:, PC_MIX:PC_MIX + 8].unsqueeze(2).broadcast_to([128, 8, 16]),
                                        op=ALU.mult), reads=[R_wdt32, R_par], writes=[R_wdt])

        pre_seq = ["X0", "X1", "X2", "V0", "V1", "H0", "H1", "H2", "H3"]
        main_seq = (["X0", "X1", "X2", "Z0", "Z1", "V0", "V1", "G0", "G1", "H0", "H1", "H2", "H3",
                     "OUT00", "OUT01", "OUT10", "OUT11", "Q0", "Q1", "O0", "O1"]
                    + ["GU%d" % j for j in range(11)] + ["D00", "D01", "D02", "D10", "D11", "D12"])
        wseq = ["KV0", "KV1", "KV2", "KV3"] + pre_seq * NPRE + main_seq * NT
        wstate = {"issued": 0, "got": 0}

        def wissue():
            i = wstate["issued"]
            c = cid[wseq[i]]
            sl = i % NWS
            P.dma("sp", "wb%d" % sl, wbuf[sl][:], wsc[c].rearrange("p (k n) -> p k n", k=8),
                  reads=[R_wsc[c]], writes=[R_wbuf[sl]])
            wstate["issued"] += 1

        def wget(name):
            i = wstate["got"]
            assert wseq[i] == name, (wseq[i], name)
            while wstate["issued"] < min(len(wseq), i + NWS):
                wissue()
            wstate["got"] += 1
            return wbuf[i % NWS], R_wbuf[i % NWS]

        def rstd_from_ss(ssv, n, Rs, inv_n):
            P.dve(lambda e: e.tensor_scalar(out=ssv, in0=ssv, scalar1=inv_n, scalar2=EPS, op0=ALU.mult, op1=ALU.add),
                  reads=[Rs], writes=[Rs])
            P.act(lambda e: e.activation(out=ssv, in_=ssv, func=AF.Sqrt), reads=[Rs], writes=[Rs])
            P.dve(lambda e: e.reciprocal(out=ssv, in_=ssv), reads=[Rs], writes=[Rs])

        rms_ctr = [0]
        R_rms = [Res(), Res()]
        R_ssf = Res()
        R_scp = [Res(), Res()]

        def rms_T(src, Rsrc, nsub, dstT, R_dst):
            k = rms_ctr[0] % 2
            rms_ctr[0] += 1
            ss = stat[:, 8 * k:8 * k + nsub]
            Rss = R_rms[k]
            P.pool(lambda e: e.memset(ss, 0.0), writes=[Rss])
            for s in range(nsub):
                P.act(lambda e, s=s: e.activation(out=junk[:], in_=src(s), func=AF.Square, accum_out=ss[:, s:s + 1]),
                      reads=[Rsrc[s], Rss], writes=[R_junk, Rss])
            rstd_from_ss(ss, nsub, Rss, 1.0 / D)
            for s in range(nsub):
                b = s % 2
                P.dve(lambda e, s=s, b=b: e.tensor_scalar(out=hn[b][:], in0=src(s), scalar1=ss[:, s:s + 1], scalar2=None,
                                                          op0=ALU.mult), reads=[Rsrc[s], Rss], writes=[R_hn[b]])
                pt, Rp = psum()
                pv = bfv(pt)
                for kt in range(8):
                    P.pe(lambda e, kt=kt, b=b, pv=pv: e.transpose(out=pv[:, kt * 128:(kt + 1) * 128],
                                                                  in_=hn[b][:, kt * 128:(kt + 1) * 128], identity=ident[:]),
                         reads=[R_hn[b], R_ident], writes=[Rp])
                P.act(lambda e, s=s, pv=pv: e.activation(out=dstT[:, :, s * 128:(s + 1) * 128],
                                                         in_=pv.rearrange("p (k t) -> p k t", k=8), func=AF.Copy),
                      reads=[Rp], writes=[R_dst])

        def proj_fm(wt, Rw, j, xT, RxT, ncols=512):
            pt, Rp = psum()
            for kt in range(8):
                P.pe(lambda e, kt=kt, pt=pt: e.matmul(pt[:, 0:ncols], lhsT=wt[:, kt, j * 128:(j + 1) * 128], rhs=xT[:, kt, 0:ncols],
                                                      start=(kt == 0), stop=(kt == 7)), reads=[Rw, RxT], writes=[Rp])
            return pt, Rp

        def proj_tm(wt, Rw, s, xT, RxT, ncols=512):
            pt, Rp = psum()
            for kt in range(8):
                P.pe(lambda e, kt=kt, pt=pt: e.matmul(pt[:, 0:ncols], lhsT=xT[:, kt, s * 128:(s + 1) * 128], rhs=wt[:, kt, 0:ncols],
                                                      start=(kt == 0), stop=(kt == 7)), reads=[Rw, RxT], writes=[Rp])
            return pt, Rp

        mem_t = ar.f32(2 * D).rearrange("p (s d) -> p s d", s=2); R_mem = [Res(), Res()]
        mT = ar.bf(8 * 256).rearrange("p (k t) -> p k t", k=8); R_mT = Res()
        for s in range(2):
            P.dma("sp", "mem%d" % s, mem_t[:, s, :], memd[s * 128:(s + 1) * 128, :], writes=[R_mem[s]])
        rms_T(lambda s: mem_t[:, s, :], R_mem, 2, mT, R_mT)
        for jc in range(2):
            wt, Rw = wget("KV%d" % jc)
            for j in range(4):
                pt, Rp = proj_fm(wt, Rw, j, mT, R_mT, ncols=256)
                P.act(lambda e, pt=pt, jc=jc, j=j: e.activation(out=kmT[:, 4 * jc + j, :], in_=pt[:, 0:256], func=AF.Copy),
                      reads=[Rp], writes=[R_kmT])
        for jc in range(2):
            wt, Rw = wget("KV%d" % (2 + jc))
            for s in range(2):
                pt, Rp = proj_tm(wt, Rw, s, mT, R_mT)
                P.act(lambda e, pt=pt, jc=jc, s=s: e.activation(out=vm[:, s, jc * 512:(jc + 1) * 512], in_=pt[:, 0:512], func=AF.Copy),
                      reads=[Rp], writes=[R_vm])
        P.barrier(extra=pro_dmas[-2:])

        out_dmas = []
        tile_ctr = [0]

        def do_tile(xsrc_d, row0, is_pre, pre_idx, out_row0):
            ti = tile_ctr[0]
            tile_ctr[0] += 1
            A = Arena()
            raw = A.f32(4 * 515).rearrange("p (j t) -> p j t", j=4); R_raw = Res()
            cacc = [A.f32(512) for _ in range(2)]; R_cacc = [Res(), Res()]
            xsT = A.bf(8 * 512).rearrange("p (k t) -> p k t", k=8); R_xsT = Res()
            BT = A.bf(2 * 512).rearrange("p (k t) -> p k t", k=2); R_BT = Res()
            CT = A.bf(2 * 512).rearrange("p (k t) -> p k t", k=2); R_CT = Res()
            xs_tm = A.bf(4 * D).rearrange("p (s d) -> p s d", s=4); R_xs = [Res() for _ in range(4)]
            B_tm = A.bf(4 * 256).rearrange("p (s d) -> p s d", s=4); R_Btm = Res()
            zs = A.bf(4 * D).rearrange("p (s d) -> p s d", s=4); R_zs = [Res() for _ in range(4)]
            vt = A.bf(4 * D).rearrange("p (s d) -> p s d", s=4); R_vt = [Res() for _ in range(4)]
            gs = A.bf(4 * D).rearrange("p (s d) -> p s d", s=4); R_gs = [Res() for _ in range(4)]
            dtr = A.f32(64).rearrange("p (s h) -> p s h", s=4); R_dtr = Res()
            dtA = A.f32(64).rearrange("p (s h) -> p s h", s=4); R_dtA = Res()
            acs = [A.f32(96) for _ in range(2)]; R_acs = [Res(), Res()]
            Lseg = [A.f32(512) for _ in range(2)]; R_Lseg = [Res(), Res()]
            MT = A.bf(16 * 128).rearrange("p (h l) -> p h l", h=16); R_MT = Res()
            cbm = A.f32(256).rearrange("p (g l) -> p g l", g=2); R_cbm = Res()
            xdt = A.bf(D); R_xdt = Res()
            xdtd = A.bf(D); R_xdtd = Res()
            t1 = A.f32(D); R_t1 = Res()
            t3 = A.f32(D); R_t3 = Res()
            yn = A.bf(D); R_yn = Res()
            qf = A.f32(1024).rearrange("p (i t) -> p i t", i=2); R_qf = Res()
            gl = A.f32(1024).rearrange("p (i t) -> p i t", i=2); R_gl = Res()
            kf = A.f32(1024).rearrange("p (i t) -> p i t", i=2); R_kf = Res()
            bt = [A.f32(513) for _ in range(2)]; R_bt = [Res(), Res()]
            etmp = [A.f32(128) for _ in range(4)]; R_et = [Res() for _ in range(4)]
            qt_ = [A.bf(128) for _ in range(2)]; R_qt = [Res(), Res()]
            KA = [A.bf(128) for _ in range(2)]; R_KA = [Res(), Res()]
            KB = [A.bf(128) for _ in range(2)]; R_KB = [Res(), Res()]
            KC = [A.bf(128) for _ in range(2)]; R_KC = [Res(), Res()]
            QC = [A.bf(64) for _ in range(2)]; R_QC = [Res(), Res()]
            if not is_pre:
                for i in range(2):
                    P.pool(lambda e, i=i: e.memset(KA[i], 0.0), writes=[R_KA[i]])
                    P.pool(lambda e, i=i: e.memset(KB[i], 0.0), writes=[R_KB[i]])
                    P.pool(lambda e, i=i: e.memset(KC[i], 0.0), writes=[R_KC[i]])
            qh = [A.bf(128) for _ in range(2)]; R_qh = [Res(), Res()]
            kh = [A.bf(128) for _ in range(2)]; R_kh = [Res(), Res()]
            khtm = [A.bf(128) for _ in range(2)]; R_khtm = [Res(), Res()]
            attm = [A.bf(128) for _ in range(2)]; R_attm = [Res(), Res()]
            otmp = A.f32(256); R_otmp = Res()
            og = A.bf(256); R_og = Res()
            sst = A.f32(32); R_sst = Res()

            for s in range(4):
                P.dma("sp", "x%d" % s, x_tm[:, s, :], xsrc_d[row0 + s * 128:row0 + (s + 1) * 128, :], writes=[R_x[s]])
            rms_T(lambda s: x_tm[:, s, :], R_x, 4, hT, R_hT)

            for c3 in range(3):
                wt, Rw = wget("X%d" % c3)
                P.pool(lambda e, c3=c3: e.tensor_copy(out=raw[:, :, 0:3], in_=halo[:, 4 * c3:4 * c3 + 4, :]),
                       reads=[R_halo], writes=[R_raw])
                for j in range(4):
                    ct = 4 * c3 + j
                    pt, Rp = proj_fm(wt, Rw, j, hT, R_hT)
                    P.act(lambda e, pt=pt, j=j: e.activation(out=raw[:, j, 3:515], in_=pt[:, 0:512], func=AF.Copy),
                          reads=[Rp], writes=[R_raw])
                P.pool(lambda e, c3=c3: e.tensor_copy(out=halo[:, 4 * c3:4 * c3 + 4, :], in_=raw[:, :, 512:515]),
                       reads=[R_raw], writes=[R_halo])
                for j in range(4):
                    ct = 4 * c3 + j
                    ca, Rca = cacc[j % 2], R_cacc[j % 2]
                    P.dve(lambda e, j=j, ct=ct, ca=ca: e.tensor_scalar(
                        out=ca, in0=raw[:, j, 0:512], scalar1=par[:, PC_CW + 4 * ct:PC_CW + 4 * ct + 1],
                        scalar2=par[:, PC_CB + ct:PC_CB + ct + 1], op0=ALU.mult, op1=ALU.add), reads=[R_raw, R_par], writes=[Rca])
                    for k in range(1, 4):
                        P.dve(lambda e, j=j, ct=ct, k=k, ca=ca: e.scalar_tensor_tensor(
                            out=ca, in0=raw[:, j, k:k + 512], scalar=par[:, PC_CW + 4 * ct + k:PC_CW + 4 * ct + k + 1],
                            in1=ca, op0=ALU.mult, op1=ALU.add), reads=[R_raw, R_par, Rca], writes=[Rca])
                    if ct < 8:
                        dst, Rd = xsT[:, ct, :], R_xsT
                    elif ct < 10:
                        dst, Rd = BT[:, ct - 8, :], R_BT
                    else:
                        dst, Rd = CT[:, ct - 10, :], R_CT
                    P.act(lambda e, ca=ca, dst=dst: e.activation(out=dst, in_=ca, func=AF.Silu), reads=[Rca], writes=[Rd])
            for s in range(4):
                pt, Rp = psum()
                pv = bfv(pt)
                for kt in range(8):
                    P.pe(lambda e, kt=kt, s=s, pv=pv: e.transpose(out=pv[:, kt * 128:(kt + 1) * 128],
                                                                  in_=xsT[:, kt, s * 128:(s + 1) * 128], identity=ident[:]),
                         reads=[R_xsT, R_ident], writes=[Rp])
                P.act(lambda e, s=s, pv=pv: e.activation(out=xs_tm[:, s, :], in_=pv, func=AF.Copy), reads=[Rp], writes=[R_xs[s]])
            pt, Rp = psum()
            pv = bfv(pt)
            for s in range(4):
                for g in range(2):
                    P.pe(lambda e, s=s, g=g, pv=pv: e.transpose(out=pv[:, s * 256 + g * 128:s * 256 + (g + 1) * 128],
                                                                in_=BT[:, g, s * 128:(s + 1) * 128], identity=ident[:]),
                         reads=[R_BT, R_ident], writes=[Rp])
            P.act(lambda e, pv=pv: e.activation(out=B_tm[:], in_=pv.rearrange("p (s d) -> p s d", s=4), func=AF.Copy),
                  reads=[Rp], writes=[R_Btm])

            pt, Rp = psum()
            for s in range(4):
                for kt in range(8):
                    P.pe(lambda e, s=s, kt=kt, pt=pt: e.matmul(pt[:, s * 16:(s + 1) * 16], lhsT=hT[:, kt, s * 128:(s + 1) * 128],
                                                               rhs=wdt[:, kt, :], start=(kt == 0), stop=(kt == 7)),
                         reads=[R_hT, R_wdt], writes=[Rp])
            P.dve(lambda e, pt=pt: e.tensor_tensor(out=dtr[:], in0=pt[:, 0:64].rearrange("p (s h) -> p s h", s=4),
                                                   in1=par[:, PC_DTB:PC_DTB + 16].unsqueeze(1).broadcast_to([128, 4, 16]), op=ALU.add),
                  reads=[Rp, R_par], writes=[R_dtr])
            P.act(lambda e: e.activation(out=dtr[:], in_=dtr[:], func=AF.Exp), reads=[R_dtr], writes=[R_dtr])
            P.act(lambda e: e.activation(out=dtr[:], in_=dtr[:], func=AF.Ln, bias=1.0), reads=[R_dtr], writes=[R_dtr])
            if is_pre:
                P.dve(lambda e: e.tensor_scalar(out=dtr[:], in0=dtr[:], scalar1=par[:, PC_MASK + pre_idx:PC_MASK + pre_idx + 1],
                                                scalar2=None, op0=ALU.mult), reads=[R_dtr, R_par], writes=[R_dtr])
            P.dve(lambda e: e.tensor_tensor(out=dtA[:], in0=dtr[:], in1=aneg.unsqueeze(1).broadcast_to([128, 4, 16]), op=ALU.mult),
                  reads=[R_dtr, R_cst], writes=[R_dtA])

            def tm_chunks(nm, dst, Rdst, func):
                for jc in range(2):
                    wt, Rw = wget("%s%d" % (nm, jc))
                    for s in range(4):
                        pt, Rp = proj_tm(wt, Rw, s, hT, R_hT)
                        P.act(lambda e, pt=pt, s=s, jc=jc: e.activation(out=dst[:, s, jc * 512:(jc + 1) * 512], in_=pt[:, 0:512], func=func),
                              reads=[Rp], writes=[Rdst[s]])
            if not is_pre:
                tm_chunks("Z", zs, R_zs, AF.Silu)
            tm_chunks("V", vt, R_vt, AF.Copy)
            if not is_pre:
                tm_chunks("G", gs, R_gs, AF.Silu)

            for c in range(4):
                ac, Rac = acs[c % 2], R_acs[c % 2]
                pt, Rp = psum()
                P.pe(lambda e, c=c, pt=pt: e.matmul(pt[:, 0:16], lhsT=U[:], rhs=dtA[:, c, :], start=True, stop=True),
                     reads=[R_U, R_dtA], writes=[Rp])
                P.pe(lambda e, c=c, pt=pt: e.matmul(pt[:, 16:32], lhsT=ones[:, 0:128], rhs=dtA[:, c, :], start=True, stop=True),
                     reads=[R_ones, R_dtA], writes=[Rp])
                P.dve(lambda e, pt=pt, ac=ac: e.tensor_copy(out=ac[:, 0:32], in_=pt[:, 0:32]), reads=[Rp], writes=[Rac])
                P.dve(lambda e, ac=ac: e.tensor_tensor(out=ac[:, 48:64], in0=ac[:, 16:32], in1=ac[:, 0:16], op=ALU.subtract),
                      reads=[Rac], writes=[Rac])
                P.act(lambda e, ac=ac: e.activation(out=ac[:, 32:48], in_=ac[:, 0:16], func=AF.Exp), reads=[Rac], writes=[Rac])
                P.act(lambda e, ac=ac: e.activation(out=ac[:, 48:64], in_=ac[:, 48:64], func=AF.Exp), reads=[Rac], writes=[Rac])
                P.act(lambda e, ac=ac: e.activation(out=ac[:, 64:80], in_=ac[:, 16:32], func=AF.Exp), reads=[Rac], writes=[Rac])
                P.dve(lambda e, c=c: e.tensor_tensor(out=xdt.rearrange("p (h d) -> p h d", h=16),
                                                     in0=xs_tm[:, c, :].rearrange("p (h d) -> p h d", h=16),
                                                     in1=dtr[:, c, :].unsqueeze(2).broadcast_to([128, 16, 64]), op=ALU.mult),
                      reads=[R_xs[c], R_dtr], writes=[R_xdt])
                if not is_pre:
                    pt, Rp = psum()
                    for g in range(2):
                        P.pe(lambda e, c=c, g=g, pt=pt: e.matmul(pt[:, g * 128:(g + 1) * 128], lhsT=BT[:, g, c * 128:(c + 1) * 128],
                                                                 rhs=CT[:, g, c * 128:(c + 1) * 128], start=True, stop=True),
                             reads=[R_BT, R_CT], writes=[Rp])
                    P.dve(lambda e, pt=pt: e.tensor_tensor(out=cbm[:], in0=pt[:, 0:256].rearrange("p (g l) -> p g l", g=2),
                                                           in1=U[:].unsqueeze(1).broadcast_to([128, 2, 128]), op=ALU.mult),
                          reads=[Rp, R_U], writes=[R_cbm])
                    for hb in range(4):
                        Ls, RLs = Lseg[hb % 2], R_Lseg[hb % 2]
                        pt, Rp = psum()
                        for i in range(4):
                            h = hb * 4 + i
                            P.pe(lambda e, c=c, h=h, i=i, pt=pt: e.matmul(pt[:, i * 128:(i + 1) * 128],
                                                                          lhsT=dtA[:, c, h:h + 1].broadcast_to([128, 128]), rhs=U[:],
                                                                          start=True, stop=True), reads=[R_dtA, R_U], writes=[Rp])
                        P.dve(lambda e, pt=pt, hb=hb, ac=ac, Ls=Ls: e.tensor_tensor(
                            out=Ls.rearrange("p (h l) -> p h l", h=4), in0=pt[:, 0:512].rearrange("p (h l) -> p h l", h=4),
                            in1=ac[:, 4 * hb:4 * hb + 4].unsqueeze(2).broadcast_to([128, 4, 128]), op=ALU.subtract),
                            reads=[Rp, Rac], writes=[RLs])
                        P.pool(lambda e, Ls=Ls: e.tensor_scalar(out=Ls, in0=Ls, scalar1=0.0, scalar2=None, op0=ALU.min),
                               reads=[RLs], writes=[RLs])
                        P.act(lambda e, Ls=Ls: e.activation(out=Ls, in_=Ls, func=AF.Exp), reads=[RLs], writes=[RLs])
                        g = hb // 2
                        P.pool(lambda e, hb=hb, g=g, Ls=Ls: e.tensor_tensor(
                            out=MT[:, 4 * hb:4 * hb + 4, :], in0=Ls.rearrange("p (h l) -> p h l", h=4),
                            in1=cbm[:, g, :].unsqueeze(1).broadcast_to([128, 4, 128]), op=ALU.mult),
                            reads=[RLs, R_cbm], writes=[R_MT])
                    py = [psum(), psum()]
                    for h in range(16):
                        ptt, Rpp = py[h // 8]
                        P.pe(lambda e, h=h, ptt=ptt: e.matmul(ptt[:, (h % 8) * 64:(h % 8 + 1) * 64], lhsT=MT[:, h, :],
                                                              rhs=xdt[:, h * 64:(h + 1) * 64], start=True, stop=True),
                             reads=[R_MT, R_xdt], writes=[Rpp])
                    po = [psum(), psum()]
                    for g in range(2):
                        ptt, Rpp = po[g]
                        P.pe(lambda e, g=g, c=c, ptt=ptt: e.matmul(ptt[:, 0:512], lhsT=CT[:, g, c * 128:(c + 1) * 128],
                                                                   rhs=Ssdb[:, g * 512:(g + 1) * 512], start=True, stop=True),
                             reads=[R_CT, R_Ssdb], writes=[Rpp])
                    for g in range(2):
                        P.dve(lambda e, g=g, ac=ac, ptt=po[g][0]: e.tensor_tensor(
                            out=t1[:, g * 512:(g + 1) * 512].rearrange("p (h d) -> p h d", h=8),
                            in0=ptt[:, 0:512].rearrange("p (h d) -> p h d", h=8),
                            in1=ac[:, 32 + 8 * g:40 + 8 * g].unsqueeze(2).broadcast_to([128, 8, 64]), op=ALU.mult),
                            reads=[po[g][1], Rac], writes=[R_t1])
                    for g in range(2):
                        P.dve(lambda e, g=g, ptt=py[g][0]: e.tensor_tensor(out=t1[:, g * 512:(g + 1) * 512], in0=ptt[:, 0:512],
                                                                           in1=t1[:, g * 512:(g + 1) * 512], op=ALU.add),
                              reads=[py[g][1], R_t1], writes=[R_t1])
                    P.pool(lambda e, c=c: e.tensor_tensor(out=t3.rearrange("p (h d) -> p h d", h=16),
                                                          in0=xs_tm[:, c, :].rearrange("p (h d) -> p h d", h=16),
                                                          in1=par[:, PC_DSK:PC_DSK + 16].unsqueeze(2).broadcast_to([128, 16, 64]), op=ALU.mult),
                           reads=[R_xs[c], R_par], writes=[R_t3])
                    P.pool(lambda e: e.tensor_tensor(out=t1, in0=t1, in1=t3, op=ALU.add), reads=[R_t1, R_t3], writes=[R_t1])
                    P.dve(lambda e, c=c: e.tensor_tensor(out=t3, in0=t1, in1=zs[:, c, :], op=ALU.mult),
                          reads=[R_t1, R_zs[c], R_t3], writes=[R_t3])
                    P.pool(lambda e: e.memset(sst[:, 0:2], 0.0), writes=[R_sst])
                    for g in range(2):
                        P.act(lambda e, g=g: e.activation(out=junk[:, 0:512], in_=t3[:, g * 512:(g + 1) * 512], func=AF.Square,
                                                          accum_out=sst[:, g:g + 1]), reads=[R_t3, R_sst], writes=[R_junk, R_sst])
                    rstd_from_ss(sst[:, 0:2], 2, R_sst, 1.0 / 512)
                    for g in range(2):
                        P.dve(lambda e, g=g: e.tensor_scalar(out=yn[:, g * 512:(g + 1) * 512], in0=t3[:, g * 512:(g + 1) * 512],
                                                             scalar1=sst[:, g:g + 1], scalar2=None, op0=ALU.mult),
                              reads=[R_t3, R_sst], writes=[R_yn])
                    pt, Rp = psum()
                    pv = bfv(pt)
                    for kt in range(8):
                        P.pe(lambda e, kt=kt, pv=pv: e.transpose(out=pv[:, kt * 128:(kt + 1) * 128], in_=yn[:, kt * 128:(kt + 1) * 128],
                                                                 identity=ident[:]), reads=[R_yn, R_ident], writes=[Rp])
                    P.act(lambda e, c=c, pv=pv: e.activation(out=mixedT[:, 0:8, c * 128:(c + 1) * 128],
                                                             in_=pv.rearrange("p (k t) -> p k t", k=8), func=AF.Copy),
                          reads=[Rp], writes=[R_mixT[c]])
                P.dve(lambda e, ac=ac: e.tensor_tensor(out=xdtd.rearrange("p (h d) -> p h d", h=16),
                                                       in0=xdt.rearrange("p (h d) -> p h d", h=16),
                                                       in1=ac[:, 48:64].unsqueeze(2).broadcast_to([128, 16, 64]), op=ALU.mult),
                      reads=[R_xdt, Rac], writes=[R_xdtd])
                pss = [psum(), psum()]
                for g in range(2):
                    ptt, Rpp = pss[g]
                    P.pe(lambda e, g=g, c=c, ptt=ptt: e.matmul(ptt[:, 0:512], lhsT=B_tm[:, c, g * 128:(g + 1) * 128],
                                                               rhs=xdtd[:, g * 512:(g + 1) * 512], start=True, stop=True),
                         reads=[R_Btm, R_xdtd], writes=[Rpp])
                P.dve(lambda e, ac=ac: e.tensor_tensor(out=Ssd.rearrange("p (h d) -> p h d", h=16),
                                                       in0=Ssd.rearrange("p (h d) -> p h d", h=16),
                                                       in1=ac[:, 64:80].unsqueeze(2).broadcast_to([128, 16, 64]), op=ALU.mult),
                      reads=[R_Ssd, Rac], writes=[R_Ssd])
                for g in range(2):
                    P.dve(lambda e, g=g, ptt=pss[g][0]: e.tensor_tensor(out=Ssd[:, g * 512:(g + 1) * 512], in0=ptt[:, 0:512],
                                                                        in1=Ssd[:, g * 512:(g + 1) * 512], op=ALU.add),
                          reads=[pss[g][1], R_Ssd], writes=[R_Ssd])
                P.act(lambda e: e.activation(out=Ssdb[:], in_=Ssd[:], func=AF.Copy), reads=[R_Ssd], writes=[R_Ssdb])

            for a in range(4):
                wt, Rw = wget("H%d" % a)
                pq = [None, None]
                if not is_pre:
                    for i in range(2):
                        pt, Rp = proj_fm(wt, Rw, i, hT, R_hT)
                        P.act(lambda e, pt=pt, i=i: e.activation(out=qf[:, i, :], in_=pt[:, 0:512], func=AF.Silu), reads=[Rp], writes=[R_qf])
                for i in range(2):
                    h = 2 * a + i
                    pt, Rp = proj_fm(wt, Rw, 2 + i, hT, R_hT)
                    P.act(lambda e, pt=pt, i=i: e.activation(out=gl[:, i, :], in_=pt[:, 0:512], func=AF.Sigmoid), reads=[Rp], writes=[R_gl])
                    P.act(lambda e, pt=pt, i=i: e.activation(out=kf[:, i, :], in_=pt[:, 0:512], func=AF.Sigmoid, scale=-1.0),
                          reads=[Rp], writes=[R_kf])
                    P.dve(lambda e, i=i, h=h: e.tensor_scalar(out=gl[:, i, :], in0=gl[:, i, :], scalar1=oml[:, h:h + 1], scalar2=lb[:, h:h + 1],
                                                              op0=ALU.mult, op1=ALU.add), reads=[R_gl, R_cst], writes=[R_gl])
                    P.act(lambda e, i=i: e.activation(out=gl[:, i, :], in_=gl[:, i, :], func=AF.Ln), reads=[R_gl], writes=[R_gl])
                    if is_pre:
                        P.dve(lambda e, i=i, h=h: e.tensor_scalar(out=kf[:, i, :], in0=kf[:, i, :], scalar1=oml[:, h:h + 1],
                                                                  scalar2=par[:, PC_MASK + pre_idx:PC_MASK + pre_idx + 1],
                                                                  op0=ALU.mult, op1=ALU.mult), reads=[R_kf, R_cst, R_par], writes=[R_kf])
                    else:
                        P.dve(lambda e, i=i, h=h: e.tensor_scalar(out=kf[:, i, :], in0=kf[:, i, :], scalar1=oml[:, h:h + 1], scalar2=None,
                                                                  op0=ALU.mult), reads=[R_kf, R_cst], writes=[R_kf])
                    P.pool(lambda e, i=i: e.memset(bt[i][:, 0:1], 0.0), writes=[R_bt[i]])
                    P.dve(lambda e, i=i: e.tensor_tensor_scan(out=bt[i][:, 1:513], data0=ones[:, 0:512], data1=gl[:, i, :], initial=0.0,
                                                              op0=ALU.mult, op1=ALU.add), reads=[R_ones, R_gl, R_bt[i]], writes=[R_bt[i]])
                for c in range(4):
                    pso = None
                    if not is_pre:
                        pso = psum()
                        P.pool(lambda e: e.memset(sst[:, 8:10], 0.0), writes=[R_sst])
                    for i in range(2):
                        h = 2 * a + i
                        b_ = bt[i]
                        Rb = R_bt[i]
                        c0 = c * 128
                        bseg = b_[:, c0 + 1:c0 + 129]
                        bmid = b_[:, c0 + 64:c0 + 65]
                        blast = b_[:, c0 + 128:c0 + 129]
                        bprev = b_[:, c0:c0 + 1]
                        sc = stat[:, 40 + 8 * i:40 + 8 * i + 8]
                        Rsc = R_scp[i]
                        b31 = b_[:, c0 + 32:c0 + 33]
                        b63 = b_[:, c0 + 64:c0 + 65]
                        b95 = b_[:, c0 + 96:c0 + 97]
                        P.dve(lambda e, sc=sc, b31=b31: e.tensor_scalar(out=sc[:, 0:1], in0=b31, scalar1=-1.0, scalar2=None, op0=ALU.mult),
                              reads=[Rb, Rsc], writes=[Rsc])
                        P.dve(lambda e, sc=sc, b95=b95: e.tensor_scalar(out=sc[:, 1:2], in0=b95, scalar1=-1.0, scalar2=None, op0=ALU.mult),
                              reads=[Rb, Rsc], writes=[Rsc])
                        P.dve(lambda e, sc=sc, bprev=bprev: e.tensor_scalar(out=sc[:, 2:3], in0=bprev, scalar1=-1.0, scalar2=None, op0=ALU.mult),
                              reads=[Rb, Rsc], writes=[Rsc])
                        P.dve(lambda e, sc=sc, bprev=bprev, blast=blast: e.tensor_tensor(out=sc[:, 3:4], in0=blast, in1=bprev, op=ALU.subtract),
                              reads=[Rb, Rsc], writes=[Rsc])
                        P.dve(lambda e, sc=sc, b63=b63, b31=b31: e.tensor_tensor(out=sc[:, 4:5], in0=b63, in1=b31, op=ALU.subtract),
                              reads=[Rb, Rsc], writes=[Rsc])
                        P.dve(lambda e, sc=sc, b63=b63, b95=b95: e.tensor_tensor(out=sc[:, 5:6], in0=b95, in1=b63, op=ALU.subtract),
                              reads=[Rb, Rsc], writes=[Rsc])
                        P.act(lambda e, sc=sc: e.activation(out=sc[:, 3:6], in_=sc[:, 3:6], func=AF.Exp), reads=[Rsc], writes=[Rsc])
                        kfc = kf[:, i, c0:c0 + 128]
                        if not is_pre:
                            qfc = qf[:, i, c0:c0 + 128]
                            e0, e1, e2 = etmp[0], etmp[1], etmp[2]
                            P.act(lambda e, e0=e0, bseg=bseg, sc=sc: e.activation(out=e0[:, 0:64], in_=bseg[:, 0:64], func=AF.Exp, bias=sc[:, 0:1], scale=1.0),
                                  reads=[Rb, Rsc], writes=[R_et[0]])
                            P.act(lambda e, e0=e0, bseg=bseg, sc=sc: e.activation(out=e0[:, 64:128], in_=bseg[:, 64:128], func=AF.Exp, bias=sc[:, 1:2], scale=1.0),
                                  reads=[Rb, Rsc], writes=[R_et[0]])
                            P.dve(lambda e, i=i, e0=e0, qfc=qfc: e.tensor_tensor(out=qt_[i], in0=qfc, in1=e0, op=ALU.mult),
                                  reads=[R_qf, R_et[0]], writes=[R_qt[i]])
                            P.dve(lambda e, i=i, sc=sc: e.tensor_scalar(out=QC[i], in0=qt_[i][:, 64:128], scalar1=sc[:, 5:6], scalar2=None, op0=ALU.mult),
                                  reads=[R_qt[i], Rsc], writes=[R_QC[i]])
                            P.act(lambda e, e1=e1, bseg=bseg, b31=b31: e.activation(out=e1[:, 0:64], in_=bseg[:, 0:64], func=AF.Exp, bias=b31, scale=-1.0),
                                  reads=[Rb], writes=[R_et[1]])
                            P.act(lambda e, e1=e1, bseg=bseg, b95=b95: e.activation(out=e1[:, 64:128], in_=bseg[:, 64:128], func=AF.Exp, bias=b95, scale=-1.0),
                                  reads=[Rb], writes=[R_et[1]])
                            P.dve(lambda e, i=i, e1=e1, kfc=kfc: e.tensor_tensor(out=KA[i][:, 0:64], in0=kfc[:, 0:64], in1=e1[:, 0:64], op=ALU.mult),
                                  reads=[R_kf, R_et[1]], writes=[R_KA[i]])
                            P.dve(lambda e, i=i, e1=e1, kfc=kfc: e.tensor_tensor(out=KB[i][:, 64:128], in0=kfc[:, 64:128], in1=e1[:, 64:128], op=ALU.mult),
                                  reads=[R_kf, R_et[1]], writes=[R_KB[i]])
                            P.dve(lambda e, i=i, sc=sc: e.tensor_scalar(out=KC[i][:, 0:64], in0=KA[i][:, 0:64], scalar1=sc[:, 4:5], scalar2=None, op0=ALU.mult),
                                  reads=[R_KA[i], Rsc], writes=[R_KC[i]])
                            P.act(lambda e, e2=e2, bseg=bseg, sc=sc: e.activation(out=e2, in_=bseg, func=AF.Exp, bias=sc[:, 2:3], scale=1.0),
                                  reads=[Rb, Rsc], writes=[R_et[2]])
                            P.dve(lambda e, i=i, e2=e2, qfc=qfc: e.tensor_tensor(out=qh[i], in0=qfc, in1=e2, op=ALU.mult),
                                  reads=[R_qf, R_et[2]], writes=[R_qh[i]])
                        e3 = etmp[3]
                        P.act(lambda e, e3=e3, bseg=bseg, blast=blast: e.activation(out=e3, in_=bseg, func=AF.Exp, bias=blast, scale=-1.0),
                              reads=[Rb], writes=[R_et[3]])
                        P.dve(lambda e, i=i, e3=e3, kfc=kfc: e.tensor_tensor(out=kh[i], in0=kfc, in1=e3, op=ALU.mult),
                              reads=[R_kf, R_et[3]], writes=[R_kh[i]])
                        vch = vt[:, c, h * 128:(h + 1) * 128]
                        if not is_pre:
                            pt, Rp = psum()
                            P.pe(lambda e, i=i, pt=pt: e.matmul(pt[:, 0:64], lhsT=KA[i], rhs=qt_[i][:, 0:64], start=True, stop=True),
                                 reads=[R_KA[i], R_qt[i]], writes=[Rp])
                            P.pe(lambda e, i=i, pt=pt: e.matmul(pt[:, 64:128], lhsT=KB[i], rhs=qt_[i][:, 64:128], start=True, stop=False),
                                 reads=[R_KB[i], R_qt[i]], writes=[Rp])
                            P.pe(lambda e, i=i, pt=pt: e.matmul(pt[:, 64:128], lhsT=KC[i], rhs=QC[i], start=False, stop=True),
                                 reads=[R_KC[i], R_QC[i]], writes=[Rp])
                            P.dve(lambda e, i=i, pt=pt: e.tensor_tensor(out=attm[i], in0=pt[:, 0:128], in1=U[:], op=ALU.mult),
                                  reads=[Rp, R_U], writes=[R_attm[i]])
                            P.pe(lambda e, i=i, vch=vch, ptt=pso[0]: e.matmul(ptt[:, i * 128:(i + 1) * 128], lhsT=attm[i], rhs=vch,
                                                                              start=True, stop=False),
                                 reads=[R_attm[i], R_vt[c]], writes=[pso[1]])
                            P.pe(lambda e, i=i, h=h, ptt=pso[0]: e.matmul(ptt[:, i * 128:(i + 1) * 128], lhsT=qh[i], rhs=Shgb[:, h, :],
                                                                          start=False, stop=True),
                                 reads=[R_qh[i], R_Shgb[h]], writes=[pso[1]])
                        pt, Rp = psum()
                        pv = bfv(pt)
                        P.pe(lambda e, i=i, pv=pv: e.transpose(out=pv[:, 0:128], in_=kh[i], identity=ident[:]),
                             reads=[R_kh[i], R_ident], writes=[Rp])
                        P.act(lambda e, i=i, pv=pv: e.activation(out=khtm[i], in_=pv[:, 0:128], func=AF.Copy), reads=[Rp], writes=[R_khtm[i]])
                        pt2, Rp2 = psum()
                        P.pe(lambda e, i=i, vch=vch, pt2=pt2: e.matmul(pt2[:, 0:128], lhsT=khtm[i], rhs=vch, start=True, stop=True),
                             reads=[R_khtm[i], R_vt[c]], writes=[Rp2])
                        P.dve(lambda e, h=h, sc=sc, pt2=pt2: e.scalar_tensor_tensor(out=Shg[:, h, :], in0=Shg[:, h, :], scalar=sc[:, 3:4],
                                                                                   in1=pt2[:, 0:128], op0=ALU.mult, op1=ALU.add),
                              reads=[R_Shg[h], Rsc, Rp2], writes=[R_Shg[h]])
                        P.act(lambda e, h=h: e.activation(out=Shgb[:, h, :], in_=Shg[:, h, :], func=AF.Copy), reads=[R_Shg[h]], writes=[R_Shgb[h]])
                    if not is_pre:
                        ptt, Rpp = pso
                        for i in range(2):
                            P.act(lambda e, i=i, ptt=ptt: e.activation(out=junk[:, 0:128], in_=ptt[:, i * 128:(i + 1) * 128], func=AF.Square,
                                                                       accum_out=sst[:, 8 + i:9 + i]), reads=[Rpp, R_sst], writes=[R_junk, R_sst])
                        rstd_from_ss(sst[:, 8:10], 2, R_sst, 1.0 / 128)
                        P.dve(lambda e, ptt=ptt: e.tensor_tensor(out=otmp.rearrange("p (i v) -> p i v", i=2),
                                                                 in0=ptt[:, 0:256].rearrange("p (i v) -> p i v", i=2),
                                                                 in1=sst[:, 8:10].unsqueeze(2).broadcast_to([128, 2, 128]), op=ALU.mult),
                              reads=[Rpp, R_sst], writes=[R_otmp])
                        P.dve(lambda e, a=a, c=c: e.tensor_tensor(out=og, in0=otmp, in1=gs[:, c, 256 * a:256 * a + 256], op=ALU.mult),
                              reads=[R_otmp, R_gs[c]], writes=[R_og])
                        pt, Rp = psum()
                        pv = bfv(pt)
                        for i in range(2):
                            P.pe(lambda e, i=i, pv=pv: e.transpose(out=pv[:, i * 128:(i + 1) * 128], in_=og[:, i * 128:(i + 1) * 128],
                                                                   identity=ident[:]), reads=[R_og, R_ident], writes=[Rp])
                        P.act(lambda e, a=a, c=c, pv=pv: e.activation(out=mixedT[:, 8 + 2 * a:10 + 2 * a, c * 128:(c + 1) * 128],
                                                                      in_=pv[:, 0:256].rearrange("p (k t) -> p k t", k=2), func=AF.Copy),
                              reads=[Rp], writes=[R_mixT[c]])
            P.barrier()
            if is_pre:
                return

            A = Arena()
            qT = A.bf(8 * 512).rearrange("p (k t) -> p k t", k=8); R_qT = Res()
            pe_ = [A.f32(1024).rearrange("p (h k) -> p h k", h=4) for _ in range(2)]; R_pe = [Res(), Res()]
            pn = [A.bf(1024).rearrange("p (h k) -> p h k", h=4) for _ in range(2)]; R_pn = [Res(), Res()]
            prT = A.bf(2 * 4 * 512).rearrange("p (k h t) -> p k h t", k=2, h=4); R_prT = Res()
            oT = A.bf(8 * 512).rearrange("p (k t) -> p k t", k=8); R_oT = Res()
            actT = A.bf(22 * 512).rearrange("p (k t) -> p k t", k=22); R_actT = Res()
            sgt = [A.f32(512) for _ in range(2)]; R_sgt = [Res(), Res()]
            nfw = A.f32(D); R_nfw = Res()
            sa = A.f32(32); R_sa = [Res(), Res()]
            P.dma("sp", "nfw", nfw, nfwd, writes=[R_nfw])

            for j in range(2):
                banks = [psum() for _ in range(4)]
                for i in range(2):
                    wt, Rw = wget("OUT%d%d" % (j, i))
                    for s in range(4):
                        ptt, Rpp = banks[s]
                        for kt in range(8):
                            P.pe(lambda e, kt=kt, s=s, i=i, ptt=ptt, wt=wt: e.matmul(
                                ptt[:, 0:512], lhsT=mixedT[:, 8 * i + kt, s * 128:(s + 1) * 128], rhs=wt[:, kt, :],
                                start=(i == 0 and kt == 0), stop=(i == 1 and kt == 7)), reads=[R_mixT[s], Rw], writes=[Rpp])
                for s in range(4):
                    ptt, Rpp = banks[s]
                    P.dve(lambda e, s=s, j=j, ptt=ptt: e.tensor_tensor(out=x_tm[:, s, j * 512:(j + 1) * 512], in0=ptt[:, 0:512],
                                                                       in1=x_tm[:, s, j * 512:(j + 1) * 512], op=ALU.add),
                          reads=[Rpp, R_x[s]], writes=[R_x[s]])
            rms_T(lambda s: x_tm[:, s, :], R_x, 4, hT, R_hT)
            for jc in range(2):
                wt, Rw = wget("Q%d" % jc)
                for j in range(4):
                    pt, Rp = proj_fm(wt, Rw, j, hT, R_hT)
                    P.act(lambda e, pt=pt, jc=jc, j=j: e.activation(out=qT[:, 4 * jc + j, :], in_=pt[:, 0:512], func=AF.Copy),
                          reads=[Rp], writes=[R_qT])
            for s in range(4):
                b = s % 2
                scb = [psum(), psum()]
                for h in range(4):
                    ptt, Rpp = scb[h // 2]
                    for d2 in range(2):
                        P.pe(lambda e, h=h, d2=d2, s=s, ptt=ptt: e.matmul(ptt[:, (h % 2) * 256:(h % 2) * 256 + 256],
                                                                          lhsT=qT[:, 2 * h + d2, s * 128:(s + 1) * 128], rhs=kmT[:, 2 * h + d2, :],
                                                                          start=(d2 == 0), stop=(d2 == 1)), reads=[R_qT, R_kmT], writes=[Rpp])
                sav = sa[:, 16 * b:16 * b + 16]
                Rsa = R_sa[b]
                for hb in range(2):
                    P.dve(lambda e, hb=hb, sav=sav, ptt=scb[hb][0]: e.tensor_reduce(out=sav[:, 2 * hb:2 * hb + 2],
                                                                                   in_=ptt[:, 0:512].rearrange("p (h k) -> p h k", h=2),
                                                                                   axis=AX.X, op=ALU.max), reads=[scb[hb][1], Rsa], writes=[Rsa])
                P.dve(lambda e, sav=sav: e.tensor_scalar(out=sav[:, 0:4], in0=sav[:, 0:4], scalar1=-1.0 / 16, scalar2=None, op0=ALU.mult),
                      reads=[Rsa], writes=[Rsa])
                P.pool(lambda e, sav=sav: e.memset(sav[:, 4:8], 0.0), reads=[Rsa], writes=[Rsa])
                for h in range(4):
                    ptt, Rpp = scb[h // 2]
                    P.act(lambda e, h=h, b=b, sav=sav, ptt=ptt: e.activation(out=pe_[b][:, h, :], in_=ptt[:, (h % 2) * 256:(h % 2) * 256 + 256],
                                                                             func=AF.Exp, bias=sav[:, h:h + 1], scale=1.0 / 16,
                                                                             accum_out=sav[:, 4 + h:5 + h]), reads=[Rpp, Rsa], writes=[R_pe[b], Rsa])
                P.dve(lambda e, sav=sav: e.reciprocal(out=sav[:, 4:8], in_=sav[:, 4:8]), reads=[Rsa], writes=[Rsa])
                P.dve(lambda e, b=b, sav=sav: e.tensor_tensor(out=pn[b], in0=pe_[b], in1=sav[:, 4:8].unsqueeze(2).broadcast_to([128, 4, 256]),
                                                              op=ALU.mult), reads=[R_pe[b], Rsa], writes=[R_pn[b]])
                pt, Rp = psum()
                pv = bfv(pt)
                for k2 in range(2):
                    for h in range(4):
                        P.pe(lambda e, k2=k2, h=h, b=b, pv=pv: e.transpose(out=pv[:, (k2 * 4 + h) * 128:(k2 * 4 + h + 1) * 128],
                                                                          in_=pn[b][:, h, k2 * 128:(k2 + 1) * 128], identity=ident[:]),
                             reads=[R_pn[b], R_ident], writes=[Rp])
                for k2 in range(2):
                    P.act(lambda e, s=s, k2=k2, pv=pv: e.activation(out=prT[:, k2, :, s * 128:(s + 1) * 128],
                                                                    in_=pv[:, k2 * 512:(k2 + 1) * 512].rearrange("p (h t) -> p h t", h=4),
                                                                    func=AF.Copy), reads=[Rp], writes=[R_prT])
            for h in range(4):
                for d2 in range(2):
                    pt, Rp = psum()
                    for k2 in range(2):
                        P.pe(lambda e, h=h, d2=d2, k2=k2, pt=pt: e.matmul(pt[:, 0:512], lhsT=vm[:, k2, h * 256 + d2 * 128:h * 256 + (d2 + 1) * 128],
                                                                          rhs=prT[:, k2, h, :], start=(k2 == 0), stop=(k2 == 1)),
                             reads=[R_vm, R_prT], writes=[Rp])
                    P.act(lambda e, h=h, d2=d2, pt=pt: e.activation(out=oT[:, 2 * h + d2, :], in_=pt[:, 0:512], func=AF.Copy),
                          reads=[Rp], writes=[R_oT])
            for j in range(2):
                wt, Rw = wget("O%d" % j)
                for s in range(4):
                    pt, Rp = proj_tm(wt, Rw, s, oT, R_oT)
                    P.dve(lambda e, s=s, j=j, pt=pt: e.tensor_tensor(out=x_tm[:, s, j * 512:(j + 1) * 512], in0=pt[:, 0:512],
                                                                     in1=x_tm[:, s, j * 512:(j + 1) * 512], op=ALU.add),
                          reads=[Rp, R_x[s]], writes=[R_x[s]])
            rms_T(lambda s: x_tm[:, s, :], R_x, 4, hT, R_hT)
            for jc in range(11):
                wt, Rw = wget("GU%d" % jc)
                for jj in range(2):
                    pg, Rpg = proj_fm(wt, Rw, jj, hT, R_hT)
                    pu, Rpu = proj_fm(wt, Rw, 2 + jj, hT, R_hT)
                    sg, Rsg = sgt[jj], R_sgt[jj]
                    P.act(lambda e, pg=pg, sg=sg: e.activation(out=sg, in_=pg[:, 0:512], func=AF.Silu), reads=[Rpg], writes=[Rsg])
                    P.dve(lambda e, pu=pu, sg=sg, jc=jc, jj=jj: e.tensor_tensor(out=actT[:, 2 * jc + jj, :], in0=pu[:, 0:512], in1=sg, op=ALU.mult),
                          reads=[Rpu, Rsg], writes=[R_actT])
            for j in range(2):
                banks = [psum() for _ in range(4)]
                for i in range(3):
                    wt, Rw = wget("D%d%d" % (j, i))
                    nk = 8 if i < 2 else 6
                    for s in range(4):
                        ptt, Rpp = banks[s]
                        for kt in range(nk):
                            P.pe(lambda e, kt=kt, s=s, i=i, nk=nk, ptt=ptt, wt=wt: e.matmul(
                                ptt[:, 0:512], lhsT=actT[:, 8 * i + kt, s * 128:(s + 1) * 128], rhs=wt[:, kt, :],
                                start=(i == 0 and kt == 0), stop=(i == 2 and kt == nk - 1)), reads=[R_actT, Rw], writes=[Rpp])
                for s in range(4):
                    ptt, Rpp = banks[s]
                    P.dve(lambda e, s=s, j=j, ptt=ptt: e.tensor_tensor(out=x_tm[:, s, j * 512:(j + 1) * 512], in0=ptt[:, 0:512],
                                                                       in1=x_tm[:, s, j * 512:(j + 1) * 512], op=ALU.add),
                          reads=[Rpp, R_x[s]], writes=[R_x[s]])
            ssf = stat[:, 32:36]
            Rsf = R_ssf
            P.pool(lambda e: e.memset(ssf, 0.0), writes=[Rsf])
            for s in range(4):
                P.act(lambda e, s=s: e.activation(out=junk[:], in_=x_tm[:, s, :], func=AF.Square, accum_out=ssf[:, s:s + 1]),
                      reads=[R_x[s], Rsf], writes=[R_junk, Rsf])
            rstd_from_ss(ssf, 4, Rsf, 1.0 / D)
            for s in range(4):
                b = 0
                P.dve(lambda e, s=s, b=b: e.scalar_tensor_tensor(out=ost[b][:], in0=x_tm[:, s, :], scalar=ssf[:, s:s + 1], in1=nfw,
                                                                 op0=ALU.mult, op1=ALU.mult), reads=[R_x[s], Rsf, R_nfw], writes=[R_ost[b]])
                out_dmas.append(P.dma("sp", "ost%d" % b, outd[out_row0 + s * 128:out_row0 + (s + 1) * 128, :], ost[b][:],
                                      reads=[R_ost[b]]))
            P.barrier()

        for t in range(NPRE):
            do_tile(xp, t * 512, True, t, 0)
        for t in range(NT):
            do_tile(xm, t * 512, False, 0, t * 512)
        assert wstate["got"] == len(wseq), (wstate["got"], len(wseq))
        P.emit(final_wait_ops=out_dmas)
    return nc


def make_par(inp, NPRE, premask):
    f = lambda a: np.asarray(a, dtype=np.float32)
    par = np.zeros((128, PC_MASK + max(NPRE, 1)), np.float32)
    cw = f(inp["conv_w"])[0]
    par[:, PC_CW:PC_CW + 48] = cw.reshape(4, 12, 128).transpose(2, 1, 0).reshape(128, 48)
    par[:, PC_CB:PC_CB + 12] = f(inp["conv_b"])[0].reshape(12, 128).T
    par[:, PC_DTB:PC_DTB + 16] = f(inp["dt_bias"])[0][None, :]
    par[:, PC_ALOG:PC_ALOG + 16] = f(inp["a_log"])[0][None, :]
    par[:, PC_DSK:PC_DSK + 16] = f(inp["d_skip"])[0][None, :]
    hlb = f(inp["hg_lower_bounds"])
    par[:, PC_HLB0:PC_HLB0 + 8] = hlb[0].reshape(8, 128).T
    par[:, PC_HLB1:PC_HLB1 + 8] = hlb[1].reshape(8, 128).T
    par[:, PC_MIX:PC_MIX + 8] = f(inp["norm_mix_w"])[0].reshape(8, 128).T
    par[:, PC_XA:PC_XA + 8] = f(inp["norm_xa_w"])[0].reshape(8, 128).T
    par[:, PC_MEM:PC_MEM + 8] = f(inp["norm_mem_w"])[0].reshape(8, 128).T
    par[:, PC_FFN:PC_FFN + 8] = f(inp["norm_ffn_w"])[0].reshape(8, 128).T
    par[:, PC_WOUT:PC_WOUT + 8] = f(inp["ssd_norm_w"])[0].reshape(8, 128).T
    par[:, PC_WOUT + 8:PC_WOUT + 16] = f(inp["hg_norm_w"])[0][:, None]
    par[:, PC_MASK:PC_MASK + len(premask)] = np.asarray(premask, np.float32)[None, :]
    return par


_NC_CACHE = {}


def run(inp, T, NPRE, nseg):
    x = np.asarray(inp["x"], np.float32)
    mem = np.asarray(inp["mem"], np.float32)
    B, L, _ = x.shape
    assert L == nseg * T
    key = (T, NPRE)
    if key not in _NC_CACHE:
        _NC_CACHE[key] = build(T, NPRE)
    nc = _NC_CACHE[key]
    shared = {
        "w_in": np.ascontiguousarray(inp["w_in"][0], np.float32), "w_out": np.ascontiguousarray(inp["w_out"][0], np.float32),
        "wq": np.ascontiguousarray(inp["xa_wq"][0], np.float32), "wkv": np.ascontiguousarray(inp["xa_wkv"][0], np.float32),
        "wo": np.ascontiguousarray(inp["xa_wo"][0], np.float32), "wg": np.ascontiguousarray(inp["ffn_w_gate"][0], np.float32),
        "wu": np.ascontiguousarray(inp["ffn_w_up"][0], np.float32), "wd": np.ascontiguousarray(inp["ffn_w_down"][0], np.float32),
        "nfw": np.ascontiguousarray(np.broadcast_to(np.asarray(inp["norm_final_w"], np.float32)[None, :], (128, D))),
    }
    in_maps = []
    npre_tok = max(NPRE, 1) * 512
    for b in range(B):
        for sg in range(nseg):
            start = sg * T
            xpre = np.zeros((npre_tok, D), np.float32)
            premask = np.zeros(max(NPRE, 1), np.float32)
            lo = start - NPRE * 512
            for t in range(NPRE):
                p0 = lo + t * 512
                if p0 >= 0:
                    xpre[t * 512:(t + 1) * 512] = x[b, p0:p0 + 512]
                    premask[t] = 1.0
            m = dict(shared)
            m["xm"] = np.ascontiguousarray(x[b, start:start + T])
            m["xp"] = xpre
            m["mem"] = np.ascontiguousarray(mem[b])
            m["par"] = make_par(inp, NPRE, premask)
            in_maps.append(m)
    res = run_bass_kernel_spmd(nc, in_maps, core_ids=list(range(B * nseg)))
    out = np.zeros((B, L, D), np.float32)
    k = 0
    for b in range(B):
        for sg in range(nseg):
            out[b, sg * T:(sg + 1) * T] = res.results[k]["out"]
            k += 1
    return out


def kernel(**inputs):
    return run(inputs, 4096, 24, 4)
```

```python
import contextlib
import numpy as np
import concourse.bass as bass
import concourse.mybir as mybir
from concourse.bass_utils import run_bass_kernel_spmd

F32 = mybir.dt.float32
BF16 = mybir.dt.bfloat16
AF = mybir.ActivationFunctionType
ALU = mybir.AluOpType
AX = mybir.AxisListType

D = 1024
FF = 2816
EPS = 1e-6
S1, S2, S3, S4, S5, S6 = 1024, 2560, 2576, 3600, 4624, 5648
NWS = 3


class Res:
    __slots__ = ("name", "last_w", "readers")

    def __init__(self, name=""):
        self.name = name
        self.last_w = None
        self.readers = {}


class Op:
    __slots__ = ("eng", "fn", "idx", "deps", "dma_key", "dma_cnt", "needs_inc", "inc_val")

    def __init__(self, eng, fn, idx, dma_key=None):
        self.eng = eng
        self.fn = fn
        self.idx = idx
        self.deps = {}
        self.dma_key = dma_key
        self.dma_cnt = 0
        self.needs_inc = False
        self.inc_val = 0


COMPUTE = ("pe", "act", "dve", "pool")


class Prog:
    def __init__(self, nc):
        self.nc = nc
        self.ops = []
        self.dma_counts = {}
        self.last_real = {}

    def add(self, eng, fn, reads=(), writes=(), dma_key=None):
        op = Op(eng, fn, len(self.ops), dma_key)
        if dma_key is not None:
            c = self.dma_counts.get(dma_key, 0) + 1
            self.dma_counts[dma_key] = c
            op.dma_cnt = c
        deps = op.deps
        for r in reads:
            if r.last_w is not None:
                deps.setdefault(r.last_w, set()).add("raw")
        for w in writes:
            lw = w.last_w
            if lw is not None:
                if not (dma_key is not None and lw.dma_key == dma_key):
                    deps.setdefault(lw, set()).add("waw")
            for rd in w.readers.values():
                deps.setdefault(rd, set()).add("war")
        k = ("dma", op.idx) if dma_key is not None else eng
        for r in reads:
            r.readers[k] = op
        for w in writes:
            w.last_w = op
            w.readers = {}
        self.ops.append(op)
        if dma_key is None:
            self.last_real[eng] = op
        return op

    def pe(self, fn, reads=(), writes=()):
        return self.add("pe", fn, reads, writes)

    def act(self, fn, reads=(), writes=()):
        return self.add("act", fn, reads, writes)

    def dve(self, fn, reads=(), writes=()):
        return self.add("dve", fn, reads, writes)

    def pool(self, fn, reads=(), writes=()):
        return self.add("pool", fn, reads, writes)

    def dma(self, queue, key, out, in_, reads=(), writes=()):
        return self.add(queue, lambda e: e.dma_start(out=out, in_=in_), reads, writes, dma_key=key)

    def barrier(self, extra=()):
        lasts = dict(self.last_real)
        for e in COMPUTE:
            op = Op(e, None, len(self.ops))
            for e2, lo in lasts.items():
                if e2 != e:
                    op.deps[lo] = {"raw"}
            for x in extra:
                op.deps[x] = {"raw"}
            self.ops.append(op)

    def emit(self, final_wait_ops=()):
        nc = self.nc
        fin = Op("sp", None, len(self.ops))
        for o in final_wait_ops:
            fin.deps[o] = {"raw"}
        ops = self.ops + [fin]
        for op in ops:
            real = {}
            for d, kinds in op.deps.items():
                if d.dma_key is None and d.eng == op.eng and op.dma_key is None:
                    if op.eng == "pe" or kinds == {"war"}:
                        continue
                real[d] = kinds
            op.deps = real
            for d in real:
                if d.dma_key is None:
                    d.needs_inc = True
        cnt = {e: 0 for e in COMPUTE + ("sp",)}
        for op in ops:
            if op.dma_key is None and op.needs_inc:
                cnt[op.eng] += 1
                op.inc_val = cnt[op.eng]
        dma_keys = sorted(self.dma_counts.keys())
        with contextlib.ExitStack() as st:
            esem = {e: st.enter_context(nc.semaphore("s_" + e)) for e in cnt}
            dsem = {k: st.enter_context(nc.semaphore("d_%d" % i)) for i, k in enumerate(dma_keys)}
            block = st.enter_context(nc.Block())
            engs = {"pe": block.tensor, "act": block.scalar, "dve": block.vector,
                    "pool": block.gpsimd, "sp": block.sync}
            for ename, deco in engs.items():
                my = [o for o in ops if o.eng == ename]
                if not my:
                    continue

                def body(e, my=my, ename=ename):
                    waited = {}
                    for op in my:
                        need = {}
                        for d in op.deps:
                            if d.dma_key is not None:
                                s, v = ("d", d.dma_key), 16 * d.dma_cnt
                            else:
                                s, v = ("e", d.eng), d.inc_val
                            if need.get(s, 0) < v:
                                need[s] = v
                        for s, v in need.items():
                            if waited.get(s, 0) >= v:
                                continue
                            waited[s] = v
                            e.wait_ge(dsem[s[1]] if s[0] == "d" else esem[s[1]], v)
                        if op.fn is None:
                            continue
                        ins = op.fn(e)
                        if op.dma_key is not None:
                            ins.then_inc(dsem[op.dma_key], 16)
                        elif op.needs_inc:
                            ins.then_inc(esem[ename], 1)

                deco(body)


def chunk_catalog():
    cat = []
    for j in range(2):
        cat.append(("Z%d" % j, "w_in", 0, 8, [(0, 512, 512 * j)], "mix"))
    for j in range(3):
        cat.append(("X%d" % j, "w_in", 0, 8, [(0, 512, S1 + 512 * j)], "mix"))
    for a in range(4):
        cat.append(("H%d" % a, "w_in", 0, 8, [(0, 256, S3 + 256 * a), (256, 256, S4 + 256 * a)], "mix"))
    for j in range(2):
        cat.append(("V%d" % j, "w_in", 0, 8, [(0, 512, S5 + 512 * j)], "mix"))
    for j in range(2):
        cat.append(("G%d" % j, "w_in", 0, 8, [(0, 512, S6 + 512 * j)], "mix"))
    for j in range(2):
        for i in range(2):
            cat.append(("OUT%d%d" % (j, i), "w_out", 8 * i, 8, [(0, 512, 512 * j)], "wout%d" % i))
    for j in range(2):
        cat.append(("Q%d" % j, "wq", 0, 8, [(0, 512, 512 * j)], "xa"))
    for j in range(4):
        cat.append(("KV%d" % j, "wkv", 0, 8, [(0, 512, 512 * j)], "mem"))
    for j in range(2):
        cat.append(("O%d" % j, "wo", 0, 8, [(0, 512, 512 * j)], None))
    for j in range(11):
        cat.append(("GU%d" % j, "wgu", 0, 8, [(0, 256, 256 * j), (256, 256, 256 * j)], "ffn"))
    for j in range(2):
        for i in range(3):
            cat.append(("D%d%d" % (j, i), "wd", 8 * i, 8 if i < 2 else 6, [(0, 512, 512 * j)], None))
    return cat


PC_CW, PC_CB, PC_DTB, PC_ALOG, PC_DSK, PC_HLB0, PC_HLB1 = 0, 48, 60, 76, 92, 108, 116
PC_MIX, PC_XA, PC_MEM, PC_FFN, PC_WOUT, PC_MASK = 124, 132, 140, 148, 156, 172


def build(T, NPRE):
    NT = T // 512
    NPAR = PC_MASK + max(NPRE, 1)
    nc = bass.Bass("TRN2", target_bir_lowering=False)
    xm = nc.dram_tensor("xm", [T, D], F32, kind="ExternalInput").ap()
    xp = nc.dram_tensor("xp", [max(NPRE, 1) * 512, D], F32, kind="ExternalInput").ap()
    memd = nc.dram_tensor("mem", [256, D], F32, kind="ExternalInput").ap()
    pard = nc.dram_tensor("par", [128, NPAR], F32, kind="ExternalInput").ap()
    nfwd = nc.dram_tensor("nfw", [128, D], F32, kind="ExternalInput").ap()
    wd_ = {
        "w_in": nc.dram_tensor("w_in", [D, 6672], F32, kind="ExternalInput").ap(),
        "w_out": nc.dram_tensor("w_out", [2048, D], F32, kind="ExternalInput").ap(),
        "wq": nc.dram_tensor("wq", [D, D], F32, kind="ExternalInput").ap(),
        "wkv": nc.dram_tensor("wkv", [D, 2048], F32, kind="ExternalInput").ap(),
        "wo": nc.dram_tensor("wo", [D, D], F32, kind="ExternalInput").ap(),
        "wg": nc.dram_tensor("wg", [D, FF], F32, kind="ExternalInput").ap(),
        "wu": nc.dram_tensor("wu", [D, FF], F32, kind="ExternalInput").ap(),
        "wd": nc.dram_tensor("wd", [FF, D], F32, kind="ExternalInput").ap(),
    }
    outd = nc.dram_tensor("out", [T, D], F32, kind="ExternalOutput").ap()
    cat = chunk_catalog()
    cid = {c[0]: i for i, c in enumerate(cat)}
    wsc = nc.dram_tensor("wsc", [len(cat), 128, 4096], BF16, kind="Internal").ap()
    R_wsc = [Res("wsc%d" % i) for i in range(len(cat))]

    P = Prog(nc)
    with contextlib.ExitStack() as st:
        def sb(name, shape, dt):
            return st.enter_context(nc.sbuf_tensor("sb_" + name, shape, dt))

        par = sb("par", [128, NPAR], F32); R_par = Res()
        ident = sb("ident", [128, 128], BF16); R_ident = Res()
        U = sb("U", [128, 128], F32); R_U = Res()
        ones = sb("ones", [128, 512], F32); R_ones = Res()
        cst = sb("cst", [128, 64], F32); R_cst = Res()
        wdt = sb("wdt", [128, 8, 16], BF16); R_wdt = Res()
        x_tm = sb("x_tm", [128, 4, D], F32); R_x = [Res() for _ in range(4)]
        hT = sb("hT", [128, 8, 512], BF16); R_hT = Res()
        hn = [sb("hn%d" % i, [128, D], BF16) for i in range(2)]; R_hn = [Res(), Res()]
        junk = sb("junk", [128, D], BF16); R_junk = Res()
        wbuf = [sb("wbuf%d" % i, [128, 8, 512], BF16) for i in range(NWS)]; R_wbuf = [Res() for _ in range(NWS)]
        Ssd = sb("Ssd", [128, D], F32); R_Ssd = Res()
        Ssdb = sb("Ssdb", [128, D], BF16); R_Ssdb = Res()
        Shg = sb("Shg", [128, 8, 128], F32); R_Shg = [Res() for _ in range(8)]
        Shgb = sb("Shgb", [128, 8, 128], BF16); R_Shgb = [Res() for _ in range(8)]
        halo = sb("halo", [128, 12, 3], F32); R_halo = Res()
        mixedT = sb("mixedT", [128, 16, 512], BF16); R_mixT = [Res() for _ in range(4)]
        kmT = sb("kmT", [128, 8, 256], BF16); R_kmT = Res()
        vm = sb("vm", [128, 2, D], BF16); R_vm = Res()
        ost = [sb("ost0", [128, D], F32)]; R_ost = [Res()]
        stat = sb("stat", [128, 64], F32)
        ARENA = 27500
        arena = sb("arena", [128, ARENA], F32)
        psb = [st.enter_context(nc.psum_tensor("ps%d" % i, [128, 512], F32)) for i in range(8)]
        R_ps = [Res() for _ in range(8)]
        pctr = [0]

        def psum():
            i = pctr[0] % 8
            pctr[0] += 1
            return psb[i], R_ps[i]

        def bfv(pt):
            return pt[:, 0:512].bitcast(BF16)

        class Arena:
            def __init__(self):
                self.off = 0

            def f32(self, n):
                a = arena[:, self.off:self.off + n]
                self.off += n
                assert self.off <= ARENA, self.off
                return a

            def bf(self, n):
                n32 = (n + 1) // 2
                a = arena[:, self.off:self.off + n32].bitcast(BF16)
                self.off += n32
                assert self.off <= ARENA, self.off
                return a

        d_par = P.dma("sp", "par", par[:], pard, writes=[R_par])
        P.pool(lambda e: e.memset(ones[:], 1.0), writes=[R_ones])
        P.pool(lambda e: e.memset(U[:], 1.0), writes=[R_U])
        P.pool(lambda e: e.affine_select(out=U[:], in_=U[:], pattern=[[1, 128]], compare_op=ALU.is_ge,
                                         fill=0.0, base=0, channel_multiplier=-1), reads=[R_U], writes=[R_U])
        idf = arena[:, 0:128]
        R_idf = Res()
        P.pool(lambda e: e.memset(idf, 0.0), writes=[R_idf])
        P.pool(lambda e: e.affine_select(out=idf, in_=ones[:, 0:128], pattern=[[1, 128]], compare_op=ALU.is_equal,
                                         fill=0.0, base=0, channel_multiplier=-1), reads=[R_ones, R_idf], writes=[R_idf])
        P.dve(lambda e: e.tensor_copy(out=ident[:], in_=idf), reads=[R_idf], writes=[R_ident])
        P.pool(lambda e: e.memset(Ssd[:], 0.0), writes=[R_Ssd])
        P.pool(lambda e: e.memset(Ssdb[:], 0.0), writes=[R_Ssdb])
        P.pool(lambda e: e.memset(Shg[:], 0.0), writes=R_Shg)
        P.pool(lambda e: e.memset(Shgb[:], 0.0), writes=R_Shgb)
        P.pool(lambda e: e.memset(halo[:], 0.0), writes=[R_halo])
        P.pool(lambda e: e.memset(cst[:, 32:40], 1.0), writes=[R_cst])
        P.dve(lambda e: e.tensor_tensor(out=cst[:, 40:48], in0=par[:, PC_HLB0:PC_HLB0 + 8],
                                        in1=par[:, PC_HLB1:PC_HLB1 + 8], op=ALU.subtract), reads=[R_par, R_cst], writes=[R_cst])
        P.act(lambda e: e.activation(out=cst[:, 0:8], in_=cst[:, 40:48], func=AF.Sigmoid), reads=[R_cst], writes=[R_cst])
        P.act(lambda e: e.activation(out=cst[:, 8:16], in_=cst[:, 40:48], func=AF.Sigmoid, scale=-1.0), reads=[R_cst], writes=[R_cst])
        P.act(lambda e: e.activation(out=cst[:, 48:64], in_=par[:, PC_ALOG:PC_ALOG + 16], func=AF.Exp), reads=[R_par, R_cst], writes=[R_cst])
        P.dve(lambda e: e.tensor_scalar(out=cst[:, 16:32], in0=cst[:, 48:64], scalar1=-1.0, scalar2=None, op0=ALU.mult),
              reads=[R_cst], writes=[R_cst])
        lb, oml, aneg, onesb = cst[:, 0:8], cst[:, 8:16], cst[:, 16:32], cst[:, 32:40]
        nhalf = sb("nhalf", [128, 8], F32); R_nhalf = Res()
        P.pool(lambda e: e.memset(nhalf[:], -0.5), writes=[R_nhalf])
        hcst = sb("hcst", [128, 40], F32); R_hcst = Res(); R_hm = Res()
        P.dve(lambda e: e.tensor_scalar(out=hcst[:, 0:8], in0=oml, scalar1=0.5, scalar2=None, op0=ALU.mult), reads=[R_cst], writes=[R_hcst])
        P.dve(lambda e: e.tensor_tensor(out=hcst[:, 8:16], in0=hcst[:, 0:8], in1=lb, op=ALU.add), reads=[R_cst, R_hcst], writes=[R_hcst])
        P.dve(lambda e: e.tensor_scalar(out=hcst[:, 16:24], in0=oml, scalar1=-0.5, scalar2=None, op0=ALU.mult), reads=[R_cst, R_hcst], writes=[R_hcst])

        scale_ap = {"mix": par[:, PC_MIX:PC_MIX + 8], "xa": par[:, PC_XA:PC_XA + 8], "mem": par[:, PC_MEM:PC_MEM + 8],
                    "ffn": par[:, PC_FFN:PC_FFN + 8], "wout0": par[:, PC_WOUT:PC_WOUT + 8],
                    "wout1": par[:, PC_WOUT + 8:PC_WOUT + 16], None: onesb}
        ar = Arena(); ar.off = 128
        NSTG = 3
        stg = [ar.f32(4096).rearrange("p (k n) -> p k n", k=8) for _ in range(NSTG)]
        stgb = [ar.bf(4096).rearrange("p (k n) -> p k n", k=8) for _ in range(NSTG)]
        wdt32 = ar.f32(128).rearrange("p (k n) -> p k n", k=8)
        R_stg = [Res() for _ in range(NSTG)]; R_stgb = [Res() for _ in range(NSTG)]; R_wdt32 = Res()
        pro_dmas = []

        def wsrc(key, kt0, nkt, c0, cw):
            return wd_[key].rearrange("(kt p) n -> p kt n", p=128)[:, kt0:kt0 + nkt, c0:c0 + cw]

        def pro_load(ci):
            name, key, kt0, nkt, pieces, sk = cat[ci]
            sl = ci % NSTG
            for pi, (dc, cw, sc) in enumerate(pieces):
                k2 = key
                if key == "wgu":
                    k2 = "wg" if pi == 0 else "wu"
                P.dma("sp", "stg%d" % sl, stg[sl][:, 0:nkt, dc:dc + cw], wsrc(k2, kt0, nkt, sc, cw), writes=[R_stg[sl]])

        def pro_cast_store(ci):
            name, key, kt0, nkt, pieces, sk = cat[ci]
            sl = ci % NSTG
            sap = scale_ap[sk]
            f = (lambda e, sl=sl, nkt=nkt, sap=sap: e.tensor_tensor(
                out=stgb[sl][:, 0:nkt, :], in0=stg[sl][:, 0:nkt, :],
                in1=sap[:, 0:nkt].unsqueeze(2).broadcast_to([128, nkt, 512]), op=ALU.mult))
            (P.pool if ci % 3 == 2 else P.dve)(f, reads=[R_stg[sl], R_par, R_cst], writes=[R_stgb[sl]])
            pro_dmas.append(P.dma("sp", "wscw%d" % sl, wsc[ci].rearrange("p (k n) -> p k n", k=8)[:, 0:nkt, :],
                                  stgb[sl][:, 0:nkt, :], reads=[R_stgb[sl]], writes=[R_wsc[ci]]))

        pro_load(0)
        pro_load(1)
        for ci in range(len(cat)):
            if ci + 2 < len(cat):
                pro_load(ci + 2)
            pro_cast_store(ci)
        P.dma("sp", "wdt32", wdt32, wsrc("w_in", 0, 8, S2, 16), writes=[R_wdt32])
        P.dve(lambda e: e.tensor_tensor(out=wdt[:], in0=wdt32, in1=par[:, PC_MIX:PC_MIX + 8].unsqueeze(2).broadcast_to([128, 8, 16]),
                                        op=ALU.mult), reads=[R_wdt32, R_par], writes=[R_wdt])

        pre_seq = ["X0", "V0", "X1", "V1", "X2", "H0", "H1", "H2", "H3"]
        main_seq = (["X0", "Z0", "X1", "Z1", "X2", "V0", "V1", "G0", "G1", "H0", "H1", "H2", "H3",
                     "OUT00", "OUT01", "OUT10", "OUT11", "Q0", "Q1", "O0", "O1"]
                    + ["GU%d" % j for j in range(11)] + ["D00", "D01", "D02", "D10", "D11", "D12"])
        wseq = ["KV0", "KV1", "KV2", "KV3"] + pre_seq * NPRE + main_seq * NT
        wstate = {"issued": 0, "got": 0}

        def wissue():
            i = wstate["issued"]
            c = cid[wseq[i]]
            sl = i % NWS
            P.dma("sp", "wb%d" % sl, wbuf[sl][:], wsc[c].rearrange("p (k n) -> p k n", k=8),
                  reads=[R_wsc[c]], writes=[R_wbuf[sl]])
            wstate["issued"] += 1

        def wget(name):
            i = wstate["got"]
            assert wseq[i] == name, (wseq[i], name)
            while wstate["issued"] < min(len(wseq), i + NWS):
                wissue()
            wstate["got"] += 1
            return wbuf[i % NWS], R_wbuf[i % NWS]

        def rstd_from_ss(ssv, n, Rs, inv_n):
            P.pool(lambda e: e.tensor_scalar(out=ssv, in0=ssv, scalar1=inv_n, scalar2=EPS, op0=ALU.mult, op1=ALU.add),
                   reads=[Rs], writes=[Rs])
            P.pool(lambda e: e.tensor_tensor(out=ssv, in0=ssv, in1=nhalf[:, 0:n], op=ALU.pow), reads=[Rs, R_nhalf], writes=[Rs])

        rms_ctr = [0]
        R_rms = [Res(), Res()]
        R_ssf = Res()
        R_scp = [Res(), Res()]
        R_prekh = [Res(), Res()]
        R_prekhtm = [Res(), Res()]

        def rms_T(src, Rsrc, nsub, dstT, R_dst):
            k = rms_ctr[0] % 2
            rms_ctr[0] += 1
            ss = stat[:, 8 * k:8 * k + nsub]
            Rss = R_rms[k]
            P.pool(lambda e: e.memset(ss, 0.0), writes=[Rss])
            for s in range(nsub):
                P.act(lambda e, s=s: e.activation(out=junk[:], in_=src(s), func=AF.Square, accum_out=ss[:, s:s + 1]),
                      reads=[Rsrc[s], Rss], writes=[R_junk, Rss])
            rstd_from_ss(ss, nsub, Rss, 1.0 / D)
            for s in range(nsub):
                b = s % 2
                P.dve(lambda e, s=s, b=b: e.tensor_scalar(out=hn[b][:], in0=src(s), scalar1=ss[:, s:s + 1], scalar2=None,
                                                          op0=ALU.mult), reads=[Rsrc[s], Rss], writes=[R_hn[b]])
                pt, Rp = psum()
                pv = bfv(pt)
                for kt in range(8):
                    P.pe(lambda e, kt=kt, b=b, pv=pv: e.transpose(out=pv[:, kt * 128:(kt + 1) * 128],
                                                                  in_=hn[b][:, kt * 128:(kt + 1) * 128], identity=ident[:]),
                         reads=[R_hn[b], R_ident], writes=[Rp])
                P.act(lambda e, s=s, pv=pv: e.activation(out=dstT[:, :, s * 128:(s + 1) * 128],
                                                         in_=pv.rearrange("p (k t) -> p k t", k=8), func=AF.Copy),
                      reads=[Rp], writes=[R_dst])

        def proj_fm(wt, Rw, j, xT, RxT, ncols=512):
            pt, Rp = psum()
            for kt in range(8):
                P.pe(lambda e, kt=kt, pt=pt: e.matmul(pt[:, 0:ncols], lhsT=wt[:, kt, j * 128:(j + 1) * 128], rhs=xT[:, kt, 0:ncols],
                                                      start=(kt == 0), stop=(kt == 7)), reads=[Rw, RxT], writes=[Rp])
            return pt, Rp

        def proj_tm(wt, Rw, s, xT, RxT, ncols=512):
            pt, Rp = psum()
            for kt in range(8):
                P.pe(lambda e, kt=kt, pt=pt: e.matmul(pt[:, 0:ncols], lhsT=xT[:, kt, s * 128:(s + 1) * 128], rhs=wt[:, kt, 0:ncols],
                                                      start=(kt == 0), stop=(kt == 7)), reads=[Rw, RxT], writes=[Rp])
            return pt, Rp

        mem_t = ar.f32(2 * D).rearrange("p (s d) -> p s d", s=2); R_mem = [Res(), Res()]
        mT = ar.bf(8 * 256).rearrange("p (k t) -> p k t", k=8); R_mT = Res()
        for s in range(2):
            P.dma("sp", "mem%d" % s, mem_t[:, s, :], memd[s * 128:(s + 1) * 128, :], writes=[R_mem[s]])
        rms_T(lambda s: mem_t[:, s, :], R_mem, 2, mT, R_mT)
        for jc in range(2):
            wt, Rw = wget("KV%d" % jc)
            for j in range(4):
                pt, Rp = proj_fm(wt, Rw, j, mT, R_mT, ncols=256)
                P.act(lambda e, pt=pt, jc=jc, j=j: e.activation(out=kmT[:, 4 * jc + j, :], in_=pt[:, 0:256], func=AF.Copy),
                      reads=[Rp], writes=[R_kmT])
        for jc in range(2):
            wt, Rw = wget("KV%d" % (2 + jc))
            for s in range(2):
                pt, Rp = proj_tm(wt, Rw, s, mT, R_mT)
                P.act(lambda e, pt=pt, jc=jc, s=s: e.activation(out=vm[:, s, jc * 512:(jc + 1) * 512], in_=pt[:, 0:512], func=AF.Copy),
                      reads=[Rp], writes=[R_vm])
        P.barrier(extra=pro_dmas[-3:])

        out_dmas = []
        tile_ctr = [0]
        mres_store = []

        def mk_mres():
            idx = [0]

            def mres():
                i = idx[0]
                idx[0] += 1
                if i >= len(mres_store):
                    mres_store.append(Res())
                return mres_store[i]
            return mres

        def do_tile(xsrc_d, row0, is_pre, pre_idx, out_row0):
            ti = tile_ctr[0]
            tile_ctr[0] += 1
            A = Arena()
            mres = mk_mres()
            raw = A.f32(4 * 515).rearrange("p (j t) -> p j t", j=4); R_raw = mres()
            cacc = [A.f32(512) for _ in range(2)]; R_cacc = [mres(), mres()]
            xsT = A.bf(8 * 512).rearrange("p (k t) -> p k t", k=8); R_xsT = mres()
            BT = A.bf(2 * 512).rearrange("p (k t) -> p k t", k=2); R_BT = mres()
            CT = A.bf(2 * 512).rearrange("p (k t) -> p k t", k=2); R_CT = mres()
            xs_tm = A.bf(4 * D).rearrange("p (s d) -> p s d", s=4); R_xs = [mres() for _ in range(4)]
            B_tm = A.bf(4 * 256).rearrange("p (s d) -> p s d", s=4); R_Btm = mres()
            zs = A.bf(4 * D).rearrange("p (s d) -> p s d", s=4); R_zs = [mres() for _ in range(4)]
            vt = A.bf(4 * D).rearrange("p (s d) -> p s d", s=4); R_vt = [mres() for _ in range(4)]
            gs = A.bf(4 * D).rearrange("p (s d) -> p s d", s=4); R_gs = [mres() for _ in range(4)]
            dtr = A.f32(64).rearrange("p (s h) -> p s h", s=4); R_dtr = mres()
            dtA = A.f32(64).rearrange("p (s h) -> p s h", s=4); R_dtA = mres()
            acs = [A.f32(96) for _ in range(2)]; R_acs = [mres(), mres()]
            Lseg = [A.f32(512) for _ in range(2)]; R_Lseg = [mres(), mres()]
            MT = A.bf(16 * 128).rearrange("p (h l) -> p h l", h=16); R_MT = mres()
            cbm = A.f32(256).rearrange("p (g l) -> p g l", g=2); R_cbm = mres()
            xdt = A.bf(D); R_xdt = mres()
            xdtd = A.bf(D); R_xdtd = mres()
            t1 = A.f32(D); R_t1 = mres()
            t3 = A.f32(D); R_t3 = mres()
            yn = A.bf(D); R_yn = mres()
            qf = A.f32(1024).rearrange("p (i t) -> p i t", i=2); R_qf = mres()
            gl = A.f32(1024).rearrange("p (i t) -> p i t", i=2); R_gl = mres()
            kf = A.f32(1024).rearrange("p (i t) -> p i t", i=2); R_kf = mres()
            bt = [A.f32(513) for _ in range(2)]; R_bt = [mres(), mres()]
            etmp = [A.f32(128) for _ in range(4)]; R_et = [mres() for _ in range(4)]
            qt_ = [A.bf(128) for _ in range(2)]; R_qt = [mres(), mres()]
            KA = [A.bf(128) for _ in range(2)]; R_KA = [mres(), mres()]
            KB = [A.bf(128) for _ in range(2)]; R_KB = [mres(), mres()]
            KC = [A.bf(128) for _ in range(2)]; R_KC = [mres(), mres()]
            QC = [A.bf(64) for _ in range(2)]; R_QC = [mres(), mres()]
            if not is_pre:
                for i in range(2):
                    P.pool(lambda e, i=i: e.memset(KA[i], 0.0), writes=[R_KA[i]])
                    P.pool(lambda e, i=i: e.memset(KB[i], 0.0), writes=[R_KB[i]])
                    P.pool(lambda e, i=i: e.memset(KC[i], 0.0), writes=[R_KC[i]])
            qh = [A.bf(128) for _ in range(2)]; R_qh = [mres(), mres()]
            kh = [A.bf(128) for _ in range(2)]; R_kh = [mres(), mres()]
            khtm = [A.bf(128) for _ in range(2)]; R_khtm = [mres(), mres()]
            attm = [A.bf(128) for _ in range(2)]; R_attm = [mres(), mres()]
            otmp = A.f32(256); R_otmp = mres()
            og = A.bf(256); R_og = mres()
            sst = A.f32(32); R_sst = mres()

            qfb = qf.rearrange("p i t -> p (i t)").bitcast(BF16)
            pre_kh = [qfb[:, 0:512], qfb[:, 512:1024]]
            pre_khtm = [qfb[:, 1024:1536], qfb[:, 1536:2048]]

            for s in range(4):
                P.dma("sp", "x%d" % s, x_tm[:, s, :], xsrc_d[row0 + s * 128:row0 + (s + 1) * 128, :], writes=[R_x[s]])
            rms_T(lambda s: x_tm[:, s, :], R_x, 4, hT, R_hT)

            def tm_chunk(nm, jc, dst, Rdst, func):
                wt, Rw = wget("%s%d" % (nm, jc))
                for s in range(4):
                    pt, Rp = proj_tm(wt, Rw, s, hT, R_hT)
                    P.act(lambda e, pt=pt, s=s, jc=jc: e.activation(out=dst[:, s, jc * 512:(jc + 1) * 512], in_=pt[:, 0:512], func=func),
                          reads=[Rp], writes=[Rdst[s]])
            if is_pre:
                tm_list = [("V", 0, vt, R_vt, AF.Copy), ("V", 1, vt, R_vt, AF.Copy)]
            else:
                tm_list = [("Z", 0, zs, R_zs, AF.Silu), ("Z", 1, zs, R_zs, AF.Silu), ("V", 0, vt, R_vt, AF.Copy),
                           ("V", 1, vt, R_vt, AF.Copy), ("G", 0, gs, R_gs, AF.Silu), ("G", 1, gs, R_gs, AF.Silu)]
            for c3 in range(3):
                wt, Rw = wget("X%d" % c3)
                P.pool(lambda e, c3=c3: e.tensor_copy(out=raw[:, :, 0:3], in_=halo[:, 4 * c3:4 * c3 + 4, :]),
                       reads=[R_halo], writes=[R_raw])
                for j in range(4):
                    pt, Rp = proj_fm(wt, Rw, j, hT, R_hT)
                    P.act(lambda e, pt=pt, j=j: e.activation(out=raw[:, j, 3:515], in_=pt[:, 0:512], func=AF.Copy),
                          reads=[Rp], writes=[R_raw])
                P.pool(lambda e, c3=c3: e.tensor_copy(out=halo[:, 4 * c3:4 * c3 + 4, :], in_=raw[:, :, 512:515]),
                       reads=[R_raw], writes=[R_halo])
                if tm_list:
                    tm_chunk(*tm_list.pop(0))
                for j in range(4):
                    ct = 4 * c3 + j
                    ca, Rca = cacc[j % 2], R_cacc[j % 2]
                    P.dve(lambda e, j=j, ct=ct, ca=ca: e.tensor_scalar(
                        out=ca, in0=raw[:, j, 0:512], scalar1=par[:, PC_CW + 4 * ct:PC_CW + 4 * ct + 1],
                        scalar2=par[:, PC_CB + ct:PC_CB + ct + 1], op0=ALU.mult, op1=ALU.add), reads=[R_raw, R_par], writes=[Rca])
                    for k in range(1, 4):
                        P.dve(lambda e, j=j, ct=ct, k=k, ca=ca: e.scalar_tensor_tensor(
                            out=ca, in0=raw[:, j, k:k + 512], scalar=par[:, PC_CW + 4 * ct + k:PC_CW + 4 * ct + k + 1],
                            in1=ca, op0=ALU.mult, op1=ALU.add), reads=[R_raw, R_par, Rca], writes=[Rca])
                    if ct < 8:
                        dst, Rd = xsT[:, ct, :], R_xsT
                    elif ct < 10:
                        dst, Rd = BT[:, ct - 8, :], R_BT
                    else:
                        dst, Rd = CT[:, ct - 10, :], R_CT
                    P.act(lambda e, ca=ca, dst=dst: e.activation(out=dst, in_=ca, func=AF.Silu), reads=[Rca], writes=[Rd])
            while tm_list:
                tm_chunk(*tm_list.pop(0))
            for s in range(4):
                pt, Rp = psum()
                pv = bfv(pt)
                for kt in range(8):
                    P.pe(lambda e, kt=kt, s=s, pv=pv: e.transpose(out=pv[:, kt * 128:(kt + 1) * 128],
                                                                  in_=xsT[:, kt, s * 128:(s + 1) * 128], identity=ident[:]),
                         reads=[R_xsT, R_ident], writes=[Rp])
                P.act(lambda e, s=s, pv=pv: e.activation(out=xs_tm[:, s, :], in_=pv, func=AF.Copy), reads=[Rp], writes=[R_xs[s]])
            pt, Rp = psum()
            pv = bfv(pt)
            for s in range(4):
                for g in range(2):
                    P.pe(lambda e, s=s, g=g, pv=pv: e.transpose(out=pv[:, s * 256 + g * 128:s * 256 + (g + 1) * 128],
                                                                in_=BT[:, g, s * 128:(s + 1) * 128], identity=ident[:]),
                         reads=[R_BT, R_ident], writes=[Rp])
            P.act(lambda e, pv=pv: e.activation(out=B_tm[:], in_=pv.rearrange("p (s d) -> p s d", s=4), func=AF.Copy),
                  reads=[Rp], writes=[R_Btm])

            pt, Rp = psum()
            for s in range(4):
                for kt in range(8):
                    P.pe(lambda e, s=s, kt=kt, pt=pt: e.matmul(pt[:, s * 16:(s + 1) * 16], lhsT=hT[:, kt, s * 128:(s + 1) * 128],
                                                               rhs=wdt[:, kt, :], start=(kt == 0), stop=(kt == 7)),
                         reads=[R_hT, R_wdt], writes=[Rp])
            P.dve(lambda e, pt=pt: e.tensor_tensor(out=dtr[:], in0=pt[:, 0:64].rearrange("p (s h) -> p s h", s=4),
                                                   in1=par[:, PC_DTB:PC_DTB + 16].unsqueeze(1).broadcast_to([128, 4, 16]), op=ALU.add),
                  reads=[Rp, R_par], writes=[R_dtr])
            P.act(lambda e: e.activation(out=dtr[:], in_=dtr[:], func=AF.Exp), reads=[R_dtr], writes=[R_dtr])
            P.act(lambda e: e.activation(out=dtr[:], in_=dtr[:], func=AF.Ln, bias=1.0), reads=[R_dtr], writes=[R_dtr])
            if is_pre:
                P.dve(lambda e: e.tensor_scalar(out=dtr[:], in0=dtr[:], scalar1=par[:, PC_MASK + pre_idx:PC_MASK + pre_idx + 1],
                                                scalar2=None, op0=ALU.mult), reads=[R_dtr, R_par], writes=[R_dtr])
            P.dve(lambda e: e.tensor_tensor(out=dtA[:], in0=dtr[:], in1=aneg.unsqueeze(1).broadcast_to([128, 4, 16]), op=ALU.mult),
                  reads=[R_dtr, R_cst], writes=[R_dtA])

            if is_pre:
                ac = acs[0]; Rac = R_acs[0]
                pa, Rpa = psum()
                for j in range(4):
                    P.pe(lambda e, j=j, pa=pa: e.matmul(pa[:, j * 16:(j + 1) * 16], lhsT=U[:], rhs=dtA[:, j, :], start=True, stop=True),
                         reads=[R_U, R_dtA], writes=[Rpa])
                    P.pe(lambda e, j=j, pa=pa: e.matmul(pa[:, 64 + j * 16:64 + (j + 1) * 16], lhsT=ones[:, 0:128], rhs=dtA[:, j, :],
                                                        start=True, stop=True), reads=[R_ones, R_dtA], writes=[Rpa])
                suf = acs[1]; Rsuf = R_acs[1]
                P.dve(lambda e, pa=pa: e.tensor_copy(out=suf[:, 0:64], in_=pa[:, 64:128]), reads=[Rpa], writes=[Rsuf])
                for j in (2, 1, 0):
                    P.dve(lambda e, j=j: e.tensor_tensor(out=suf[:, j * 16:(j + 1) * 16], in0=suf[:, j * 16:(j + 1) * 16],
                                                         in1=suf[:, (j + 1) * 16:(j + 2) * 16], op=ALU.add), reads=[Rsuf], writes=[Rsuf])
                P.dve(lambda e, pa=pa: e.tensor_tensor(out=ac[:, 0:64], in0=suf[:, 0:64], in1=pa[:, 0:64], op=ALU.subtract),
                      reads=[Rpa, Rsuf], writes=[Rac])
                P.act(lambda e: e.activation(out=ac[:, 0:64], in_=ac[:, 0:64], func=AF.Exp), reads=[Rac], writes=[Rac])
                P.act(lambda e: e.activation(out=ac[:, 80:96], in_=suf[:, 0:16], func=AF.Exp), reads=[Rsuf, Rac], writes=[Rac])
                P.dve(lambda e: e.tensor_tensor(out=ac[:, 0:64], in0=ac[:, 0:64], in1=dtr[:].rearrange("p s h -> p (s h)"), op=ALU.mult),
                      reads=[Rac, R_dtr], writes=[Rac])
                for j in range(4):
                    P.dve(lambda e, j=j: e.tensor_tensor(out=zs[:, j, :].rearrange("p (h d) -> p h d", h=16),
                                                         in0=xs_tm[:, j, :].rearrange("p (h d) -> p h d", h=16),
                                                         in1=ac[:, j * 16:(j + 1) * 16].unsqueeze(2).broadcast_to([128, 16, 64]), op=ALU.mult),
                          reads=[R_xs[j], Rac], writes=[R_zs[j]])
                pss = [psum(), psum()]
                for g in range(2):
                    ptt, Rpp = pss[g]
                    for j in range(4):
                        P.pe(lambda e, g=g, j=j, ptt=ptt: e.matmul(ptt[:, 0:512], lhsT=B_tm[:, j, g * 128:(g + 1) * 128],
                                                                   rhs=zs[:, j, g * 512:(g + 1) * 512], start=(j == 0), stop=(j == 3)),
                             reads=[R_Btm, R_zs[j]], writes=[Rpp])
                P.dve(lambda e: e.tensor_tensor(out=Ssd.rearrange("p (h d) -> p h d", h=16),
                                                in0=Ssd.rearrange("p (h d) -> p h d", h=16),
                                                in1=ac[:, 80:96].unsqueeze(2).broadcast_to([128, 16, 64]), op=ALU.mult),
                      reads=[R_Ssd, Rac], writes=[R_Ssd])
                for g in range(2):
                    P.dve(lambda e, g=g, ptt=pss[g][0]: e.tensor_tensor(out=Ssd[:, g * 512:(g + 1) * 512], in0=ptt[:, 0:512],
                                                                        in1=Ssd[:, g * 512:(g + 1) * 512], op=ALU.add),
                          reads=[pss[g][1], R_Ssd], writes=[R_Ssd])
                P.act(lambda e: e.activation(out=Ssdb[:], in_=Ssd[:], func=AF.Copy), reads=[R_Ssd], writes=[R_Ssdb])
            for c in (range(0) if is_pre else range(4)):
                ac, Rac = acs[c % 2], R_acs[c % 2]
                pt, Rp = psum()
                P.pe(lambda e, c=c, pt=pt: e.matmul(pt[:, 0:16], lhsT=U[:], rhs=dtA[:, c, :], start=True, stop=True),
                     reads=[R_U, R_dtA], writes=[Rp])
                P.pe(lambda e, c=c, pt=pt: e.matmul(pt[:, 16:32], lhsT=ones[:, 0:128], rhs=dtA[:, c, :], start=True, stop=True),
                     reads=[R_ones, R_dtA], writes=[Rp])
                P.dve(lambda e, pt=pt, ac=ac: e.tensor_copy(out=ac[:, 0:32], in_=pt[:, 0:32]), reads=[Rp], writes=[Rac])
                P.dve(lambda e, ac=ac: e.tensor_tensor(out=ac[:, 48:64], in0=ac[:, 16:32], in1=ac[:, 0:16], op=ALU.subtract),
                      reads=[Rac], writes=[Rac])
                P.act(lambda e, ac=ac: e.activation(out=ac[:, 32:48], in_=ac[:, 0:16], func=AF.Exp), reads=[Rac], writes=[Rac])
                P.act(lambda e, ac=ac: e.activation(out=ac[:, 48:64], in_=ac[:, 48:64], func=AF.Exp), reads=[Rac], writes=[Rac])
                P.act(lambda e, ac=ac: e.activation(out=ac[:, 64:80], in_=ac[:, 16:32], func=AF.Exp), reads=[Rac], writes=[Rac])
                P.dve(lambda e, c=c: e.tensor_tensor(out=xdt.rearrange("p (h d) -> p h d", h=16),
                                                     in0=xs_tm[:, c, :].rearrange("p (h d) -> p h d", h=16),
                                                     in1=dtr[:, c, :].unsqueeze(2).broadcast_to([128, 16, 64]), op=ALU.mult),
                      reads=[R_xs[c], R_dtr], writes=[R_xdt])
                if not is_pre:
                    pt, Rp = psum()
                    for g in range(2):
                        P.pe(lambda e, c=c, g=g, pt=pt: e.matmul(pt[:, g * 128:(g + 1) * 128], lhsT=BT[:, g, c * 128:(c + 1) * 128],
                                                                 rhs=CT[:, g, c * 128:(c + 1) * 128], start=True, stop=True),
                             reads=[R_BT, R_CT], writes=[Rp])
                    P.dve(lambda e, pt=pt: e.tensor_tensor(out=cbm[:], in0=pt[:, 0:256].rearrange("p (g l) -> p g l", g=2),
                                                           in1=U[:].unsqueeze(1).broadcast_to([128, 2, 128]), op=ALU.mult),
                          reads=[Rp, R_U], writes=[R_cbm])
                    for hb in range(4):
                        Ls, RLs = Lseg[hb % 2], R_Lseg[hb % 2]
                        pt, Rp = psum()
                        for i in range(4):
                            h = hb * 4 + i
                            P.pe(lambda e, c=c, h=h, i=i, pt=pt: e.matmul(pt[:, i * 128:(i + 1) * 128],
                                                                          lhsT=dtA[:, c, h:h + 1].broadcast_to([128, 128]), rhs=U[:],
                                                                          start=True, stop=True), reads=[R_dtA, R_U], writes=[Rp])
                        P.dve(lambda e, pt=pt, hb=hb, ac=ac, Ls=Ls: e.tensor_tensor(
                            out=Ls.rearrange("p (h l) -> p h l", h=4), in0=pt[:, 0:512].rearrange("p (h l) -> p h l", h=4),
                            in1=ac[:, 4 * hb:4 * hb + 4].unsqueeze(2).broadcast_to([128, 4, 128]), op=ALU.subtract),
                            reads=[Rp, Rac], writes=[RLs])
                        P.dve(lambda e, Ls=Ls: e.tensor_scalar(out=Ls, in0=Ls, scalar1=0.0, scalar2=None, op0=ALU.min),
                              reads=[RLs], writes=[RLs])
                        P.act(lambda e, Ls=Ls: e.activation(out=Ls, in_=Ls, func=AF.Exp), reads=[RLs], writes=[RLs])
                        g = hb // 2
                        P.pool(lambda e, hb=hb, g=g, Ls=Ls: e.tensor_tensor(
                            out=MT[:, 4 * hb:4 * hb + 4, :], in0=Ls.rearrange("p (h l) -> p h l", h=4),
                            in1=cbm[:, g, :].unsqueeze(1).broadcast_to([128, 4, 128]), op=ALU.mult),
                            reads=[RLs, R_cbm], writes=[R_MT])
                    py = [psum(), psum()]
                    for h in range(16):
                        ptt, Rpp = py[h // 8]
                        P.pe(lambda e, h=h, ptt=ptt: e.matmul(ptt[:, (h % 8) * 64:(h % 8 + 1) * 64], lhsT=MT[:, h, :],
                                                              rhs=xdt[:, h * 64:(h + 1) * 64], start=True, stop=True),
                             reads=[R_MT, R_xdt], writes=[Rpp])
                    po = [psum(), psum()]
                    for g in range(2):
                        ptt, Rpp = po[g]
                        P.pe(lambda e, g=g, c=c, ptt=ptt: e.matmul(ptt[:, 0:512], lhsT=CT[:, g, c * 128:(c + 1) * 128],
                                                                   rhs=Ssdb[:, g * 512:(g + 1) * 512], start=True, stop=True),
                             reads=[R_CT, R_Ssdb], writes=[Rpp])
                    for g in range(2):
                        P.dve(lambda e, g=g, ac=ac, ptt=po[g][0]: e.tensor_tensor(
                            out=t1[:, g * 512:(g + 1) * 512].rearrange("p (h d) -> p h d", h=8),
                            in0=ptt[:, 0:512].rearrange("p (h d) -> p h d", h=8),
                            in1=ac[:, 32 + 8 * g:40 + 8 * g].unsqueeze(2).broadcast_to([128, 8, 64]), op=ALU.mult),
                            reads=[po[g][1], Rac], writes=[R_t1])
                    for g in range(2):
                        P.dve(lambda e, g=g, ptt=py[g][0]: e.tensor_tensor(out=t1[:, g * 512:(g + 1) * 512], in0=ptt[:, 0:512],
                                                                           in1=t1[:, g * 512:(g + 1) * 512], op=ALU.add),
                              reads=[py[g][1], R_t1], writes=[R_t1])
                    P.pool(lambda e, c=c: e.tensor_tensor(out=t3.rearrange("p (h d) -> p h d", h=16),
                                                          in0=xs_tm[:, c, :].rearrange("p (h d) -> p h d", h=16),
                                                          in1=par[:, PC_DSK:PC_DSK + 16].unsqueeze(2).broadcast_to([128, 16, 64]), op=ALU.mult),
                           reads=[R_xs[c], R_par], writes=[R_t3])
                    P.pool(lambda e: e.tensor_tensor(out=t1, in0=t1, in1=t3, op=ALU.add), reads=[R_t1, R_t3], writes=[R_t1])
                    P.dve(lambda e, c=c: e.tensor_tensor(out=t3, in0=t1, in1=zs[:, c, :], op=ALU.mult),
                          reads=[R_t1, R_zs[c], R_t3], writes=[R_t3])
                    P.pool(lambda e: e.memset(sst[:, 0:2], 0.0), writes=[R_sst])
                    for g in range(2):
                        P.act(lambda e, g=g: e.activation(out=junk[:, 0:512], in_=t3[:, g * 512:(g + 1) * 512], func=AF.Square,
                                                          accum_out=sst[:, g:g + 1]), reads=[R_t3, R_sst], writes=[R_junk, R_sst])
                    rstd_from_ss(sst[:, 0:2], 2, R_sst, 1.0 / 512)
                    for g in range(2):
                        P.dve(lambda e, g=g: e.tensor_scalar(out=yn[:, g * 512:(g + 1) * 512], in0=t3[:, g * 512:(g + 1) * 512],
                                                             scalar1=sst[:, g:g + 1], scalar2=None, op0=ALU.mult),
                              reads=[R_t3, R_sst], writes=[R_yn])
                    pt, Rp = psum()
                    pv = bfv(pt)
                    for kt in range(8):
                        P.pe(lambda e, kt=kt, pv=pv: e.transpose(out=pv[:, kt * 128:(kt + 1) * 128], in_=yn[:, kt * 128:(kt + 1) * 128],
                                                                 identity=ident[:]), reads=[R_yn, R_ident], writes=[Rp])
                    P.act(lambda e, c=c, pv=pv: e.activation(out=mixedT[:, 0:8, c * 128:(c + 1) * 128],
                                                             in_=pv.rearrange("p (k t) -> p k t", k=8), func=AF.Copy),
                          reads=[Rp], writes=[R_mixT[c]])
                P.dve(lambda e, ac=ac: e.tensor_tensor(out=xdtd.rearrange("p (h d) -> p h d", h=16),
                                                       in0=xdt.rearrange("p (h d) -> p h d", h=16),
                                                       in1=ac[:, 48:64].unsqueeze(2).broadcast_to([128, 16, 64]), op=ALU.mult),
                      reads=[R_xdt, Rac], writes=[R_xdtd])
                pss = [psum(), psum()]
                for g in range(2):
                    ptt, Rpp = pss[g]
                    P.pe(lambda e, g=g, c=c, ptt=ptt: e.matmul(ptt[:, 0:512], lhsT=B_tm[:, c, g * 128:(g + 1) * 128],
                                                               rhs=xdtd[:, g * 512:(g + 1) * 512], start=True, stop=True),
                         reads=[R_Btm, R_xdtd], writes=[Rpp])
                P.dve(lambda e, ac=ac: e.tensor_tensor(out=Ssd.rearrange("p (h d) -> p h d", h=16),
                                                       in0=Ssd.rearrange("p (h d) -> p h d", h=16),
                                                       in1=ac[:, 64:80].unsqueeze(2).broadcast_to([128, 16, 64]), op=ALU.mult),
                      reads=[R_Ssd, Rac], writes=[R_Ssd])
                for g in range(2):
                    P.dve(lambda e, g=g, ptt=pss[g][0]: e.tensor_tensor(out=Ssd[:, g * 512:(g + 1) * 512], in0=ptt[:, 0:512],
                                                                        in1=Ssd[:, g * 512:(g + 1) * 512], op=ALU.add),
                          reads=[pss[g][1], R_Ssd], writes=[R_Ssd])
                P.act(lambda e: e.activation(out=Ssdb[:], in_=Ssd[:], func=AF.Copy), reads=[R_Ssd], writes=[R_Ssdb])

            if is_pre:
                mcol = par[:, PC_MASK + pre_idx:PC_MASK + pre_idx + 1]
                P.dve(lambda e: e.tensor_scalar(out=hcst[:, 24:32], in0=hcst[:, 0:8], scalar1=mcol, scalar2=None, op0=ALU.mult),
                      reads=[R_hcst, R_par, R_hm], writes=[R_hm])
                P.dve(lambda e: e.tensor_scalar(out=hcst[:, 32:40], in0=hcst[:, 16:24], scalar1=mcol, scalar2=None, op0=ALU.mult),
                      reads=[R_hcst, R_par, R_hm], writes=[R_hm])
            for a in range(4):
                wt, Rw = wget("H%d" % a)
                pq = [None, None]
                if not is_pre:
                    for i in range(2):
                        pt, Rp = proj_fm(wt, Rw, i, hT, R_hT)
                        P.act(lambda e, pt=pt, i=i: e.activation(out=qf[:, i, :], in_=pt[:, 0:512], func=AF.Silu), reads=[Rp], writes=[R_qf])
                for i in range(2):
                    h = 2 * a + i
                    pt, Rp = proj_fm(wt, Rw, 2 + i, hT, R_hT)
                    P.act(lambda e, pt=pt, i=i: e.activation(out=kf[:, i, :], in_=pt[:, 0:512], func=AF.Tanh, scale=0.5), reads=[Rp], writes=[R_kf])
                    P.dve(lambda e, i=i, h=h: e.tensor_scalar(out=gl[:, i, :], in0=kf[:, i, :], scalar1=hcst[:, h:h + 1], scalar2=hcst[:, 8 + h:9 + h],
                                                              op0=ALU.mult, op1=ALU.add), reads=[R_kf, R_hcst], writes=[R_gl])
                    P.act(lambda e, i=i: e.activation(out=gl[:, i, :], in_=gl[:, i, :], func=AF.Ln), reads=[R_gl], writes=[R_gl])
                    if is_pre:
                        P.dve(lambda e, i=i, h=h: e.tensor_scalar(out=kf[:, i, :], in0=kf[:, i, :], scalar1=hcst[:, 32 + h:33 + h], scalar2=hcst[:, 24 + h:25 + h],
                                                                  op0=ALU.mult, op1=ALU.add), reads=[R_kf, R_hm], writes=[R_kf])
                    else:
                        P.dve(lambda e, i=i, h=h: e.tensor_scalar(out=kf[:, i, :], in0=kf[:, i, :], scalar1=hcst[:, 16 + h:17 + h], scalar2=hcst[:, h:h + 1],
                                                                  op0=ALU.mult, op1=ALU.add), reads=[R_kf, R_hcst], writes=[R_kf])
                    P.pool(lambda e, i=i: e.memset(bt[i][:, 0:1], 0.0), writes=[R_bt[i]])
                    P.dve(lambda e, i=i: e.tensor_tensor_scan(out=bt[i][:, 1:513], data0=ones[:, 0:512], data1=gl[:, i, :], initial=0.0,
                                                              op0=ALU.mult, op1=ALU.add), reads=[R_ones, R_gl, R_bt[i]], writes=[R_bt[i]])
                if is_pre:
                    for i in range(2):
                        h = 2 * a + i
                        b_ = bt[i]
                        Rb = R_bt[i]
                        khf = pre_kh[i]; khtmf = pre_khtm[i]
                        P.act(lambda e, i=i, b_=b_: e.activation(out=gl[:, i, :], in_=b_[:, 1:513], func=AF.Exp, bias=b_[:, 512:513], scale=-1.0),
                              reads=[Rb, R_gl], writes=[R_gl])
                        P.dve(lambda e, i=i, khf=khf: e.tensor_tensor(out=khf, in0=kf[:, i, :], in1=gl[:, i, :], op=ALU.mult),
                              reads=[R_kf, R_gl], writes=[R_prekh[i]])
                        sc = stat[:, 40 + 8 * i:40 + 8 * i + 8]
                        Rsc = R_scp[i]
                        P.act(lambda e, sc=sc, b_=b_: e.activation(out=sc[:, 3:4], in_=b_[:, 512:513], func=AF.Exp), reads=[Rb, Rsc], writes=[Rsc])
                        pt, Rp = psum()
                        pv = bfv(pt)
                        for j in range(4):
                            P.pe(lambda e, j=j, pv=pv, khf=khf: e.transpose(out=pv[:, j * 128:(j + 1) * 128], in_=khf[:, j * 128:(j + 1) * 128],
                                                                            identity=ident[:]), reads=[R_prekh[i], R_ident], writes=[Rp])
                        P.act(lambda e, pv=pv, khtmf=khtmf: e.activation(out=khtmf, in_=pv[:, 0:512], func=AF.Copy), reads=[Rp], writes=[R_prekhtm[i]])
                        pt2, Rp2 = psum()
                        for j in range(4):
                            P.pe(lambda e, j=j, h=h, pt2=pt2, khtmf=khtmf: e.matmul(pt2[:, 0:128], lhsT=khtmf[:, j * 128:(j + 1) * 128],
                                                                                    rhs=vt[:, j, h * 128:(h + 1) * 128], start=(j == 0), stop=(j == 3)),
                                 reads=[R_prekhtm[i], R_vt[j]], writes=[Rp2])
                        P.dve(lambda e, h=h, sc=sc, pt2=pt2: e.scalar_tensor_tensor(out=Shg[:, h, :], in0=Shg[:, h, :], scalar=sc[:, 3:4],
                                                                                   in1=pt2[:, 0:128], op0=ALU.mult, op1=ALU.add),
                              reads=[R_Shg[h], Rsc, Rp2], writes=[R_Shg[h]])
                        P.act(lambda e, h=h: e.activation(out=Shgb[:, h, :], in_=Shg[:, h, :], func=AF.Copy), reads=[R_Shg[h]], writes=[R_Shgb[h]])
                for c in (range(0) if is_pre else range(4)):
                    pso = None
                    if not is_pre:
                        pso = psum()
                        P.pool(lambda e: e.memset(sst[:, 8:10], 0.0), writes=[R_sst])
                    for i in range(2):
                        h = 2 * a + i
                        b_ = bt[i]
                        Rb = R_bt[i]
                        c0 = c * 128
                        bseg = b_[:, c0 + 1:c0 + 129]
                        bmid = b_[:, c0 + 64:c0 + 65]
                        blast = b_[:, c0 + 128:c0 + 129]
                        bprev = b_[:, c0:c0 + 1]
                        sc = stat[:, 40 + 8 * i:40 + 8 * i + 8]
                        Rsc = R_scp[i]
                        b31 = b_[:, c0 + 32:c0 + 33]
                        b63 = b_[:, c0 + 64:c0 + 65]
                        b95 = b_[:, c0 + 96:c0 + 97]
                        P.dve(lambda e, sc=sc, b31=b31: e.tensor_scalar(out=sc[:, 0:1], in0=b31, scalar1=-1.0, scalar2=None, op0=ALU.mult),
                              reads=[Rb, Rsc], writes=[Rsc])
                        P.dve(lambda e, sc=sc, b95=b95: e.tensor_scalar(out=sc[:, 1:2], in0=b95, scalar1=-1.0, scalar2=None, op0=ALU.mult),
                              reads=[Rb, Rsc], writes=[Rsc])
                        P.dve(lambda e, sc=sc, bprev=bprev: e.tensor_scalar(out=sc[:, 2:3], in0=bprev, scalar1=-1.0, scalar2=None, op0=ALU.mult),
                              reads=[Rb, Rsc], writes=[Rsc])
                        P.dve(lambda e, sc=sc, bprev=bprev, blast=blast: e.tensor_tensor(out=sc[:, 3:4], in0=blast, in1=bprev, op=ALU.subtract),
                              reads=[Rb, Rsc], writes=[Rsc])
                        P.dve(lambda e, sc=sc, b63=b63, b31=b31: e.tensor_tensor(out=sc[:, 4:5], in0=b63, in1=b31, op=ALU.subtract),
                              reads=[Rb, Rsc], writes=[Rsc])
                        P.dve(lambda e, sc=sc, b63=b63, b95=b95: e.tensor_tensor(out=sc[:, 5:6], in0=b95, in1=b63, op=ALU.subtract),
                              reads=[Rb, Rsc], writes=[Rsc])
                        P.act(lambda e, sc=sc: e.activation(out=sc[:, 3:6], in_=sc[:, 3:6], func=AF.Exp), reads=[Rsc], writes=[Rsc])
                        kfc = kf[:, i, c0:c0 + 128]
                        if not is_pre:
                            qfc = qf[:, i, c0:c0 + 128]
                            e0, e1, e2 = etmp[0], etmp[1], etmp[2]
                            P.act(lambda e, e0=e0, bseg=bseg, sc=sc: e.activation(out=e0[:, 0:64], in_=bseg[:, 0:64], func=AF.Exp, bias=sc[:, 0:1], scale=1.0),
                                  reads=[Rb, Rsc], writes=[R_et[0]])
                            P.act(lambda e, e0=e0, bseg=bseg, sc=sc: e.activation(out=e0[:, 64:128], in_=bseg[:, 64:128], func=AF.Exp, bias=sc[:, 1:2], scale=1.0),
                                  reads=[Rb, Rsc], writes=[R_et[0]])
                            P.dve(lambda e, i=i, e0=e0, qfc=qfc: e.tensor_tensor(out=qt_[i], in0=qfc, in1=e0, op=ALU.mult),
                                  reads=[R_qf, R_et[0]], writes=[R_qt[i]])
                            P.dve(lambda e, i=i, sc=sc: e.tensor_scalar(out=QC[i], in0=qt_[i][:, 64:128], scalar1=sc[:, 5:6], scalar2=None, op0=ALU.mult),
                                  reads=[R_qt[i], Rsc], writes=[R_QC[i]])
                            P.act(lambda e, e1=e1, bseg=bseg, b31=b31: e.activation(out=e1[:, 0:64], in_=bseg[:, 0:64], func=AF.Exp, bias=b31, scale=-1.0),
                                  reads=[Rb], writes=[R_et[1]])
                            P.act(lambda e, e1=e1, bseg=bseg, b95=b95: e.activation(out=e1[:, 64:128], in_=bseg[:, 64:128], func=AF.Exp, bias=b95, scale=-1.0),
                                  reads=[Rb], writes=[R_et[1]])
                            P.dve(lambda e, i=i, e1=e1, kfc=kfc: e.tensor_tensor(out=KA[i][:, 0:64], in0=kfc[:, 0:64], in1=e1[:, 0:64], op=ALU.mult),
                                  reads=[R_kf, R_et[1]], writes=[R_KA[i]])
                            P.dve(lambda e, i=i, e1=e1, kfc=kfc: e.tensor_tensor(out=KB[i][:, 64:128], in0=kfc[:, 64:128], in1=e1[:, 64:128], op=ALU.mult),
                                  reads=[R_kf, R_et[1]], writes=[R_KB[i]])
                            P.dve(lambda e, i=i, sc=sc: e.tensor_scalar(out=KC[i][:, 0:64], in0=KA[i][:, 0:64], scalar1=sc[:, 4:5], scalar2=None, op0=ALU.mult),
                                  reads=[R_KA[i], Rsc], writes=[R_KC[i]])
                            P.act(lambda e, e2=e2, bseg=bseg, sc=sc: e.activation(out=e2, in_=bseg, func=AF.Exp, bias=sc[:, 2:3], scale=1.0),
                                  reads=[Rb, Rsc], writes=[R_et[2]])
                            P.dve(lambda e, i=i, e2=e2, qfc=qfc: e.tensor_tensor(out=qh[i], in0=qfc, in1=e2, op=ALU.mult),
                                  reads=[R_qf, R_et[2]], writes=[R_qh[i]])
                        e3 = etmp[3]
                        P.act(lambda e, e3=e3, bseg=bseg, blast=blast: e.activation(out=e3, in_=bseg, func=AF.Exp, bias=blast, scale=-1.0),
                              reads=[Rb], writes=[R_et[3]])
                        P.dve(lambda e, i=i, e3=e3, kfc=kfc: e.tensor_tensor(out=kh[i], in0=kfc, in1=e3, op=ALU.mult),
                              reads=[R_kf, R_et[3]], writes=[R_kh[i]])
                        vch = vt[:, c, h * 128:(h + 1) * 128]
                        if not is_pre:
                            pt, Rp = psum()
                            P.pe(lambda e, i=i, pt=pt: e.matmul(pt[:, 0:64], lhsT=KA[i], rhs=qt_[i][:, 0:64], start=True, stop=True),
                                 reads=[R_KA[i], R_qt[i]], writes=[Rp])
                            P.pe(lambda e, i=i, pt=pt: e.matmul(pt[:, 64:128], lhsT=KB[i], rhs=qt_[i][:, 64:128], start=True, stop=False),
                                 reads=[R_KB[i], R_qt[i]], writes=[Rp])
                            P.pe(lambda e, i=i, pt=pt: e.matmul(pt[:, 64:128], lhsT=KC[i], rhs=QC[i], start=False, stop=True),
                                 reads=[R_KC[i], R_QC[i]], writes=[Rp])
                            P.dve(lambda e, i=i, pt=pt: e.tensor_tensor(out=attm[i], in0=pt[:, 0:128], in1=U[:], op=ALU.mult),
                                  reads=[Rp, R_U], writes=[R_attm[i]])
                            P.pe(lambda e, i=i, vch=vch, ptt=pso[0]: e.matmul(ptt[:, i * 128:(i + 1) * 128], lhsT=attm[i], rhs=vch,
                                                                              start=True, stop=False),
                                 reads=[R_attm[i], R_vt[c]], writes=[pso[1]])
                            P.pe(lambda e, i=i, h=h, ptt=pso[0]: e.matmul(ptt[:, i * 128:(i + 1) * 128], lhsT=qh[i], rhs=Shgb[:, h, :],
                                                                          start=False, stop=True),
                                 reads=[R_qh[i], R_Shgb[h]], writes=[pso[1]])
                        pt, Rp = psum()
                        pv = bfv(pt)
                        P.pe(lambda e, i=i, pv=pv: e.transpose(out=pv[:, 0:128], in_=kh[i], identity=ident[:]),
                             reads=[R_kh[i], R_ident], writes=[Rp])
                        P.act(lambda e, i=i, pv=pv: e.activation(out=khtm[i], in_=pv[:, 0:128], func=AF.Copy), reads=[Rp], writes=[R_khtm[i]])
                        pt2, Rp2 = psum()
                        P.pe(lambda e, i=i, vch=vch, pt2=pt2: e.matmul(pt2[:, 0:128], lhsT=khtm[i], rhs=vch, start=True, stop=True),
                             reads=[R_khtm[i], R_vt[c]], writes=[Rp2])
                        P.dve(lambda e, h=h, sc=sc, pt2=pt2: e.scalar_tensor_tensor(out=Shg[:, h, :], in0=Shg[:, h, :], scalar=sc[:, 3:4],
                                                                                   in1=pt2[:, 0:128], op0=ALU.mult, op1=ALU.add),
                              reads=[R_Shg[h], Rsc, Rp2], writes=[R_Shg[h]])
                        P.act(lambda e, h=h: e.activation(out=Shgb[:, h, :], in_=Shg[:, h, :], func=AF.Copy), reads=[R_Shg[h]], writes=[R_Shgb[h]])
                    if not is_pre:
                        ptt, Rpp = pso
                        for i in range(2):
                            P.act(lambda e, i=i, ptt=ptt: e.activation(out=junk[:, 0:128], in_=ptt[:, i * 128:(i + 1) * 128], func=AF.Square,
                                                                       accum_out=sst[:, 8 + i:9 + i]), reads=[Rpp, R_sst], writes=[R_junk, R_sst])
                        rstd_from_ss(sst[:, 8:10], 2, R_sst, 1.0 / 128)
                        P.dve(lambda e, ptt=ptt: e.tensor_tensor(out=otmp.rearrange("p (i v) -> p i v", i=2),
                                                                 in0=ptt[:, 0:256].rearrange("p (i v) -> p i v", i=2),
                                                                 in1=sst[:, 8:10].unsqueeze(2).broadcast_to([128, 2, 128]), op=ALU.mult),
                              reads=[Rpp, R_sst], writes=[R_otmp])
                        P.dve(lambda e, a=a, c=c: e.tensor_tensor(out=og, in0=otmp, in1=gs[:, c, 256 * a:256 * a + 256], op=ALU.mult),
                              reads=[R_otmp, R_gs[c]], writes=[R_og])
                        pt, Rp = psum()
                        pv = bfv(pt)
                        for i in range(2):
                            P.pe(lambda e, i=i, pv=pv: e.transpose(out=pv[:, i * 128:(i + 1) * 128], in_=og[:, i * 128:(i + 1) * 128],
                                                                   identity=ident[:]), reads=[R_og, R_ident], writes=[Rp])
                        P.act(lambda e, a=a, c=c, pv=pv: e.activation(out=mixedT[:, 8 + 2 * a:10 + 2 * a, c * 128:(c + 1) * 128],
                                                                      in_=pv[:, 0:256].rearrange("p (k t) -> p k t", k=2), func=AF.Copy),
                              reads=[Rp], writes=[R_mixT[c]])
            if is_pre:
                P.pool(lambda e: e.memset(qf[:, 0, 0:1], 0.0), reads=R_prekh + R_prekhtm, writes=[R_qf] + R_prekh + R_prekhtm)
                return
            P.barrier()

            A = Arena()
            qT = A.bf(8 * 512).rearrange("p (k t) -> p k t", k=8); R_qT = Res()
            pe_ = [A.f32(1024).rearrange("p (h k) -> p h k", h=4) for _ in range(2)]; R_pe = [Res(), Res()]
            pn = [A.bf(1024).rearrange("p (h k) -> p h k", h=4) for _ in range(2)]; R_pn = [Res(), Res()]
            prT = A.bf(2 * 4 * 512).rearrange("p (k h t) -> p k h t", k=2, h=4); R_prT = Res()
            oT = A.bf(8 * 512).rearrange("p (k t) -> p k t", k=8); R_oT = Res()
            actT = A.bf(22 * 512).rearrange("p (k t) -> p k t", k=22); R_actT = Res()
            sgt = [A.f32(512) for _ in range(2)]; R_sgt = [Res(), Res()]
            nfw = A.f32(D); R_nfw = Res()
            sa = A.f32(32); R_sa = [Res(), Res()]
            P.dma("sp", "nfw", nfw, nfwd, writes=[R_nfw])

            for j in range(2):
                banks = [psum() for _ in range(4)]
                for i in range(2):
                    wt, Rw = wget("OUT%d%d" % (j, i))
                    for s in range(4):
                        ptt, Rpp = banks[s]
                        for kt in range(8):
                            P.pe(lambda e, kt=kt, s=s, i=i, ptt=ptt, wt=wt: e.matmul(
                                ptt[:, 0:512], lhsT=mixedT[:, 8 * i + kt, s * 128:(s + 1) * 128], rhs=wt[:, kt, :],
                                start=(i == 0 and kt == 0), stop=(i == 1 and kt == 7)), reads=[R_mixT[s], Rw], writes=[Rpp])
                for s in range(4):
                    ptt, Rpp = banks[s]
                    P.dve(lambda e, s=s, j=j, ptt=ptt: e.tensor_tensor(out=x_tm[:, s, j * 512:(j + 1) * 512], in0=ptt[:, 0:512],
                                                                       in1=x_tm[:, s, j * 512:(j + 1) * 512], op=ALU.add),
                          reads=[Rpp, R_x[s]], writes=[R_x[s]])
            rms_T(lambda s: x_tm[:, s, :], R_x, 4, hT, R_hT)
            for jc in range(2):
                wt, Rw = wget("Q%d" % jc)
                for j in range(4):
                    pt, Rp = proj_fm(wt, Rw, j, hT, R_hT)
                    P.act(lambda e, pt=pt, jc=jc, j=j: e.activation(out=qT[:, 4 * jc + j, :], in_=pt[:, 0:512], func=AF.Copy),
                          reads=[Rp], writes=[R_qT])
            for s in range(4):
                b = s % 2
                scb = [psum(), psum()]
                for h in range(4):
                    ptt, Rpp = scb[h // 2]
                    for d2 in range(2):
                        P.pe(lambda e, h=h, d2=d2, s=s, ptt=ptt: e.matmul(ptt[:, (h % 2) * 256:(h % 2) * 256 + 256],
                                                                          lhsT=qT[:, 2 * h + d2, s * 128:(s + 1) * 128], rhs=kmT[:, 2 * h + d2, :],
                                                                          start=(d2 == 0), stop=(d2 == 1)), reads=[R_qT, R_kmT], writes=[Rpp])
                sav = sa[:, 16 * b:16 * b + 16]
                Rsa = R_sa[b]
                for hb in range(2):
                    P.dve(lambda e, hb=hb, sav=sav, ptt=scb[hb][0]: e.tensor_reduce(out=sav[:, 2 * hb:2 * hb + 2],
                                                                                   in_=ptt[:, 0:512].rearrange("p (h k) -> p h k", h=2),
                                                                                   axis=AX.X, op=ALU.max), reads=[scb[hb][1], Rsa], writes=[Rsa])
                P.dve(lambda e, sav=sav: e.tensor_scalar(out=sav[:, 0:4], in0=sav[:, 0:4], scalar1=-1.0 / 16, scalar2=None, op0=ALU.mult),
                      reads=[Rsa], writes=[Rsa])
                P.pool(lambda e, sav=sav: e.memset(sav[:, 4:8], 0.0), reads=[Rsa], writes=[Rsa])
                for h in range(4):
                    ptt, Rpp = scb[h // 2]
                    P.act(lambda e, h=h, b=b, sav=sav, ptt=ptt: e.activation(out=pe_[b][:, h, :], in_=ptt[:, (h % 2) * 256:(h % 2) * 256 + 256],
                                                                             func=AF.Exp, bias=sav[:, h:h + 1], scale=1.0 / 16,
                                                                             accum_out=sav[:, 4 + h:5 + h]), reads=[Rpp, Rsa], writes=[R_pe[b], Rsa])
                P.dve(lambda e, sav=sav: e.reciprocal(out=sav[:, 4:8], in_=sav[:, 4:8]), reads=[Rsa], writes=[Rsa])
                P.dve(lambda e, b=b, sav=sav: e.tensor_tensor(out=pn[b], in0=pe_[b], in1=sav[:, 4:8].unsqueeze(2).broadcast_to([128, 4, 256]),
                                                              op=ALU.mult), reads=[R_pe[b], Rsa], writes=[R_pn[b]])
                pt, Rp = psum()
                pv = bfv(pt)
                for k2 in range(2):
                    for h in range(4):
                        P.pe(lambda e, k2=k2, h=h, b=b, pv=pv: e.transpose(out=pv[:, (k2 * 4 + h) * 128:(k2 * 4 + h + 1) * 128],
                                                                          in_=pn[b][:, h, k2 * 128:(k2 + 1) * 128], identity=ident[:]),
                             reads=[R_pn[b], R_ident], writes=[Rp])
                for k2 in range(2):
                    P.act(lambda e, s=s, k2=k2, pv=pv: e.activation(out=prT[:, k2, :, s * 128:(s + 1) * 128],
                                                                    in_=pv[:, k2 * 512:(k2 + 1) * 512].rearrange("p (h t) -> p h t", h=4),
                                                                    func=AF.Copy), reads=[Rp], writes=[R_prT])
            for h in range(4):
                for d2 in range(2):
                    pt, Rp = psum()
                    for k2 in range(2):
                        P.pe(lambda e, h=h, d2=d2, k2=k2, pt=pt: e.matmul(pt[:, 0:512], lhsT=vm[:, k2, h * 256 + d2 * 128:h * 256 + (d2 + 1) * 128],
                                                                          rhs=prT[:, k2, h, :], start=(k2 == 0), stop=(k2 == 1)),
                             reads=[R_vm, R_prT], writes=[Rp])
                    P.act(lambda e, h=h, d2=d2, pt=pt: e.activation(out=oT[:, 2 * h + d2, :], in_=pt[:, 0:512], func=AF.Copy),
                          reads=[Rp], writes=[R_oT])
            for j in range(2):
                wt, Rw = wget("O%d" % j)
                for s in range(4):
                    pt, Rp = proj_tm(wt, Rw, s, oT, R_oT)
                    P.dve(lambda e, s=s, j=j, pt=pt: e.tensor_tensor(out=x_tm[:, s, j * 512:(j + 1) * 512], in0=pt[:, 0:512],
                                                                     in1=x_tm[:, s, j * 512:(j + 1) * 512], op=ALU.add),
                          reads=[Rp, R_x[s]], writes=[R_x[s]])
            rms_T(lambda s: x_tm[:, s, :], R_x, 4, hT, R_hT)
            for jc in range(11):
                wt, Rw = wget("GU%d" % jc)
                for jj in range(2):
                    pg, Rpg = proj_fm(wt, Rw, jj, hT, R_hT)
                    pu, Rpu = proj_fm(wt, Rw, 2 + jj, hT, R_hT)
                    sg, Rsg = sgt[jj], R_sgt[jj]
                    P.act(lambda e, pg=pg, sg=sg: e.activation(out=sg, in_=pg[:, 0:512], func=AF.Silu), reads=[Rpg], writes=[Rsg])
                    P.dve(lambda e, pu=pu, sg=sg, jc=jc, jj=jj: e.tensor_tensor(out=actT[:, 2 * jc + jj, :], in0=pu[:, 0:512], in1=sg, op=ALU.mult),
                          reads=[Rpu, Rsg], writes=[R_actT])
            for j in range(2):
                banks = [psum() for _ in range(4)]
                for i in range(3):
                    wt, Rw = wget("D%d%d" % (j, i))
                    nk = 8 if i < 2 else 6
                    for s in range(4):
                        ptt, Rpp = banks[s]
                        for kt in range(nk):
                            P.pe(lambda e, kt=kt, s=s, i=i, nk=nk, ptt=ptt, wt=wt: e.matmul(
                                ptt[:, 0:512], lhsT=actT[:, 8 * i + kt, s * 128:(s + 1) * 128], rhs=wt[:, kt, :],
                                start=(i == 0 and kt == 0), stop=(i == 2 and kt == nk - 1)), reads=[R_actT, Rw], writes=[Rpp])
                for s in range(4):
                    ptt, Rpp = banks[s]
                    P.dve(lambda e, s=s, j=j, ptt=ptt: e.tensor_tensor(out=x_tm[:, s, j * 512:(j + 1) * 512], in0=ptt[:, 0:512],
                                                                       in1=x_tm[:, s, j * 512:(j + 1) * 512], op=ALU.add),
                          reads=[Rpp, R_x[s]], writes=[R_x[s]])
            ssf = stat[:, 32:36]
            Rsf = R_ssf
            P.pool(lambda e: e.memset(ssf, 0.0), writes=[Rsf])
            for s in range(4):
                P.act(lambda e, s=s: e.activation(out=junk[:], in_=x_tm[:, s, :], func=AF.Square, accum_out=ssf[:, s:s + 1]),
                      reads=[R_x[s], Rsf], writes=[R_junk, Rsf])
            rstd_from_ss(ssf, 4, Rsf, 1.0 / D)
            for s in range(4):
                b = 0
                P.dve(lambda e, s=s, b=b: e.scalar_tensor_tensor(out=ost[b][:], in0=x_tm[:, s, :], scalar=ssf[:, s:s + 1], in1=nfw,
                                                                 op0=ALU.mult, op1=ALU.mult), reads=[R_x[s], Rsf, R_nfw], writes=[R_ost[b]])
                out_dmas.append(P.dma("sp", "ost%d" % b, outd[out_row0 + s * 128:out_row0 + (s + 1) * 128, :], ost[b][:],
                                      reads=[R_ost[b]]))
            P.barrier()

        for t in range(NPRE):
            do_tile(xp, t * 512, True, t, 0)
        for t in range(NT):
            do_tile(xm, t * 512, False, 0, t * 512)
        assert wstate["got"] == len(wseq), (wstate["got"], len(wseq))
        P.emit(final_wait_ops=out_dmas)
    return nc


def make_par(inp, NPRE, premask):
    f = lambda a: np.asarray(a, dtype=np.float32)
    par = np.zeros((128, PC_MASK + max(NPRE, 1)), np.float32)
    cw = f(inp["conv_w"])[0]
    par[:, PC_CW:PC_CW + 48] = cw.reshape(4, 12, 128).transpose(2, 1, 0).reshape(128, 48)
    par[:, PC_CB:PC_CB + 12] = f(inp["conv_b"])[0].reshape(12, 128).T
    par[:, PC_DTB:PC_DTB + 16] = f(inp["dt_bias"])[0][None, :]
    par[:, PC_ALOG:PC_ALOG + 16] = f(inp["a_log"])[0][None, :]
    par[:, PC_DSK:PC_DSK + 16] = f(inp["d_skip"])[0][None, :]
    hlb = f(inp["hg_lower_bounds"])
    par[:, PC_HLB0:PC_HLB0 + 8] = hlb[0].reshape(8, 128).T
    par[:, PC_HLB1:PC_HLB1 + 8] = hlb[1].reshape(8, 128).T
    par[:, PC_MIX:PC_MIX + 8] = f(inp["norm_mix_w"])[0].reshape(8, 128).T
    par[:, PC_XA:PC_XA + 8] = f(inp["norm_xa_w"])[0].reshape(8, 128).T
    par[:, PC_MEM:PC_MEM + 8] = f(inp["norm_mem_w"])[0].reshape(8, 128).T
    par[:, PC_FFN:PC_FFN + 8] = f(inp["norm_ffn_w"])[0].reshape(8, 128).T
    par[:, PC_WOUT:PC_WOUT + 8] = f(inp["ssd_norm_w"])[0].reshape(8, 128).T
    par[:, PC_WOUT + 8:PC_WOUT + 16] = f(inp["hg_norm_w"])[0][:, None]
    par[:, PC_MASK:PC_MASK + len(premask)] = np.asarray(premask, np.float32)[None, :]
    return par


_NC_CACHE = {}


def run(inp, T, NPRE, nseg):
    x = np.asarray(inp["x"], np.float32)
    mem = np.asarray(inp["mem"], np.float32)
    B, L, _ = x.shape
    assert L == nseg * T
    key = (T, NPRE)
    if key not in _NC_CACHE:
        _NC_CACHE[key] = build(T, NPRE)
    nc = _NC_CACHE[key]
    shared = {
        "w_in": np.ascontiguousarray(inp["w_in"][0], np.float32), "w_out": np.ascontiguousarray(inp["w_out"][0], np.float32),
        "wq": np.ascontiguousarray(inp["xa_wq"][0], np.float32), "wkv": np.ascontiguousarray(inp["xa_wkv"][0], np.float32),
        "wo": np.ascontiguousarray(inp["xa_wo"][0], np.float32), "wg": np.ascontiguousarray(inp["ffn_w_gate"][0], np.float32),
        "wu": np.ascontiguousarray(inp["ffn_w_up"][0], np.float32), "wd": np.ascontiguousarray(inp["ffn_w_down"][0], np.float32),
        "nfw": np.ascontiguousarray(np.broadcast_to(np.asarray(inp["norm_final_w"], np.float32)[None, :], (128, D))),
    }
    in_maps = []
    npre_tok = max(NPRE, 1) * 512
    for b in range(B):
        for sg in range(nseg):
            start = sg * T
            xpre = np.zeros((npre_tok, D), np.float32)
            premask = np.zeros(max(NPRE, 1), np.float32)
            lo = start - NPRE * 512
            for t in range(NPRE):
                p0 = lo + t * 512
                if p0 >= 0:
                    xpre[t * 512:(t + 1) * 512] = x[b, p0:p0 + 512]
                    premask[t] = 1.0
            m = dict(shared)
            m["xm"] = np.ascontiguousarray(x[b, start:start + T])
            m["xp"] = xpre
            m["mem"] = np.ascontiguousarray(mem[b])
            m["par"] = make_par(inp, NPRE, premask)
            in_maps.append(m)
    res = run_bass_kernel_spmd(nc, in_maps, core_ids=list(range(B * nseg)))
    out = np.zeros((B, L, D), np.float32)
    k = 0
    for b in range(B):
        for sg in range(nseg):
            out[b, sg * T:(sg + 1) * T] = res.results[k]["out"]
            k += 1
    return out


def kernel(**inputs):
    return run(inputs, 4096, 24, 4)
```

```python
import contextlib
import threading
import numpy as np
import concourse.bass as bass
import concourse.mybir as mybir
from concourse.bass_utils import run_bass_kernel_spmd

F32 = mybir.dt.float32
BF16 = mybir.dt.bfloat16
AF = mybir.ActivationFunctionType
ALU = mybir.AluOpType
AX = mybir.AxisListType

D = 1024
FF = 2816
EPS = 1e-6
S1, S2, S3, S4, S5, S6 = 1024, 2560, 2576, 3600, 4624, 5648
NWS = 3


class Res:
    __slots__ = ("name", "last_w", "readers")

    def __init__(self, name=""):
        self.name = name
        self.last_w = None
        self.readers = {}


class Op:
    __slots__ = ("eng", "fn", "idx", "deps", "dma_key", "dma_cnt", "needs_inc", "inc_val")

    def __init__(self, eng, fn, idx, dma_key=None):
        self.eng = eng
        self.fn = fn
        self.idx = idx
        self.deps = {}
        self.dma_key = dma_key
        self.dma_cnt = 0
        self.needs_inc = False
        self.inc_val = 0


COMPUTE = ("pe", "act", "dve", "pool")


class Prog:
    def __init__(self, nc):
        self.nc = nc
        self.ops = []
        self.dma_counts = {}
        self.last_real = {}
        self.after_add = None

    def add(self, eng, fn, reads=(), writes=(), dma_key=None):
        op = Op(eng, fn, len(self.ops), dma_key)
        if dma_key is not None:
            c = self.dma_counts.get(dma_key, 0) + 1
            self.dma_counts[dma_key] = c
            op.dma_cnt = c
        deps = op.deps
        for r in reads:
            if r.last_w is not None:
                deps.setdefault(r.last_w, set()).add("raw")
        for w in writes:
            lw = w.last_w
            if lw is not None:
                if not (dma_key is not None and lw.dma_key == dma_key):
                    deps.setdefault(lw, set()).add("waw")
            for rd in w.readers.values():
                deps.setdefault(rd, set()).add("war")
        k = ("dma", op.idx) if dma_key is not None else eng
        for r in reads:
            r.readers[k] = op
        for w in writes:
            w.last_w = op
            w.readers = {}
        self.ops.append(op)
        if dma_key is None:
            self.last_real[eng] = op
        if self.after_add is not None:
            self.after_add()
        return op

    def pe(self, fn, reads=(), writes=()):
        return self.add("pe", fn, reads, writes)

    def act(self, fn, reads=(), writes=()):
        return self.add("act", fn, reads, writes)

    def dve(self, fn, reads=(), writes=()):
        return self.add("dve", fn, reads, writes)

    def pool(self, fn, reads=(), writes=()):
        return self.add("pool", fn, reads, writes)

    def dma(self, queue, key, out, in_, reads=(), writes=()):
        return self.add(queue, lambda e: e.dma_start(out=out, in_=in_), reads, writes, dma_key=key)

    def barrier(self, extra=()):
        lasts = dict(self.last_real)
        for e in COMPUTE:
            op = Op(e, None, len(self.ops))
            for e2, lo in lasts.items():
                if e2 != e:
                    op.deps[lo] = {"raw"}
            for x in extra:
                op.deps[x] = {"raw"}
            self.ops.append(op)

    def emit(self, final_wait_ops=()):
        nc = self.nc
        fin = Op("sp", None, len(self.ops))
        for o in final_wait_ops:
            fin.deps[o] = {"raw"}
        ops = self.ops + [fin]
        for op in ops:
            real = {}
            for d, kinds in op.deps.items():
                if d.dma_key is None and d.eng == op.eng and op.dma_key is None:
                    if op.eng == "pe" or kinds == {"war"}:
                        continue
                real[d] = kinds
            op.deps = real
            for d in real:
                if d.dma_key is None:
                    d.needs_inc = True
        cnt = {e: 0 for e in COMPUTE + ("sp",)}
        for op in ops:
            if op.dma_key is None and op.needs_inc:
                cnt[op.eng] += 1
                op.inc_val = cnt[op.eng]
        dma_keys = sorted(self.dma_counts.keys())
        with contextlib.ExitStack() as st:
            esem = {e: st.enter_context(nc.semaphore("s_" + e)) for e in cnt}
            dsem = {k: st.enter_context(nc.semaphore("d_%d" % i)) for i, k in enumerate(dma_keys)}
            block = st.enter_context(nc.Block())
            engs = {"pe": block.tensor, "act": block.scalar, "dve": block.vector,
                    "pool": block.gpsimd, "sp": block.sync}
            for ename, deco in engs.items():
                my = [o for o in ops if o.eng == ename]
                if not my:
                    continue

                def body(e, my=my, ename=ename):
                    waited = {}
                    for op in my:
                        need = {}
                        for d in op.deps:
                            if d.dma_key is not None:
                                s, v = ("d", d.dma_key), 16 * d.dma_cnt
                            else:
                                s, v = ("e", d.eng), d.inc_val
                            if need.get(s, 0) < v:
                                need[s] = v
                        for s, v in need.items():
                            if waited.get(s, 0) >= v:
                                continue
                            waited[s] = v
                            e.wait_ge(dsem[s[1]] if s[0] == "d" else esem[s[1]], v)
                        if op.fn is None:
                            continue
                        ins = op.fn(e)
                        if op.dma_key is not None:
                            ins.then_inc(dsem[op.dma_key], 16)
                        elif op.needs_inc:
                            ins.then_inc(esem[ename], 1)

                deco(body)


def chunk_catalog():
    cat = []
    for j in range(2):
        cat.append(("Z%d" % j, "w_in", 0, 8, [(0, 512, 512 * j)], "mix"))
    for j in range(3):
        cat.append(("X%d" % j, "w_in", 0, 8, [(0, 512, S1 + 512 * j)], "mix"))
    for a in range(4):
        cat.append(("H%d" % a, "w_in", 0, 8, [(0, 256, S3 + 256 * a), (256, 256, S4 + 256 * a)], "mix"))
    for j in range(2):
        cat.append(("V%d" % j, "w_in", 0, 8, [(0, 512, S5 + 512 * j)], "mix"))
    for j in range(2):
        cat.append(("G%d" % j, "w_in", 0, 8, [(0, 512, S6 + 512 * j)], "mix"))
    for j in range(2):
        for i in range(2):
            cat.append(("OUT%d%d" % (j, i), "w_out", 8 * i, 8, [(0, 512, 512 * j)], "wout%d" % i))
    for j in range(2):
        cat.append(("Q%d" % j, "wq", 0, 8, [(0, 512, 512 * j)], "xa"))
    for j in range(4):
        cat.append(("KV%d" % j, "wkv", 0, 8, [(0, 512, 512 * j)], "mem"))
    for j in range(2):
        cat.append(("O%d" % j, "wo", 0, 8, [(0, 512, 512 * j)], None))
    for j in range(11):
        cat.append(("GU%d" % j, "wgu", 0, 8, [(0, 256, 256 * j), (256, 256, 256 * j)], "ffn"))
    for j in range(2):
        for i in range(3):
            cat.append(("D%d%d" % (j, i), "wd", 8 * i, 8 if i < 2 else 6, [(0, 512, 512 * j)], None))
    return cat


PC_CW, PC_CB, PC_DTB, PC_ALOG, PC_DSK, PC_HLB0, PC_HLB1 = 0, 48, 60, 76, 92, 108, 116
PC_MIX, PC_XA, PC_MEM, PC_FFN, PC_WOUT, PC_MASK = 124, 132, 140, 148, 156, 172


def build(T, NPRE):
    NT = T // 512
    NPAR = PC_MASK + max(NPRE, 1)
    nc = bass.Bass("TRN2", target_bir_lowering=False)
    xm = nc.dram_tensor("xm", [T, D], F32, kind="ExternalInput").ap()
    xp = nc.dram_tensor("xp", [max(NPRE, 1) * 512, D], F32, kind="ExternalInput").ap()
    memd = nc.dram_tensor("mem", [256, D], F32, kind="ExternalInput").ap()
    pard = nc.dram_tensor("par", [128, NPAR], F32, kind="ExternalInput").ap()
    nfwd = nc.dram_tensor("nfw", [128, D], F32, kind="ExternalInput").ap()
    wd_ = {
        "w_in": nc.dram_tensor("w_in", [D, 6672], F32, kind="ExternalInput").ap(),
        "w_out": nc.dram_tensor("w_out", [2048, D], F32, kind="ExternalInput").ap(),
        "wq": nc.dram_tensor("wq", [D, D], F32, kind="ExternalInput").ap(),
        "wkv": nc.dram_tensor("wkv", [D, 2048], F32, kind="ExternalInput").ap(),
        "wo": nc.dram_tensor("wo", [D, D], F32, kind="ExternalInput").ap(),
        "wg": nc.dram_tensor("wg", [D, FF], F32, kind="ExternalInput").ap(),
        "wu": nc.dram_tensor("wu", [D, FF], F32, kind="ExternalInput").ap(),
        "wd": nc.dram_tensor("wd", [FF, D], F32, kind="ExternalInput").ap(),
    }
    outd = nc.dram_tensor("out", [T, D], F32, kind="ExternalOutput").ap()
    cat = chunk_catalog()
    cid = {c[0]: i for i, c in enumerate(cat)}
    wsc = nc.dram_tensor("wsc", [len(cat), 128, 4096], BF16, kind="Internal").ap()
    R_wsc = [Res("wsc%d" % i) for i in range(len(cat))]

    P = Prog(nc)
    with contextlib.ExitStack() as st:
        def sb(name, shape, dt):
            return st.enter_context(nc.sbuf_tensor("sb_" + name, shape, dt))

        par = sb("par", [128, NPAR], F32); R_par = Res()
        ident = sb("ident", [128, 128], BF16); R_ident = Res()
        U = sb("U", [128, 128], F32); R_U = Res()
        ones = sb("ones", [128, 512], F32); R_ones = Res()
        cst = sb("cst", [128, 64], F32); R_cst = Res()
        wdt = sb("wdt", [128, 8, 16], BF16); R_wdt = Res()
        x_tm = sb("x_tm", [128, 4, D], F32); R_x = [Res() for _ in range(4)]
        hT = sb("hT", [128, 8, 512], BF16); R_hT = Res()
        hn = [sb("hn%d" % i, [128, D], BF16) for i in range(2)]; R_hn = [Res(), Res()]
        junk = sb("junk", [128, D], BF16); R_junk = Res()
        wbuf = [sb("wbuf%d" % i, [128, 8, 512], BF16) for i in range(NWS)]; R_wbuf = [Res() for _ in range(NWS)]
        Ssd = sb("Ssd", [128, D], F32); R_Ssd = Res()
        Ssdb = sb("Ssdb", [128, D], BF16); R_Ssdb = Res()
        Shg = sb("Shg", [128, 8, 128], F32); R_Shg = [Res() for _ in range(8)]
        Shgb = sb("Shgb", [128, 8, 128], BF16); R_Shgb = [Res() for _ in range(8)]
        halo = sb("halo", [128, 12, 3], F32); R_halo = Res()
        mixedT = sb("mixedT", [128, 16, 512], BF16); R_mixT = [Res() for _ in range(4)]; R_mixTh2 = [[Res() for _ in range(4)] for _ in range(2)]; R_junkh2 = [Res(), Res()]
        kmT = sb("kmT", [128, 8, 256], BF16); R_kmT = Res()
        vm = sb("vm", [128, 2, D], BF16); R_vm = Res()
        ost = [sb("ost0", [128, D], F32)]; R_ost = [Res()]
        stat = sb("stat", [128, 64], F32)
        ARENA = 27720
        arena = sb("arena", [128, ARENA], F32)
        psb = [st.enter_context(nc.psum_tensor("ps%d" % i, [128, 512], F32)) for i in range(8)]
        R_ps = [Res() for _ in range(8)]
        pctr = [0]

        tl = threading.local()

        def psum():
            pool = getattr(tl, "pool", None)
            if pool is None:
                i = pctr[0] % 8
                pctr[0] += 1
            else:
                i = pool["banks"][pool["ctr"] % len(pool["banks"])]
                pool["ctr"] += 1
            return psb[i], R_ps[i]

        def psum_ded():
            pool = getattr(tl, "pool", None)
            if pool is None or pool.get("ded") is None:
                return psum()
            return psb[pool["ded"]], R_ps[pool["ded"]]

        def run_interleaved(funcs, pools):
            n = len(funcs)
            st_ = {"turn": 0, "alive": [True] * n, "err": None}
            cv = threading.Condition()

            def advance(k):
                for d in range(1, n + 1):
                    j = (k + d) % n
                    if st_["alive"][j]:
                        st_["turn"] = j
                        return
                st_["turn"] = -1

            def yield_turn():
                k = getattr(tl, "sid", None)
                if k is None:
                    return
                with cv:
                    advance(k)
                    cv.notify_all()
                    while st_["turn"] != k:
                        cv.wait()

            def runner(k):
                tl.pool = pools[k]
                tl.sid = k
                with cv:
                    while st_["turn"] != k:
                        cv.wait()
                try:
                    funcs[k]()
                except BaseException as ex:
                    st_["err"] = ex
                finally:
                    with cv:
                        st_["alive"][k] = False
                        advance(k)
                        cv.notify_all()

            P.after_add = yield_turn
            ths = [threading.Thread(target=runner, args=(k,)) for k in range(n)]
            for t in ths:
                t.start()
            for t in ths:
                t.join()
            P.after_add = None
            if st_["err"] is not None:
                raise st_["err"]

        def bfv(pt):
            return pt[:, 0:512].bitcast(BF16)

        class Arena:
            def __init__(self):
                self.off = 0

            def f32(self, n):
                a = arena[:, self.off:self.off + n]
                self.off += n
                assert self.off <= ARENA, self.off
                return a

            def bf(self, n):
                n32 = (n + 1) // 2
                a = arena[:, self.off:self.off + n32].bitcast(BF16)
                self.off += n32
                assert self.off <= ARENA, self.off
                return a

        d_par = P.dma("sp", "par", par[:], pard, writes=[R_par])
        P.pool(lambda e: e.memset(ones[:], 1.0), writes=[R_ones])
        P.pool(lambda e: e.memset(U[:], 1.0), writes=[R_U])
        P.pool(lambda e: e.affine_select(out=U[:], in_=U[:], pattern=[[1, 128]], compare_op=ALU.is_ge,
                                         fill=0.0, base=0, channel_multiplier=-1), reads=[R_U], writes=[R_U])
        idf = arena[:, 0:128]
        R_idf = Res()
        P.pool(lambda e: e.memset(idf, 0.0), writes=[R_idf])
        P.pool(lambda e: e.affine_select(out=idf, in_=ones[:, 0:128], pattern=[[1, 128]], compare_op=ALU.is_equal,
                                         fill=0.0, base=0, channel_multiplier=-1), reads=[R_ones, R_idf], writes=[R_idf])
        P.dve(lambda e: e.tensor_copy(out=ident[:], in_=idf), reads=[R_idf], writes=[R_ident])
        P.pool(lambda e: e.memset(Ssd[:], 0.0), writes=[R_Ssd])
        P.pool(lambda e: e.memset(Ssdb[:], 0.0), writes=[R_Ssdb])
        P.pool(lambda e: e.memset(Shg[:], 0.0), writes=R_Shg)
        P.pool(lambda e: e.memset(Shgb[:], 0.0), writes=R_Shgb)
        P.pool(lambda e: e.memset(halo[:], 0.0), writes=[R_halo])
        P.pool(lambda e: e.memset(cst[:, 32:40], 1.0), writes=[R_cst])
        P.dve(lambda e: e.tensor_tensor(out=cst[:, 40:48], in0=par[:, PC_HLB0:PC_HLB0 + 8],
                                        in1=par[:, PC_HLB1:PC_HLB1 + 8], op=ALU.subtract), reads=[R_par, R_cst], writes=[R_cst])
        P.act(lambda e: e.activation(out=cst[:, 0:8], in_=cst[:, 40:48], func=AF.Sigmoid), reads=[R_cst], writes=[R_cst])
        P.act(lambda e: e.activation(out=cst[:, 8:16], in_=cst[:, 40:48], func=AF.Sigmoid, scale=-1.0), reads=[R_cst], writes=[R_cst])
        P.act(lambda e: e.activation(out=cst[:, 48:64], in_=par[:, PC_ALOG:PC_ALOG + 16], func=AF.Exp), reads=[R_par, R_cst], writes=[R_cst])
        P.dve(lambda e: e.tensor_scalar(out=cst[:, 16:32], in0=cst[:, 48:64], scalar1=-1.0, scalar2=None, op0=ALU.mult),
              reads=[R_cst], writes=[R_cst])
        lb, oml, aneg, onesb = cst[:, 0:8], cst[:, 8:16], cst[:, 16:32], cst[:, 32:40]
        nhalf = sb("nhalf", [128, 8], F32); R_nhalf = Res()
        P.pool(lambda e: e.memset(nhalf[:], -0.5), writes=[R_nhalf])
        hcst = sb("hcst", [128, 40], F32); R_hcst = Res(); R_hm = Res()
        P.dve(lambda e: e.tensor_scalar(out=hcst[:, 0:8], in0=oml, scalar1=0.5, scalar2=None, op0=ALU.mult), reads=[R_cst], writes=[R_hcst])
        P.dve(lambda e: e.tensor_tensor(out=hcst[:, 8:16], in0=hcst[:, 0:8], in1=lb, op=ALU.add), reads=[R_cst, R_hcst], writes=[R_hcst])
        P.dve(lambda e: e.tensor_scalar(out=hcst[:, 16:24], in0=oml, scalar1=-0.5, scalar2=None, op0=ALU.mult), reads=[R_cst, R_hcst], writes=[R_hcst])

        scale_ap = {"mix": par[:, PC_MIX:PC_MIX + 8], "xa": par[:, PC_XA:PC_XA + 8], "mem": par[:, PC_MEM:PC_MEM + 8],
                    "ffn": par[:, PC_FFN:PC_FFN + 8], "wout0": par[:, PC_WOUT:PC_WOUT + 8],
                    "wout1": par[:, PC_WOUT + 8:PC_WOUT + 16], None: onesb}
        ar = Arena(); ar.off = 128
        NSTG = 3
        stg = [ar.f32(4096).rearrange("p (k n) -> p k n", k=8) for _ in range(NSTG)]
        stgb = [ar.bf(4096).rearrange("p (k n) -> p k n", k=8) for _ in range(NSTG)]
        wdt32 = ar.f32(128).rearrange("p (k n) -> p k n", k=8)
        R_stg = [Res() for _ in range(NSTG)]; R_stgb = [Res() for _ in range(NSTG)]; R_wdt32 = Res()
        pro_dmas = []

        def wsrc(key, kt0, nkt, c0, cw):
            return wd_[key].rearrange("(kt p) n -> p kt n", p=128)[:, kt0:kt0 + nkt, c0:c0 + cw]

        def pro_load(ci):
            name, key, kt0, nkt, pieces, sk = cat[ci]
            sl = ci % NSTG
            for pi, (dc, cw, sc) in enumerate(pieces):
                k2 = key
                if key == "wgu":
                    k2 = "wg" if pi == 0 else "wu"
                P.dma("sp", "stg%d" % sl, stg[sl][:, 0:nkt, dc:dc + cw], wsrc(k2, kt0, nkt, sc, cw), writes=[R_stg[sl]])

        def pro_cast_store(ci):
            name, key, kt0, nkt, pieces, sk = cat[ci]
            sl = ci % NSTG
            sap = scale_ap[sk]
            f = (lambda e, sl=sl, nkt=nkt, sap=sap: e.tensor_tensor(
                out=stgb[sl][:, 0:nkt, :], in0=stg[sl][:, 0:nkt, :],
                in1=sap[:, 0:nkt].unsqueeze(2).broadcast_to([128, nkt, 512]), op=ALU.mult))
            (P.pool if ci % 3 == 2 else P.dve)(f, reads=[R_stg[sl], R_par, R_cst], writes=[R_stgb[sl]])
            pro_dmas.append(P.dma("sp", "wscw%d" % sl, wsc[ci].rearrange("p (k n) -> p k n", k=8)[:, 0:nkt, :],
                                  stgb[sl][:, 0:nkt, :], reads=[R_stgb[sl]], writes=[R_wsc[ci]]))

        pro_load(0)
        pro_load(1)
        for ci in range(len(cat)):
            if ci + 2 < len(cat):
                pro_load(ci + 2)
            pro_cast_store(ci)
        P.dma("sp", "wdt32", wdt32, wsrc("w_in", 0, 8, S2, 16), writes=[R_wdt32])
        P.dve(lambda e: e.tensor_tensor(out=wdt[:], in0=wdt32, in1=par[:, PC_MIX:PC_MIX + 8].unsqueeze(2).broadcast_to([128, 8, 16]),
                                        op=ALU.mult), reads=[R_wdt32, R_par], writes=[R_wdt])

        pre_seq = ["X0", "V0", "X1", "V1", "X2", "H0", "H1", "H2", "H3"]
        main_seq = (["X0", "Z0", "X1", "Z1", "X2", "V0", "V1", "G0", "G1", "H0", "H1", "H2", "H3",
                     "OUT00", "OUT01", "OUT10", "OUT11", "Q0", "Q1", "O0", "O1"]
                    + ["GU%d" % j for j in range(11)] + ["D00", "D01", "D02", "D10", "D11", "D12"])
        wseq = ["KV0", "KV1", "KV2", "KV3"] + pre_seq * NPRE + main_seq * NT
        wstate = {"issued": 0, "got": 0}

        def wissue():
            i = wstate["issued"]
            c = cid[wseq[i]]
            sl = i % NWS
            P.dma("sp", "wb%d" % sl, wbuf[sl][:], wsc[c].rearrange("p (k n) -> p k n", k=8),
                  reads=[R_wsc[c]], writes=[R_wbuf[sl]])
            wstate["issued"] += 1

        def wget(name):
            i = wstate["got"]
            assert wseq[i] == name, (wseq[i], name)
            while wstate["issued"] < min(len(wseq), i + NWS):
                wissue()
            wstate["got"] += 1
            return wbuf[i % NWS], R_wbuf[i % NWS]

        def rstd_from_ss(ssv, n, Rs, inv_n):
            P.pool(lambda e: e.tensor_scalar(out=ssv, in0=ssv, scalar1=inv_n, scalar2=EPS, op0=ALU.mult, op1=ALU.add),
                   reads=[Rs], writes=[Rs])
            P.pool(lambda e: e.tensor_tensor(out=ssv, in0=ssv, in1=nhalf[:, 0:n], op=ALU.pow), reads=[Rs, R_nhalf], writes=[Rs])

        rms_ctr = [0]
        R_rms = [Res(), Res()]
        R_ssf = Res()
        R_scp = [Res(), Res()]
        R_prekh = [Res(), Res()]
        R_prekhtm = [Res(), Res()]

        def rms_T(src, Rsrc, nsub, dstT, R_dst):
            k = rms_ctr[0] % 2
            rms_ctr[0] += 1
            ss = stat[:, 8 * k:8 * k + nsub]
            Rss = R_rms[k]
            P.pool(lambda e: e.memset(ss, 0.0), writes=[Rss])
            for s in range(nsub):
                P.act(lambda e, s=s: e.activation(out=junk[:], in_=src(s), func=AF.Square, accum_out=ss[:, s:s + 1]),
                      reads=[Rsrc[s], Rss], writes=[R_junk, R_junkh2[0], R_junkh2[1], Rss])
            rstd_from_ss(ss, nsub, Rss, 1.0 / D)
            for s in range(nsub):
                b = s % 2
                P.dve(lambda e, s=s, b=b: e.tensor_scalar(out=hn[b][:], in0=src(s), scalar1=ss[:, s:s + 1], scalar2=None,
                                                          op0=ALU.mult), reads=[Rsrc[s], Rss], writes=[R_hn[b]])
                pt, Rp = psum()
                pv = bfv(pt)
                for kt in range(8):
                    P.pe(lambda e, kt=kt, b=b, pv=pv: e.transpose(out=pv[:, kt * 128:(kt + 1) * 128],
                                                                  in_=hn[b][:, kt * 128:(kt + 1) * 128], identity=ident[:]),
                         reads=[R_hn[b], R_ident], writes=[Rp])
                P.act(lambda e, s=s, pv=pv: e.activation(out=dstT[:, :, s * 128:(s + 1) * 128],
                                                         in_=pv.rearrange("p (k t) -> p k t", k=8), func=AF.Copy),
                      reads=[Rp], writes=[R_dst])

        def proj_fm(wt, Rw, j, xT, RxT, ncols=512):
            pt, Rp = psum()
            for kt in range(8):
                P.pe(lambda e, kt=kt, pt=pt: e.matmul(pt[:, 0:ncols], lhsT=wt[:, kt, j * 128:(j + 1) * 128], rhs=xT[:, kt, 0:ncols],
                                                      start=(kt == 0), stop=(kt == 7)), reads=[Rw, RxT], writes=[Rp])
            return pt, Rp

        def proj_tm(wt, Rw, s, xT, RxT, ncols=512):
            pt, Rp = psum()
            for kt in range(8):
                P.pe(lambda e, kt=kt, pt=pt: e.matmul(pt[:, 0:ncols], lhsT=xT[:, kt, s * 128:(s + 1) * 128], rhs=wt[:, kt, 0:ncols],
                                                      start=(kt == 0), stop=(kt == 7)), reads=[Rw, RxT], writes=[Rp])
            return pt, Rp

        mem_t = ar.f32(2 * D).rearrange("p (s d) -> p s d", s=2); R_mem = [Res(), Res()]
        mT = ar.bf(8 * 256).rearrange("p (k t) -> p k t", k=8); R_mT = Res()
        for s in range(2):
            P.dma("sp", "mem%d" % s, mem_t[:, s, :], memd[s * 128:(s + 1) * 128, :], writes=[R_mem[s]])
        rms_T(lambda s: mem_t[:, s, :], R_mem, 2, mT, R_mT)
        for jc in range(2):
            wt, Rw = wget("KV%d" % jc)
            for j in range(4):
                pt, Rp = proj_fm(wt, Rw, j, mT, R_mT, ncols=256)
                P.act(lambda e, pt=pt, jc=jc, j=j: e.activation(out=kmT[:, 4 * jc + j, :], in_=pt[:, 0:256], func=AF.Copy),
                      reads=[Rp], writes=[R_kmT])
        for jc in range(2):
            wt, Rw = wget("KV%d" % (2 + jc))
            for s in range(2):
                pt, Rp = proj_tm(wt, Rw, s, mT, R_mT)
                P.act(lambda e, pt=pt, jc=jc, s=s: e.activation(out=vm[:, s, jc * 512:(jc + 1) * 512], in_=pt[:, 0:512], func=AF.Copy),
                      reads=[Rp], writes=[R_vm])
        P.barrier(extra=pro_dmas[-3:])

        out_dmas = []
        tile_ctr = [0]
        mres_store = []

        def mk_mres():
            idx = [0]

            def mres():
                i = idx[0]
                idx[0] += 1
                if i >= len(mres_store):
                    mres_store.append(Res())
                return mres_store[i]
            return mres

        def do_tile(xsrc_d, row0, is_pre, pre_idx, out_row0):
            ti = tile_ctr[0]
            tile_ctr[0] += 1
            A = Arena()
            mres = mk_mres()
            raw = A.f32(4 * 515).rearrange("p (j t) -> p j t", j=4); R_raw = mres()
            cacc = [A.f32(512) for _ in range(2)]; R_cacc = [mres(), mres()]
            xsT = A.bf(8 * 512).rearrange("p (k t) -> p k t", k=8); R_xsT = mres()
            BT = A.bf(2 * 512).rearrange("p (k t) -> p k t", k=2); R_BT = mres()
            CT = A.bf(2 * 512).rearrange("p (k t) -> p k t", k=2); R_CT = mres()
            xs_tm = A.bf(4 * D).rearrange("p (s d) -> p s d", s=4); R_xs = [mres() for _ in range(4)]
            B_tm = A.bf(4 * 256).rearrange("p (s d) -> p s d", s=4); R_Btm = mres()
            zs = A.bf(4 * D).rearrange("p (s d) -> p s d", s=4); R_zs = [mres() for _ in range(4)]
            vt = A.bf(4 * D).rearrange("p (s d) -> p s d", s=4); R_vt = [mres() for _ in range(4)]
            gs = A.bf(4 * D).rearrange("p (s d) -> p s d", s=4); R_gs = [mres() for _ in range(4)]
            dtr = A.f32(64).rearrange("p (s h) -> p s h", s=4); R_dtr = mres()
            dtA = A.f32(64).rearrange("p (s h) -> p s h", s=4); R_dtA = mres()
            acs = [A.f32(96) for _ in range(2)]; R_acs = [mres(), mres()]
            Lseg = [A.f32(512) for _ in range(2)]; R_Lseg = [mres(), mres()]
            MT = A.bf(16 * 128).rearrange("p (h l) -> p h l", h=16); R_MT = mres()
            cbm = A.f32(256).rearrange("p (g l) -> p g l", g=2); R_cbm = mres()
            xdt = A.bf(D); R_xdt = mres()
            xdtd = A.bf(D); R_xdtd = mres()
            t1 = A.f32(D); R_t1 = mres()
            t3 = A.f32(D); R_t3 = mres()
            yn = A.bf(D); R_yn = mres()
            qf = A.f32(1024).rearrange("p (i t) -> p i t", i=2); R_qf = mres()
            gl = A.f32(1024).rearrange("p (i t) -> p i t", i=2); R_gl = mres()
            kf = A.f32(1024).rearrange("p (i t) -> p i t", i=2); R_kf = mres()
            bt = [A.f32(513) for _ in range(2)]; R_bt = [mres(), mres()]
            etmp = [A.f32(128) for _ in range(8)]; R_et = [mres() for _ in range(8)]
            qt_ = [A.bf(128) for _ in range(2)]; R_qt = [mres(), mres()]
            KA = [A.bf(128) for _ in range(2)]; R_KA = [mres(), mres()]
            KB = [A.bf(128) for _ in range(2)]; R_KB = [mres(), mres()]
            KC = [A.bf(128) for _ in range(2)]; R_KC = [mres(), mres()]
            QC = [A.bf(64) for _ in range(2)]; R_QC = [mres(), mres()]
            if not is_pre:
                for i in range(2):
                    P.pool(lambda e, i=i: e.memset(KA[i], 0.0), writes=[R_KA[i]])
                    P.pool(lambda e, i=i: e.memset(KB[i], 0.0), writes=[R_KB[i]])
                    P.pool(lambda e, i=i: e.memset(KC[i], 0.0), writes=[R_KC[i]])
            qh = [A.bf(128) for _ in range(2)]; R_qh = [mres(), mres()]
            kh = [A.bf(128) for _ in range(2)]; R_kh = [mres(), mres()]
            khtm = [A.bf(128) for _ in range(2)]; R_khtm = [mres(), mres()]
            attm = [A.bf(128) for _ in range(2)]; R_attm = [mres(), mres()]
            otmp = A.f32(256); R_otmp = mres()
            og = A.bf(256); R_og = mres()
            sst = A.f32(32); R_sst = mres(); R_ssth2 = [mres(), mres()]
            R_qf2 = [mres(), mres()]; R_gl2 = [mres(), mres()]; R_kf2 = [mres(), mres()]; R_otmp2 = [mres(), mres()]; R_og2 = [mres(), mres()]

            qfb = qf.rearrange("p i t -> p (i t)").bitcast(BF16)
            pre_kh = [qfb[:, 0:512], qfb[:, 512:1024]]
            pre_khtm = [qfb[:, 1024:1536], qfb[:, 1536:2048]]

            for s in range(4):
                P.dma("sp", "x%d" % s, x_tm[:, s, :], xsrc_d[row0 + s * 128:row0 + (s + 1) * 128, :], writes=[R_x[s]])
            rms_T(lambda s: x_tm[:, s, :], R_x, 4, hT, R_hT)

            def tm_chunk(nm, jc, dst, Rdst, func):
                wt, Rw = wget("%s%d" % (nm, jc))
                for s in range(4):
                    pt, Rp = proj_tm(wt, Rw, s, hT, R_hT)
                    P.act(lambda e, pt=pt, s=s, jc=jc: e.activation(out=dst[:, s, jc * 512:(jc + 1) * 512], in_=pt[:, 0:512], func=func),
                          reads=[Rp], writes=[Rdst[s]])
            if is_pre:
                tm_list = [("V", 0, vt, R_vt, AF.Copy), ("V", 1, vt, R_vt, AF.Copy)]
            else:
                tm_list = [("Z", 0, zs, R_zs, AF.Silu), ("Z", 1, zs, R_zs, AF.Silu), ("V", 0, vt, R_vt, AF.Copy),
                           ("V", 1, vt, R_vt, AF.Copy), ("G", 0, gs, R_gs, AF.Silu), ("G", 1, gs, R_gs, AF.Silu)]
            for c3 in range(3):
                wt, Rw = wget("X%d" % c3)
                P.pool(lambda e, c3=c3: e.tensor_copy(out=raw[:, :, 0:3], in_=halo[:, 4 * c3:4 * c3 + 4, :]),
                       reads=[R_halo], writes=[R_raw])
                for j in range(4):
                    pt, Rp = proj_fm(wt, Rw, j, hT, R_hT)
                    P.act(lambda e, pt=pt, j=j: e.activation(out=raw[:, j, 3:515], in_=pt[:, 0:512], func=AF.Copy),
                          reads=[Rp], writes=[R_raw])
                P.pool(lambda e, c3=c3: e.tensor_copy(out=halo[:, 4 * c3:4 * c3 + 4, :], in_=raw[:, :, 512:515]),
                       reads=[R_raw], writes=[R_halo])
                if tm_list:
                    tm_chunk(*tm_list.pop(0))
                for j in range(4):
                    ct = 4 * c3 + j
                    ca, Rca = cacc[j % 2], R_cacc[j % 2]
                    P.dve(lambda e, j=j, ct=ct, ca=ca: e.tensor_scalar(
                        out=ca, in0=raw[:, j, 0:512], scalar1=par[:, PC_CW + 4 * ct:PC_CW + 4 * ct + 1],
                        scalar2=par[:, PC_CB + ct:PC_CB + ct + 1], op0=ALU.mult, op1=ALU.add), reads=[R_raw, R_par], writes=[Rca])
                    for k in range(1, 4):
                        P.dve(lambda e, j=j, ct=ct, k=k, ca=ca: e.scalar_tensor_tensor(
                            out=ca, in0=raw[:, j, k:k + 512], scalar=par[:, PC_CW + 4 * ct + k:PC_CW + 4 * ct + k + 1],
                            in1=ca, op0=ALU.mult, op1=ALU.add), reads=[R_raw, R_par, Rca], writes=[Rca])
                    if ct < 8:
                        dst, Rd = xsT[:, ct, :], R_xsT
                    elif ct < 10:
                        dst, Rd = BT[:, ct - 8, :], R_BT
                    else:
                        dst, Rd = CT[:, ct - 10, :], R_CT
                    P.act(lambda e, ca=ca, dst=dst: e.activation(out=dst, in_=ca, func=AF.Silu), reads=[Rca], writes=[Rd])
            while tm_list:
                tm_chunk(*tm_list.pop(0))
            for s in range(4):
                pt, Rp = psum()
                pv = bfv(pt)
                for kt in range(8):
                    P.pe(lambda e, kt=kt, s=s, pv=pv: e.transpose(out=pv[:, kt * 128:(kt + 1) * 128],
                                                                  in_=xsT[:, kt, s * 128:(s + 1) * 128], identity=ident[:]),
                         reads=[R_xsT, R_ident], writes=[Rp])
                P.act(lambda e, s=s, pv=pv: e.activation(out=xs_tm[:, s, :], in_=pv, func=AF.Copy), reads=[Rp], writes=[R_xs[s]])
            pt, Rp = psum()
            pv = bfv(pt)
            for s in range(4):
                for g in range(2):
                    P.pe(lambda e, s=s, g=g, pv=pv: e.transpose(out=pv[:, s * 256 + g * 128:s * 256 + (g + 1) * 128],
                                                                in_=BT[:, g, s * 128:(s + 1) * 128], identity=ident[:]),
                         reads=[R_BT, R_ident], writes=[Rp])
            P.act(lambda e, pv=pv: e.activation(out=B_tm[:], in_=pv.rearrange("p (s d) -> p s d", s=4), func=AF.Copy),
                  reads=[Rp], writes=[R_Btm])

            pt, Rp = psum()
            for s in range(4):
                for kt in range(8):
                    P.pe(lambda e, s=s, kt=kt, pt=pt: e.matmul(pt[:, s * 16:(s + 1) * 16], lhsT=hT[:, kt, s * 128:(s + 1) * 128],
                                                               rhs=wdt[:, kt, :], start=(kt == 0), stop=(kt == 7)),
                         reads=[R_hT, R_wdt], writes=[Rp])
            P.dve(lambda e, pt=pt: e.tensor_tensor(out=dtr[:], in0=pt[:, 0:64].rearrange("p (s h) -> p s h", s=4),
                                                   in1=par[:, PC_DTB:PC_DTB + 16].unsqueeze(1).broadcast_to([128, 4, 16]), op=ALU.add),
                  reads=[Rp, R_par], writes=[R_dtr])
            P.act(lambda e: e.activation(out=dtr[:], in_=dtr[:], func=AF.Exp), reads=[R_dtr], writes=[R_dtr])
            P.act(lambda e: e.activation(out=dtr[:], in_=dtr[:], func=AF.Ln, bias=1.0), reads=[R_dtr], writes=[R_dtr])
            if is_pre:
                P.dve(lambda e: e.tensor_scalar(out=dtr[:], in0=dtr[:], scalar1=par[:, PC_MASK + pre_idx:PC_MASK + pre_idx + 1],
                                                scalar2=None, op0=ALU.mult), reads=[R_dtr, R_par], writes=[R_dtr])
            P.dve(lambda e: e.tensor_tensor(out=dtA[:], in0=dtr[:], in1=aneg.unsqueeze(1).broadcast_to([128, 4, 16]), op=ALU.mult),
                  reads=[R_dtr, R_cst], writes=[R_dtA])

            def sec_ssd():
                if is_pre:
                    ac = acs[0]; Rac = R_acs[0]
                    pa, Rpa = psum()
                    for j in range(4):
                        P.pe(lambda e, j=j, pa=pa: e.matmul(pa[:, j * 16:(j + 1) * 16], lhsT=U[:], rhs=dtA[:, j, :], start=True, stop=True),
                             reads=[R_U, R_dtA], writes=[Rpa])
                        P.pe(lambda e, j=j, pa=pa: e.matmul(pa[:, 64 + j * 16:64 + (j + 1) * 16], lhsT=ones[:, 0:128], rhs=dtA[:, j, :],
                                                            start=True, stop=True), reads=[R_ones, R_dtA], writes=[Rpa])
                    suf = acs[1]; Rsuf = R_acs[1]
                    P.dve(lambda e, pa=pa: e.tensor_copy(out=suf[:, 0:64], in_=pa[:, 64:128]), reads=[Rpa], writes=[Rsuf])
                    for j in (2, 1, 0):
                        P.dve(lambda e, j=j: e.tensor_tensor(out=suf[:, j * 16:(j + 1) * 16], in0=suf[:, j * 16:(j + 1) * 16],
                                                             in1=suf[:, (j + 1) * 16:(j + 2) * 16], op=ALU.add), reads=[Rsuf], writes=[Rsuf])
                    P.dve(lambda e, pa=pa: e.tensor_tensor(out=ac[:, 0:64], in0=suf[:, 0:64], in1=pa[:, 0:64], op=ALU.subtract),
                          reads=[Rpa, Rsuf], writes=[Rac])
                    P.act(lambda e: e.activation(out=ac[:, 0:64], in_=ac[:, 0:64], func=AF.Exp), reads=[Rac], writes=[Rac])
                    P.act(lambda e: e.activation(out=ac[:, 80:96], in_=suf[:, 0:16], func=AF.Exp), reads=[Rsuf, Rac], writes=[Rac])
                    P.dve(lambda e: e.tensor_tensor(out=ac[:, 0:64], in0=ac[:, 0:64], in1=dtr[:].rearrange("p s h -> p (s h)"), op=ALU.mult),
                          reads=[Rac, R_dtr], writes=[Rac])
                    for j in range(4):
                        P.dve(lambda e, j=j: e.tensor_tensor(out=zs[:, j, :].rearrange("p (h d) -> p h d", h=16),
                                                             in0=xs_tm[:, j, :].rearrange("p (h d) -> p h d", h=16),
                                                             in1=ac[:, j * 16:(j + 1) * 16].unsqueeze(2).broadcast_to([128, 16, 64]), op=ALU.mult),
                              reads=[R_xs[j], Rac], writes=[R_zs[j]])
                    pss = [psum(), psum()]
                    for g in range(2):
                        ptt, Rpp = pss[g]
                        for j in range(4):
                            P.pe(lambda e, g=g, j=j, ptt=ptt: e.matmul(ptt[:, 0:512], lhsT=B_tm[:, j, g * 128:(g + 1) * 128],
                                                                       rhs=zs[:, j, g * 512:(g + 1) * 512], start=(j == 0), stop=(j == 3)),
                                 reads=[R_Btm, R_zs[j]], writes=[Rpp])
                    P.dve(lambda e: e.tensor_tensor(out=Ssd.rearrange("p (h d) -> p h d", h=16),
                                                    in0=Ssd.rearrange("p (h d) -> p h d", h=16),
                                                    in1=ac[:, 80:96].unsqueeze(2).broadcast_to([128, 16, 64]), op=ALU.mult),
                          reads=[R_Ssd, Rac], writes=[R_Ssd])
                    for g in range(2):
                        P.dve(lambda e, g=g, ptt=pss[g][0]: e.tensor_tensor(out=Ssd[:, g * 512:(g + 1) * 512], in0=ptt[:, 0:512],
                                                                            in1=Ssd[:, g * 512:(g + 1) * 512], op=ALU.add),
                              reads=[pss[g][1], R_Ssd], writes=[R_Ssd])
                    P.act(lambda e: e.activation(out=Ssdb[:], in_=Ssd[:], func=AF.Copy), reads=[R_Ssd], writes=[R_Ssdb])
                for c in (range(0) if is_pre else range(4)):
                    ac, Rac = acs[c % 2], R_acs[c % 2]
                    pt, Rp = psum()
                    P.pe(lambda e, c=c, pt=pt: e.matmul(pt[:, 0:16], lhsT=U[:], rhs=dtA[:, c, :], start=True, stop=True),
                         reads=[R_U, R_dtA], writes=[Rp])
                    P.pe(lambda e, c=c, pt=pt: e.matmul(pt[:, 16:32], lhsT=ones[:, 0:128], rhs=dtA[:, c, :], start=True, stop=True),
                         reads=[R_ones, R_dtA], writes=[Rp])
                    P.dve(lambda e, pt=pt, ac=ac: e.tensor_copy(out=ac[:, 0:32], in_=pt[:, 0:32]), reads=[Rp], writes=[Rac])
                    P.dve(lambda e, ac=ac: e.tensor_tensor(out=ac[:, 48:64], in0=ac[:, 16:32], in1=ac[:, 0:16], op=ALU.subtract),
                          reads=[Rac], writes=[Rac])
                    P.act(lambda e, ac=ac: e.activation(out=ac[:, 32:48], in_=ac[:, 0:16], func=AF.Exp), reads=[Rac], writes=[Rac])
                    P.act(lambda e, ac=ac: e.activation(out=ac[:, 48:64], in_=ac[:, 48:64], func=AF.Exp), reads=[Rac], writes=[Rac])
                    P.act(lambda e, ac=ac: e.activation(out=ac[:, 64:80], in_=ac[:, 16:32], func=AF.Exp), reads=[Rac], writes=[Rac])
                    P.dve(lambda e, c=c: e.tensor_tensor(out=xdt.rearrange("p (h d) -> p h d", h=16),
                                                         in0=xs_tm[:, c, :].rearrange("p (h d) -> p h d", h=16),
                                                         in1=dtr[:, c, :].unsqueeze(2).broadcast_to([128, 16, 64]), op=ALU.mult),
                          reads=[R_xs[c], R_dtr], writes=[R_xdt])
                    if not is_pre:
                        pt, Rp = psum()
                        for g in range(2):
                            P.pe(lambda e, c=c, g=g, pt=pt: e.matmul(pt[:, g * 128:(g + 1) * 128], lhsT=BT[:, g, c * 128:(c + 1) * 128],
                                                                     rhs=CT[:, g, c * 128:(c + 1) * 128], start=True, stop=True),
                                 reads=[R_BT, R_CT], writes=[Rp])
                        P.dve(lambda e, pt=pt: e.tensor_tensor(out=cbm[:], in0=pt[:, 0:256].rearrange("p (g l) -> p g l", g=2),
                                                               in1=U[:].unsqueeze(1).broadcast_to([128, 2, 128]), op=ALU.mult),
                              reads=[Rp, R_U], writes=[R_cbm])
                        for hb in range(4):
                            Ls, RLs = Lseg[hb % 2], R_Lseg[hb % 2]
                            pt, Rp = psum()
                            for i in range(4):
                                h = hb * 4 + i
                                P.pe(lambda e, c=c, h=h, i=i, pt=pt: e.matmul(pt[:, i * 128:(i + 1) * 128],
                                                                              lhsT=dtA[:, c, h:h + 1].broadcast_to([128, 128]), rhs=U[:],
                                                                              start=True, stop=True), reads=[R_dtA, R_U], writes=[Rp])
                            P.dve(lambda e, pt=pt, hb=hb, ac=ac, Ls=Ls: e.tensor_tensor(
                                out=Ls.rearrange("p (h l) -> p h l", h=4), in0=pt[:, 0:512].rearrange("p (h l) -> p h l", h=4),
                                in1=ac[:, 4 * hb:4 * hb + 4].unsqueeze(2).broadcast_to([128, 4, 128]), op=ALU.subtract),
                                reads=[Rp, Rac], writes=[RLs])
                            P.dve(lambda e, Ls=Ls: e.tensor_scalar(out=Ls, in0=Ls, scalar1=0.0, scalar2=None, op0=ALU.min),
                                  reads=[RLs], writes=[RLs])
                            P.act(lambda e, Ls=Ls: e.activation(out=Ls, in_=Ls, func=AF.Exp), reads=[RLs], writes=[RLs])
                            g = hb // 2
                            P.pool(lambda e, hb=hb, g=g, Ls=Ls: e.tensor_tensor(
                                out=MT[:, 4 * hb:4 * hb + 4, :], in0=Ls.rearrange("p (h l) -> p h l", h=4),
                                in1=cbm[:, g, :].unsqueeze(1).broadcast_to([128, 4, 128]), op=ALU.mult),
                                reads=[RLs, R_cbm], writes=[R_MT])
                        py = [psum(), psum()]
                        for h in range(16):
                            ptt, Rpp = py[h // 8]
                            P.pe(lambda e, h=h, ptt=ptt: e.matmul(ptt[:, (h % 8) * 64:(h % 8 + 1) * 64], lhsT=MT[:, h, :],
                                                                  rhs=xdt[:, h * 64:(h + 1) * 64], start=True, stop=True),
                                 reads=[R_MT, R_xdt], writes=[Rpp])
                        po = [psum(), psum()]
                        for g in range(2):
                            ptt, Rpp = po[g]
                            P.pe(lambda e, g=g, c=c, ptt=ptt: e.matmul(ptt[:, 0:512], lhsT=CT[:, g, c * 128:(c + 1) * 128],
                                                                       rhs=Ssdb[:, g * 512:(g + 1) * 512], start=True, stop=True),
                                 reads=[R_CT, R_Ssdb], writes=[Rpp])
                        for g in range(2):
                            P.dve(lambda e, g=g, ac=ac, ptt=po[g][0]: e.tensor_tensor(
                                out=t1[:, g * 512:(g + 1) * 512].rearrange("p (h d) -> p h d", h=8),
                                in0=ptt[:, 0:512].rearrange("p (h d) -> p h d", h=8),
                                in1=ac[:, 32 + 8 * g:40 + 8 * g].unsqueeze(2).broadcast_to([128, 8, 64]), op=ALU.mult),
                                reads=[po[g][1], Rac], writes=[R_t1])
                        for g in range(2):
                            P.dve(lambda e, g=g, ptt=py[g][0]: e.tensor_tensor(out=t1[:, g * 512:(g + 1) * 512], in0=ptt[:, 0:512],
                                                                               in1=t1[:, g * 512:(g + 1) * 512], op=ALU.add),
                                  reads=[py[g][1], R_t1], writes=[R_t1])
                        P.pool(lambda e, c=c: e.tensor_tensor(out=t3.rearrange("p (h d) -> p h d", h=16),
                                                              in0=xs_tm[:, c, :].rearrange("p (h d) -> p h d", h=16),
                                                              in1=par[:, PC_DSK:PC_DSK + 16].unsqueeze(2).broadcast_to([128, 16, 64]), op=ALU.mult),
                               reads=[R_xs[c], R_par], writes=[R_t3])
                        P.pool(lambda e: e.tensor_tensor(out=t1, in0=t1, in1=t3, op=ALU.add), reads=[R_t1, R_t3], writes=[R_t1])
                        P.dve(lambda e, c=c: e.tensor_tensor(out=t3, in0=t1, in1=zs[:, c, :], op=ALU.mult),
                              reads=[R_t1, R_zs[c], R_t3], writes=[R_t3])
                        P.pool(lambda e: e.memset(sst[:, 0:2], 0.0), writes=[R_sst])
                        for g in range(2):
                            P.act(lambda e, g=g: e.activation(out=junk[:, 0:512], in_=t3[:, g * 512:(g + 1) * 512], func=AF.Square,
                                                              accum_out=sst[:, g:g + 1]), reads=[R_t3, R_sst], writes=[R_junk, R_sst])
                        rstd_from_ss(sst[:, 0:2], 2, R_sst, 1.0 / 512)
                        for g in range(2):
                            P.dve(lambda e, g=g: e.tensor_scalar(out=yn[:, g * 512:(g + 1) * 512], in0=t3[:, g * 512:(g + 1) * 512],
                                                                 scalar1=sst[:, g:g + 1], scalar2=None, op0=ALU.mult),
                                  reads=[R_t3, R_sst], writes=[R_yn])
                        pt, Rp = psum()
                        pv = bfv(pt)
                        for kt in range(8):
                            P.pe(lambda e, kt=kt, pv=pv: e.transpose(out=pv[:, kt * 128:(kt + 1) * 128], in_=yn[:, kt * 128:(kt + 1) * 128],
                                                                     identity=ident[:]), reads=[R_yn, R_ident], writes=[Rp])
                        P.act(lambda e, c=c, pv=pv: e.activation(out=mixedT[:, 0:8, c * 128:(c + 1) * 128],
                                                                 in_=pv.rearrange("p (k t) -> p k t", k=8), func=AF.Copy),
                              reads=[Rp], writes=[R_mixT[c]])
                    P.dve(lambda e, ac=ac: e.tensor_tensor(out=xdtd.rearrange("p (h d) -> p h d", h=16),
                                                           in0=xdt.rearrange("p (h d) -> p h d", h=16),
                                                           in1=ac[:, 48:64].unsqueeze(2).broadcast_to([128, 16, 64]), op=ALU.mult),
                          reads=[R_xdt, Rac], writes=[R_xdtd])
                    pss = [psum(), psum()]
                    for g in range(2):
                        ptt, Rpp = pss[g]
                        P.pe(lambda e, g=g, c=c, ptt=ptt: e.matmul(ptt[:, 0:512], lhsT=B_tm[:, c, g * 128:(g + 1) * 128],
                                                                   rhs=xdtd[:, g * 512:(g + 1) * 512], start=True, stop=True),
                             reads=[R_Btm, R_xdtd], writes=[Rpp])
                    P.dve(lambda e, ac=ac: e.tensor_tensor(out=Ssd.rearrange("p (h d) -> p h d", h=16),
                                                           in0=Ssd.rearrange("p (h d) -> p h d", h=16),
                                                           in1=ac[:, 64:80].unsqueeze(2).broadcast_to([128, 16, 64]), op=ALU.mult),
                          reads=[R_Ssd, Rac], writes=[R_Ssd])
                    for g in range(2):
                        P.dve(lambda e, g=g, ptt=pss[g][0]: e.tensor_tensor(out=Ssd[:, g * 512:(g + 1) * 512], in0=ptt[:, 0:512],
                                                                            in1=Ssd[:, g * 512:(g + 1) * 512], op=ALU.add),
                              reads=[pss[g][1], R_Ssd], writes=[R_Ssd])
                    P.act(lambda e: e.activation(out=Ssdb[:], in_=Ssd[:], func=AF.Copy), reads=[R_Ssd], writes=[R_Ssdb])

            if is_pre:
                mcol = par[:, PC_MASK + pre_idx:PC_MASK + pre_idx + 1]
                P.dve(lambda e: e.tensor_scalar(out=hcst[:, 24:32], in0=hcst[:, 0:8], scalar1=mcol, scalar2=None, op0=ALU.mult),
                      reads=[R_hcst, R_par, R_hm], writes=[R_hm])
                P.dve(lambda e: e.tensor_scalar(out=hcst[:, 32:40], in0=hcst[:, 16:24], scalar1=mcol, scalar2=None, op0=ALU.mult),
                      reads=[R_hcst, R_par, R_hm], writes=[R_hm])
            hw = {}

            def getH(a):
                if a not in hw:
                    sid = getattr(tl, "sid", None)
                    tl.sid = None
                    try:
                        hw[a] = wget("H%d" % a)
                    finally:
                        tl.sid = sid
                return hw[a]

            def sec_hg_head(i):
                b_ = bt[i]
                Rb = R_bt[i]
                sc = stat[:, 40 + 8 * i:40 + 8 * i + 8]
                Rsc = R_scp[i]
                e0, e1, e2, e3 = etmp[4 * i:4 * i + 4]
                Re0, Re1, Re2, Re3 = R_et[4 * i:4 * i + 4]
                for a in range(4):
                    h = 2 * a + i
                    wt, Rw = getH(a)
                    if not is_pre:
                        pt, Rp = proj_fm(wt, Rw, i, hT, R_hT)
                        P.act(lambda e, pt=pt: e.activation(out=qf[:, i, :], in_=pt[:, 0:512], func=AF.Silu), reads=[Rp], writes=[R_qf2[i]])
                    pt, Rp = proj_fm(wt, Rw, 2 + i, hT, R_hT)
                    P.act(lambda e, pt=pt: e.activation(out=kf[:, i, :], in_=pt[:, 0:512], func=AF.Tanh, scale=0.5), reads=[Rp], writes=[R_kf2[i]])
                    P.dve(lambda e, h=h: e.tensor_scalar(out=gl[:, i, :], in0=kf[:, i, :], scalar1=hcst[:, h:h + 1], scalar2=hcst[:, 8 + h:9 + h],
                                                         op0=ALU.mult, op1=ALU.add), reads=[R_kf2[i], R_hcst], writes=[R_gl2[i]])
                    P.act(lambda e: e.activation(out=gl[:, i, :], in_=gl[:, i, :], func=AF.Ln), reads=[R_gl2[i]], writes=[R_gl2[i]])
                    if is_pre:
                        P.dve(lambda e, h=h: e.tensor_scalar(out=kf[:, i, :], in0=kf[:, i, :], scalar1=hcst[:, 32 + h:33 + h], scalar2=hcst[:, 24 + h:25 + h],
                                                             op0=ALU.mult, op1=ALU.add), reads=[R_kf2[i], R_hm], writes=[R_kf2[i]])
                    else:
                        P.dve(lambda e, h=h: e.tensor_scalar(out=kf[:, i, :], in0=kf[:, i, :], scalar1=hcst[:, 16 + h:17 + h], scalar2=hcst[:, h:h + 1],
                                                             op0=ALU.mult, op1=ALU.add), reads=[R_kf2[i], R_hcst], writes=[R_kf2[i]])
                    P.pool(lambda e: e.memset(b_[:, 0:1], 0.0), writes=[Rb])
                    P.dve(lambda e: e.tensor_tensor_scan(out=b_[:, 1:513], data0=ones[:, 0:512], data1=gl[:, i, :], initial=0.0,
                                                         op0=ALU.mult, op1=ALU.add), reads=[R_ones, R_gl2[i], Rb], writes=[Rb])
                    if is_pre:
                        khf = pre_kh[i]; khtmf = pre_khtm[i]
                        P.act(lambda e: e.activation(out=gl[:, i, :], in_=b_[:, 1:513], func=AF.Exp, bias=b_[:, 512:513], scale=-1.0),
                              reads=[Rb, R_gl2[i]], writes=[R_gl2[i]])
                        P.dve(lambda e, khf=khf: e.tensor_tensor(out=khf, in0=kf[:, i, :], in1=gl[:, i, :], op=ALU.mult),
                              reads=[R_kf2[i], R_gl2[i]], writes=[R_prekh[i]])
                        P.act(lambda e: e.activation(out=sc[:, 3:4], in_=b_[:, 512:513], func=AF.Exp), reads=[Rb, Rsc], writes=[Rsc])
                        pt, Rp = psum()
                        pv = bfv(pt)
                        for j in range(4):
                            P.pe(lambda e, j=j, pv=pv, khf=khf: e.transpose(out=pv[:, j * 128:(j + 1) * 128], in_=khf[:, j * 128:(j + 1) * 128],
                                                                            identity=ident[:]), reads=[R_prekh[i], R_ident], writes=[Rp])
                        P.act(lambda e, pv=pv, khtmf=khtmf: e.activation(out=khtmf, in_=pv[:, 0:512], func=AF.Copy), reads=[Rp], writes=[R_prekhtm[i]])
                        pt2, Rp2 = psum()
                        for j in range(4):
                            P.pe(lambda e, j=j, h=h, pt2=pt2, khtmf=khtmf: e.matmul(pt2[:, 0:128], lhsT=khtmf[:, j * 128:(j + 1) * 128],
                                                                                    rhs=vt[:, j, h * 128:(h + 1) * 128], start=(j == 0), stop=(j == 3)),
                                 reads=[R_prekhtm[i], R_vt[j]], writes=[Rp2])
                        P.dve(lambda e, h=h, pt2=pt2: e.scalar_tensor_tensor(out=Shg[:, h, :], in0=Shg[:, h, :], scalar=sc[:, 3:4],
                                                                            in1=pt2[:, 0:128], op0=ALU.mult, op1=ALU.add),
                              reads=[R_Shg[h], Rsc, Rp2], writes=[R_Shg[h]])
                        P.act(lambda e, h=h: e.activation(out=Shgb[:, h, :], in_=Shg[:, h, :], func=AF.Copy), reads=[R_Shg[h]], writes=[R_Shgb[h]])
                        continue
                    for c in range(4):
                        pso = psum_ded()
                        P.pool(lambda e: e.memset(sst[:, 8 + i:9 + i], 0.0), writes=[R_ssth2[i]])
                        c0 = c * 128
                        bseg = b_[:, c0 + 1:c0 + 129]
                        blast = b_[:, c0 + 128:c0 + 129]
                        bprev = b_[:, c0:c0 + 1]
                        b31 = b_[:, c0 + 32:c0 + 33]
                        b63 = b_[:, c0 + 64:c0 + 65]
                        b95 = b_[:, c0 + 96:c0 + 97]
                        P.dve(lambda e, b31=b31: e.tensor_scalar(out=sc[:, 0:1], in0=b31, scalar1=-1.0, scalar2=None, op0=ALU.mult),
                              reads=[Rb, Rsc], writes=[Rsc])
                        P.dve(lambda e, b95=b95: e.tensor_scalar(out=sc[:, 1:2], in0=b95, scalar1=-1.0, scalar2=None, op0=ALU.mult),
                              reads=[Rb, Rsc], writes=[Rsc])
                        P.dve(lambda e, bprev=bprev: e.tensor_scalar(out=sc[:, 2:3], in0=bprev, scalar1=-1.0, scalar2=None, op0=ALU.mult),
                              reads=[Rb, Rsc], writes=[Rsc])
                        P.dve(lambda e, bprev=bprev, blast=blast: e.tensor_tensor(out=sc[:, 3:4], in0=blast, in1=bprev, op=ALU.subtract),
                              reads=[Rb, Rsc], writes=[Rsc])
                        P.dve(lambda e, b63=b63, b31=b31: e.tensor_tensor(out=sc[:, 4:5], in0=b63, in1=b31, op=ALU.subtract),
                              reads=[Rb, Rsc], writes=[Rsc])
                        P.dve(lambda e, b63=b63, b95=b95: e.tensor_tensor(out=sc[:, 5:6], in0=b95, in1=b63, op=ALU.subtract),
                              reads=[Rb, Rsc], writes=[Rsc])
                        P.act(lambda e: e.activation(out=sc[:, 3:6], in_=sc[:, 3:6], func=AF.Exp), reads=[Rsc], writes=[Rsc])
                        kfc = kf[:, i, c0:c0 + 128]
                        qfc = qf[:, i, c0:c0 + 128]
                        P.act(lambda e, bseg=bseg: e.activation(out=e0[:, 0:64], in_=bseg[:, 0:64], func=AF.Exp, bias=sc[:, 0:1], scale=1.0),
                              reads=[Rb, Rsc], writes=[Re0])
                        P.act(lambda e, bseg=bseg: e.activation(out=e0[:, 64:128], in_=bseg[:, 64:128], func=AF.Exp, bias=sc[:, 1:2], scale=1.0),
                              reads=[Rb, Rsc], writes=[Re0])
                        P.dve(lambda e, qfc=qfc: e.tensor_tensor(out=qt_[i], in0=qfc, in1=e0, op=ALU.mult),
                              reads=[R_qf2[i], Re0], writes=[R_qt[i]])
                        P.dve(lambda e: e.tensor_scalar(out=QC[i], in0=qt_[i][:, 64:128], scalar1=sc[:, 5:6], scalar2=None, op0=ALU.mult),
                              reads=[R_qt[i], Rsc], writes=[R_QC[i]])
                        P.act(lambda e, bseg=bseg, b31=b31: e.activation(out=e1[:, 0:64], in_=bseg[:, 0:64], func=AF.Exp, bias=b31, scale=-1.0),
                              reads=[Rb], writes=[Re1])
                        P.act(lambda e, bseg=bseg, b95=b95: e.activation(out=e1[:, 64:128], in_=bseg[:, 64:128], func=AF.Exp, bias=b95, scale=-1.0),
                              reads=[Rb], writes=[Re1])
                        P.dve(lambda e, kfc=kfc: e.tensor_tensor(out=KA[i][:, 0:64], in0=kfc[:, 0:64], in1=e1[:, 0:64], op=ALU.mult),
                              reads=[R_kf2[i], Re1], writes=[R_KA[i]])
                        P.dve(lambda e, kfc=kfc: e.tensor_tensor(out=KB[i][:, 64:128], in0=kfc[:, 64:128], in1=e1[:, 64:128], op=ALU.mult),
                              reads=[R_kf2[i], Re1], writes=[R_KB[i]])
                        P.dve(lambda e: e.tensor_scalar(out=KC[i][:, 0:64], in0=KA[i][:, 0:64], scalar1=sc[:, 4:5], scalar2=None, op0=ALU.mult),
                              reads=[R_KA[i], Rsc], writes=[R_KC[i]])
                        P.act(lambda e, bseg=bseg: e.activation(out=e2, in_=bseg, func=AF.Exp, bias=sc[:, 2:3], scale=1.0),
                              reads=[Rb, Rsc], writes=[Re2])
                        P.dve(lambda e, qfc=qfc: e.tensor_tensor(out=qh[i], in0=qfc, in1=e2, op=ALU.mult),
                              reads=[R_qf2[i], Re2], writes=[R_qh[i]])
                        P.act(lambda e, bseg=bseg, blast=blast: e.activation(out=e3, in_=bseg, func=AF.Exp, bias=blast, scale=-1.0),
                              reads=[Rb], writes=[Re3])
                        P.dve(lambda e, kfc=kfc: e.tensor_tensor(out=kh[i], in0=kfc, in1=e3, op=ALU.mult),
                              reads=[R_kf2[i], Re3], writes=[R_kh[i]])
                        vch = vt[:, c, h * 128:(h + 1) * 128]
                        pt, Rp = psum()
                        P.pe(lambda e, pt=pt: e.matmul(pt[:, 0:64], lhsT=KA[i], rhs=qt_[i][:, 0:64], start=True, stop=True),
                             reads=[R_KA[i], R_qt[i]], writes=[Rp])
                        P.pe(lambda e, pt=pt: e.matmul(pt[:, 64:128], lhsT=KB[i], rhs=qt_[i][:, 64:128], start=True, stop=False),
                             reads=[R_KB[i], R_qt[i]], writes=[Rp])
                        P.pe(lambda e, pt=pt: e.matmul(pt[:, 64:128], lhsT=KC[i], rhs=QC[i], start=False, stop=True),
                             reads=[R_KC[i], R_QC[i]], writes=[Rp])
                        P.dve(lambda e, pt=pt: e.tensor_tensor(out=attm[i], in0=pt[:, 0:128], in1=U[:], op=ALU.mult),
                              reads=[Rp, R_U], writes=[R_attm[i]])
                        P.pe(lambda e, vch=vch, ptt=pso[0]: e.matmul(ptt[:, 0:128], lhsT=attm[i], rhs=vch, start=True, stop=False),
                             reads=[R_attm[i], R_vt[c]], writes=[pso[1]])
                        P.pe(lambda e, h=h, ptt=pso[0]: e.matmul(ptt[:, 0:128], lhsT=qh[i], rhs=Shgb[:, h, :], start=False, stop=True),
                             reads=[R_qh[i], R_Shgb[h]], writes=[pso[1]])
                        pt, Rp = psum()
                        pv = bfv(pt)
                        P.pe(lambda e, pv=pv: e.transpose(out=pv[:, 0:128], in_=kh[i], identity=ident[:]),
                             reads=[R_kh[i], R_ident], writes=[Rp])
                        P.act(lambda e, pv=pv: e.activation(out=khtm[i], in_=pv[:, 0:128], func=AF.Copy), reads=[Rp], writes=[R_khtm[i]])
                        pt2, Rp2 = psum()
                        P.pe(lambda e, vch=vch, pt2=pt2: e.matmul(pt2[:, 0:128], lhsT=khtm[i], rhs=vch, start=True, stop=True),
                             reads=[R_khtm[i], R_vt[c]], writes=[Rp2])
                        P.dve(lambda e, h=h, pt2=pt2: e.scalar_tensor_tensor(out=Shg[:, h, :], in0=Shg[:, h, :], scalar=sc[:, 3:4],
                                                                            in1=pt2[:, 0:128], op0=ALU.mult, op1=ALU.add),
                              reads=[R_Shg[h], Rsc, Rp2], writes=[R_Shg[h]])
                        P.act(lambda e, h=h: e.activation(out=Shgb[:, h, :], in_=Shg[:, h, :], func=AF.Copy), reads=[R_Shg[h]], writes=[R_Shgb[h]])
                        ptt, Rpp = pso
                        P.act(lambda e, ptt=ptt: e.activation(out=junk[:, 512 + 128 * i:640 + 128 * i], in_=ptt[:, 0:128], func=AF.Square,
                                                              accum_out=sst[:, 8 + i:9 + i]), reads=[Rpp, R_ssth2[i]], writes=[R_junkh2[i], R_ssth2[i]])
                        rstd_from_ss(sst[:, 8 + i:9 + i], 1, R_ssth2[i], 1.0 / 128)
                        ot = otmp[:, 128 * i:128 * i + 128]
                        ogi = og[:, 128 * i:128 * i + 128]
                        P.dve(lambda e, ptt=ptt, ot=ot: e.tensor_scalar(out=ot, in0=ptt[:, 0:128], scalar1=sst[:, 8 + i:9 + i], scalar2=None, op0=ALU.mult),
                              reads=[Rpp, R_ssth2[i]], writes=[R_otmp2[i]])
                        P.dve(lambda e, h=h, c=c, ot=ot, ogi=ogi: e.tensor_tensor(out=ogi, in0=ot, in1=gs[:, c, 128 * h:128 * h + 128], op=ALU.mult),
                              reads=[R_otmp2[i], R_gs[c]], writes=[R_og2[i]])
                        pt, Rp = psum()
                        pv = bfv(pt)
                        P.pe(lambda e, pv=pv, ogi=ogi: e.transpose(out=pv[:, 0:128], in_=ogi, identity=ident[:]), reads=[R_og2[i], R_ident], writes=[Rp])
                        P.act(lambda e, h=h, c=c, pv=pv: e.activation(out=mixedT[:, 8 + h, c * 128:(c + 1) * 128], in_=pv[:, 0:128], func=AF.Copy),
                              reads=[Rp], writes=[R_mixTh2[i][c]])
            run_interleaved([sec_ssd, lambda: sec_hg_head(0), lambda: sec_hg_head(1)],
                            [{'banks': [0, 1, 2, 3], 'ctr': 0, 'ded': None}, {'banks': [5], 'ctr': 0, 'ded': 4}, {'banks': [7], 'ctr': 0, 'ded': 6}])
            if is_pre:
                P.pool(lambda e: e.memset(qf[:, 0, 0:1], 0.0), reads=R_prekh + R_prekhtm, writes=R_qf2 + R_prekh + R_prekhtm)
                return
            P.barrier()

            A = Arena()
            qT = A.bf(8 * 512).rearrange("p (k t) -> p k t", k=8); R_qT = Res()
            pe_ = [A.f32(1024).rearrange("p (h k) -> p h k", h=4) for _ in range(2)]; R_pe = [Res(), Res()]
            pn = [A.bf(1024).rearrange("p (h k) -> p h k", h=4) for _ in range(2)]; R_pn = [Res(), Res()]
            prT = A.bf(2 * 4 * 512).rearrange("p (k h t) -> p k h t", k=2, h=4); R_prT = Res()
            oT = A.bf(8 * 512).rearrange("p (k t) -> p k t", k=8); R_oT = Res()
            actT = A.bf(22 * 512).rearrange("p (k t) -> p k t", k=22); R_actT = Res()
            sgt = [A.f32(512) for _ in range(2)]; R_sgt = [Res(), Res()]
            nfw = A.f32(D); R_nfw = Res()
            sa = A.f32(32); R_sa = [Res(), Res()]
            P.dma("sp", "nfw", nfw, nfwd, writes=[R_nfw])

            for j in range(2):
                banks = [psum() for _ in range(4)]
                for i in range(2):
                    wt, Rw = wget("OUT%d%d" % (j, i))
                    for s in range(4):
                        ptt, Rpp = banks[s]
                        for kt in range(8):
                            P.pe(lambda e, kt=kt, s=s, i=i, ptt=ptt, wt=wt: e.matmul(
                                ptt[:, 0:512], lhsT=mixedT[:, 8 * i + kt, s * 128:(s + 1) * 128], rhs=wt[:, kt, :],
                                start=(i == 0 and kt == 0), stop=(i == 1 and kt == 7)), reads=[R_mixT[s], R_mixTh2[0][s], R_mixTh2[1][s], Rw], writes=[Rpp])
                for s in range(4):
                    ptt, Rpp = banks[s]
                    P.dve(lambda e, s=s, j=j, ptt=ptt: e.tensor_tensor(out=x_tm[:, s, j * 512:(j + 1) * 512], in0=ptt[:, 0:512],
                                                                       in1=x_tm[:, s, j * 512:(j + 1) * 512], op=ALU.add),
                          reads=[Rpp, R_x[s]], writes=[R_x[s]])
            rms_T(lambda s: x_tm[:, s, :], R_x, 4, hT, R_hT)
            for jc in range(2):
                wt, Rw = wget("Q%d" % jc)
                for j in range(4):
                    pt, Rp = proj_fm(wt, Rw, j, hT, R_hT)
                    P.act(lambda e, pt=pt, jc=jc, j=j: e.activation(out=qT[:, 4 * jc + j, :], in_=pt[:, 0:512], func=AF.Copy),
                          reads=[Rp], writes=[R_qT])
            for s in range(4):
                b = s % 2
                scb = [psum(), psum()]
                for h in range(4):
                    ptt, Rpp = scb[h // 2]
                    for d2 in range(2):
                        P.pe(lambda e, h=h, d2=d2, s=s, ptt=ptt: e.matmul(ptt[:, (h % 2) * 256:(h % 2) * 256 + 256],
                                                                          lhsT=qT[:, 2 * h + d2, s * 128:(s + 1) * 128], rhs=kmT[:, 2 * h + d2, :],
                                                                          start=(d2 == 0), stop=(d2 == 1)), reads=[R_qT, R_kmT], writes=[Rpp])
                sav = sa[:, 16 * b:16 * b + 16]
                Rsa = R_sa[b]
                for hb in range(2):
                    P.dve(lambda e, hb=hb, sav=sav, ptt=scb[hb][0]: e.tensor_reduce(out=sav[:, 2 * hb:2 * hb + 2],
                                                                                   in_=ptt[:, 0:512].rearrange("p (h k) -> p h k", h=2),
                                                                                   axis=AX.X, op=ALU.max), reads=[scb[hb][1], Rsa], writes=[Rsa])
                P.dve(lambda e, sav=sav: e.tensor_scalar(out=sav[:, 0:4], in0=sav[:, 0:4], scalar1=-1.0 / 16, scalar2=None, op0=ALU.mult),
                      reads=[Rsa], writes=[Rsa])
                P.pool(lambda e, sav=sav: e.memset(sav[:, 4:8], 0.0), reads=[Rsa], writes=[Rsa])
                for h in range(4):
                    ptt, Rpp = scb[h // 2]
                    P.act(lambda e, h=h, b=b, sav=sav, ptt=ptt: e.activation(out=pe_[b][:, h, :], in_=ptt[:, (h % 2) * 256:(h % 2) * 256 + 256],
                                                                             func=AF.Exp, bias=sav[:, h:h + 1], scale=1.0 / 16,
                                                                             accum_out=sav[:, 4 + h:5 + h]), reads=[Rpp, Rsa], writes=[R_pe[b], Rsa])
                P.dve(lambda e, sav=sav: e.reciprocal(out=sav[:, 4:8], in_=sav[:, 4:8]), reads=[Rsa], writes=[Rsa])
                P.dve(lambda e, b=b, sav=sav: e.tensor_tensor(out=pn[b], in0=pe_[b], in1=sav[:, 4:8].unsqueeze(2).broadcast_to([128, 4, 256]),
                                                              op=ALU.mult), reads=[R_pe[b], Rsa], writes=[R_pn[b]])
                pt, Rp = psum()
                pv = bfv(pt)
                for k2 in range(2):
                    for h in range(4):
                        P.pe(lambda e, k2=k2, h=h, b=b, pv=pv: e.transpose(out=pv[:, (k2 * 4 + h) * 128:(k2 * 4 + h + 1) * 128],
                                                                          in_=pn[b][:, h, k2 * 128:(k2 + 1) * 128], identity=ident[:]),
                             reads=[R_pn[b], R_ident], writes=[Rp])
                for k2 in range(2):
                    P.act(lambda e, s=s, k2=k2, pv=pv: e.activation(out=prT[:, k2, :, s * 128:(s + 1) * 128],
                                                                    in_=pv[:, k2 * 512:(k2 + 1) * 512].rearrange("p (h t) -> p h t", h=4),
                                                                    func=AF.Copy), reads=[Rp], writes=[R_prT])
            for h in range(4):
                for d2 in range(2):
                    pt, Rp = psum()
                    for k2 in range(2):
                        P.pe(lambda e, h=h, d2=d2, k2=k2, pt=pt: e.matmul(pt[:, 0:512], lhsT=vm[:, k2, h * 256 + d2 * 128:h * 256 + (d2 + 1) * 128],
                                                                          rhs=prT[:, k2, h, :], start=(k2 == 0), stop=(k2 == 1)),
                             reads=[R_vm, R_prT], writes=[Rp])
                    P.act(lambda e, h=h, d2=d2, pt=pt: e.activation(out=oT[:, 2 * h + d2, :], in_=pt[:, 0:512], func=AF.Copy),
                          reads=[Rp], writes=[R_oT])
            for j in range(2):
                wt, Rw = wget("O%d" % j)
                for s in range(4):
                    pt, Rp = proj_tm(wt, Rw, s, oT, R_oT)
                    P.dve(lambda e, s=s, j=j, pt=pt: e.tensor_tensor(out=x_tm[:, s, j * 512:(j + 1) * 512], in0=pt[:, 0:512],
                                                                     in1=x_tm[:, s, j * 512:(j + 1) * 512], op=ALU.add),
                          reads=[Rp, R_x[s]], writes=[R_x[s]])
            rms_T(lambda s: x_tm[:, s, :], R_x, 4, hT, R_hT)
            for jc in range(11):
                wt, Rw = wget("GU%d" % jc)
                for jj in range(2):
                    pg, Rpg = proj_fm(wt, Rw, jj, hT, R_hT)
                    pu, Rpu = proj_fm(wt, Rw, 2 + jj, hT, R_hT)
                    sg, Rsg = sgt[jj], R_sgt[jj]
                    P.act(lambda e, pg=pg, sg=sg: e.activation(out=sg, in_=pg[:, 0:512], func=AF.Silu), reads=[Rpg], writes=[Rsg])
                    P.dve(lambda e, pu=pu, sg=sg, jc=jc, jj=jj: e.tensor_tensor(out=actT[:, 2 * jc + jj, :], in0=pu[:, 0:512], in1=sg, op=ALU.mult),
                          reads=[Rpu, Rsg], writes=[R_actT])
            for j in range(2):
                banks = [psum() for _ in range(4)]
                for i in range(3):
                    wt, Rw = wget("D%d%d" % (j, i))
                    nk = 8 if i < 2 else 6
                    for s in range(4):
                        ptt, Rpp = banks[s]
                        for kt in range(nk):
                            P.pe(lambda e, kt=kt, s=s, i=i, nk=nk, ptt=ptt, wt=wt: e.matmul(
                                ptt[:, 0:512], lhsT=actT[:, 8 * i + kt, s * 128:(s + 1) * 128], rhs=wt[:, kt, :],
                                start=(i == 0 and kt == 0), stop=(i == 2 and kt == nk - 1)), reads=[R_actT, Rw], writes=[Rpp])
                for s in range(4):
                    ptt, Rpp = banks[s]
                    P.dve(lambda e, s=s, j=j, ptt=ptt: e.tensor_tensor(out=x_tm[:, s, j * 512:(j + 1) * 512], in0=ptt[:, 0:512],
                                                                       in1=x_tm[:, s, j * 512:(j + 1) * 512], op=ALU.add),
                          reads=[Rpp, R_x[s]], writes=[R_x[s]])
            ssf = stat[:, 32:36]
            Rsf = R_ssf
            P.pool(lambda e: e.memset(ssf, 0.0), writes=[Rsf])
            for s in range(4):
                P.act(lambda e, s=s: e.activation(out=junk[:], in_=x_tm[:, s, :], func=AF.Square, accum_out=ssf[:, s:s + 1]),
                      reads=[R_x[s], Rsf], writes=[R_junk, R_junkh2[0], R_junkh2[1], Rsf])
            rstd_from_ss(ssf, 4, Rsf, 1.0 / D)
            for s in range(4):
                b = 0
                P.dve(lambda e, s=s, b=b: e.scalar_tensor_tensor(out=ost[b][:], in0=x_tm[:, s, :], scalar=ssf[:, s:s + 1], in1=nfw,
                                                                 op0=ALU.mult, op1=ALU.mult), reads=[R_x[s], Rsf, R_nfw], writes=[R_ost[b]])
                out_dmas.append(P.dma("sp", "ost%d" % b, outd[out_row0 + s * 128:out_row0 + (s + 1) * 128, :], ost[b][:],
                                      reads=[R_ost[b]]))
            P.barrier()

        for t in range(NPRE):
            do_tile(xp, t * 512, True, t, 0)
        for t in range(NT):
            do_tile(xm, t * 512, False, 0, t * 512)
        assert wstate["got"] == len(wseq), (wstate["got"], len(wseq))
        P.emit(final_wait_ops=out_dmas)
    return nc


def make_par(inp, NPRE, premask):
    f = lambda a: np.asarray(a, dtype=np.float32)
    par = np.zeros((128, PC_MASK + max(NPRE, 1)), np.float32)
    cw = f(inp["conv_w"])[0]
    par[:, PC_CW:PC_CW + 48] = cw.reshape(4, 12, 128).transpose(2, 1, 0).reshape(128, 48)
    par[:, PC_CB:PC_CB + 12] = f(inp["conv_b"])[0].reshape(12, 128).T
    par[:, PC_DTB:PC_DTB + 16] = f(inp["dt_bias"])[0][None, :]
    par[:, PC_ALOG:PC_ALOG + 16] = f(inp["a_log"])[0][None, :]
    par[:, PC_DSK:PC_DSK + 16] = f(inp["d_skip"])[0][None, :]
    hlb = f(inp["hg_lower_bounds"])
    par[:, PC_HLB0:PC_HLB0 + 8] = hlb[0].reshape(8, 128).T
    par[:, PC_HLB1:PC_HLB1 + 8] = hlb[1].reshape(8, 128).T
    par[:, PC_MIX:PC_MIX + 8] = f(inp["norm_mix_w"])[0].reshape(8, 128).T
    par[:, PC_XA:PC_XA + 8] = f(inp["norm_xa_w"])[0].reshape(8, 128).T
    par[:, PC_MEM:PC_MEM + 8] = f(inp["norm_mem_w"])[0].reshape(8, 128).T
    par[:, PC_FFN:PC_FFN + 8] = f(inp["norm_ffn_w"])[0].reshape(8, 128).T
    par[:, PC_WOUT:PC_WOUT + 8] = f(inp["ssd_norm_w"])[0].reshape(8, 128).T
    par[:, PC_WOUT + 8:PC_WOUT + 16] = f(inp["hg_norm_w"])[0][:, None]
    par[:, PC_MASK:PC_MASK + len(premask)] = np.asarray(premask, np.float32)[None, :]
    return par


_NC_CACHE = {}


def run(inp, T, NPRE, nseg):
    x = np.asarray(inp["x"], np.float32)
    mem = np.asarray(inp["mem"], np.float32)
    B, L, _ = x.shape
    assert L == nseg * T
    key = (T, NPRE)
    if key not in _NC_CACHE:
        _NC_CACHE[key] = build(T, NPRE)
    nc = _NC_CACHE[key]
    shared = {
        "w_in": np.ascontiguousarray(inp["w_in"][0], np.float32), "w_out": np.ascontiguousarray(inp["w_out"][0], np.float32),
        "wq": np.ascontiguousarray(inp["xa_wq"][0], np.float32), "wkv": np.ascontiguousarray(inp["xa_wkv"][0], np.float32),
        "wo": np.ascontiguousarray(inp["xa_wo"][0], np.float32), "wg": np.ascontiguousarray(inp["ffn_w_gate"][0], np.float32),
        "wu": np.ascontiguousarray(inp["ffn_w_up"][0], np.float32), "wd": np.ascontiguousarray(inp["ffn_w_down"][0], np.float32),
        "nfw": np.ascontiguousarray(np.broadcast_to(np.asarray(inp["norm_final_w"], np.float32)[None, :], (128, D))),
    }
    in_maps = []
    npre_tok = max(NPRE, 1) * 512
    for b in range(B):
        for sg in range(nseg):
            start = sg * T
            xpre = np.zeros((npre_tok, D), np.float32)
            premask = np.zeros(max(NPRE, 1), np.float32)
            lo = start - NPRE * 512
            for t in range(NPRE):
                p0 = lo + t * 512
                if p0 >= 0:
                    xpre[t * 512:(t + 1) * 512] = x[b, p0:p0 + 512]
                    premask[t] = 1.0
            m = dict(shared)
            m["xm"] = np.ascontiguousarray(x[b, start:start + T])
            m["xp"] = xpre
            m["mem"] = np.ascontiguousarray(mem[b])
            m["par"] = make_par(inp, NPRE, premask)
            in_maps.append(m)
    res = run_bass_kernel_spmd(nc, in_maps, core_ids=list(range(B * nseg)))
    out = np.zeros((B, L, D), np.float32)
    k = 0
    for b in range(B):
        for sg in range(nseg):
            out[b, sg * T:(sg + 1) * T] = res.results[k]["out"]
            k += 1
    return out


def kernel(**inputs):
    return run(inputs, 4096, 24, 4)
```

```python
import contextlib
import threading
import numpy as np
import concourse.bass as bass
import concourse.mybir as mybir
from concourse.bass_utils import run_bass_kernel_spmd

F32 = mybir.dt.float32
BF16 = mybir.dt.bfloat16
AF = mybir.ActivationFunctionType
ALU = mybir.AluOpType
AX = mybir.AxisListType

D = 1024
FF = 2816
EPS = 1e-6
S1, S2, S3, S4, S5, S6 = 1024, 2560, 2576, 3600, 4624, 5648
NWS = 3
import os as _os
_SEQ_DEBUG = bool(_os.environ.get('KSEQ'))


class Res:
    __slots__ = ("name", "last_w", "readers")

    def __init__(self, name=""):
        self.name = name
        self.last_w = None
        self.readers = {}


class Op:
    __slots__ = ("eng", "fn", "idx", "deps", "dma_key", "dma_cnt", "needs_inc", "inc_val")

    def __init__(self, eng, fn, idx, dma_key=None):
        self.eng = eng
        self.fn = fn
        self.idx = idx
        self.deps = {}
        self.dma_key = dma_key
        self.dma_cnt = 0
        self.needs_inc = False
        self.inc_val = 0


COMPUTE = ("pe", "act", "dve", "pool")


class Prog:
    def __init__(self, nc):
        self.nc = nc
        self.ops = []
        self.dma_counts = {}
        self.last_real = {}
        self.after_add = None

    def add(self, eng, fn, reads=(), writes=(), dma_key=None):
        op = Op(eng, fn, len(self.ops), dma_key)
        if dma_key is not None:
            c = self.dma_counts.get(dma_key, 0) + 1
            self.dma_counts[dma_key] = c
            op.dma_cnt = c
        deps = op.deps
        for r in reads:
            if r.last_w is not None:
                deps.setdefault(r.last_w, set()).add("raw")
        for w in writes:
            lw = w.last_w
            if lw is not None:
                if not (dma_key is not None and lw.dma_key == dma_key):
                    deps.setdefault(lw, set()).add("waw")
            for rd in w.readers.values():
                deps.setdefault(rd, set()).add("war")
        k = ("dma", op.idx) if dma_key is not None else eng
        for r in reads:
            r.readers[k] = op
        for w in writes:
            w.last_w = op
            w.readers = {}
        self.ops.append(op)
        if dma_key is None:
            self.last_real[eng] = op
        if self.after_add is not None:
            self.after_add()
        return op

    def pe(self, fn, reads=(), writes=()):
        return self.add("pe", fn, reads, writes)

    def act(self, fn, reads=(), writes=()):
        return self.add("act", fn, reads, writes)

    def dve(self, fn, reads=(), writes=()):
        return self.add("dve", fn, reads, writes)

    def pool(self, fn, reads=(), writes=()):
        return self.add("pool", fn, reads, writes)

    def dma(self, queue, key, out, in_, reads=(), writes=()):
        return self.add(queue, lambda e: e.dma_start(out=out, in_=in_), reads, writes, dma_key=key)

    def barrier(self, extra=()):
        lasts = dict(self.last_real)
        for e in COMPUTE:
            op = Op(e, None, len(self.ops))
            for e2, lo in lasts.items():
                if e2 != e:
                    op.deps[lo] = {"raw"}
            for x in extra:
                op.deps[x] = {"raw"}
            self.ops.append(op)

    def emit(self, final_wait_ops=()):
        nc = self.nc
        fin = Op("sp", None, len(self.ops))
        for o in final_wait_ops:
            fin.deps[o] = {"raw"}
        ops = self.ops + [fin]
        for op in ops:
            real = {}
            for d, kinds in op.deps.items():
                if d.dma_key is None and d.eng == op.eng and op.dma_key is None:
                    if op.eng == "pe" or kinds == {"war"}:
                        continue
                real[d] = kinds
            op.deps = real
            for d in real:
                if d.dma_key is None:
                    d.needs_inc = True
        cnt = {e: 0 for e in COMPUTE + ("sp",)}
        for op in ops:
            if op.dma_key is None and op.needs_inc:
                cnt[op.eng] += 1
                op.inc_val = cnt[op.eng]
        dma_keys = sorted(self.dma_counts.keys())
        with contextlib.ExitStack() as st:
            esem = {e: st.enter_context(nc.semaphore("s_" + e)) for e in cnt}
            dsem = {k: st.enter_context(nc.semaphore("d_%d" % i)) for i, k in enumerate(dma_keys)}
            block = st.enter_context(nc.Block())
            engs = {"pe": block.tensor, "act": block.scalar, "dve": block.vector,
                    "pool": block.gpsimd, "sp": block.sync}
            for ename, deco in engs.items():
                my = [o for o in ops if o.eng == ename]
                if not my:
                    continue

                def body(e, my=my, ename=ename):
                    waited = {}
                    for op in my:
                        need = {}
                        for d in op.deps:
                            if d.dma_key is not None:
                                s, v = ("d", d.dma_key), 16 * d.dma_cnt
                            else:
                                s, v = ("e", d.eng), d.inc_val
                            if need.get(s, 0) < v:
                                need[s] = v
                        for s, v in need.items():
                            if waited.get(s, 0) >= v:
                                continue
                            waited[s] = v
                            e.wait_ge(dsem[s[1]] if s[0] == "d" else esem[s[1]], v)
                        if op.fn is None:
                            continue
                        ins = op.fn(e)
                        if op.dma_key is not None:
                            ins.then_inc(dsem[op.dma_key], 16)
                        elif op.needs_inc:
                            ins.then_inc(esem[ename], 1)

                deco(body)


def chunk_catalog():
    cat = []
    for j in range(2):
        cat.append(("Z%d" % j, "w_in", 0, 8, [(0, 512, 512 * j)], "mix"))
    for j in range(3):
        cat.append(("X%d" % j, "w_in", 0, 8, [(0, 512, S1 + 512 * j)], "mix"))
    for a in range(4):
        cat.append(("H%d" % a, "w_in", 0, 8, [(0, 256, S3 + 256 * a), (256, 256, S4 + 256 * a)], "mix"))
    for j in range(2):
        cat.append(("V%d" % j, "w_in", 0, 8, [(0, 512, S5 + 512 * j)], "mix"))
    for j in range(2):
        cat.append(("G%d" % j, "w_in", 0, 8, [(0, 512, S6 + 512 * j)], "mix"))
    for j in range(2):
        for i in range(2):
            cat.append(("OUT%d%d" % (j, i), "w_out", 8 * i, 8, [(0, 512, 512 * j)], "wout%d" % i))
    for j in range(2):
        cat.append(("Q%d" % j, "wq", 0, 8, [(0, 512, 512 * j)], "xa"))
    for j in range(4):
        cat.append(("KV%d" % j, "wkv", 0, 8, [(0, 512, 512 * j)], "mem"))
    for j in range(2):
        cat.append(("O%d" % j, "wo", 0, 8, [(0, 512, 512 * j)], None))
    for j in range(11):
        cat.append(("GU%d" % j, "wgu", 0, 8, [(0, 256, 256 * j), (256, 256, 256 * j)], "ffn"))
    for j in range(2):
        for i in range(3):
            cat.append(("D%d%d" % (j, i), "wd", 8 * i, 8 if i < 2 else 6, [(0, 512, 512 * j)], None))
    return cat


PC_CW, PC_CB, PC_DTB, PC_ALOG, PC_DSK, PC_HLB0, PC_HLB1 = 0, 48, 60, 76, 92, 108, 116
PC_MIX, PC_XA, PC_MEM, PC_FFN, PC_WOUT, PC_MASK = 124, 132, 140, 148, 156, 172


def build(T, NPRE):
    _, rec = _build(T, NPRE, None)
    nc, _ = _build(T, NPRE, rec)
    return nc


def _build(T, NPRE, wseq_in):
    NT = T // 512
    NPAR = PC_MASK + max(NPRE, 1)
    nc = bass.Bass("TRN2", target_bir_lowering=False)
    xm = nc.dram_tensor("xm", [T, D], F32, kind="ExternalInput").ap()
    xp = nc.dram_tensor("xp", [max(NPRE, 1) * 512, D], F32, kind="ExternalInput").ap()
    memd = nc.dram_tensor("mem", [256, D], F32, kind="ExternalInput").ap()
    pard = nc.dram_tensor("par", [128, NPAR], F32, kind="ExternalInput").ap()
    nfwd = nc.dram_tensor("nfw", [128, D], F32, kind="ExternalInput").ap()
    wd_ = {
        "w_in": nc.dram_tensor("w_in", [D, 6672], F32, kind="ExternalInput").ap(),
        "w_out": nc.dram_tensor("w_out", [2048, D], F32, kind="ExternalInput").ap(),
        "wq": nc.dram_tensor("wq", [D, D], F32, kind="ExternalInput").ap(),
        "wkv": nc.dram_tensor("wkv", [D, 2048], F32, kind="ExternalInput").ap(),
        "wo": nc.dram_tensor("wo", [D, D], F32, kind="ExternalInput").ap(),
        "wg": nc.dram_tensor("wg", [D, FF], F32, kind="ExternalInput").ap(),
        "wu": nc.dram_tensor("wu", [D, FF], F32, kind="ExternalInput").ap(),
        "wd": nc.dram_tensor("wd", [FF, D], F32, kind="ExternalInput").ap(),
    }
    outd = nc.dram_tensor("out", [T, D], F32, kind="ExternalOutput").ap()
    cat = chunk_catalog()
    cid = {c[0]: i for i, c in enumerate(cat)}
    wsc = nc.dram_tensor("wsc", [len(cat), 128, 4096], BF16, kind="Internal").ap()
    R_wsc = [Res("wsc%d" % i) for i in range(len(cat))]

    P = Prog(nc)
    with contextlib.ExitStack() as st:
        def sb(name, shape, dt):
            return st.enter_context(nc.sbuf_tensor("sb_" + name, shape, dt))

        par = sb("par", [128, NPAR], F32); R_par = Res()
        ident = sb("ident", [128, 128], BF16); R_ident = Res()
        U = sb("U", [128, 128], F32); R_U = Res()
        ones = sb("ones", [128, 512], F32); R_ones = Res()
        cst = sb("cst", [128, 64], F32); R_cst = Res()
        wdt = sb("wdt", [128, 8, 16], BF16); R_wdt = Res()
        x_tm = sb("x_tm", [128, 4, D], F32); R_x = [Res() for _ in range(4)]
        hT = sb("hT", [128, 8, 512], BF16); R_hT = Res()
        hn = [sb("hn%d" % i, [128, D], BF16) for i in range(2)]; R_hn = [Res(), Res()]
        junk = sb("junk", [128, D], BF16); R_junk = Res()
        wbuf = [sb("wbuf%d" % i, [128, 8, 512], BF16) for i in range(NWS)]; R_wbuf = [Res() for _ in range(NWS)]
        Ssd = sb("Ssd", [128, D], F32); R_Ssd = Res()
        Ssdb = sb("Ssdb", [128, D], BF16); R_Ssdb = Res()
        Shg = sb("Shg", [128, 8, 128], F32); R_Shg = [Res() for _ in range(8)]
        Shgb = sb("Shgb", [128, 8, 128], BF16); R_Shgb = [Res() for _ in range(8)]
        halo = sb("halo", [128, 12, 3], F32); R_halo = Res()
        mixedT = sb("mixedT", [128, 16, 512], BF16); R_mixT = [Res() for _ in range(4)]; R_mixTh2 = [[Res() for _ in range(4)] for _ in range(2)]; R_junkh2 = [Res(), Res()]
        kmT = sb("kmT", [128, 8, 256], BF16); R_kmT = Res()
        vm = sb("vm", [128, 2, D], BF16); R_vm = Res()
        ost = [sb("ost0", [128, D], F32)]; R_ost = [Res()]
        stat = sb("stat", [128, 64], F32)
        ARENA = 27720
        arena = sb("arena", [128, ARENA], F32)
        psb = [st.enter_context(nc.psum_tensor("ps%d" % i, [128, 512], F32)) for i in range(8)]
        R_ps = [Res() for _ in range(8)]
        pctr = [0]

        tl = threading.local()
        rec_state = {"yield": None, "flags": set()}

        def rec_set(name):
            rec_state["flags"].add(name)

        def rec_wait(name):
            while name not in rec_state["flags"]:
                rec_state["yield"]()

        def psum():
            pool = getattr(tl, "pool", None)
            if pool is None:
                i = pctr[0] % 8
                pctr[0] += 1
            else:
                i = pool["banks"][pool["ctr"] % len(pool["banks"])]
                pool["ctr"] += 1
            return psb[i], R_ps[i]

        def psum_ded():
            pool = getattr(tl, "pool", None)
            if pool is None or pool.get("ded") is None:
                return psum()
            return psb[pool["ded"]], R_ps[pool["ded"]]

        def run_interleaved(funcs, pools):
            n = len(funcs)
            st_ = {"turn": 0, "alive": [True] * n, "err": None}
            cv = threading.Condition()

            def advance(k):
                for d in range(1, n + 1):
                    j = (k + d) % n
                    if st_["alive"][j]:
                        st_["turn"] = j
                        return
                st_["turn"] = -1

            def yield_turn():
                k = getattr(tl, "sid", None)
                if k is None:
                    return
                with cv:
                    advance(k)
                    cv.notify_all()
                    while st_["turn"] != k:
                        cv.wait()

            def runner(k):
                tl.pool = pools[k]
                tl.sid = k
                with cv:
                    while st_["turn"] != k:
                        cv.wait()
                try:
                    funcs[k]()
                except BaseException as ex:
                    st_["err"] = ex
                finally:
                    with cv:
                        st_["alive"][k] = False
                        advance(k)
                        cv.notify_all()

            P.after_add = None if _os.environ.get('KCOARSE') else yield_turn
            rec_state['yield'] = yield_turn
            rec_state['flags'] = set()
            ths = [threading.Thread(target=runner, args=(k,)) for k in range(n)]
            for t in ths:
                t.start()
            for t in ths:
                t.join()
            P.after_add = None
            if st_["err"] is not None:
                raise st_["err"]

        def bfv(pt):
            return pt[:, 0:512].bitcast(BF16)

        class Arena:
            def __init__(self):
                self.off = 0

            def f32(self, n):
                a = arena[:, self.off:self.off + n]
                self.off += n
                assert self.off <= ARENA, self.off
                return a

            def bf(self, n):
                n32 = (n + 1) // 2
                a = arena[:, self.off:self.off + n32].bitcast(BF16)
                self.off += n32
                assert self.off <= ARENA, self.off
                return a

        d_par = P.dma("sp", "par", par[:], pard, writes=[R_par])
        P.pool(lambda e: e.memset(ones[:], 1.0), writes=[R_ones])
        P.pool(lambda e: e.memset(U[:], 1.0), writes=[R_U])
        P.pool(lambda e: e.affine_select(out=U[:], in_=U[:], pattern=[[1, 128]], compare_op=ALU.is_ge,
                                         fill=0.0, base=0, channel_multiplier=-1), reads=[R_U], writes=[R_U])
        idf = arena[:, 0:128]
        R_idf = Res()
        P.pool(lambda e: e.memset(idf, 0.0), writes=[R_idf])
        P.pool(lambda e: e.affine_select(out=idf, in_=ones[:, 0:128], pattern=[[1, 128]], compare_op=ALU.is_equal,
                                         fill=0.0, base=0, channel_multiplier=-1), reads=[R_ones, R_idf], writes=[R_idf])
        P.dve(lambda e: e.tensor_copy(out=ident[:], in_=idf), reads=[R_idf], writes=[R_ident])
        P.pool(lambda e: e.memset(Ssd[:], 0.0), writes=[R_Ssd])
        P.pool(lambda e: e.memset(Ssdb[:], 0.0), writes=[R_Ssdb])
        P.pool(lambda e: e.memset(Shg[:], 0.0), writes=R_Shg)
        P.pool(lambda e: e.memset(Shgb[:], 0.0), writes=R_Shgb)
        P.pool(lambda e: e.memset(halo[:], 0.0), writes=[R_halo])
        P.pool(lambda e: e.memset(cst[:, 32:40], 1.0), writes=[R_cst])
        P.dve(lambda e: e.tensor_tensor(out=cst[:, 40:48], in0=par[:, PC_HLB0:PC_HLB0 + 8],
                                        in1=par[:, PC_HLB1:PC_HLB1 + 8], op=ALU.subtract), reads=[R_par, R_cst], writes=[R_cst])
        P.act(lambda e: e.activation(out=cst[:, 0:8], in_=cst[:, 40:48], func=AF.Sigmoid), reads=[R_cst], writes=[R_cst])
        P.act(lambda e: e.activation(out=cst[:, 8:16], in_=cst[:, 40:48], func=AF.Sigmoid, scale=-1.0), reads=[R_cst], writes=[R_cst])
        P.act(lambda e: e.activation(out=cst[:, 48:64], in_=par[:, PC_ALOG:PC_ALOG + 16], func=AF.Exp), reads=[R_par, R_cst], writes=[R_cst])
        P.dve(lambda e: e.tensor_scalar(out=cst[:, 16:32], in0=cst[:, 48:64], scalar1=-1.0, scalar2=None, op0=ALU.mult),
              reads=[R_cst], writes=[R_cst])
        lb, oml, aneg, onesb = cst[:, 0:8], cst[:, 8:16], cst[:, 16:32], cst[:, 32:40]
        nhalf = sb("nhalf", [128, 8], F32); R_nhalf = Res()
        P.pool(lambda e: e.memset(nhalf[:], -0.5), writes=[R_nhalf])
        hcst = sb("hcst", [128, 40], F32); R_hcst = Res(); R_hm = Res()
        P.dve(lambda e: e.tensor_scalar(out=hcst[:, 0:8], in0=oml, scalar1=0.5, scalar2=None, op0=ALU.mult), reads=[R_cst], writes=[R_hcst])
        P.dve(lambda e: e.tensor_tensor(out=hcst[:, 8:16], in0=hcst[:, 0:8], in1=lb, op=ALU.add), reads=[R_cst, R_hcst], writes=[R_hcst])
        P.dve(lambda e: e.tensor_scalar(out=hcst[:, 16:24], in0=oml, scalar1=-0.5, scalar2=None, op0=ALU.mult), reads=[R_cst, R_hcst], writes=[R_hcst])

        scale_ap = {"mix": par[:, PC_MIX:PC_MIX + 8], "xa": par[:, PC_XA:PC_XA + 8], "mem": par[:, PC_MEM:PC_MEM + 8],
                    "ffn": par[:, PC_FFN:PC_FFN + 8], "wout0": par[:, PC_WOUT:PC_WOUT + 8],
                    "wout1": par[:, PC_WOUT + 8:PC_WOUT + 16], None: onesb}
        ar = Arena(); ar.off = 128
        NSTG = 3
        stg = [ar.f32(4096).rearrange("p (k n) -> p k n", k=8) for _ in range(NSTG)]
        stgb = [ar.bf(4096).rearrange("p (k n) -> p k n", k=8) for _ in range(NSTG)]
        wdt32 = ar.f32(128).rearrange("p (k n) -> p k n", k=8)
        R_stg = [Res() for _ in range(NSTG)]; R_stgb = [Res() for _ in range(NSTG)]; R_wdt32 = Res()
        pro_dmas = []

        def wsrc(key, kt0, nkt, c0, cw):
            return wd_[key].rearrange("(kt p) n -> p kt n", p=128)[:, kt0:kt0 + nkt, c0:c0 + cw]

        def pro_load(ci):
            name, key, kt0, nkt, pieces, sk = cat[ci]
            sl = ci % NSTG
            for pi, (dc, cw, sc) in enumerate(pieces):
                k2 = key
                if key == "wgu":
                    k2 = "wg" if pi == 0 else "wu"
                P.dma("sp", "stg%d" % sl, stg[sl][:, 0:nkt, dc:dc + cw], wsrc(k2, kt0, nkt, sc, cw), writes=[R_stg[sl]])

        def pro_cast_store(ci):
            name, key, kt0, nkt, pieces, sk = cat[ci]
            sl = ci % NSTG
            sap = scale_ap[sk]
            f = (lambda e, sl=sl, nkt=nkt, sap=sap: e.tensor_tensor(
                out=stgb[sl][:, 0:nkt, :], in0=stg[sl][:, 0:nkt, :],
                in1=sap[:, 0:nkt].unsqueeze(2).broadcast_to([128, nkt, 512]), op=ALU.mult))
            (P.pool if ci % 3 == 2 else P.dve)(f, reads=[R_stg[sl], R_par, R_cst], writes=[R_stgb[sl]])
            pro_dmas.append(P.dma("sp", "wscw%d" % sl, wsc[ci].rearrange("p (k n) -> p k n", k=8)[:, 0:nkt, :],
                                  stgb[sl][:, 0:nkt, :], reads=[R_stgb[sl]], writes=[R_wsc[ci]]))

        pro_load(0)
        pro_load(1)
        for ci in range(len(cat)):
            if ci + 2 < len(cat):
                pro_load(ci + 2)
            pro_cast_store(ci)
        P.dma("sp", "wdt32", wdt32, wsrc("w_in", 0, 8, S2, 16), writes=[R_wdt32])
        P.dve(lambda e: e.tensor_tensor(out=wdt[:], in0=wdt32, in1=par[:, PC_MIX:PC_MIX + 8].unsqueeze(2).broadcast_to([128, 8, 16]),
                                        op=ALU.mult), reads=[R_wdt32, R_par], writes=[R_wdt])

        wrec = []
        wseq = wseq_in
        wstate = {"issued": 0, "got": 0}

        def wissue():
            i = wstate["issued"]
            names = wseq if wseq is not None else wrec
            c = cid[names[i]]
            sl = i % NWS
            P.dma("sp", "wb%d" % sl, wbuf[sl][:], wsc[c].rearrange("p (k n) -> p k n", k=8),
                  reads=[R_wsc[c]], writes=[R_wbuf[sl]])
            slot_content[sl] = i
            wstate["issued"] += 1

        slot_content = {}
        occ_done = [0] * NWS
        ref_left = {}

        def can_issue(k):
            return occ_done[k % NWS] == k // NWS

        def wdone(i):
            assert slot_content.get(i % NWS) == i, ("evicted before release", i, slot_content)
            ref_left[i] -= 1
            if ref_left[i] == 0:
                occ_done[i % NWS] += 1

        def wdone_cur():
            cur = getattr(tl, "cur", None)
            if cur is not None:
                wdone(cur)
                tl.cur = None

        def wget(name, auto=True, nref=1):
            if auto:
                wdone_cur()
            i = wstate["got"]
            wstate["got"] += 1
            wrec.append(name)
            ref_left[i] = nref
            if wseq is not None:
                assert wseq[i] == name, (i, wseq[i], name)
            while wstate["issued"] <= i:
                if can_issue(wstate["issued"]):
                    sid = getattr(tl, "sid", None)
                    tl.sid = None
                    try:
                        wissue()
                    finally:
                        tl.sid = sid
                else:
                    rec_state["yield"]()
            if wseq is not None:
                sid = getattr(tl, "sid", None)
                tl.sid = None
                try:
                    while wstate["issued"] < min(len(wseq), i + NWS) and can_issue(wstate["issued"]):
                        wissue()
                finally:
                    tl.sid = sid
            if auto:
                tl.cur = i
            assert slot_content.get(i % NWS) == i, ("not resident at obtain", i, slot_content)
            return wbuf[i % NWS], R_wbuf[i % NWS], i

        def rstd_from_ss(ssv, n, Rs, inv_n):
            P.pool(lambda e: e.tensor_scalar(out=ssv, in0=ssv, scalar1=inv_n, scalar2=EPS, op0=ALU.mult, op1=ALU.add),
                   reads=[Rs], writes=[Rs])
            P.pool(lambda e: e.tensor_tensor(out=ssv, in0=ssv, in1=nhalf[:, 0:n], op=ALU.pow), reads=[Rs, R_nhalf], writes=[Rs])

        rms_ctr = [0]
        R_rms = [Res(), Res()]
        R_ssf = Res()
        R_scp = [Res(), Res()]
        R_prekh = [Res(), Res()]
        R_prekhtm = [Res(), Res()]

        def rms_T(src, Rsrc, nsub, dstT, R_dst):
            k = rms_ctr[0] % 2
            rms_ctr[0] += 1
            ss = stat[:, 8 * k:8 * k + nsub]
            Rss = R_rms[k]
            P.pool(lambda e: e.memset(ss, 0.0), writes=[Rss])
            for s in range(nsub):
                P.act(lambda e, s=s: e.activation(out=junk[:], in_=src(s), func=AF.Square, accum_out=ss[:, s:s + 1]),
                      reads=[Rsrc[s], Rss], writes=[R_junk, R_junkh2[0], R_junkh2[1], Rss])
            rstd_from_ss(ss, nsub, Rss, 1.0 / D)
            for s in range(nsub):
                b = s % 2
                P.dve(lambda e, s=s, b=b: e.tensor_scalar(out=hn[b][:], in0=src(s), scalar1=ss[:, s:s + 1], scalar2=None,
                                                          op0=ALU.mult), reads=[Rsrc[s], Rss], writes=[R_hn[b]])
                pt, Rp = psum()
                pv = bfv(pt)
                for kt in range(8):
                    P.pe(lambda e, kt=kt, b=b, pv=pv: e.transpose(out=pv[:, kt * 128:(kt + 1) * 128],
                                                                  in_=hn[b][:, kt * 128:(kt + 1) * 128], identity=ident[:]),
                         reads=[R_hn[b], R_ident], writes=[Rp])
                P.act(lambda e, s=s, pv=pv: e.activation(out=dstT[:, :, s * 128:(s + 1) * 128],
                                                         in_=pv.rearrange("p (k t) -> p k t", k=8), func=AF.Copy),
                      reads=[Rp], writes=[R_dst])

        def proj_fm(wt, Rw, j, xT, RxT, ncols=512):
            pt, Rp = psum()
            for kt in range(8):
                P.pe(lambda e, kt=kt, pt=pt: e.matmul(pt[:, 0:ncols], lhsT=wt[:, kt, j * 128:(j + 1) * 128], rhs=xT[:, kt, 0:ncols],
                                                      start=(kt == 0), stop=(kt == 7)), reads=[Rw, RxT], writes=[Rp])
            return pt, Rp

        def proj_tm(wt, Rw, s, xT, RxT, ncols=512):
            pt, Rp = psum()
            for kt in range(8):
                P.pe(lambda e, kt=kt, pt=pt: e.matmul(pt[:, 0:ncols], lhsT=xT[:, kt, s * 128:(s + 1) * 128], rhs=wt[:, kt, 0:ncols],
                                                      start=(kt == 0), stop=(kt == 7)), reads=[Rw, RxT], writes=[Rp])
            return pt, Rp

        mem_t = ar.f32(2 * D).rearrange("p (s d) -> p s d", s=2); R_mem = [Res(), Res()]
        mT = ar.bf(8 * 256).rearrange("p (k t) -> p k t", k=8); R_mT = Res()
        for s in range(2):
            P.dma("sp", "mem%d" % s, mem_t[:, s, :], memd[s * 128:(s + 1) * 128, :], writes=[R_mem[s]])
        rms_T(lambda s: mem_t[:, s, :], R_mem, 2, mT, R_mT)
        for jc in range(2):
            wt, Rw, _ = wget("KV%d" % jc)
            for j in range(4):
                pt, Rp = proj_fm(wt, Rw, j, mT, R_mT, ncols=256)
                P.act(lambda e, pt=pt, jc=jc, j=j: e.activation(out=kmT[:, 4 * jc + j, :], in_=pt[:, 0:256], func=AF.Copy),
                      reads=[Rp], writes=[R_kmT])
        for jc in range(2):
            wt, Rw, _ = wget("KV%d" % (2 + jc))
            for s in range(2):
                pt, Rp = proj_tm(wt, Rw, s, mT, R_mT)
                P.act(lambda e, pt=pt, jc=jc, s=s: e.activation(out=vm[:, s, jc * 512:(jc + 1) * 512], in_=pt[:, 0:512], func=AF.Copy),
                      reads=[Rp], writes=[R_vm])
        wdone_cur()
        P.barrier(extra=pro_dmas[-3:])

        out_dmas = []
        tile_ctr = [0]
        mres_store = []

        def mk_mres():
            idx = [0]

            def mres():
                i = idx[0]
                idx[0] += 1
                if i >= len(mres_store):
                    mres_store.append(Res())
                return mres_store[i]
            return mres

        def do_tile(xsrc_d, row0, is_pre, pre_idx, out_row0):
            ti = tile_ctr[0]
            tile_ctr[0] += 1
            A = Arena()
            mres = mk_mres()
            raw = A.f32(4 * 515).rearrange("p (j t) -> p j t", j=4); R_raw = mres()
            cacc = [A.f32(512) for _ in range(2)]; R_cacc = [mres(), mres()]
            xsT = A.bf(8 * 512).rearrange("p (k t) -> p k t", k=8); R_xsT = mres()
            BT = A.bf(2 * 512).rearrange("p (k t) -> p k t", k=2); R_BT = mres()
            CT = A.bf(2 * 512).rearrange("p (k t) -> p k t", k=2); R_CT = mres()
            xs_tm = A.bf(4 * D).rearrange("p (s d) -> p s d", s=4); R_xs = [mres() for _ in range(4)]
            B_tm = A.bf(4 * 256).rearrange("p (s d) -> p s d", s=4); R_Btm = mres()
            zs = A.bf(4 * D).rearrange("p (s d) -> p s d", s=4); R_zs = [mres() for _ in range(4)]
            vt = A.bf(4 * D).rearrange("p (s d) -> p s d", s=4); R_vt = [mres() for _ in range(4)]
            gs = A.bf(4 * D).rearrange("p (s d) -> p s d", s=4); R_gs = [mres() for _ in range(4)]
            dtr = A.f32(64).rearrange("p (s h) -> p s h", s=4); R_dtr = mres()
            dtA = A.f32(64).rearrange("p (s h) -> p s h", s=4); R_dtA = mres()
            acs = [A.f32(96) for _ in range(2)]; R_acs = [mres(), mres()]
            Lseg = [A.f32(512) for _ in range(2)]; R_Lseg = [mres(), mres()]
            MT = A.bf(16 * 128).rearrange("p (h l) -> p h l", h=16); R_MT = mres()
            cbm = A.f32(256).rearrange("p (g l) -> p g l", g=2); R_cbm = mres()
            xdt = A.bf(D); R_xdt = mres()
            xdtd = A.bf(D); R_xdtd = mres()
            t1 = A.f32(D); R_t1 = mres()
            t3 = A.f32(D); R_t3 = mres()
            yn = A.bf(D); R_yn = mres()
            qf = A.f32(1024).rearrange("p (i t) -> p i t", i=2); R_qf = mres()
            gl = A.f32(1024).rearrange("p (i t) -> p i t", i=2); R_gl = mres()
            kf = A.f32(1024).rearrange("p (i t) -> p i t", i=2); R_kf = mres()
            bt = [A.f32(513) for _ in range(2)]; R_bt = [mres(), mres()]
            etmp = [A.f32(128) for _ in range(8)]; R_et = [mres() for _ in range(8)]
            qt_ = [A.bf(128) for _ in range(2)]; R_qt = [mres(), mres()]
            KA = [A.bf(128) for _ in range(2)]; R_KA = [mres(), mres()]
            KB = [A.bf(128) for _ in range(2)]; R_KB = [mres(), mres()]
            KC = [A.bf(128) for _ in range(2)]; R_KC = [mres(), mres()]
            QC = [A.bf(64) for _ in range(2)]; R_QC = [mres(), mres()]
            if not is_pre:
                for i in range(2):
                    P.pool(lambda e, i=i: e.memset(KA[i], 0.0), writes=[R_KA[i]])
                    P.pool(lambda e, i=i: e.memset(KB[i], 0.0), writes=[R_KB[i]])
                    P.pool(lambda e, i=i: e.memset(KC[i], 0.0), writes=[R_KC[i]])
            qh = [A.bf(128) for _ in range(2)]; R_qh = [mres(), mres()]
            kh = [A.bf(128) for _ in range(2)]; R_kh = [mres(), mres()]
            khtm = [A.bf(128) for _ in range(2)]; R_khtm = [mres(), mres()]
            attm = [A.bf(128) for _ in range(2)]; R_attm = [mres(), mres()]
            otmp = A.f32(256); R_otmp = mres()
            og = A.bf(256); R_og = mres()
            sst = A.f32(32); R_sst = mres(); R_ssth2 = [mres(), mres()]
            R_qf2 = [mres(), mres()]; R_gl2 = [mres(), mres()]; R_kf2 = [mres(), mres()]; R_otmp2 = [mres(), mres()]; R_og2 = [mres(), mres()]

            qfb = qf.rearrange("p i t -> p (i t)").bitcast(BF16)
            pre_kh = [qfb[:, 0:512], qfb[:, 512:1024]]
            pre_khtm = [qfb[:, 1024:1536], qfb[:, 1536:2048]]

            for s in range(4):
                P.dma("sp", "x%d" % s, x_tm[:, s, :], xsrc_d[row0 + s * 128:row0 + (s + 1) * 128, :], writes=[R_x[s]])
            rms_T(lambda s: x_tm[:, s, :], R_x, 4, hT, R_hT)

            def tm_chunk(nm, jc, dst, Rdst, func):
                wt, Rw, _ = wget("%s%d" % (nm, jc))
                for s in range(4):
                    pt, Rp = proj_tm(wt, Rw, s, hT, R_hT)
                    P.act(lambda e, pt=pt, s=s, jc=jc: e.activation(out=dst[:, s, jc * 512:(jc + 1) * 512], in_=pt[:, 0:512], func=func),
                          reads=[Rp], writes=[Rdst[s]])
            if is_pre:
                tm_list = [("V", 0, vt, R_vt, AF.Copy), ("V", 1, vt, R_vt, AF.Copy)]
            else:
                tm_list = [("Z", 0, zs, R_zs, AF.Silu), ("Z", 1, zs, R_zs, AF.Silu), ("V", 0, vt, R_vt, AF.Copy),
                           ("V", 1, vt, R_vt, AF.Copy), ("G", 0, gs, R_gs, AF.Silu), ("G", 1, gs, R_gs, AF.Silu)]
            def secA():
                for c3 in range(3):
                    wt, Rw, _ = wget("X%d" % c3)
                    P.pool(lambda e, c3=c3: e.tensor_copy(out=raw[:, :, 0:3], in_=halo[:, 4 * c3:4 * c3 + 4, :]),
                           reads=[R_halo], writes=[R_raw])
                    for j in range(4):
                        pt, Rp = proj_fm(wt, Rw, j, hT, R_hT)
                        P.act(lambda e, pt=pt, j=j: e.activation(out=raw[:, j, 3:515], in_=pt[:, 0:512], func=AF.Copy),
                              reads=[Rp], writes=[R_raw])
                    P.pool(lambda e, c3=c3: e.tensor_copy(out=halo[:, 4 * c3:4 * c3 + 4, :], in_=raw[:, :, 512:515]),
                           reads=[R_raw], writes=[R_halo])
                    for j in range(4):
                        ct = 4 * c3 + j
                        ca, Rca = cacc[j % 2], R_cacc[j % 2]
                        P.dve(lambda e, j=j, ct=ct, ca=ca: e.tensor_scalar(
                            out=ca, in0=raw[:, j, 0:512], scalar1=par[:, PC_CW + 4 * ct:PC_CW + 4 * ct + 1],
                            scalar2=par[:, PC_CB + ct:PC_CB + ct + 1], op0=ALU.mult, op1=ALU.add), reads=[R_raw, R_par], writes=[Rca])
                        for k in range(1, 4):
                            P.dve(lambda e, j=j, ct=ct, k=k, ca=ca: e.scalar_tensor_tensor(
                                out=ca, in0=raw[:, j, k:k + 512], scalar=par[:, PC_CW + 4 * ct + k:PC_CW + 4 * ct + k + 1],
                                in1=ca, op0=ALU.mult, op1=ALU.add), reads=[R_raw, R_par, Rca], writes=[Rca])
                        if ct < 8:
                            dst, Rd = xsT[:, ct, :], R_xsT
                        elif ct < 10:
                            dst, Rd = BT[:, ct - 8, :], R_BT
                        else:
                            dst, Rd = CT[:, ct - 10, :], R_CT
                        P.act(lambda e, ca=ca, dst=dst: e.activation(out=dst, in_=ca, func=AF.Silu), reads=[Rca], writes=[Rd])
                for s in range(4):
                    pt, Rp = psum()
                    pv = bfv(pt)
                    for kt in range(8):
                        P.pe(lambda e, kt=kt, s=s, pv=pv: e.transpose(out=pv[:, kt * 128:(kt + 1) * 128],
                                                                      in_=xsT[:, kt, s * 128:(s + 1) * 128], identity=ident[:]),
                             reads=[R_xsT, R_ident], writes=[Rp])
                    P.act(lambda e, s=s, pv=pv: e.activation(out=xs_tm[:, s, :], in_=pv, func=AF.Copy), reads=[Rp], writes=[R_xs[s]])
                pt, Rp = psum()
                pv = bfv(pt)
                for s in range(4):
                    for g in range(2):
                        P.pe(lambda e, s=s, g=g, pv=pv: e.transpose(out=pv[:, s * 256 + g * 128:s * 256 + (g + 1) * 128],
                                                                    in_=BT[:, g, s * 128:(s + 1) * 128], identity=ident[:]),
                             reads=[R_BT, R_ident], writes=[Rp])
                P.act(lambda e, pv=pv: e.activation(out=B_tm[:], in_=pv.rearrange("p (s d) -> p s d", s=4), func=AF.Copy),
                      reads=[Rp], writes=[R_Btm])

                pt, Rp = psum()
                for s in range(4):
                    for kt in range(8):
                        P.pe(lambda e, s=s, kt=kt, pt=pt: e.matmul(pt[:, s * 16:(s + 1) * 16], lhsT=hT[:, kt, s * 128:(s + 1) * 128],
                                                                   rhs=wdt[:, kt, :], start=(kt == 0), stop=(kt == 7)),
                             reads=[R_hT, R_wdt], writes=[Rp])
                P.dve(lambda e, pt=pt: e.tensor_tensor(out=dtr[:], in0=pt[:, 0:64].rearrange("p (s h) -> p s h", s=4),
                                                       in1=par[:, PC_DTB:PC_DTB + 16].unsqueeze(1).broadcast_to([128, 4, 16]), op=ALU.add),
                      reads=[Rp, R_par], writes=[R_dtr])
                P.act(lambda e: e.activation(out=dtr[:], in_=dtr[:], func=AF.Exp), reads=[R_dtr], writes=[R_dtr])
                P.act(lambda e: e.activation(out=dtr[:], in_=dtr[:], func=AF.Ln, bias=1.0), reads=[R_dtr], writes=[R_dtr])
                if is_pre:
                    P.dve(lambda e: e.tensor_scalar(out=dtr[:], in0=dtr[:], scalar1=par[:, PC_MASK + pre_idx:PC_MASK + pre_idx + 1],
                                                    scalar2=None, op0=ALU.mult), reads=[R_dtr, R_par], writes=[R_dtr])
                P.dve(lambda e: e.tensor_tensor(out=dtA[:], in0=dtr[:], in1=aneg.unsqueeze(1).broadcast_to([128, 4, 16]), op=ALU.mult),
                      reads=[R_dtr, R_cst], writes=[R_dtA])


            def secB():
                while tm_list:
                    tm_chunk(*tm_list.pop(0))

            def sec_ssd():
                if is_pre:
                    ac = acs[0]; Rac = R_acs[0]
                    pa, Rpa = psum()
                    for j in range(4):
                        P.pe(lambda e, j=j, pa=pa: e.matmul(pa[:, j * 16:(j + 1) * 16], lhsT=U[:], rhs=dtA[:, j, :], start=True, stop=True),
                             reads=[R_U, R_dtA], writes=[Rpa])
                        P.pe(lambda e, j=j, pa=pa: e.matmul(pa[:, 64 + j * 16:64 + (j + 1) * 16], lhsT=ones[:, 0:128], rhs=dtA[:, j, :],
                                                            start=True, stop=True), reads=[R_ones, R_dtA], writes=[Rpa])
                    suf = acs[1]; Rsuf = R_acs[1]
                    P.dve(lambda e, pa=pa: e.tensor_copy(out=suf[:, 0:64], in_=pa[:, 64:128]), reads=[Rpa], writes=[Rsuf])
                    for j in (2, 1, 0):
                        P.dve(lambda e, j=j: e.tensor_tensor(out=suf[:, j * 16:(j + 1) * 16], in0=suf[:, j * 16:(j + 1) * 16],
                                                             in1=suf[:, (j + 1) * 16:(j + 2) * 16], op=ALU.add), reads=[Rsuf], writes=[Rsuf])
                    P.dve(lambda e, pa=pa: e.tensor_tensor(out=ac[:, 0:64], in0=suf[:, 0:64], in1=pa[:, 0:64], op=ALU.subtract),
                          reads=[Rpa, Rsuf], writes=[Rac])
                    P.act(lambda e: e.activation(out=ac[:, 0:64], in_=ac[:, 0:64], func=AF.Exp), reads=[Rac], writes=[Rac])
                    P.act(lambda e: e.activation(out=ac[:, 80:96], in_=suf[:, 0:16], func=AF.Exp), reads=[Rsuf, Rac], writes=[Rac])
                    P.dve(lambda e: e.tensor_tensor(out=ac[:, 0:64], in0=ac[:, 0:64], in1=dtr[:].rearrange("p s h -> p (s h)"), op=ALU.mult),
                          reads=[Rac, R_dtr], writes=[Rac])
                    for j in range(4):
                        P.dve(lambda e, j=j: e.tensor_tensor(out=zs[:, j, :].rearrange("p (h d) -> p h d", h=16),
                                                             in0=xs_tm[:, j, :].rearrange("p (h d) -> p h d", h=16),
                                                             in1=ac[:, j * 16:(j + 1) * 16].unsqueeze(2).broadcast_to([128, 16, 64]), op=ALU.mult),
                              reads=[R_xs[j], Rac], writes=[R_zs[j]])
                    pss = [psum(), psum()]
                    for g in range(2):
                        ptt, Rpp = pss[g]
                        for j in range(4):
                            P.pe(lambda e, g=g, j=j, ptt=ptt: e.matmul(ptt[:, 0:512], lhsT=B_tm[:, j, g * 128:(g + 1) * 128],
                                                                       rhs=zs[:, j, g * 512:(g + 1) * 512], start=(j == 0), stop=(j == 3)),
                                 reads=[R_Btm, R_zs[j]], writes=[Rpp])
                    P.dve(lambda e: e.tensor_tensor(out=Ssd.rearrange("p (h d) -> p h d", h=16),
                                                    in0=Ssd.rearrange("p (h d) -> p h d", h=16),
                                                    in1=ac[:, 80:96].unsqueeze(2).broadcast_to([128, 16, 64]), op=ALU.mult),
                          reads=[R_Ssd, Rac], writes=[R_Ssd])
                    for g in range(2):
                        P.dve(lambda e, g=g, ptt=pss[g][0]: e.tensor_tensor(out=Ssd[:, g * 512:(g + 1) * 512], in0=ptt[:, 0:512],
                                                                            in1=Ssd[:, g * 512:(g + 1) * 512], op=ALU.add),
                              reads=[pss[g][1], R_Ssd], writes=[R_Ssd])
                    P.act(lambda e: e.activation(out=Ssdb[:], in_=Ssd[:], func=AF.Copy), reads=[R_Ssd], writes=[R_Ssdb])
                for c in (range(0) if is_pre else range(4)):
                    ac, Rac = acs[c % 2], R_acs[c % 2]
                    pt, Rp = psum()
                    P.pe(lambda e, c=c, pt=pt: e.matmul(pt[:, 0:16], lhsT=U[:], rhs=dtA[:, c, :], start=True, stop=True),
                         reads=[R_U, R_dtA], writes=[Rp])
                    P.pe(lambda e, c=c, pt=pt: e.matmul(pt[:, 16:32], lhsT=ones[:, 0:128], rhs=dtA[:, c, :], start=True, stop=True),
                         reads=[R_ones, R_dtA], writes=[Rp])
                    P.dve(lambda e, pt=pt, ac=ac: e.tensor_copy(out=ac[:, 0:32], in_=pt[:, 0:32]), reads=[Rp], writes=[Rac])
                    P.dve(lambda e, ac=ac: e.tensor_tensor(out=ac[:, 48:64], in0=ac[:, 16:32], in1=ac[:, 0:16], op=ALU.subtract),
                          reads=[Rac], writes=[Rac])
                    P.act(lambda e, ac=ac: e.activation(out=ac[:, 32:48], in_=ac[:, 0:16], func=AF.Exp), reads=[Rac], writes=[Rac])
                    P.act(lambda e, ac=ac: e.activation(out=ac[:, 48:64], in_=ac[:, 48:64], func=AF.Exp), reads=[Rac], writes=[Rac])
                    P.act(lambda e, ac=ac: e.activation(out=ac[:, 64:80], in_=ac[:, 16:32], func=AF.Exp), reads=[Rac], writes=[Rac])
                    P.dve(lambda e, c=c: e.tensor_tensor(out=xdt.rearrange("p (h d) -> p h d", h=16),
                                                         in0=xs_tm[:, c, :].rearrange("p (h d) -> p h d", h=16),
                                                         in1=dtr[:, c, :].unsqueeze(2).broadcast_to([128, 16, 64]), op=ALU.mult),
                          reads=[R_xs[c], R_dtr], writes=[R_xdt])
                    if not is_pre:
                        pt, Rp = psum()
                        for g in range(2):
                            P.pe(lambda e, c=c, g=g, pt=pt: e.matmul(pt[:, g * 128:(g + 1) * 128], lhsT=BT[:, g, c * 128:(c + 1) * 128],
                                                                     rhs=CT[:, g, c * 128:(c + 1) * 128], start=True, stop=True),
                                 reads=[R_BT, R_CT], writes=[Rp])
                        P.dve(lambda e, pt=pt: e.tensor_tensor(out=cbm[:], in0=pt[:, 0:256].rearrange("p (g l) -> p g l", g=2),
                                                               in1=U[:].unsqueeze(1).broadcast_to([128, 2, 128]), op=ALU.mult),
                              reads=[Rp, R_U], writes=[R_cbm])
                        for hb in range(4):
                            Ls, RLs = Lseg[hb % 2], R_Lseg[hb % 2]
                            pt, Rp = psum()
                            for i in range(4):
                                h = hb * 4 + i
                                P.pe(lambda e, c=c, h=h, i=i, pt=pt: e.matmul(pt[:, i * 128:(i + 1) * 128],
                                                                              lhsT=dtA[:, c, h:h + 1].broadcast_to([128, 128]), rhs=U[:],
                                                                              start=True, stop=True), reads=[R_dtA, R_U], writes=[Rp])
                            P.dve(lambda e, pt=pt, hb=hb, ac=ac, Ls=Ls: e.tensor_tensor(
                                out=Ls.rearrange("p (h l) -> p h l", h=4), in0=pt[:, 0:512].rearrange("p (h l) -> p h l", h=4),
                                in1=ac[:, 4 * hb:4 * hb + 4].unsqueeze(2).broadcast_to([128, 4, 128]), op=ALU.subtract),
                                reads=[Rp, Rac], writes=[RLs])
                            P.dve(lambda e, Ls=Ls: e.tensor_scalar(out=Ls, in0=Ls, scalar1=0.0, scalar2=None, op0=ALU.min),
                                  reads=[RLs], writes=[RLs])
                            P.act(lambda e, Ls=Ls: e.activation(out=Ls, in_=Ls, func=AF.Exp), reads=[RLs], writes=[RLs])
                            g = hb // 2
                            P.pool(lambda e, hb=hb, g=g, Ls=Ls: e.tensor_tensor(
                                out=MT[:, 4 * hb:4 * hb + 4, :], in0=Ls.rearrange("p (h l) -> p h l", h=4),
                                in1=cbm[:, g, :].unsqueeze(1).broadcast_to([128, 4, 128]), op=ALU.mult),
                                reads=[RLs, R_cbm], writes=[R_MT])
                        py = [psum(), psum()]
                        for h in range(16):
                            ptt, Rpp = py[h // 8]
                            P.pe(lambda e, h=h, ptt=ptt: e.matmul(ptt[:, (h % 8) * 64:(h % 8 + 1) * 64], lhsT=MT[:, h, :],
                                                                  rhs=xdt[:, h * 64:(h + 1) * 64], start=True, stop=True),
                                 reads=[R_MT, R_xdt], writes=[Rpp])
                        po = [psum(), psum()]
                        for g in range(2):
                            ptt, Rpp = po[g]
                            P.pe(lambda e, g=g, c=c, ptt=ptt: e.matmul(ptt[:, 0:512], lhsT=CT[:, g, c * 128:(c + 1) * 128],
                                                                       rhs=Ssdb[:, g * 512:(g + 1) * 512], start=True, stop=True),
                                 reads=[R_CT, R_Ssdb], writes=[Rpp])
                        for g in range(2):
                            P.dve(lambda e, g=g, ac=ac, ptt=po[g][0]: e.tensor_tensor(
                                out=t1[:, g * 512:(g + 1) * 512].rearrange("p (h d) -> p h d", h=8),
                                in0=ptt[:, 0:512].rearrange("p (h d) -> p h d", h=8),
                                in1=ac[:, 32 + 8 * g:40 + 8 * g].unsqueeze(2).broadcast_to([128, 8, 64]), op=ALU.mult),
                                reads=[po[g][1], Rac], writes=[R_t1])
                        for g in range(2):
                            P.dve(lambda e, g=g, ptt=py[g][0]: e.tensor_tensor(out=t1[:, g * 512:(g + 1) * 512], in0=ptt[:, 0:512],
                                                                               in1=t1[:, g * 512:(g + 1) * 512], op=ALU.add),
                                  reads=[py[g][1], R_t1], writes=[R_t1])
                        P.pool(lambda e, c=c: e.tensor_tensor(out=t3.rearrange("p (h d) -> p h d", h=16),
                                                              in0=xs_tm[:, c, :].rearrange("p (h d) -> p h d", h=16),
                                                              in1=par[:, PC_DSK:PC_DSK + 16].unsqueeze(2).broadcast_to([128, 16, 64]), op=ALU.mult),
                               reads=[R_xs[c], R_par], writes=[R_t3])
                        P.pool(lambda e: e.tensor_tensor(out=t1, in0=t1, in1=t3, op=ALU.add), reads=[R_t1, R_t3], writes=[R_t1])
                        P.dve(lambda e, c=c: e.tensor_tensor(out=t3, in0=t1, in1=zs[:, c, :], op=ALU.mult),
                              reads=[R_t1, R_zs[c], R_t3], writes=[R_t3])
                        P.pool(lambda e: e.memset(sst[:, 0:2], 0.0), writes=[R_sst])
                        for g in range(2):
                            P.act(lambda e, g=g: e.activation(out=junk[:, 0:512], in_=t3[:, g * 512:(g + 1) * 512], func=AF.Square,
                                                              accum_out=sst[:, g:g + 1]), reads=[R_t3, R_sst], writes=[R_junk, R_sst])
                        rstd_from_ss(sst[:, 0:2], 2, R_sst, 1.0 / 512)
                        for g in range(2):
                            P.dve(lambda e, g=g: e.tensor_scalar(out=yn[:, g * 512:(g + 1) * 512], in0=t3[:, g * 512:(g + 1) * 512],
                                                                 scalar1=sst[:, g:g + 1], scalar2=None, op0=ALU.mult),
                                  reads=[R_t3, R_sst], writes=[R_yn])
                        pt, Rp = psum()
                        pv = bfv(pt)
                        for kt in range(8):
                            P.pe(lambda e, kt=kt, pv=pv: e.transpose(out=pv[:, kt * 128:(kt + 1) * 128], in_=yn[:, kt * 128:(kt + 1) * 128],
                                                                     identity=ident[:]), reads=[R_yn, R_ident], writes=[Rp])
                        P.act(lambda e, c=c, pv=pv: e.activation(out=mixedT[:, 0:8, c * 128:(c + 1) * 128],
                                                                 in_=pv.rearrange("p (k t) -> p k t", k=8), func=AF.Copy),
                              reads=[Rp], writes=[R_mixT[c]])
                    P.dve(lambda e, ac=ac: e.tensor_tensor(out=xdtd.rearrange("p (h d) -> p h d", h=16),
                                                           in0=xdt.rearrange("p (h d) -> p h d", h=16),
                                                           in1=ac[:, 48:64].unsqueeze(2).broadcast_to([128, 16, 64]), op=ALU.mult),
                          reads=[R_xdt, Rac], writes=[R_xdtd])
                    pss = [psum(), psum()]
                    for g in range(2):
                        ptt, Rpp = pss[g]
                        P.pe(lambda e, g=g, c=c, ptt=ptt: e.matmul(ptt[:, 0:512], lhsT=B_tm[:, c, g * 128:(g + 1) * 128],
                                                                   rhs=xdtd[:, g * 512:(g + 1) * 512], start=True, stop=True),
                             reads=[R_Btm, R_xdtd], writes=[Rpp])
                    P.dve(lambda e, ac=ac: e.tensor_tensor(out=Ssd.rearrange("p (h d) -> p h d", h=16),
                                                           in0=Ssd.rearrange("p (h d) -> p h d", h=16),
                                                           in1=ac[:, 64:80].unsqueeze(2).broadcast_to([128, 16, 64]), op=ALU.mult),
                          reads=[R_Ssd, Rac], writes=[R_Ssd])
                    for g in range(2):
                        P.dve(lambda e, g=g, ptt=pss[g][0]: e.tensor_tensor(out=Ssd[:, g * 512:(g + 1) * 512], in0=ptt[:, 0:512],
                                                                            in1=Ssd[:, g * 512:(g + 1) * 512], op=ALU.add),
                              reads=[pss[g][1], R_Ssd], writes=[R_Ssd])
                    P.act(lambda e: e.activation(out=Ssdb[:], in_=Ssd[:], func=AF.Copy), reads=[R_Ssd], writes=[R_Ssdb])

            if is_pre:
                mcol = par[:, PC_MASK + pre_idx:PC_MASK + pre_idx + 1]
                P.dve(lambda e: e.tensor_scalar(out=hcst[:, 24:32], in0=hcst[:, 0:8], scalar1=mcol, scalar2=None, op0=ALU.mult),
                      reads=[R_hcst, R_par, R_hm], writes=[R_hm])
                P.dve(lambda e: e.tensor_scalar(out=hcst[:, 32:40], in0=hcst[:, 16:24], scalar1=mcol, scalar2=None, op0=ALU.mult),
                      reads=[R_hcst, R_par, R_hm], writes=[R_hm])
            hw = {}

            def getH(a):
                if a not in hw:
                    hw[a] = None
                    hw[a] = wget("H%d" % a, auto=False, nref=2)
                while hw[a] is None:
                    rec_state["yield"]()
                return hw[a]

            def sec_hg_head(i):
                b_ = bt[i]
                Rb = R_bt[i]
                sc = stat[:, 40 + 8 * i:40 + 8 * i + 8]
                Rsc = R_scp[i]
                e0, e1, e2, e3 = etmp[4 * i:4 * i + 4]
                Re0, Re1, Re2, Re3 = R_et[4 * i:4 * i + 4]
                for a in range(4):
                    h = 2 * a + i
                    if a > 0:
                        wdone(hw[a - 1][2])
                    wt, Rw, _ = getH(a)
                    if not is_pre:
                        pt, Rp = proj_fm(wt, Rw, i, hT, R_hT)
                        P.act(lambda e, pt=pt: e.activation(out=qf[:, i, :], in_=pt[:, 0:512], func=AF.Silu), reads=[Rp], writes=[R_qf2[i]])
                    pt, Rp = proj_fm(wt, Rw, 2 + i, hT, R_hT)
                    P.act(lambda e, pt=pt: e.activation(out=kf[:, i, :], in_=pt[:, 0:512], func=AF.Tanh, scale=0.5), reads=[Rp], writes=[R_kf2[i]])
                    P.dve(lambda e, h=h: e.tensor_scalar(out=gl[:, i, :], in0=kf[:, i, :], scalar1=hcst[:, h:h + 1], scalar2=hcst[:, 8 + h:9 + h],
                                                         op0=ALU.mult, op1=ALU.add), reads=[R_kf2[i], R_hcst], writes=[R_gl2[i]])
                    P.act(lambda e: e.activation(out=gl[:, i, :], in_=gl[:, i, :], func=AF.Ln), reads=[R_gl2[i]], writes=[R_gl2[i]])
                    if is_pre:
                        P.dve(lambda e, h=h: e.tensor_scalar(out=kf[:, i, :], in0=kf[:, i, :], scalar1=hcst[:, 32 + h:33 + h], scalar2=hcst[:, 24 + h:25 + h],
                                                             op0=ALU.mult, op1=ALU.add), reads=[R_kf2[i], R_hm], writes=[R_kf2[i]])
                    else:
                        P.dve(lambda e, h=h: e.tensor_scalar(out=kf[:, i, :], in0=kf[:, i, :], scalar1=hcst[:, 16 + h:17 + h], scalar2=hcst[:, h:h + 1],
                                                             op0=ALU.mult, op1=ALU.add), reads=[R_kf2[i], R_hcst], writes=[R_kf2[i]])
                    P.pool(lambda e: e.memset(b_[:, 0:1], 0.0), writes=[Rb])
                    P.dve(lambda e: e.tensor_tensor_scan(out=b_[:, 1:513], data0=ones[:, 0:512], data1=gl[:, i, :], initial=0.0,
                                                         op0=ALU.mult, op1=ALU.add), reads=[R_ones, R_gl2[i], Rb], writes=[Rb])
                    if is_pre:
                        khf = pre_kh[i]; khtmf = pre_khtm[i]
                        P.act(lambda e: e.activation(out=gl[:, i, :], in_=b_[:, 1:513], func=AF.Exp, bias=b_[:, 512:513], scale=-1.0),
                              reads=[Rb, R_gl2[i]], writes=[R_gl2[i]])
                        P.dve(lambda e, khf=khf: e.tensor_tensor(out=khf, in0=kf[:, i, :], in1=gl[:, i, :], op=ALU.mult),
                              reads=[R_kf2[i], R_gl2[i]], writes=[R_prekh[i]])
                        P.act(lambda e: e.activation(out=sc[:, 3:4], in_=b_[:, 512:513], func=AF.Exp), reads=[Rb, Rsc], writes=[Rsc])
                        pt, Rp = psum()
                        pv = bfv(pt)
                        for j in range(4):
                            P.pe(lambda e, j=j, pv=pv, khf=khf: e.transpose(out=pv[:, j * 128:(j + 1) * 128], in_=khf[:, j * 128:(j + 1) * 128],
                                                                            identity=ident[:]), reads=[R_prekh[i], R_ident], writes=[Rp])
                        P.act(lambda e, pv=pv, khtmf=khtmf: e.activation(out=khtmf, in_=pv[:, 0:512], func=AF.Copy), reads=[Rp], writes=[R_prekhtm[i]])
                        pt2, Rp2 = psum()
                        for j in range(4):
                            P.pe(lambda e, j=j, h=h, pt2=pt2, khtmf=khtmf: e.matmul(pt2[:, 0:128], lhsT=khtmf[:, j * 128:(j + 1) * 128],
                                                                                    rhs=vt[:, j, h * 128:(h + 1) * 128], start=(j == 0), stop=(j == 3)),
                                 reads=[R_prekhtm[i], R_vt[j]], writes=[Rp2])
                        P.dve(lambda e, h=h, pt2=pt2: e.scalar_tensor_tensor(out=Shg[:, h, :], in0=Shg[:, h, :], scalar=sc[:, 3:4],
                                                                            in1=pt2[:, 0:128], op0=ALU.mult, op1=ALU.add),
                              reads=[R_Shg[h], Rsc, Rp2], writes=[R_Shg[h]])
                        P.act(lambda e, h=h: e.activation(out=Shgb[:, h, :], in_=Shg[:, h, :], func=AF.Copy), reads=[R_Shg[h]], writes=[R_Shgb[h]])
                        continue
                    for c in range(4):
                        pso = psum_ded()
                        P.pool(lambda e: e.memset(sst[:, 8 + i:9 + i], 0.0), writes=[R_ssth2[i]])
                        c0 = c * 128
                        bseg = b_[:, c0 + 1:c0 + 129]
                        blast = b_[:, c0 + 128:c0 + 129]
                        bprev = b_[:, c0:c0 + 1]
                        b31 = b_[:, c0 + 32:c0 + 33]
                        b63 = b_[:, c0 + 64:c0 + 65]
                        b95 = b_[:, c0 + 96:c0 + 97]
                        P.dve(lambda e, b31=b31: e.tensor_scalar(out=sc[:, 0:1], in0=b31, scalar1=-1.0, scalar2=None, op0=ALU.mult),
                              reads=[Rb, Rsc], writes=[Rsc])
                        P.dve(lambda e, b95=b95: e.tensor_scalar(out=sc[:, 1:2], in0=b95, scalar1=-1.0, scalar2=None, op0=ALU.mult),
                              reads=[Rb, Rsc], writes=[Rsc])
                        P.dve(lambda e, bprev=bprev: e.tensor_scalar(out=sc[:, 2:3], in0=bprev, scalar1=-1.0, scalar2=None, op0=ALU.mult),
                              reads=[Rb, Rsc], writes=[Rsc])
                        P.dve(lambda e, bprev=bprev, blast=blast: e.tensor_tensor(out=sc[:, 3:4], in0=blast, in1=bprev, op=ALU.subtract),
                              reads=[Rb, Rsc], writes=[Rsc])
                        P.dve(lambda e, b63=b63, b31=b31: e.tensor_tensor(out=sc[:, 4:5], in0=b63, in1=b31, op=ALU.subtract),
                              reads=[Rb, Rsc], writes=[Rsc])
                        P.dve(lambda e, b63=b63, b95=b95: e.tensor_tensor(out=sc[:, 5:6], in0=b95, in1=b63, op=ALU.subtract),
                              reads=[Rb, Rsc], writes=[Rsc])
                        P.act(lambda e: e.activation(out=sc[:, 3:6], in_=sc[:, 3:6], func=AF.Exp), reads=[Rsc], writes=[Rsc])
                        kfc = kf[:, i, c0:c0 + 128]
                        qfc = qf[:, i, c0:c0 + 128]
                        P.act(lambda e, bseg=bseg: e.activation(out=e0[:, 0:64], in_=bseg[:, 0:64], func=AF.Exp, bias=sc[:, 0:1], scale=1.0),
                              reads=[Rb, Rsc], writes=[Re0])
                        P.act(lambda e, bseg=bseg: e.activation(out=e0[:, 64:128], in_=bseg[:, 64:128], func=AF.Exp, bias=sc[:, 1:2], scale=1.0),
                              reads=[Rb, Rsc], writes=[Re0])
                        P.dve(lambda e, qfc=qfc: e.tensor_tensor(out=qt_[i], in0=qfc, in1=e0, op=ALU.mult),
                              reads=[R_qf2[i], Re0], writes=[R_qt[i]])
                        P.dve(lambda e: e.tensor_scalar(out=QC[i], in0=qt_[i][:, 64:128], scalar1=sc[:, 5:6], scalar2=None, op0=ALU.mult),
                              reads=[R_qt[i], Rsc], writes=[R_QC[i]])
                        P.act(lambda e, bseg=bseg, b31=b31: e.activation(out=e1[:, 0:64], in_=bseg[:, 0:64], func=AF.Exp, bias=b31, scale=-1.0),
                              reads=[Rb], writes=[Re1])
                        P.act(lambda e, bseg=bseg, b95=b95: e.activation(out=e1[:, 64:128], in_=bseg[:, 64:128], func=AF.Exp, bias=b95, scale=-1.0),
                              reads=[Rb], writes=[Re1])
                        P.dve(lambda e, kfc=kfc: e.tensor_tensor(out=KA[i][:, 0:64], in0=kfc[:, 0:64], in1=e1[:, 0:64], op=ALU.mult),
                              reads=[R_kf2[i], Re1], writes=[R_KA[i]])
                        P.dve(lambda e, kfc=kfc: e.tensor_tensor(out=KB[i][:, 64:128], in0=kfc[:, 64:128], in1=e1[:, 64:128], op=ALU.mult),
                              reads=[R_kf2[i], Re1], writes=[R_KB[i]])
                        P.dve(lambda e: e.tensor_scalar(out=KC[i][:, 0:64], in0=KA[i][:, 0:64], scalar1=sc[:, 4:5], scalar2=None, op0=ALU.mult),
                              reads=[R_KA[i], Rsc], writes=[R_KC[i]])
                        P.act(lambda e, bseg=bseg: e.activation(out=e2, in_=bseg, func=AF.Exp, bias=sc[:, 2:3], scale=1.0),
                              reads=[Rb, Rsc], writes=[Re2])
                        P.dve(lambda e, qfc=qfc: e.tensor_tensor(out=qh[i], in0=qfc, in1=e2, op=ALU.mult),
                              reads=[R_qf2[i], Re2], writes=[R_qh[i]])
                        P.act(lambda e, bseg=bseg, blast=blast: e.activation(out=e3, in_=bseg, func=AF.Exp, bias=blast, scale=-1.0),
                              reads=[Rb], writes=[Re3])
                        P.dve(lambda e, kfc=kfc: e.tensor_tensor(out=kh[i], in0=kfc, in1=e3, op=ALU.mult),
                              reads=[R_kf2[i], Re3], writes=[R_kh[i]])
                        vch = vt[:, c, h * 128:(h + 1) * 128]
                        pt, Rp = psum()
                        P.pe(lambda e, pt=pt: e.matmul(pt[:, 0:64], lhsT=KA[i], rhs=qt_[i][:, 0:64], start=True, stop=True),
                             reads=[R_KA[i], R_qt[i]], writes=[Rp])
                        P.pe(lambda e, pt=pt: e.matmul(pt[:, 64:128], lhsT=KB[i], rhs=qt_[i][:, 64:128], start=True, stop=False),
                             reads=[R_KB[i], R_qt[i]], writes=[Rp])
                        P.pe(lambda e, pt=pt: e.matmul(pt[:, 64:128], lhsT=KC[i], rhs=QC[i], start=False, stop=True),
                             reads=[R_KC[i], R_QC[i]], writes=[Rp])
                        P.dve(lambda e, pt=pt: e.tensor_tensor(out=attm[i], in0=pt[:, 0:128], in1=U[:], op=ALU.mult),
                              reads=[Rp, R_U], writes=[R_attm[i]])
                        P.pe(lambda e, vch=vch, ptt=pso[0]: e.matmul(ptt[:, 0:128], lhsT=attm[i], rhs=vch, start=True, stop=False),
                             reads=[R_attm[i], R_vt[c]], writes=[pso[1]])
                        P.pe(lambda e, h=h, ptt=pso[0]: e.matmul(ptt[:, 0:128], lhsT=qh[i], rhs=Shgb[:, h, :], start=False, stop=True),
                             reads=[R_qh[i], R_Shgb[h]], writes=[pso[1]])
                        pt, Rp = psum()
                        pv = bfv(pt)
                        P.pe(lambda e, pv=pv: e.transpose(out=pv[:, 0:128], in_=kh[i], identity=ident[:]),
                             reads=[R_kh[i], R_ident], writes=[Rp])
                        P.act(lambda e, pv=pv: e.activation(out=khtm[i], in_=pv[:, 0:128], func=AF.Copy), reads=[Rp], writes=[R_khtm[i]])
                        pt2, Rp2 = psum()
                        P.pe(lambda e, vch=vch, pt2=pt2: e.matmul(pt2[:, 0:128], lhsT=khtm[i], rhs=vch, start=True, stop=True),
                             reads=[R_khtm[i], R_vt[c]], writes=[Rp2])
                        P.dve(lambda e, h=h, pt2=pt2: e.scalar_tensor_tensor(out=Shg[:, h, :], in0=Shg[:, h, :], scalar=sc[:, 3:4],
                                                                            in1=pt2[:, 0:128], op0=ALU.mult, op1=ALU.add),
                              reads=[R_Shg[h], Rsc, Rp2], writes=[R_Shg[h]])
                        P.act(lambda e, h=h: e.activation(out=Shgb[:, h, :], in_=Shg[:, h, :], func=AF.Copy), reads=[R_Shg[h]], writes=[R_Shgb[h]])
                        ptt, Rpp = pso
                        P.act(lambda e, ptt=ptt: e.activation(out=junk[:, 512 + 128 * i:640 + 128 * i], in_=ptt[:, 0:128], func=AF.Square,
                                                              accum_out=sst[:, 8 + i:9 + i]), reads=[Rpp, R_ssth2[i]], writes=[R_junkh2[i], R_ssth2[i]])
                        rstd_from_ss(sst[:, 8 + i:9 + i], 1, R_ssth2[i], 1.0 / 128)
                        ot = otmp[:, 128 * i:128 * i + 128]
                        ogi = og[:, 128 * i:128 * i + 128]
                        P.dve(lambda e, ptt=ptt, ot=ot: e.tensor_scalar(out=ot, in0=ptt[:, 0:128], scalar1=sst[:, 8 + i:9 + i], scalar2=None, op0=ALU.mult),
                              reads=[Rpp, R_ssth2[i]], writes=[R_otmp2[i]])
                        P.dve(lambda e, h=h, c=c, ot=ot, ogi=ogi: e.tensor_tensor(out=ogi, in0=ot, in1=gs[:, c, 128 * h:128 * h + 128], op=ALU.mult),
                              reads=[R_otmp2[i], R_gs[c]], writes=[R_og2[i]])
                        pt, Rp = psum()
                        pv = bfv(pt)
                        P.pe(lambda e, pv=pv, ogi=ogi: e.transpose(out=pv[:, 0:128], in_=ogi, identity=ident[:]), reads=[R_og2[i], R_ident], writes=[Rp])
                        P.act(lambda e, h=h, c=c, pv=pv: e.activation(out=mixedT[:, 8 + h, c * 128:(c + 1) * 128], in_=pv[:, 0:128], func=AF.Copy),
                              reads=[Rp], writes=[R_mixTh2[i][c]])
            def stream1():
                secA()
                wdone_cur()
                rec_wait("B")
                sec_ssd()

            def stream2():
                secB()
                rec_set("B")
                wdone_cur()
                tl.pool = {'banks': [5], 'ctr': 0, 'ded': 4}
                sec_hg_head(0)
                wdone(hw[3][2])

            def stream3():
                rec_wait("B")
                sec_hg_head(1)
                wdone(hw[3][2])

            wdone_cur()
            if _SEQ_DEBUG:
                tl.pool = {'banks': [0, 1, 2, 3], 'ctr': 0, 'ded': None}
                secA(); wdone_cur()
                tl.pool = {'banks': [4, 5, 6, 7], 'ctr': 0, 'ded': None}
                secB(); wdone_cur()
                tl.pool = {'banks': [0, 1, 2, 3], 'ctr': 0, 'ded': None}
                sec_ssd()
                tl.pool = {'banks': [5], 'ctr': 0, 'ded': 4}
                sec_hg_head(0)
                tl.pool = {'banks': [7], 'ctr': 0, 'ded': 6}
                sec_hg_head(1)
                wdone(hw[3][2]); wdone(hw[3][2])
                tl.pool = None
            else:
              run_interleaved([stream1, stream2, stream3],
                            [{'banks': [0, 1, 2, 3], 'ctr': 0, 'ded': None}, {'banks': [4, 5, 6, 7], 'ctr': 0, 'ded': None},
                             {'banks': [7], 'ctr': 0, 'ded': 6}])
            if is_pre:
                P.pool(lambda e: e.memset(qf[:, 0, 0:1], 0.0), reads=R_prekh + R_prekhtm, writes=R_qf2 + R_prekh + R_prekhtm)
                return
            P.barrier()

            A = Arena()
            qT = A.bf(8 * 512).rearrange("p (k t) -> p k t", k=8); R_qT = Res()
            pe_ = [A.f32(1024).rearrange("p (h k) -> p h k", h=4) for _ in range(2)]; R_pe = [Res(), Res()]
            pn = [A.bf(1024).rearrange("p (h k) -> p h k", h=4) for _ in range(2)]; R_pn = [Res(), Res()]
            prT = A.bf(2 * 4 * 512).rearrange("p (k h t) -> p k h t", k=2, h=4); R_prT = Res()
            oT = A.bf(8 * 512).rearrange("p (k t) -> p k t", k=8); R_oT = Res()
            actT = A.bf(22 * 512).rearrange("p (k t) -> p k t", k=22); R_actT = Res()
            sgt = [A.f32(512) for _ in range(2)]; R_sgt = [Res(), Res()]
            nfw = A.f32(D); R_nfw = Res()
            sa = A.f32(32); R_sa = [Res(), Res()]
            nfw_op = P.dma("sp", "nfw", nfw, nfwd, writes=[R_nfw])
            for lo in P.last_real.values():
                nfw_op.deps[lo] = {"raw"}

            for j in range(2):
                banks = [psum() for _ in range(4)]
                for i in range(2):
                    wt, Rw, _ = wget("OUT%d%d" % (j, i))
                    for s in range(4):
                        ptt, Rpp = banks[s]
                        for kt in range(8):
                            P.pe(lambda e, kt=kt, s=s, i=i, ptt=ptt, wt=wt: e.matmul(
                                ptt[:, 0:512], lhsT=mixedT[:, 8 * i + kt, s * 128:(s + 1) * 128], rhs=wt[:, kt, :],
                                start=(i == 0 and kt == 0), stop=(i == 1 and kt == 7)), reads=[R_mixT[s], R_mixTh2[0][s], R_mixTh2[1][s], Rw], writes=[Rpp])
                for s in range(4):
                    ptt, Rpp = banks[s]
                    P.dve(lambda e, s=s, j=j, ptt=ptt: e.tensor_tensor(out=x_tm[:, s, j * 512:(j + 1) * 512], in0=ptt[:, 0:512],
                                                                       in1=x_tm[:, s, j * 512:(j + 1) * 512], op=ALU.add),
                          reads=[Rpp, R_x[s]], writes=[R_x[s]])
            rms_T(lambda s: x_tm[:, s, :], R_x, 4, hT, R_hT)
            for jc in range(2):
                wt, Rw, _ = wget("Q%d" % jc)
                for j in range(4):
                    pt, Rp = proj_fm(wt, Rw, j, hT, R_hT)
                    P.act(lambda e, pt=pt, jc=jc, j=j: e.activation(out=qT[:, 4 * jc + j, :], in_=pt[:, 0:512], func=AF.Copy),
                          reads=[Rp], writes=[R_qT])
            for s in range(4):
                b = s % 2
                scb = [psum(), psum()]
                for h in range(4):
                    ptt, Rpp = scb[h // 2]
                    for d2 in range(2):
                        P.pe(lambda e, h=h, d2=d2, s=s, ptt=ptt: e.matmul(ptt[:, (h % 2) * 256:(h % 2) * 256 + 256],
                                                                          lhsT=qT[:, 2 * h + d2, s * 128:(s + 1) * 128], rhs=kmT[:, 2 * h + d2, :],
                                                                          start=(d2 == 0), stop=(d2 == 1)), reads=[R_qT, R_kmT], writes=[Rpp])
                sav = sa[:, 16 * b:16 * b + 16]
                Rsa = R_sa[b]
                for hb in range(2):
                    P.dve(lambda e, hb=hb, sav=sav, ptt=scb[hb][0]: e.tensor_reduce(out=sav[:, 2 * hb:2 * hb + 2],
                                                                                   in_=ptt[:, 0:512].rearrange("p (h k) -> p h k", h=2),
                                                                                   axis=AX.X, op=ALU.max), reads=[scb[hb][1], Rsa], writes=[Rsa])
                P.dve(lambda e, sav=sav: e.tensor_scalar(out=sav[:, 0:4], in0=sav[:, 0:4], scalar1=-1.0 / 16, scalar2=None, op0=ALU.mult),
                      reads=[Rsa], writes=[Rsa])
                P.pool(lambda e, sav=sav: e.memset(sav[:, 4:8], 0.0), reads=[Rsa], writes=[Rsa])
                for h in range(4):
                    ptt, Rpp = scb[h // 2]
                    P.act(lambda e, h=h, b=b, sav=sav, ptt=ptt: e.activation(out=pe_[b][:, h, :], in_=ptt[:, (h % 2) * 256:(h % 2) * 256 + 256],
                                                                             func=AF.Exp, bias=sav[:, h:h + 1], scale=1.0 / 16,
                                                                             accum_out=sav[:, 4 + h:5 + h]), reads=[Rpp, Rsa], writes=[R_pe[b], Rsa])
                P.dve(lambda e, sav=sav: e.reciprocal(out=sav[:, 4:8], in_=sav[:, 4:8]), reads=[Rsa], writes=[Rsa])
                P.dve(lambda e, b=b, sav=sav: e.tensor_tensor(out=pn[b], in0=pe_[b], in1=sav[:, 4:8].unsqueeze(2).broadcast_to([128, 4, 256]),
                                                              op=ALU.mult), reads=[R_pe[b], Rsa], writes=[R_pn[b]])
                pt, Rp = psum()
                pv = bfv(pt)
                for k2 in range(2):
                    for h in range(4):
                        P.pe(lambda e, k2=k2, h=h, b=b, pv=pv: e.transpose(out=pv[:, (k2 * 4 + h) * 128:(k2 * 4 + h + 1) * 128],
                                                                          in_=pn[b][:, h, k2 * 128:(k2 + 1) * 128], identity=ident[:]),
                             reads=[R_pn[b], R_ident], writes=[Rp])
                for k2 in range(2):
                    P.act(lambda e, s=s, k2=k2, pv=pv: e.activation(out=prT[:, k2, :, s * 128:(s + 1) * 128],
                                                                    in_=pv[:, k2 * 512:(k2 + 1) * 512].rearrange("p (h t) -> p h t", h=4),
                                                                    func=AF.Copy), reads=[Rp], writes=[R_prT])
            for h in range(4):
                for d2 in range(2):
                    pt, Rp = psum()
                    for k2 in range(2):
                        P.pe(lambda e, h=h, d2=d2, k2=k2, pt=pt: e.matmul(pt[:, 0:512], lhsT=vm[:, k2, h * 256 + d2 * 128:h * 256 + (d2 + 1) * 128],
                                                                          rhs=prT[:, k2, h, :], start=(k2 == 0), stop=(k2 == 1)),
                             reads=[R_vm, R_prT], writes=[Rp])
                    P.act(lambda e, h=h, d2=d2, pt=pt: e.activation(out=oT[:, 2 * h + d2, :], in_=pt[:, 0:512], func=AF.Copy),
                          reads=[Rp], writes=[R_oT])
            for j in range(2):
                wt, Rw, _ = wget("O%d" % j)
                for s in range(4):
                    pt, Rp = proj_tm(wt, Rw, s, oT, R_oT)
                    P.dve(lambda e, s=s, j=j, pt=pt: e.tensor_tensor(out=x_tm[:, s, j * 512:(j + 1) * 512], in0=pt[:, 0:512],
                                                                     in1=x_tm[:, s, j * 512:(j + 1) * 512], op=ALU.add),
                          reads=[Rp, R_x[s]], writes=[R_x[s]])
            rms_T(lambda s: x_tm[:, s, :], R_x, 4, hT, R_hT)
            for jc in range(11):
                wt, Rw, _ = wget("GU%d" % jc)
                for jj in range(2):
                    pg, Rpg = proj_fm(wt, Rw, jj, hT, R_hT)
                    pu, Rpu = proj_fm(wt, Rw, 2 + jj, hT, R_hT)
                    sg, Rsg = sgt[jj], R_sgt[jj]
                    P.act(lambda e, pg=pg, sg=sg: e.activation(out=sg, in_=pg[:, 0:512], func=AF.Silu), reads=[Rpg], writes=[Rsg])
                    P.dve(lambda e, pu=pu, sg=sg, jc=jc, jj=jj: e.tensor_tensor(out=actT[:, 2 * jc + jj, :], in0=pu[:, 0:512], in1=sg, op=ALU.mult),
                          reads=[Rpu, Rsg], writes=[R_actT])
            for j in range(2):
                banks = [psum() for _ in range(4)]
                for i in range(3):
                    wt, Rw, _ = wget("D%d%d" % (j, i))
                    nk = 8 if i < 2 else 6
                    for s in range(4):
                        ptt, Rpp = banks[s]
                        for kt in range(nk):
                            P.pe(lambda e, kt=kt, s=s, i=i, nk=nk, ptt=ptt, wt=wt: e.matmul(
                                ptt[:, 0:512], lhsT=actT[:, 8 * i + kt, s * 128:(s + 1) * 128], rhs=wt[:, kt, :],
                                start=(i == 0 and kt == 0), stop=(i == 2 and kt == nk - 1)), reads=[R_actT, Rw], writes=[Rpp])
                for s in range(4):
                    ptt, Rpp = banks[s]
                    P.dve(lambda e, s=s, j=j, ptt=ptt: e.tensor_tensor(out=x_tm[:, s, j * 512:(j + 1) * 512], in0=ptt[:, 0:512],
                                                                       in1=x_tm[:, s, j * 512:(j + 1) * 512], op=ALU.add),
                          reads=[Rpp, R_x[s]], writes=[R_x[s]])
            ssf = stat[:, 32:36]
            Rsf = R_ssf
            P.pool(lambda e: e.memset(ssf, 0.0), writes=[Rsf])
            for s in range(4):
                P.act(lambda e, s=s: e.activation(out=junk[:], in_=x_tm[:, s, :], func=AF.Square, accum_out=ssf[:, s:s + 1]),
                      reads=[R_x[s], Rsf], writes=[R_junk, R_junkh2[0], R_junkh2[1], Rsf])
            rstd_from_ss(ssf, 4, Rsf, 1.0 / D)
            for s in range(4):
                b = 0
                P.dve(lambda e, s=s, b=b: e.scalar_tensor_tensor(out=ost[b][:], in0=x_tm[:, s, :], scalar=ssf[:, s:s + 1], in1=nfw,
                                                                 op0=ALU.mult, op1=ALU.mult), reads=[R_x[s], Rsf, R_nfw], writes=[R_ost[b]])
                out_dmas.append(P.dma("sp", "ost%d" % b, outd[out_row0 + s * 128:out_row0 + (s + 1) * 128, :], ost[b][:],
                                      reads=[R_ost[b]]))
            P.barrier()

        for t in range(NPRE):
            do_tile(xp, t * 512, True, t, 0)
        for t in range(NT):
            do_tile(xm, t * 512, False, 0, t * 512)
        if wseq is not None:
            assert wstate["got"] == len(wseq), (wstate["got"], len(wseq))
            P.emit(final_wait_ops=out_dmas)
    return nc, wrec


def make_par(inp, NPRE, premask):
    f = lambda a: np.asarray(a, dtype=np.float32)
    par = np.zeros((128, PC_MASK + max(NPRE, 1)), np.float32)
    cw = f(inp["conv_w"])[0]
    par[:, PC_CW:PC_CW + 48] = cw.reshape(4, 12, 128).transpose(2, 1, 0).reshape(128, 48)
    par[:, PC_CB:PC_CB + 12] = f(inp["conv_b"])[0].reshape(12, 128).T
    par[:, PC_DTB:PC_DTB + 16] = f(inp["dt_bias"])[0][None, :]
    par[:, PC_ALOG:PC_ALOG + 16] = f(inp["a_log"])[0][None, :]
    par[:, PC_DSK:PC_DSK + 16] = f(inp["d_skip"])[0][None, :]
    hlb = f(inp["hg_lower_bounds"])
    par[:, PC_HLB0:PC_HLB0 + 8] = hlb[0].reshape(8, 128).T
    par[:, PC_HLB1:PC_HLB1 + 8] = hlb[1].reshape(8, 128).T
    par[:, PC_MIX:PC_MIX + 8] = f(inp["norm_mix_w"])[0].reshape(8, 128).T
    par[:, PC_XA:PC_XA + 8] = f(inp["norm_xa_w"])[0].reshape(8, 128).T
    par[:, PC_MEM:PC_MEM + 8] = f(inp["norm_mem_w"])[0].reshape(8, 128).T
    par[:, PC_FFN:PC_FFN + 8] = f(inp["norm_ffn_w"])[0].reshape(8, 128).T
    par[:, PC_WOUT:PC_WOUT + 8] = f(inp["ssd_norm_w"])[0].reshape(8, 128).T
    par[:, PC_WOUT + 8:PC_WOUT + 16] = f(inp["hg_norm_w"])[0][:, None]
    par[:, PC_MASK:PC_MASK + len(premask)] = np.asarray(premask, np.float32)[None, :]
    return par


_NC_CACHE = {}


def run(inp, T, NPRE, nseg):
    x = np.asarray(inp["x"], np.float32)
    mem = np.asarray(inp["mem"], np.float32)
    B, L, _ = x.shape
    assert L == nseg * T
    key = (T, NPRE)
    if key not in _NC_CACHE:
        _NC_CACHE[key] = build(T, NPRE)
    nc = _NC_CACHE[key]
    shared = {
        "w_in": np.ascontiguousarray(inp["w_in"][0], np.float32), "w_out": np.ascontiguousarray(inp["w_out"][0], np.float32),
        "wq": np.ascontiguousarray(inp["xa_wq"][0], np.float32), "wkv": np.ascontiguousarray(inp["xa_wkv"][0], np.float32),
        "wo": np.ascontiguousarray(inp["xa_wo"][0], np.float32), "wg": np.ascontiguousarray(inp["ffn_w_gate"][0], np.float32),
        "wu": np.ascontiguousarray(inp["ffn_w_up"][0], np.float32), "wd": np.ascontiguousarray(inp["ffn_w_down"][0], np.float32),
        "nfw": np.ascontiguousarray(np.broadcast_to(np.asarray(inp["norm_final_w"], np.float32)[None, :], (128, D))),
    }
    in_maps = []
    npre_tok = max(NPRE, 1) * 512
    for b in range(B):
        for sg in range(nseg):
            start = sg * T
            xpre = np.zeros((npre_tok, D), np.float32)
            premask = np.zeros(max(NPRE, 1), np.float32)
            lo = start - NPRE * 512
            for t in range(NPRE):
                p0 = lo + t * 512
                if p0 >= 0:
                    xpre[t * 512:(t + 1) * 512] = x[b, p0:p0 + 512]
                    premask[t] = 1.0
            m = dict(shared)
            m["xm"] = np.ascontiguousarray(x[b, start:start + T])
            m["xp"] = xpre
            m["mem"] = np.ascontiguousarray(mem[b])
            m["par"] = make_par(inp, NPRE, premask)
            in_maps.append(m)
    res = run_bass_kernel_spmd(nc, in_maps, core_ids=list(range(B * nseg)))
    out = np.zeros((B, L, D), np.float32)
    k = 0
    for b in range(B):
        for sg in range(nseg):
            out[b, sg * T:(sg + 1) * T] = res.results[k]["out"]
            k += 1
    return out


def kernel(**inputs):
    return run(inputs, 4096, 24, 4)
```

```python
import contextlib
import threading
import numpy as np
import concourse.bass as bass
import concourse.mybir as mybir
from concourse.bass_utils import run_bass_kernel_spmd

F32 = mybir.dt.float32
BF16 = mybir.dt.bfloat16
AF = mybir.ActivationFunctionType
ALU = mybir.AluOpType
AX = mybir.AxisListType

D = 1024
FF = 2816
EPS = 1e-6
S1, S2, S3, S4, S5, S6 = 1024, 2560, 2576, 3600, 4624, 5648
NWS = 3
import os as _os
_SEQ_DEBUG = bool(_os.environ.get('KSEQ'))


class Res:
    __slots__ = ("name", "last_w", "readers")

    def __init__(self, name=""):
        self.name = name
        self.last_w = None
        self.readers = {}


class Op:
    __slots__ = ("eng", "fn", "idx", "deps", "dma_key", "dma_cnt", "needs_inc", "inc_val")

    def __init__(self, eng, fn, idx, dma_key=None):
        self.eng = eng
        self.fn = fn
        self.idx = idx
        self.deps = {}
        self.dma_key = dma_key
        self.dma_cnt = 0
        self.needs_inc = False
        self.inc_val = 0


COMPUTE = ("pe", "act", "dve", "pool")


class Prog:
    def __init__(self, nc):
        self.nc = nc
        self.ops = []
        self.dma_counts = {}
        self.last_real = {}
        self.after_add = None

    def add(self, eng, fn, reads=(), writes=(), dma_key=None):
        op = Op(eng, fn, len(self.ops), dma_key)
        if dma_key is not None:
            c = self.dma_counts.get(dma_key, 0) + 1
            self.dma_counts[dma_key] = c
            op.dma_cnt = c
        deps = op.deps
        for r in reads:
            if r.last_w is not None:
                deps.setdefault(r.last_w, set()).add("raw")
        for w in writes:
            lw = w.last_w
            if lw is not None:
                if not (dma_key is not None and lw.dma_key == dma_key):
                    deps.setdefault(lw, set()).add("waw")
            for rd in w.readers.values():
                deps.setdefault(rd, set()).add("war")
        k = ("dma", op.idx) if dma_key is not None else eng
        for r in reads:
            r.readers[k] = op
        for w in writes:
            w.last_w = op
            w.readers = {}
        self.ops.append(op)
        if dma_key is None:
            self.last_real[eng] = op
        if self.after_add is not None:
            self.after_add()
        return op

    def pe(self, fn, reads=(), writes=()):
        return self.add("pe", fn, reads, writes)

    def act(self, fn, reads=(), writes=()):
        return self.add("act", fn, reads, writes)

    def dve(self, fn, reads=(), writes=()):
        return self.add("dve", fn, reads, writes)

    def pool(self, fn, reads=(), writes=()):
        return self.add("pool", fn, reads, writes)

    def dma(self, queue, key, out, in_, reads=(), writes=()):
        return self.add(queue, lambda e: e.dma_start(out=out, in_=in_), reads, writes, dma_key=key)

    def barrier(self, extra=()):
        lasts = dict(self.last_real)
        for e in COMPUTE:
            op = Op(e, None, len(self.ops))
            for e2, lo in lasts.items():
                if e2 != e:
                    op.deps[lo] = {"raw"}
            for x in extra:
                op.deps[x] = {"raw"}
            self.ops.append(op)

    def emit(self, final_wait_ops=()):
        nc = self.nc
        fin = Op("sp", None, len(self.ops))
        for o in final_wait_ops:
            fin.deps[o] = {"raw"}
        ops = self.ops + [fin]
        for op in ops:
            real = {}
            for d, kinds in op.deps.items():
                if d.dma_key is None and d.eng == op.eng and op.dma_key is None:
                    if op.eng == "pe" or kinds == {"war"}:
                        continue
                real[d] = kinds
            op.deps = real
            for d in real:
                if d.dma_key is None:
                    d.needs_inc = True
        cnt = {e: 0 for e in COMPUTE + ("sp",)}
        for op in ops:
            if op.dma_key is None and op.needs_inc:
                cnt[op.eng] += 1
                op.inc_val = cnt[op.eng]
        dma_keys = sorted(self.dma_counts.keys())
        with contextlib.ExitStack() as st:
            esem = {e: st.enter_context(nc.semaphore("s_" + e)) for e in cnt}
            dsem = {k: st.enter_context(nc.semaphore("d_%d" % i)) for i, k in enumerate(dma_keys)}
            block = st.enter_context(nc.Block())
            engs = {"pe": block.tensor, "act": block.scalar, "dve": block.vector,
                    "pool": block.gpsimd, "sp": block.sync}
            for ename, deco in engs.items():
                my = [o for o in ops if o.eng == ename]
                if not my:
                    continue

                def body(e, my=my, ename=ename):
                    waited = {}
                    for op in my:
                        need = {}
                        for d in op.deps:
                            if d.dma_key is not None:
                                s, v = ("d", d.dma_key), 16 * d.dma_cnt
                            else:
                                s, v = ("e", d.eng), d.inc_val
                            if need.get(s, 0) < v:
                                need[s] = v
                        for s, v in need.items():
                            if waited.get(s, 0) >= v:
                                continue
                            waited[s] = v
                            e.wait_ge(dsem[s[1]] if s[0] == "d" else esem[s[1]], v)
                        if op.fn is None:
                            continue
                        ins = op.fn(e)
                        if op.dma_key is not None:
                            ins.then_inc(dsem[op.dma_key], 16)
                        elif op.needs_inc:
                            ins.then_inc(esem[ename], 1)

                deco(body)


def chunk_catalog():
    cat = []
    for j in range(2):
        cat.append(("Z%d" % j, "w_in", 0, 8, [(0, 512, 512 * j)], "mix"))
    for j in range(3):
        cat.append(("X%d" % j, "w_in", 0, 8, [(0, 512, S1 + 512 * j)], "mix"))
    for a in range(4):
        cat.append(("H%d" % a, "w_in", 0, 8, [(0, 256, S3 + 256 * a), (256, 256, S4 + 256 * a)], "mix"))
    for j in range(2):
        cat.append(("V%d" % j, "w_in", 0, 8, [(0, 512, S5 + 512 * j)], "mix"))
    for j in range(2):
        cat.append(("G%d" % j, "w_in", 0, 8, [(0, 512, S6 + 512 * j)], "mix"))
    for j in range(2):
        for i in range(2):
            cat.append(("OUT%d%d" % (j, i), "w_out", 8 * i, 8, [(0, 512, 512 * j)], "wout%d" % i))
    for j in range(2):
        cat.append(("Q%d" % j, "wq", 0, 8, [(0, 512, 512 * j)], "xa"))
    for j in range(4):
        cat.append(("KV%d" % j, "wkv", 0, 8, [(0, 512, 512 * j)], "mem"))
    for j in range(2):
        cat.append(("O%d" % j, "wo", 0, 8, [(0, 512, 512 * j)], None))
    for j in range(11):
        cat.append(("GU%d" % j, "wgu", 0, 8, [(0, 256, 256 * j), (256, 256, 256 * j)], "ffn"))
    for j in range(2):
        for i in range(3):
            cat.append(("D%d%d" % (j, i), "wd", 8 * i, 8 if i < 2 else 6, [(0, 512, 512 * j)], None))
    return cat


PC_CW, PC_CB, PC_DTB, PC_ALOG, PC_DSK, PC_HLB0, PC_HLB1 = 0, 48, 60, 76, 92, 108, 116
PC_MIX, PC_XA, PC_MEM, PC_FFN, PC_WOUT, PC_MASK = 124, 132, 140, 148, 156, 172


def build(T, NPRE):
    _, rec = _build(T, NPRE, None)
    nc, _ = _build(T, NPRE, rec)
    return nc


def _build(T, NPRE, wseq_in):
    NT = T // 512
    NPAR = PC_MASK + max(NPRE, 1)
    nc = bass.Bass("TRN2", target_bir_lowering=False)
    xm = nc.dram_tensor("xm", [T, D], F32, kind="ExternalInput").ap()
    xp = nc.dram_tensor("xp", [max(NPRE, 1) * 512, D], F32, kind="ExternalInput").ap()
    memd = nc.dram_tensor("mem", [256, D], F32, kind="ExternalInput").ap()
    pard = nc.dram_tensor("par", [128, NPAR], F32, kind="ExternalInput").ap()
    nfwd = nc.dram_tensor("nfw", [128, D], F32, kind="ExternalInput").ap()
    wd_ = {
        "w_in": nc.dram_tensor("w_in", [D, 6672], F32, kind="ExternalInput").ap(),
        "w_out": nc.dram_tensor("w_out", [2048, D], F32, kind="ExternalInput").ap(),
        "wq": nc.dram_tensor("wq", [D, D], F32, kind="ExternalInput").ap(),
        "wkv": nc.dram_tensor("wkv", [D, 2048], F32, kind="ExternalInput").ap(),
        "wo": nc.dram_tensor("wo", [D, D], F32, kind="ExternalInput").ap(),
        "wg": nc.dram_tensor("wg", [D, FF], F32, kind="ExternalInput").ap(),
        "wu": nc.dram_tensor("wu", [D, FF], F32, kind="ExternalInput").ap(),
        "wd": nc.dram_tensor("wd", [FF, D], F32, kind="ExternalInput").ap(),
    }
    outd = nc.dram_tensor("out", [T, D], F32, kind="ExternalOutput").ap()
    cat = chunk_catalog()
    cid = {c[0]: i for i, c in enumerate(cat)}
    wsc = nc.dram_tensor("wsc", [len(cat), 128, 4096], BF16, kind="Internal").ap()
    R_wsc = [Res("wsc%d" % i) for i in range(len(cat))]

    P = Prog(nc)
    with contextlib.ExitStack() as st:
        def sb(name, shape, dt):
            return st.enter_context(nc.sbuf_tensor("sb_" + name, shape, dt))

        par = sb("par", [128, NPAR], F32); R_par = Res()
        ident = sb("ident", [128, 128], BF16); R_ident = Res()
        U = sb("U", [128, 128], F32); R_U = Res()
        ones = sb("ones", [128, 512], F32); R_ones = Res()
        cst = sb("cst", [128, 64], F32); R_cst = Res()
        wdt = sb("wdt", [128, 8, 16], BF16); R_wdt = Res()
        x_tm = sb("x_tm", [128, 4, D], F32); R_x = [Res() for _ in range(4)]
        hT = sb("hT", [128, 8, 512], BF16); R_hT = Res()
        hn = [sb("hn%d" % i, [128, D], BF16) for i in range(2)]; R_hn = [Res(), Res()]
        junk = sb("junk", [128, D], BF16); R_junk = Res()
        wbuf = [sb("wbuf%d" % i, [128, 8, 512], BF16) for i in range(NWS)]; R_wbuf = [Res() for _ in range(NWS)]
        Ssd = sb("Ssd", [128, D], F32); R_Ssd = Res()
        Ssdb = sb("Ssdb", [128, D], BF16); R_Ssdb = Res()
        Shg = sb("Shg", [128, 8, 128], F32); R_Shg = [Res() for _ in range(8)]
        Shgb = sb("Shgb", [128, 8, 128], BF16); R_Shgb = [Res() for _ in range(8)]
        halo = sb("halo", [128, 12, 3], F32); R_halo = Res()
        mixedT = sb("mixedT", [128, 16, 512], BF16); R_mixT = [Res() for _ in range(4)]; R_mixTh2 = [[Res() for _ in range(4)] for _ in range(2)]; R_junkh2 = [Res(), Res()]
        kmT = sb("kmT", [128, 8, 256], BF16); R_kmT = Res()
        vm = sb("vm", [128, 2, D], BF16); R_vm = Res()
        ost = [sb("ost0", [128, D], F32)]; R_ost = [Res()]
        stat = sb("stat", [128, 128], F32)
        ARENA = 27720
        arena = sb("arena", [128, ARENA], F32)
        psb = [st.enter_context(nc.psum_tensor("ps%d" % i, [128, 512], F32)) for i in range(8)]
        R_ps = [Res() for _ in range(8)]
        pctr = [0]

        tl = threading.local()
        rec_state = {"yield": None, "flags": set()}

        def rec_set(name):
            rec_state["flags"].add(name)

        def rec_wait(name):
            while name not in rec_state["flags"]:
                rec_state["yield"]()

        def psum():
            pool = getattr(tl, "pool", None)
            if pool is None:
                i = pctr[0] % 8
                pctr[0] += 1
            else:
                i = pool["banks"][pool["ctr"] % len(pool["banks"])]
                pool["ctr"] += 1
            return psb[i], R_ps[i]

        def psum_ded():
            pool = getattr(tl, "pool", None)
            if pool is None or pool.get("ded") is None:
                return psum()
            return psb[pool["ded"]], R_ps[pool["ded"]]

        def run_interleaved(funcs, pools):
            n = len(funcs)
            st_ = {"turn": 0, "alive": [True] * n, "err": None}
            cv = threading.Condition()

            def advance(k):
                for d in range(1, n + 1):
                    j = (k + d) % n
                    if st_["alive"][j]:
                        st_["turn"] = j
                        return
                st_["turn"] = -1

            def yield_turn():
                k = getattr(tl, "sid", None)
                if k is None:
                    return
                with cv:
                    advance(k)
                    cv.notify_all()
                    while st_["turn"] != k:
                        cv.wait()

            def runner(k):
                tl.pool = pools[k]
                tl.sid = k
                with cv:
                    while st_["turn"] != k:
                        cv.wait()
                try:
                    funcs[k]()
                except BaseException as ex:
                    st_["err"] = ex
                finally:
                    with cv:
                        st_["alive"][k] = False
                        advance(k)
                        cv.notify_all()

            P.after_add = None if _os.environ.get('KCOARSE') else yield_turn
            rec_state['yield'] = yield_turn
            rec_state['flags'] = set()
            ths = [threading.Thread(target=runner, args=(k,)) for k in range(n)]
            for t in ths:
                t.start()
            for t in ths:
                t.join()
            P.after_add = None
            if st_["err"] is not None:
                raise st_["err"]

        def bfv(pt):
            return pt[:, 0:512].bitcast(BF16)

        class Arena:
            def __init__(self):
                self.off = 0

            def f32(self, n):
                a = arena[:, self.off:self.off + n]
                self.off += n
                assert self.off <= ARENA, self.off
                return a

            def bf(self, n):
                n32 = (n + 1) // 2
                a = arena[:, self.off:self.off + n32].bitcast(BF16)
                self.off += n32
                assert self.off <= ARENA, self.off
                return a

        d_par = P.dma("sp", "par", par[:], pard, writes=[R_par])
        P.pool(lambda e: e.memset(ones[:], 1.0), writes=[R_ones])
        P.pool(lambda e: e.memset(U[:], 1.0), writes=[R_U])
        P.pool(lambda e: e.affine_select(out=U[:], in_=U[:], pattern=[[1, 128]], compare_op=ALU.is_ge,
                                         fill=0.0, base=0, channel_multiplier=-1), reads=[R_U], writes=[R_U])
        idf = arena[:, 0:128]
        R_idf = Res()
        P.pool(lambda e: e.memset(idf, 0.0), writes=[R_idf])
        P.pool(lambda e: e.affine_select(out=idf, in_=ones[:, 0:128], pattern=[[1, 128]], compare_op=ALU.is_equal,
                                         fill=0.0, base=0, channel_multiplier=-1), reads=[R_ones, R_idf], writes=[R_idf])
        P.dve(lambda e: e.tensor_copy(out=ident[:], in_=idf), reads=[R_idf], writes=[R_ident])
        P.pool(lambda e: e.memset(Ssd[:], 0.0), writes=[R_Ssd])
        P.pool(lambda e: e.memset(Ssdb[:], 0.0), writes=[R_Ssdb])
        P.pool(lambda e: e.memset(Shg[:], 0.0), writes=R_Shg)
        P.pool(lambda e: e.memset(Shgb[:], 0.0), writes=R_Shgb)
        P.pool(lambda e: e.memset(halo[:], 0.0), writes=[R_halo])
        P.pool(lambda e: e.memset(cst[:, 32:40], 1.0), writes=[R_cst])
        P.dve(lambda e: e.tensor_tensor(out=cst[:, 40:48], in0=par[:, PC_HLB0:PC_HLB0 + 8],
                                        in1=par[:, PC_HLB1:PC_HLB1 + 8], op=ALU.subtract), reads=[R_par, R_cst], writes=[R_cst])
        P.act(lambda e: e.activation(out=cst[:, 0:8], in_=cst[:, 40:48], func=AF.Sigmoid), reads=[R_cst], writes=[R_cst])
        P.act(lambda e: e.activation(out=cst[:, 8:16], in_=cst[:, 40:48], func=AF.Sigmoid, scale=-1.0), reads=[R_cst], writes=[R_cst])
        P.act(lambda e: e.activation(out=cst[:, 48:64], in_=par[:, PC_ALOG:PC_ALOG + 16], func=AF.Exp), reads=[R_par, R_cst], writes=[R_cst])
        P.dve(lambda e: e.tensor_scalar(out=cst[:, 16:32], in0=cst[:, 48:64], scalar1=-1.0, scalar2=None, op0=ALU.mult),
              reads=[R_cst], writes=[R_cst])
        lb, oml, aneg, onesb = cst[:, 0:8], cst[:, 8:16], cst[:, 16:32], cst[:, 32:40]
        nhalf = sb("nhalf", [128, 8], F32); R_nhalf = Res()
        P.pool(lambda e: e.memset(nhalf[:], -0.5), writes=[R_nhalf])
        hcst = sb("hcst", [128, 40], F32); R_hcst = Res(); R_hm = Res()
        P.dve(lambda e: e.tensor_scalar(out=hcst[:, 0:8], in0=oml, scalar1=0.5, scalar2=None, op0=ALU.mult), reads=[R_cst], writes=[R_hcst])
        P.dve(lambda e: e.tensor_tensor(out=hcst[:, 8:16], in0=hcst[:, 0:8], in1=lb, op=ALU.add), reads=[R_cst, R_hcst], writes=[R_hcst])
        P.dve(lambda e: e.tensor_scalar(out=hcst[:, 16:24], in0=oml, scalar1=-0.5, scalar2=None, op0=ALU.mult), reads=[R_cst, R_hcst], writes=[R_hcst])

        scale_ap = {"mix": par[:, PC_MIX:PC_MIX + 8], "xa": par[:, PC_XA:PC_XA + 8], "mem": par[:, PC_MEM:PC_MEM + 8],
                    "ffn": par[:, PC_FFN:PC_FFN + 8], "wout0": par[:, PC_WOUT:PC_WOUT + 8],
                    "wout1": par[:, PC_WOUT + 8:PC_WOUT + 16], None: onesb}
        ar = Arena(); ar.off = 128
        NSTG = 3
        stg = [ar.f32(4096).rearrange("p (k n) -> p k n", k=8) for _ in range(NSTG)]
        stgb = [ar.bf(4096).rearrange("p (k n) -> p k n", k=8) for _ in range(NSTG)]
        wdt32 = ar.f32(128).rearrange("p (k n) -> p k n", k=8)
        R_stg = [Res() for _ in range(NSTG)]; R_stgb = [Res() for _ in range(NSTG)]; R_wdt32 = Res()
        pro_dmas = []

        def wsrc(key, kt0, nkt, c0, cw):
            return wd_[key].rearrange("(kt p) n -> p kt n", p=128)[:, kt0:kt0 + nkt, c0:c0 + cw]

        def pro_load(ci):
            name, key, kt0, nkt, pieces, sk = cat[ci]
            sl = ci % NSTG
            for pi, (dc, cw, sc) in enumerate(pieces):
                k2 = key
                if key == "wgu":
                    k2 = "wg" if pi == 0 else "wu"
                P.dma("sp", "stg%d" % sl, stg[sl][:, 0:nkt, dc:dc + cw], wsrc(k2, kt0, nkt, sc, cw), writes=[R_stg[sl]])

        def pro_cast_store(ci):
            name, key, kt0, nkt, pieces, sk = cat[ci]
            sl = ci % NSTG
            sap = scale_ap[sk]
            f = (lambda e, sl=sl, nkt=nkt, sap=sap: e.tensor_tensor(
                out=stgb[sl][:, 0:nkt, :], in0=stg[sl][:, 0:nkt, :],
                in1=sap[:, 0:nkt].unsqueeze(2).broadcast_to([128, nkt, 512]), op=ALU.mult))
            (P.pool if ci % 3 == 2 else P.dve)(f, reads=[R_stg[sl], R_par, R_cst], writes=[R_stgb[sl]])
            pro_dmas.append(P.dma("sp", "wscw%d" % sl, wsc[ci].rearrange("p (k n) -> p k n", k=8)[:, 0:nkt, :],
                                  stgb[sl][:, 0:nkt, :], reads=[R_stgb[sl]], writes=[R_wsc[ci]]))

        pro_load(0)
        pro_load(1)
        for ci in range(len(cat)):
            if ci + 2 < len(cat):
                pro_load(ci + 2)
            pro_cast_store(ci)
        P.dma("sp", "wdt32", wdt32, wsrc("w_in", 0, 8, S2, 16), writes=[R_wdt32])
        P.dve(lambda e: e.tensor_tensor(out=wdt[:], in0=wdt32, in1=par[:, PC_MIX:PC_MIX + 8].unsqueeze(2).broadcast_to([128, 8, 16]),
                                        op=ALU.mult), reads=[R_wdt32, R_par], writes=[R_wdt])

        wrec = []
        wseq = wseq_in
        wstate = {"issued": 0, "got": 0}

        def wissue():
            i = wstate["issued"]
            names = wseq if wseq is not None else wrec
            c = cid[names[i]]
            sl = i % NWS
            P.dma("sp", "wb%d" % sl, wbuf[sl][:], wsc[c].rearrange("p (k n) -> p k n", k=8),
                  reads=[R_wsc[c]], writes=[R_wbuf[sl]])
            slot_content[sl] = i
            wstate["issued"] += 1

        slot_content = {}
        occ_done = [0] * NWS
        ref_left = {}

        def can_issue(k):
            return occ_done[k % NWS] == k // NWS

        def wdone(i):
            assert slot_content.get(i % NWS) == i, ("evicted before release", i, slot_content)
            ref_left[i] -= 1
            if ref_left[i] == 0:
                occ_done[i % NWS] += 1

        def wdone_cur():
            cur = getattr(tl, "cur", None)
            if cur is not None:
                wdone(cur)
                tl.cur = None

        def wget(name, auto=True, nref=1):
            if auto:
                wdone_cur()
            i = wstate["got"]
            wstate["got"] += 1
            wrec.append(name)
            ref_left[i] = nref
            if wseq is not None:
                assert wseq[i] == name, (i, wseq[i], name)
            while wstate["issued"] <= i:
                if can_issue(wstate["issued"]):
                    sid = getattr(tl, "sid", None)
                    tl.sid = None
                    try:
                        wissue()
                    finally:
                        tl.sid = sid
                else:
                    rec_state["yield"]()
            if wseq is not None:
                sid = getattr(tl, "sid", None)
                tl.sid = None
                try:
                    while wstate["issued"] < min(len(wseq), i + NWS) and can_issue(wstate["issued"]):
                        wissue()
                finally:
                    tl.sid = sid
            if auto:
                tl.cur = i
            assert slot_content.get(i % NWS) == i, ("not resident at obtain", i, slot_content)
            return wbuf[i % NWS], R_wbuf[i % NWS], i

        def rstd_from_ss(ssv, n, Rs, inv_n):
            P.pool(lambda e: e.tensor_scalar(out=ssv, in0=ssv, scalar1=inv_n, scalar2=EPS, op0=ALU.mult, op1=ALU.add),
                   reads=[Rs], writes=[Rs])
            P.pool(lambda e: e.tensor_tensor(out=ssv, in0=ssv, in1=nhalf[:, 0:n], op=ALU.pow), reads=[Rs, R_nhalf], writes=[Rs])

        rms_ctr = [0]
        R_rms = [Res(), Res()]
        R_ssf = Res()
        R_scp = [Res(), Res()]
        R_prekh = [Res(), Res()]
        R_prekhtm = [Res(), Res()]

        def rms_T(src, Rsrc, nsub, dstT, R_dst):
            k = rms_ctr[0] % 2
            rms_ctr[0] += 1
            ss = stat[:, 8 * k:8 * k + nsub]
            Rss = R_rms[k]
            P.pool(lambda e: e.memset(ss, 0.0), writes=[Rss])
            for s in range(nsub):
                P.act(lambda e, s=s: e.activation(out=junk[:], in_=src(s), func=AF.Square, accum_out=ss[:, s:s + 1]),
                      reads=[Rsrc[s], Rss], writes=[R_junk, R_junkh2[0], R_junkh2[1], Rss])
            rstd_from_ss(ss, nsub, Rss, 1.0 / D)
            for s in range(nsub):
                b = s % 2
                P.dve(lambda e, s=s, b=b: e.tensor_scalar(out=hn[b][:], in0=src(s), scalar1=ss[:, s:s + 1], scalar2=None,
                                                          op0=ALU.mult), reads=[Rsrc[s], Rss], writes=[R_hn[b]])
                pt, Rp = psum()
                pv = bfv(pt)
                for kt in range(8):
                    P.pe(lambda e, kt=kt, b=b, pv=pv: e.transpose(out=pv[:, kt * 128:(kt + 1) * 128],
                                                                  in_=hn[b][:, kt * 128:(kt + 1) * 128], identity=ident[:]),
                         reads=[R_hn[b], R_ident], writes=[Rp])
                P.act(lambda e, s=s, pv=pv: e.activation(out=dstT[:, :, s * 128:(s + 1) * 128],
                                                         in_=pv.rearrange("p (k t) -> p k t", k=8), func=AF.Copy),
                      reads=[Rp], writes=[R_dst])

        def proj_fm(wt, Rw, j, xT, RxT, ncols=512):
            pt, Rp = psum()
            for kt in range(8):
                P.pe(lambda e, kt=kt, pt=pt: e.matmul(pt[:, 0:ncols], lhsT=wt[:, kt, j * 128:(j + 1) * 128], rhs=xT[:, kt, 0:ncols],
                                                      start=(kt == 0), stop=(kt == 7)), reads=[Rw, RxT], writes=[Rp])
            return pt, Rp

        def proj_tm(wt, Rw, s, xT, RxT, ncols=512):
            pt, Rp = psum()
            for kt in range(8):
                P.pe(lambda e, kt=kt, pt=pt: e.matmul(pt[:, 0:ncols], lhsT=xT[:, kt, s * 128:(s + 1) * 128], rhs=wt[:, kt, 0:ncols],
                                                      start=(kt == 0), stop=(kt == 7)), reads=[Rw, RxT], writes=[Rp])
            return pt, Rp

        mem_t = ar.f32(2 * D).rearrange("p (s d) -> p s d", s=2); R_mem = [Res(), Res()]
        mT = ar.bf(8 * 256).rearrange("p (k t) -> p k t", k=8); R_mT = Res()
        for s in range(2):
            P.dma("sp", "mem%d" % s, mem_t[:, s, :], memd[s * 128:(s + 1) * 128, :], writes=[R_mem[s]])
        rms_T(lambda s: mem_t[:, s, :], R_mem, 2, mT, R_mT)
        for jc in range(2):
            wt, Rw, _ = wget("KV%d" % jc)
            for j in range(4):
                pt, Rp = proj_fm(wt, Rw, j, mT, R_mT, ncols=256)
                P.act(lambda e, pt=pt, jc=jc, j=j: e.activation(out=kmT[:, 4 * jc + j, :], in_=pt[:, 0:256], func=AF.Copy),
                      reads=[Rp], writes=[R_kmT])
        for jc in range(2):
            wt, Rw, _ = wget("KV%d" % (2 + jc))
            for s in range(2):
                pt, Rp = proj_tm(wt, Rw, s, mT, R_mT)
                P.act(lambda e, pt=pt, jc=jc, s=s: e.activation(out=vm[:, s, jc * 512:(jc + 1) * 512], in_=pt[:, 0:512], func=AF.Copy),
                      reads=[Rp], writes=[R_vm])
        wdone_cur()
        P.barrier(extra=pro_dmas[-3:])

        out_dmas = []
        tile_ctr = [0]
        mres_store = []

        def mk_mres():
            idx = [0]

            def mres():
                i = idx[0]
                idx[0] += 1
                if i >= len(mres_store):
                    mres_store.append(Res())
                return mres_store[i]
            return mres

        def do_tile(xsrc_d, row0, is_pre, pre_idx, out_row0):
            ti = tile_ctr[0]
            tile_ctr[0] += 1
            A = Arena()
            mres = mk_mres()
            raw = A.f32(4 * 515).rearrange("p (j t) -> p j t", j=4); R_raw = mres()
            cacc = [A.f32(512) for _ in range(2)]; R_cacc = [mres(), mres()]
            xsT = A.bf(8 * 512).rearrange("p (k t) -> p k t", k=8); R_xsT = mres()
            BT = A.bf(2 * 512).rearrange("p (k t) -> p k t", k=2); R_BT = mres()
            CT = A.bf(2 * 512).rearrange("p (k t) -> p k t", k=2); R_CT = mres()
            xs_tm = A.bf(4 * D).rearrange("p (s d) -> p s d", s=4); R_xs = [mres() for _ in range(4)]
            B_tm = A.bf(4 * 256).rearrange("p (s d) -> p s d", s=4); R_Btm = mres()
            zs = A.bf(4 * D).rearrange("p (s d) -> p s d", s=4); R_zs = [mres() for _ in range(4)]
            vt = A.bf(4 * D).rearrange("p (s d) -> p s d", s=4); R_vt = [mres() for _ in range(4)]
            gs = A.bf(4 * D).rearrange("p (s d) -> p s d", s=4); R_gs = [mres() for _ in range(4)]
            dtr = A.f32(64).rearrange("p (s h) -> p s h", s=4); R_dtr = mres()
            dtA = A.f32(64).rearrange("p (s h) -> p s h", s=4); R_dtA = mres()
            acs = [A.f32(96) for _ in range(2)]; R_acs = [mres(), mres()]
            Lseg = [A.f32(512) for _ in range(2)]; R_Lseg = [mres(), mres()]
            MT = A.bf(16 * 128).rearrange("p (h l) -> p h l", h=16); R_MT = mres()
            cbm = A.f32(256).rearrange("p (g l) -> p g l", g=2); R_cbm = mres()
            xdt = A.bf(D); R_xdt = mres()
            xdtd = A.bf(D); R_xdtd = mres()
            t1 = A.f32(D); R_t1 = mres()
            t3 = A.f32(D); R_t3 = mres()
            yn = A.bf(D); R_yn = mres()
            qf = A.f32(1024).rearrange("p (i t) -> p i t", i=2); R_qf = mres()
            gl = A.f32(1024).rearrange("p (i t) -> p i t", i=2); R_gl = mres()
            kf = A.f32(1024).rearrange("p (i t) -> p i t", i=2); R_kf = mres()
            bt = [A.f32(513) for _ in range(2)]; R_bt = [mres(), mres()]
            etmp = [A.f32(128) for _ in range(8)]; R_et = [mres() for _ in range(8)]
            qt_ = [A.bf(128) for _ in range(2)]; R_qt = [mres(), mres()]
            KA = [A.bf(128) for _ in range(2)]; R_KA = [mres(), mres()]
            KB = [A.bf(128) for _ in range(2)]; R_KB = [mres(), mres()]
            KC = [A.bf(128) for _ in range(2)]; R_KC = [mres(), mres()]
            QC = [A.bf(64) for _ in range(2)]; R_QC = [mres(), mres()]
            if not is_pre:
                for i in range(2):
                    P.pool(lambda e, i=i: e.memset(KA[i], 0.0), writes=[R_KA[i]])
                    P.pool(lambda e, i=i: e.memset(KB[i], 0.0), writes=[R_KB[i]])
                    P.pool(lambda e, i=i: e.memset(KC[i], 0.0), writes=[R_KC[i]])
            qh = [A.bf(128) for _ in range(2)]; R_qh = [mres(), mres()]
            kh = [A.bf(128) for _ in range(2)]; R_kh = [mres(), mres()]
            khtm = [A.bf(128) for _ in range(2)]; R_khtm = [mres(), mres()]
            attm = [A.bf(128) for _ in range(2)]; R_attm = [mres(), mres()]
            otmp = A.f32(256); R_otmp = mres()
            og = A.bf(256); R_og = mres()
            sst = A.f32(32); R_sst = mres(); R_ssth2 = [mres(), mres()]
            R_qf2 = [mres(), mres()]; R_gl2 = [mres(), mres()]; R_kf2 = [mres(), mres()]; R_otmp2 = [mres(), mres()]; R_og2 = [mres(), mres()]

            qfb = qf.rearrange("p i t -> p (i t)").bitcast(BF16)
            pre_kh = [qfb[:, 0:512], qfb[:, 512:1024]]
            pre_khtm = [qfb[:, 1024:1536], qfb[:, 1536:2048]]

            for s in range(4):
                P.dma("sp", "x%d" % s, x_tm[:, s, :], xsrc_d[row0 + s * 128:row0 + (s + 1) * 128, :], writes=[R_x[s]])
            rms_T(lambda s: x_tm[:, s, :], R_x, 4, hT, R_hT)

            def tm_chunk(nm, jc, dst, Rdst, func):
                wt, Rw, _ = wget("%s%d" % (nm, jc))
                for s in range(4):
                    pt, Rp = proj_tm(wt, Rw, s, hT, R_hT)
                    P.act(lambda e, pt=pt, s=s, jc=jc: e.activation(out=dst[:, s, jc * 512:(jc + 1) * 512], in_=pt[:, 0:512], func=func),
                          reads=[Rp], writes=[Rdst[s]])
            if is_pre:
                tm_list = [("V", 0, vt, R_vt, AF.Copy), ("V", 1, vt, R_vt, AF.Copy)]
            else:
                tm_list = [("Z", 0, zs, R_zs, AF.Silu), ("Z", 1, zs, R_zs, AF.Silu), ("V", 0, vt, R_vt, AF.Copy),
                           ("V", 1, vt, R_vt, AF.Copy), ("G", 0, gs, R_gs, AF.Silu), ("G", 1, gs, R_gs, AF.Silu)]
            def secA():
                for c3 in range(3):
                    wt, Rw, _ = wget("X%d" % c3)
                    P.pool(lambda e, c3=c3: e.tensor_copy(out=raw[:, :, 0:3], in_=halo[:, 4 * c3:4 * c3 + 4, :]),
                           reads=[R_halo], writes=[R_raw])
                    for j in range(4):
                        pt, Rp = proj_fm(wt, Rw, j, hT, R_hT)
                        P.act(lambda e, pt=pt, j=j: e.activation(out=raw[:, j, 3:515], in_=pt[:, 0:512], func=AF.Copy),
                              reads=[Rp], writes=[R_raw])
                    P.pool(lambda e, c3=c3: e.tensor_copy(out=halo[:, 4 * c3:4 * c3 + 4, :], in_=raw[:, :, 512:515]),
                           reads=[R_raw], writes=[R_halo])
                    for j in range(4):
                        ct = 4 * c3 + j
                        ca, Rca = cacc[j % 2], R_cacc[j % 2]
                        P.dve(lambda e, j=j, ct=ct, ca=ca: e.tensor_scalar(
                            out=ca, in0=raw[:, j, 0:512], scalar1=par[:, PC_CW + 4 * ct:PC_CW + 4 * ct + 1],
                            scalar2=par[:, PC_CB + ct:PC_CB + ct + 1], op0=ALU.mult, op1=ALU.add), reads=[R_raw, R_par], writes=[Rca])
                        for k in range(1, 4):
                            P.dve(lambda e, j=j, ct=ct, k=k, ca=ca: e.scalar_tensor_tensor(
                                out=ca, in0=raw[:, j, k:k + 512], scalar=par[:, PC_CW + 4 * ct + k:PC_CW + 4 * ct + k + 1],
                                in1=ca, op0=ALU.mult, op1=ALU.add), reads=[R_raw, R_par, Rca], writes=[Rca])
                        if ct < 8:
                            dst, Rd = xsT[:, ct, :], R_xsT
                        elif ct < 10:
                            dst, Rd = BT[:, ct - 8, :], R_BT
                        else:
                            dst, Rd = CT[:, ct - 10, :], R_CT
                        P.act(lambda e, ca=ca, dst=dst: e.activation(out=dst, in_=ca, func=AF.Silu), reads=[Rca], writes=[Rd])
                for s in range(4):
                    pt, Rp = psum()
                    pv = bfv(pt)
                    for kt in range(8):
                        P.pe(lambda e, kt=kt, s=s, pv=pv: e.transpose(out=pv[:, kt * 128:(kt + 1) * 128],
                                                                      in_=xsT[:, kt, s * 128:(s + 1) * 128], identity=ident[:]),
                             reads=[R_xsT, R_ident], writes=[Rp])
                    P.act(lambda e, s=s, pv=pv: e.activation(out=xs_tm[:, s, :], in_=pv, func=AF.Copy), reads=[Rp], writes=[R_xs[s]])
                pt, Rp = psum()
                pv = bfv(pt)
                for s in range(4):
                    for g in range(2):
                        P.pe(lambda e, s=s, g=g, pv=pv: e.transpose(out=pv[:, s * 256 + g * 128:s * 256 + (g + 1) * 128],
                                                                    in_=BT[:, g, s * 128:(s + 1) * 128], identity=ident[:]),
                             reads=[R_BT, R_ident], writes=[Rp])
                P.act(lambda e, pv=pv: e.activation(out=B_tm[:], in_=pv.rearrange("p (s d) -> p s d", s=4), func=AF.Copy),
                      reads=[Rp], writes=[R_Btm])

                pt, Rp = psum()
                for s in range(4):
                    for kt in range(8):
                        P.pe(lambda e, s=s, kt=kt, pt=pt: e.matmul(pt[:, s * 16:(s + 1) * 16], lhsT=hT[:, kt, s * 128:(s + 1) * 128],
                                                                   rhs=wdt[:, kt, :], start=(kt == 0), stop=(kt == 7)),
                             reads=[R_hT, R_wdt], writes=[Rp])
                P.dve(lambda e, pt=pt: e.tensor_tensor(out=dtr[:], in0=pt[:, 0:64].rearrange("p (s h) -> p s h", s=4),
                                                       in1=par[:, PC_DTB:PC_DTB + 16].unsqueeze(1).broadcast_to([128, 4, 16]), op=ALU.add),
                      reads=[Rp, R_par], writes=[R_dtr])
                P.act(lambda e: e.activation(out=dtr[:], in_=dtr[:], func=AF.Exp), reads=[R_dtr], writes=[R_dtr])
                P.act(lambda e: e.activation(out=dtr[:], in_=dtr[:], func=AF.Ln, bias=1.0), reads=[R_dtr], writes=[R_dtr])
                if is_pre:
                    P.dve(lambda e: e.tensor_scalar(out=dtr[:], in0=dtr[:], scalar1=par[:, PC_MASK + pre_idx:PC_MASK + pre_idx + 1],
                                                    scalar2=None, op0=ALU.mult), reads=[R_dtr, R_par], writes=[R_dtr])
                P.dve(lambda e: e.tensor_tensor(out=dtA[:], in0=dtr[:], in1=aneg.unsqueeze(1).broadcast_to([128, 4, 16]), op=ALU.mult),
                      reads=[R_dtr, R_cst], writes=[R_dtA])


            def secB():
                while tm_list:
                    tm_chunk(*tm_list.pop(0))

            def sec_ssd():
                if is_pre:
                    ac = acs[0]; Rac = R_acs[0]
                    pa, Rpa = psum()
                    for j in range(4):
                        P.pe(lambda e, j=j, pa=pa: e.matmul(pa[:, j * 16:(j + 1) * 16], lhsT=U[:], rhs=dtA[:, j, :], start=True, stop=True),
                             reads=[R_U, R_dtA], writes=[Rpa])
                        P.pe(lambda e, j=j, pa=pa: e.matmul(pa[:, 64 + j * 16:64 + (j + 1) * 16], lhsT=ones[:, 0:128], rhs=dtA[:, j, :],
                                                            start=True, stop=True), reads=[R_ones, R_dtA], writes=[Rpa])
                    suf = acs[1]; Rsuf = R_acs[1]
                    P.dve(lambda e, pa=pa: e.tensor_copy(out=suf[:, 0:64], in_=pa[:, 64:128]), reads=[Rpa], writes=[Rsuf])
                    for j in (2, 1, 0):
                        P.dve(lambda e, j=j: e.tensor_tensor(out=suf[:, j * 16:(j + 1) * 16], in0=suf[:, j * 16:(j + 1) * 16],
                                                             in1=suf[:, (j + 1) * 16:(j + 2) * 16], op=ALU.add), reads=[Rsuf], writes=[Rsuf])
                    P.dve(lambda e, pa=pa: e.tensor_tensor(out=ac[:, 0:64], in0=suf[:, 0:64], in1=pa[:, 0:64], op=ALU.subtract),
                          reads=[Rpa, Rsuf], writes=[Rac])
                    P.act(lambda e: e.activation(out=ac[:, 0:64], in_=ac[:, 0:64], func=AF.Exp), reads=[Rac], writes=[Rac])
                    P.act(lambda e: e.activation(out=ac[:, 80:96], in_=suf[:, 0:16], func=AF.Exp), reads=[Rsuf, Rac], writes=[Rac])
                    P.dve(lambda e: e.tensor_tensor(out=ac[:, 0:64], in0=ac[:, 0:64], in1=dtr[:].rearrange("p s h -> p (s h)"), op=ALU.mult),
                          reads=[Rac, R_dtr], writes=[Rac])
                    for j in range(4):
                        P.dve(lambda e, j=j: e.tensor_tensor(out=zs[:, j, :].rearrange("p (h d) -> p h d", h=16),
                                                             in0=xs_tm[:, j, :].rearrange("p (h d) -> p h d", h=16),
                                                             in1=ac[:, j * 16:(j + 1) * 16].unsqueeze(2).broadcast_to([128, 16, 64]), op=ALU.mult),
                              reads=[R_xs[j], Rac], writes=[R_zs[j]])
                    pss = [psum(), psum()]
                    for g in range(2):
                        ptt, Rpp = pss[g]
                        for j in range(4):
                            P.pe(lambda e, g=g, j=j, ptt=ptt: e.matmul(ptt[:, 0:512], lhsT=B_tm[:, j, g * 128:(g + 1) * 128],
                                                                       rhs=zs[:, j, g * 512:(g + 1) * 512], start=(j == 0), stop=(j == 3)),
                                 reads=[R_Btm, R_zs[j]], writes=[Rpp])
                    P.dve(lambda e: e.tensor_tensor(out=Ssd.rearrange("p (h d) -> p h d", h=16),
                                                    in0=Ssd.rearrange("p (h d) -> p h d", h=16),
                                                    in1=ac[:, 80:96].unsqueeze(2).broadcast_to([128, 16, 64]), op=ALU.mult),
                          reads=[R_Ssd, Rac], writes=[R_Ssd])
                    for g in range(2):
                        P.dve(lambda e, g=g, ptt=pss[g][0]: e.tensor_tensor(out=Ssd[:, g * 512:(g + 1) * 512], in0=ptt[:, 0:512],
                                                                            in1=Ssd[:, g * 512:(g + 1) * 512], op=ALU.add),
                              reads=[pss[g][1], R_Ssd], writes=[R_Ssd])
                    P.act(lambda e: e.activation(out=Ssdb[:], in_=Ssd[:], func=AF.Copy), reads=[R_Ssd], writes=[R_Ssdb])
                for c in (range(0) if is_pre else range(4)):
                    ac, Rac = acs[c % 2], R_acs[c % 2]
                    pt, Rp = psum()
                    P.pe(lambda e, c=c, pt=pt: e.matmul(pt[:, 0:16], lhsT=U[:], rhs=dtA[:, c, :], start=True, stop=True),
                         reads=[R_U, R_dtA], writes=[Rp])
                    P.pe(lambda e, c=c, pt=pt: e.matmul(pt[:, 16:32], lhsT=ones[:, 0:128], rhs=dtA[:, c, :], start=True, stop=True),
                         reads=[R_ones, R_dtA], writes=[Rp])
                    P.dve(lambda e, pt=pt, ac=ac: e.tensor_copy(out=ac[:, 0:32], in_=pt[:, 0:32]), reads=[Rp], writes=[Rac])
                    P.dve(lambda e, ac=ac: e.tensor_tensor(out=ac[:, 48:64], in0=ac[:, 16:32], in1=ac[:, 0:16], op=ALU.subtract),
                          reads=[Rac], writes=[Rac])
                    P.act(lambda e, ac=ac: e.activation(out=ac[:, 32:48], in_=ac[:, 0:16], func=AF.Exp), reads=[Rac], writes=[Rac])
                    P.act(lambda e, ac=ac: e.activation(out=ac[:, 48:64], in_=ac[:, 48:64], func=AF.Exp), reads=[Rac], writes=[Rac])
                    P.act(lambda e, ac=ac: e.activation(out=ac[:, 64:80], in_=ac[:, 16:32], func=AF.Exp), reads=[Rac], writes=[Rac])
                    P.dve(lambda e, c=c: e.tensor_tensor(out=xdt.rearrange("p (h d) -> p h d", h=16),
                                                         in0=xs_tm[:, c, :].rearrange("p (h d) -> p h d", h=16),
                                                         in1=dtr[:, c, :].unsqueeze(2).broadcast_to([128, 16, 64]), op=ALU.mult),
                          reads=[R_xs[c], R_dtr], writes=[R_xdt])
                    if not is_pre:
                        pt, Rp = psum()
                        for g in range(2):
                            P.pe(lambda e, c=c, g=g, pt=pt: e.matmul(pt[:, g * 128:(g + 1) * 128], lhsT=BT[:, g, c * 128:(c + 1) * 128],
                                                                     rhs=CT[:, g, c * 128:(c + 1) * 128], start=True, stop=True),
                                 reads=[R_BT, R_CT], writes=[Rp])
                        P.dve(lambda e, pt=pt: e.tensor_tensor(out=cbm[:], in0=pt[:, 0:256].rearrange("p (g l) -> p g l", g=2),
                                                               in1=U[:].unsqueeze(1).broadcast_to([128, 2, 128]), op=ALU.mult),
                              reads=[Rp, R_U], writes=[R_cbm])
                        for hb in range(4):
                            Ls, RLs = Lseg[hb % 2], R_Lseg[hb % 2]
                            pt, Rp = psum()
                            for i in range(4):
                                h = hb * 4 + i
                                P.pe(lambda e, c=c, h=h, i=i, pt=pt: e.matmul(pt[:, i * 128:(i + 1) * 128],
                                                                              lhsT=dtA[:, c, h:h + 1].broadcast_to([128, 128]), rhs=U[:],
                                                                              start=True, stop=True), reads=[R_dtA, R_U], writes=[Rp])
                            P.dve(lambda e, pt=pt, hb=hb, ac=ac, Ls=Ls: e.tensor_tensor(
                                out=Ls.rearrange("p (h l) -> p h l", h=4), in0=pt[:, 0:512].rearrange("p (h l) -> p h l", h=4),
                                in1=ac[:, 4 * hb:4 * hb + 4].unsqueeze(2).broadcast_to([128, 4, 128]), op=ALU.subtract),
                                reads=[Rp, Rac], writes=[RLs])
                            P.dve(lambda e, Ls=Ls: e.tensor_scalar(out=Ls, in0=Ls, scalar1=0.0, scalar2=None, op0=ALU.min),
                                  reads=[RLs], writes=[RLs])
                            P.act(lambda e, Ls=Ls: e.activation(out=Ls, in_=Ls, func=AF.Exp), reads=[RLs], writes=[RLs])
                            g = hb // 2
                            P.pool(lambda e, hb=hb, g=g, Ls=Ls: e.tensor_tensor(
                                out=MT[:, 4 * hb:4 * hb + 4, :], in0=Ls.rearrange("p (h l) -> p h l", h=4),
                                in1=cbm[:, g, :].unsqueeze(1).broadcast_to([128, 4, 128]), op=ALU.mult),
                                reads=[RLs, R_cbm], writes=[R_MT])
                        py = [psum(), psum()]
                        for h in range(16):
                            ptt, Rpp = py[h // 8]
                            P.pe(lambda e, h=h, ptt=ptt: e.matmul(ptt[:, (h % 8) * 64:(h % 8 + 1) * 64], lhsT=MT[:, h, :],
                                                                  rhs=xdt[:, h * 64:(h + 1) * 64], start=True, stop=True),
                                 reads=[R_MT, R_xdt], writes=[Rpp])
                        po = [psum(), psum()]
                        for g in range(2):
                            ptt, Rpp = po[g]
                            P.pe(lambda e, g=g, c=c, ptt=ptt: e.matmul(ptt[:, 0:512], lhsT=CT[:, g, c * 128:(c + 1) * 128],
                                                                       rhs=Ssdb[:, g * 512:(g + 1) * 512], start=True, stop=True),
                                 reads=[R_CT, R_Ssdb], writes=[Rpp])
                        for g in range(2):
                            P.dve(lambda e, g=g, ac=ac, ptt=po[g][0]: e.tensor_tensor(
                                out=t1[:, g * 512:(g + 1) * 512].rearrange("p (h d) -> p h d", h=8),
                                in0=ptt[:, 0:512].rearrange("p (h d) -> p h d", h=8),
                                in1=ac[:, 32 + 8 * g:40 + 8 * g].unsqueeze(2).broadcast_to([128, 8, 64]), op=ALU.mult),
                                reads=[po[g][1], Rac], writes=[R_t1])
                        for g in range(2):
                            P.dve(lambda e, g=g, ptt=py[g][0]: e.tensor_tensor(out=t1[:, g * 512:(g + 1) * 512], in0=ptt[:, 0:512],
                                                                               in1=t1[:, g * 512:(g + 1) * 512], op=ALU.add),
                                  reads=[py[g][1], R_t1], writes=[R_t1])
                        P.pool(lambda e, c=c: e.tensor_tensor(out=t3.rearrange("p (h d) -> p h d", h=16),
                                                              in0=xs_tm[:, c, :].rearrange("p (h d) -> p h d", h=16),
                                                              in1=par[:, PC_DSK:PC_DSK + 16].unsqueeze(2).broadcast_to([128, 16, 64]), op=ALU.mult),
                               reads=[R_xs[c], R_par], writes=[R_t3])
                        P.pool(lambda e: e.tensor_tensor(out=t1, in0=t1, in1=t3, op=ALU.add), reads=[R_t1, R_t3], writes=[R_t1])
                        P.dve(lambda e, c=c: e.tensor_tensor(out=t3, in0=t1, in1=zs[:, c, :], op=ALU.mult),
                              reads=[R_t1, R_zs[c], R_t3], writes=[R_t3])
                        P.pool(lambda e: e.memset(sst[:, 0:2], 0.0), writes=[R_sst])
                        for g in range(2):
                            P.act(lambda e, g=g: e.activation(out=junk[:, 0:512], in_=t3[:, g * 512:(g + 1) * 512], func=AF.Square,
                                                              accum_out=sst[:, g:g + 1]), reads=[R_t3, R_sst], writes=[R_junk, R_sst])
                        rstd_from_ss(sst[:, 0:2], 2, R_sst, 1.0 / 512)
                        for g in range(2):
                            P.dve(lambda e, g=g: e.tensor_scalar(out=yn[:, g * 512:(g + 1) * 512], in0=t3[:, g * 512:(g + 1) * 512],
                                                                 scalar1=sst[:, g:g + 1], scalar2=None, op0=ALU.mult),
                                  reads=[R_t3, R_sst], writes=[R_yn])
                        pt, Rp = psum()
                        pv = bfv(pt)
                        for kt in range(8):
                            P.pe(lambda e, kt=kt, pv=pv: e.transpose(out=pv[:, kt * 128:(kt + 1) * 128], in_=yn[:, kt * 128:(kt + 1) * 128],
                                                                     identity=ident[:]), reads=[R_yn, R_ident], writes=[Rp])
                        P.act(lambda e, c=c, pv=pv: e.activation(out=mixedT[:, 0:8, c * 128:(c + 1) * 128],
                                                                 in_=pv.rearrange("p (k t) -> p k t", k=8), func=AF.Copy),
                              reads=[Rp], writes=[R_mixT[c]])
                    P.dve(lambda e, ac=ac: e.tensor_tensor(out=xdtd.rearrange("p (h d) -> p h d", h=16),
                                                           in0=xdt.rearrange("p (h d) -> p h d", h=16),
                                                           in1=ac[:, 48:64].unsqueeze(2).broadcast_to([128, 16, 64]), op=ALU.mult),
                          reads=[R_xdt, Rac], writes=[R_xdtd])
                    pss = [psum(), psum()]
                    for g in range(2):
                        ptt, Rpp = pss[g]
                        P.pe(lambda e, g=g, c=c, ptt=ptt: e.matmul(ptt[:, 0:512], lhsT=B_tm[:, c, g * 128:(g + 1) * 128],
                                                                   rhs=xdtd[:, g * 512:(g + 1) * 512], start=True, stop=True),
                             reads=[R_Btm, R_xdtd], writes=[Rpp])
                    P.dve(lambda e, ac=ac: e.tensor_tensor(out=Ssd.rearrange("p (h d) -> p h d", h=16),
                                                           in0=Ssd.rearrange("p (h d) -> p h d", h=16),
                                                           in1=ac[:, 64:80].unsqueeze(2).broadcast_to([128, 16, 64]), op=ALU.mult),
                          reads=[R_Ssd, Rac], writes=[R_Ssd])
                    for g in range(2):
                        P.dve(lambda e, g=g, ptt=pss[g][0]: e.tensor_tensor(out=Ssd[:, g * 512:(g + 1) * 512], in0=ptt[:, 0:512],
                                                                            in1=Ssd[:, g * 512:(g + 1) * 512], op=ALU.add),
                              reads=[pss[g][1], R_Ssd], writes=[R_Ssd])
                    P.act(lambda e: e.activation(out=Ssdb[:], in_=Ssd[:], func=AF.Copy), reads=[R_Ssd], writes=[R_Ssdb])

            if is_pre:
                mcol = par[:, PC_MASK + pre_idx:PC_MASK + pre_idx + 1]
                P.dve(lambda e: e.tensor_scalar(out=hcst[:, 24:32], in0=hcst[:, 0:8], scalar1=mcol, scalar2=None, op0=ALU.mult),
                      reads=[R_hcst, R_par, R_hm], writes=[R_hm])
                P.dve(lambda e: e.tensor_scalar(out=hcst[:, 32:40], in0=hcst[:, 16:24], scalar1=mcol, scalar2=None, op0=ALU.mult),
                      reads=[R_hcst, R_par, R_hm], writes=[R_hm])
            hw = {}

            def getH(a):
                if a not in hw:
                    hw[a] = None
                    hw[a] = wget("H%d" % a, auto=False, nref=2)
                while hw[a] is None:
                    rec_state["yield"]()
                return hw[a]

            def sec_hg_head(i):
                b_ = bt[i]
                Rb = R_bt[i]
                scb = stat[:, 64 + 24 * i:64 + 24 * i + 24]
                sc = stat[:, 40 + 8 * i:40 + 8 * i + 8]
                Rsc = R_scp[i]
                e0, e1, e2, e3 = etmp[4 * i:4 * i + 4]
                Re0, Re1, Re2, Re3 = R_et[4 * i:4 * i + 4]
                for a in range(4):
                    h = 2 * a + i
                    if a > 0:
                        wdone(hw[a - 1][2])
                    wt, Rw, _ = getH(a)
                    if not is_pre:
                        pt, Rp = proj_fm(wt, Rw, i, hT, R_hT)
                        P.act(lambda e, pt=pt: e.activation(out=qf[:, i, :], in_=pt[:, 0:512], func=AF.Silu), reads=[Rp], writes=[R_qf2[i]])
                    pt, Rp = proj_fm(wt, Rw, 2 + i, hT, R_hT)
                    P.act(lambda e, pt=pt: e.activation(out=kf[:, i, :], in_=pt[:, 0:512], func=AF.Tanh, scale=0.5), reads=[Rp], writes=[R_kf2[i]])
                    P.dve(lambda e, h=h: e.tensor_scalar(out=gl[:, i, :], in0=kf[:, i, :], scalar1=hcst[:, h:h + 1], scalar2=hcst[:, 8 + h:9 + h],
                                                         op0=ALU.mult, op1=ALU.add), reads=[R_kf2[i], R_hcst], writes=[R_gl2[i]])
                    P.act(lambda e: e.activation(out=gl[:, i, :], in_=gl[:, i, :], func=AF.Ln), reads=[R_gl2[i]], writes=[R_gl2[i]])
                    if is_pre:
                        P.dve(lambda e, h=h: e.tensor_scalar(out=kf[:, i, :], in0=kf[:, i, :], scalar1=hcst[:, 32 + h:33 + h], scalar2=hcst[:, 24 + h:25 + h],
                                                             op0=ALU.mult, op1=ALU.add), reads=[R_kf2[i], R_hm], writes=[R_kf2[i]])
                    else:
                        P.dve(lambda e, h=h: e.tensor_scalar(out=kf[:, i, :], in0=kf[:, i, :], scalar1=hcst[:, 16 + h:17 + h], scalar2=hcst[:, h:h + 1],
                                                             op0=ALU.mult, op1=ALU.add), reads=[R_kf2[i], R_hcst], writes=[R_kf2[i]])
                    P.pool(lambda e: e.memset(b_[:, 0:1], 0.0), writes=[Rb])
                    P.dve(lambda e: e.tensor_tensor_scan(out=b_[:, 1:513], data0=ones[:, 0:512], data1=gl[:, i, :], initial=0.0,
                                                         op0=ALU.mult, op1=ALU.add), reads=[R_ones, R_gl2[i], Rb], writes=[Rb])
                    if not is_pre:
                        v0 = b_[:, 0:512].rearrange("p (c t) -> p c t", c=4)
                        v1 = b_[:, 1:513].rearrange("p (c t) -> p c t", c=4)
                        o3 = lambda k: scb[:, 4 * k:4 * k + 4].unsqueeze(2)
                        P.pool(lambda e: e.tensor_scalar(out=o3(0), in0=v0[:, :, 32:33], scalar1=-1.0, scalar2=None, op0=ALU.mult), reads=[Rb, Rsc], writes=[Rsc])
                        P.pool(lambda e: e.tensor_scalar(out=o3(1), in0=v0[:, :, 96:97], scalar1=-1.0, scalar2=None, op0=ALU.mult), reads=[Rb, Rsc], writes=[Rsc])
                        P.pool(lambda e: e.tensor_scalar(out=o3(2), in0=v0[:, :, 0:1], scalar1=-1.0, scalar2=None, op0=ALU.mult), reads=[Rb, Rsc], writes=[Rsc])
                        P.pool(lambda e: e.tensor_tensor(out=o3(3), in0=v1[:, :, 127:128], in1=v0[:, :, 0:1], op=ALU.subtract), reads=[Rb, Rsc], writes=[Rsc])
                        P.pool(lambda e: e.tensor_tensor(out=o3(4), in0=v0[:, :, 64:65], in1=v0[:, :, 32:33], op=ALU.subtract), reads=[Rb, Rsc], writes=[Rsc])
                        P.pool(lambda e: e.tensor_tensor(out=o3(5), in0=v0[:, :, 96:97], in1=v0[:, :, 64:65], op=ALU.subtract), reads=[Rb, Rsc], writes=[Rsc])
                        P.act(lambda e: e.activation(out=scb[:, 12:24], in_=scb[:, 12:24], func=AF.Exp), reads=[Rsc], writes=[Rsc])
                    if is_pre:
                        khf = pre_kh[i]; khtmf = pre_khtm[i]
                        P.act(lambda e: e.activation(out=gl[:, i, :], in_=b_[:, 1:513], func=AF.Exp, bias=b_[:, 512:513], scale=-1.0),
                              reads=[Rb, R_gl2[i]], writes=[R_gl2[i]])
                        P.dve(lambda e, khf=khf: e.tensor_tensor(out=khf, in0=kf[:, i, :], in1=gl[:, i, :], op=ALU.mult),
                              reads=[R_kf2[i], R_gl2[i]], writes=[R_prekh[i]])
                        P.act(lambda e: e.activation(out=sc[:, 3:4], in_=b_[:, 512:513], func=AF.Exp), reads=[Rb, Rsc], writes=[Rsc])
                        pt, Rp = psum()
                        pv = bfv(pt)
                        for j in range(4):
                            P.pe(lambda e, j=j, pv=pv, khf=khf: e.transpose(out=pv[:, j * 128:(j + 1) * 128], in_=khf[:, j * 128:(j + 1) * 128],
                                                                            identity=ident[:]), reads=[R_prekh[i], R_ident], writes=[Rp])
                        P.act(lambda e, pv=pv, khtmf=khtmf: e.activation(out=khtmf, in_=pv[:, 0:512], func=AF.Copy), reads=[Rp], writes=[R_prekhtm[i]])
                        pt2, Rp2 = psum()
                        for j in range(4):
                            P.pe(lambda e, j=j, h=h, pt2=pt2, khtmf=khtmf: e.matmul(pt2[:, 0:128], lhsT=khtmf[:, j * 128:(j + 1) * 128],
                                                                                    rhs=vt[:, j, h * 128:(h + 1) * 128], start=(j == 0), stop=(j == 3)),
                                 reads=[R_prekhtm[i], R_vt[j]], writes=[Rp2])
                        P.dve(lambda e, h=h, pt2=pt2: e.scalar_tensor_tensor(out=Shg[:, h, :], in0=Shg[:, h, :], scalar=sc[:, 3:4],
                                                                            in1=pt2[:, 0:128], op0=ALU.mult, op1=ALU.add),
                              reads=[R_Shg[h], Rsc, Rp2], writes=[R_Shg[h]])
                        P.act(lambda e, h=h: e.activation(out=Shgb[:, h, :], in_=Shg[:, h, :], func=AF.Copy), reads=[R_Shg[h]], writes=[R_Shgb[h]])
                        continue
                    for c in range(4):
                        pso = psum_ded()
                        P.pool(lambda e: e.memset(sst[:, 8 + i:9 + i], 0.0), writes=[R_ssth2[i]])
                        c0 = c * 128
                        bseg = b_[:, c0 + 1:c0 + 129]
                        blast = b_[:, c0 + 128:c0 + 129]
                        bprev = b_[:, c0:c0 + 1]
                        b31 = b_[:, c0 + 32:c0 + 33]
                        b63 = b_[:, c0 + 64:c0 + 65]
                        b95 = b_[:, c0 + 96:c0 + 97]
                        kfc = kf[:, i, c0:c0 + 128]
                        qfc = qf[:, i, c0:c0 + 128]
                        P.act(lambda e, bseg=bseg, c=c: e.activation(out=e0[:, 0:64], in_=bseg[:, 0:64], func=AF.Exp, bias=scb[:, c:c + 1], scale=1.0),
                              reads=[Rb, Rsc], writes=[Re0])
                        P.act(lambda e, bseg=bseg, c=c: e.activation(out=e0[:, 64:128], in_=bseg[:, 64:128], func=AF.Exp, bias=scb[:, 4 + c:5 + c], scale=1.0),
                              reads=[Rb, Rsc], writes=[Re0])
                        P.dve(lambda e, qfc=qfc, c=c: e.tensor_tensor(out=qt_[i], in0=qfc, in1=e0, op=ALU.mult),
                              reads=[R_qf2[i], Re0], writes=[R_qt[i]])
                        P.dve(lambda e, c=c: e.tensor_scalar(out=QC[i], in0=qt_[i][:, 64:128], scalar1=scb[:, 20 + c:21 + c], scalar2=None, op0=ALU.mult),
                              reads=[R_qt[i], Rsc], writes=[R_QC[i]])
                        P.act(lambda e, bseg=bseg, b31=b31, c=c: e.activation(out=e1[:, 0:64], in_=bseg[:, 0:64], func=AF.Exp, bias=b31, scale=-1.0),
                              reads=[Rb], writes=[Re1])
                        P.act(lambda e, bseg=bseg, b95=b95, c=c: e.activation(out=e1[:, 64:128], in_=bseg[:, 64:128], func=AF.Exp, bias=b95, scale=-1.0),
                              reads=[Rb], writes=[Re1])
                        P.dve(lambda e, kfc=kfc, c=c: e.tensor_tensor(out=KA[i][:, 0:64], in0=kfc[:, 0:64], in1=e1[:, 0:64], op=ALU.mult),
                              reads=[R_kf2[i], Re1], writes=[R_KA[i]])
                        P.dve(lambda e, kfc=kfc, c=c: e.tensor_tensor(out=KB[i][:, 64:128], in0=kfc[:, 64:128], in1=e1[:, 64:128], op=ALU.mult),
                              reads=[R_kf2[i], Re1], writes=[R_KB[i]])
                        P.dve(lambda e, c=c: e.tensor_scalar(out=KC[i][:, 0:64], in0=KA[i][:, 0:64], scalar1=scb[:, 16 + c:17 + c], scalar2=None, op0=ALU.mult),
                              reads=[R_KA[i], Rsc], writes=[R_KC[i]])
                        P.act(lambda e, bseg=bseg, c=c: e.activation(out=e2, in_=bseg, func=AF.Exp, bias=scb[:, 8 + c:9 + c], scale=1.0),
                              reads=[Rb, Rsc], writes=[Re2])
                        P.dve(lambda e, qfc=qfc, c=c: e.tensor_tensor(out=qh[i], in0=qfc, in1=e2, op=ALU.mult),
                              reads=[R_qf2[i], Re2], writes=[R_qh[i]])
                        P.act(lambda e, bseg=bseg, blast=blast, c=c: e.activation(out=e3, in_=bseg, func=AF.Exp, bias=blast, scale=-1.0),
                              reads=[Rb], writes=[Re3])
                        P.dve(lambda e, kfc=kfc, c=c: e.tensor_tensor(out=kh[i], in0=kfc, in1=e3, op=ALU.mult),
                              reads=[R_kf2[i], Re3], writes=[R_kh[i]])
                        vch = vt[:, c, h * 128:(h + 1) * 128]
                        pt, Rp = psum()
                        P.pe(lambda e, pt=pt, c=c: e.matmul(pt[:, 0:64], lhsT=KA[i], rhs=qt_[i][:, 0:64], start=True, stop=True),
                             reads=[R_KA[i], R_qt[i]], writes=[Rp])
                        P.pe(lambda e, pt=pt, c=c: e.matmul(pt[:, 64:128], lhsT=KB[i], rhs=qt_[i][:, 64:128], start=True, stop=False),
                             reads=[R_KB[i], R_qt[i]], writes=[Rp])
                        P.pe(lambda e, pt=pt, c=c: e.matmul(pt[:, 64:128], lhsT=KC[i], rhs=QC[i], start=False, stop=True),
                             reads=[R_KC[i], R_QC[i]], writes=[Rp])
                        P.dve(lambda e, pt=pt, c=c: e.tensor_tensor(out=attm[i], in0=pt[:, 0:128], in1=U[:], op=ALU.mult),
                              reads=[Rp, R_U], writes=[R_attm[i]])
                        P.pe(lambda e, vch=vch, ptt=pso[0], c=c: e.matmul(ptt[:, 0:128], lhsT=attm[i], rhs=vch, start=True, stop=False),
                             reads=[R_attm[i], R_vt[c]], writes=[pso[1]])
                        P.pe(lambda e, h=h, ptt=pso[0], c=c: e.matmul(ptt[:, 0:128], lhsT=qh[i], rhs=Shgb[:, h, :], start=False, stop=True),
                             reads=[R_qh[i], R_Shgb[h]], writes=[pso[1]])
                        pt, Rp = psum()
                        pv = bfv(pt)
                        P.pe(lambda e, pv=pv, c=c: e.transpose(out=pv[:, 0:128], in_=kh[i], identity=ident[:]),
                             reads=[R_kh[i], R_ident], writes=[Rp])
                        P.act(lambda e, pv=pv, c=c: e.activation(out=khtm[i], in_=pv[:, 0:128], func=AF.Copy), reads=[Rp], writes=[R_khtm[i]])
                        pt2, Rp2 = psum()
                        P.pe(lambda e, vch=vch, pt2=pt2, c=c: e.matmul(pt2[:, 0:128], lhsT=khtm[i], rhs=vch, start=True, stop=True),
                             reads=[R_khtm[i], R_vt[c]], writes=[Rp2])
                        P.dve(lambda e, h=h, pt2=pt2, c=c: e.scalar_tensor_tensor(out=Shg[:, h, :], in0=Shg[:, h, :], scalar=scb[:, 12 + c:13 + c],
                                                                            in1=pt2[:, 0:128], op0=ALU.mult, op1=ALU.add),
                              reads=[R_Shg[h], Rsc, Rp2], writes=[R_Shg[h]])
                        P.act(lambda e, h=h, c=c: e.activation(out=Shgb[:, h, :], in_=Shg[:, h, :], func=AF.Copy), reads=[R_Shg[h]], writes=[R_Shgb[h]])
                        ptt, Rpp = pso
                        P.act(lambda e, ptt=ptt: e.activation(out=junk[:, 512 + 128 * i:640 + 128 * i], in_=ptt[:, 0:128], func=AF.Square,
                                                              accum_out=sst[:, 8 + i:9 + i]), reads=[Rpp, R_ssth2[i]], writes=[R_junkh2[i], R_ssth2[i]])
                        rstd_from_ss(sst[:, 8 + i:9 + i], 1, R_ssth2[i], 1.0 / 128)
                        ot = otmp[:, 128 * i:128 * i + 128]
                        ogi = og[:, 128 * i:128 * i + 128]
                        P.dve(lambda e, ptt=ptt, ot=ot: e.tensor_scalar(out=ot, in0=ptt[:, 0:128], scalar1=sst[:, 8 + i:9 + i], scalar2=None, op0=ALU.mult),
                              reads=[Rpp, R_ssth2[i]], writes=[R_otmp2[i]])
                        P.dve(lambda e, h=h, c=c, ot=ot, ogi=ogi: e.tensor_tensor(out=ogi, in0=ot, in1=gs[:, c, 128 * h:128 * h + 128], op=ALU.mult),
                              reads=[R_otmp2[i], R_gs[c]], writes=[R_og2[i]])
                        pt, Rp = psum()
                        pv = bfv(pt)
                        P.pe(lambda e, pv=pv, ogi=ogi: e.transpose(out=pv[:, 0:128], in_=ogi, identity=ident[:]), reads=[R_og2[i], R_ident], writes=[Rp])
                        P.act(lambda e, h=h, c=c, pv=pv: e.activation(out=mixedT[:, 8 + h, c * 128:(c + 1) * 128], in_=pv[:, 0:128], func=AF.Copy),
                              reads=[Rp], writes=[R_mixTh2[i][c]])
            def stream1():
                secA()
                wdone_cur()
                rec_wait("B")
                sec_ssd()

            def stream2():
                secB()
                rec_set("B")
                wdone_cur()
                tl.pool = {'banks': [5], 'ctr': 0, 'ded': 4}
                sec_hg_head(0)
                wdone(hw[3][2])

            def stream3():
                rec_wait("B")
                sec_hg_head(1)
                wdone(hw[3][2])

            wdone_cur()
            if _SEQ_DEBUG:
                tl.pool = {'banks': [0, 1, 2, 3], 'ctr': 0, 'ded': None}
                secA(); wdone_cur()
                tl.pool = {'banks': [4, 5, 6, 7], 'ctr': 0, 'ded': None}
                secB(); wdone_cur()
                tl.pool = {'banks': [0, 1, 2, 3], 'ctr': 0, 'ded': None}
                sec_ssd()
                tl.pool = {'banks': [5], 'ctr': 0, 'ded': 4}
                sec_hg_head(0)
                tl.pool = {'banks': [7], 'ctr': 0, 'ded': 6}
                sec_hg_head(1)
                wdone(hw[3][2]); wdone(hw[3][2])
                tl.pool = None
            else:
              run_interleaved([stream1, stream2, stream3],
                            [{'banks': [0, 1, 2, 3], 'ctr': 0, 'ded': None}, {'banks': [4, 5, 6, 7], 'ctr': 0, 'ded': None},
                             {'banks': [7], 'ctr': 0, 'ded': 6}])
            if is_pre:
                P.pool(lambda e: e.memset(qf[:, 0, 0:1], 0.0), reads=R_prekh + R_prekhtm, writes=R_qf2 + R_prekh + R_prekhtm)
                return
            P.barrier()

            A = Arena()
            qT = A.bf(8 * 512).rearrange("p (k t) -> p k t", k=8); R_qT = Res()
            pe_ = [A.f32(1024).rearrange("p (h k) -> p h k", h=4) for _ in range(2)]; R_pe = [Res(), Res()]
            pn = [A.bf(1024).rearrange("p (h k) -> p h k", h=4) for _ in range(2)]; R_pn = [Res(), Res()]
            prT = A.bf(2 * 4 * 512).rearrange("p (k h t) -> p k h t", k=2, h=4); R_prT = Res()
            oT = A.bf(8 * 512).rearrange("p (k t) -> p k t", k=8); R_oT = Res()
            actT = A.bf(22 * 512).rearrange("p (k t) -> p k t", k=22); R_actT = Res()
            sgt = [A.f32(512) for _ in range(2)]; R_sgt = [Res(), Res()]
            nfw = A.f32(D); R_nfw = Res()
            sa = A.f32(32); R_sa = [Res(), Res()]
            nfw_op = P.dma("sp", "nfw", nfw, nfwd, writes=[R_nfw])
            for lo in P.last_real.values():
                nfw_op.deps[lo] = {"raw"}

            for j in range(2):
                banks = [psum() for _ in range(4)]
                for i in range(2):
                    wt, Rw, _ = wget("OUT%d%d" % (j, i))
                    for s in range(4):
                        ptt, Rpp = banks[s]
                        for kt in range(8):
                            P.pe(lambda e, kt=kt, s=s, i=i, ptt=ptt, wt=wt: e.matmul(
                                ptt[:, 0:512], lhsT=mixedT[:, 8 * i + kt, s * 128:(s + 1) * 128], rhs=wt[:, kt, :],
                                start=(i == 0 and kt == 0), stop=(i == 1 and kt == 7)), reads=[R_mixT[s], R_mixTh2[0][s], R_mixTh2[1][s], Rw], writes=[Rpp])
                for s in range(4):
                    ptt, Rpp = banks[s]
                    P.dve(lambda e, s=s, j=j, ptt=ptt: e.tensor_tensor(out=x_tm[:, s, j * 512:(j + 1) * 512], in0=ptt[:, 0:512],
                                                                       in1=x_tm[:, s, j * 512:(j + 1) * 512], op=ALU.add),
                          reads=[Rpp, R_x[s]], writes=[R_x[s]])
            rms_T(lambda s: x_tm[:, s, :], R_x, 4, hT, R_hT)
            for jc in range(2):
                wt, Rw, _ = wget("Q%d" % jc)
                for j in range(4):
                    pt, Rp = proj_fm(wt, Rw, j, hT, R_hT)
                    P.act(lambda e, pt=pt, jc=jc, j=j: e.activation(out=qT[:, 4 * jc + j, :], in_=pt[:, 0:512], func=AF.Copy),
                          reads=[Rp], writes=[R_qT])
            for s in range(4):
                b = s % 2
                scb = [psum(), psum()]
                for h in range(4):
                    ptt, Rpp = scb[h // 2]
                    for d2 in range(2):
                        P.pe(lambda e, h=h, d2=d2, s=s, ptt=ptt: e.matmul(ptt[:, (h % 2) * 256:(h % 2) * 256 + 256],
                                                                          lhsT=qT[:, 2 * h + d2, s * 128:(s + 1) * 128], rhs=kmT[:, 2 * h + d2, :],
                                                                          start=(d2 == 0), stop=(d2 == 1)), reads=[R_qT, R_kmT], writes=[Rpp])
                sav = sa[:, 16 * b:16 * b + 16]
                Rsa = R_sa[b]
                for hb in range(2):
                    P.dve(lambda e, hb=hb, sav=sav, ptt=scb[hb][0]: e.tensor_reduce(out=sav[:, 2 * hb:2 * hb + 2],
                                                                                   in_=ptt[:, 0:512].rearrange("p (h k) -> p h k", h=2),
                                                                                   axis=AX.X, op=ALU.max), reads=[scb[hb][1], Rsa], writes=[Rsa])
                P.dve(lambda e, sav=sav: e.tensor_scalar(out=sav[:, 0:4], in0=sav[:, 0:4], scalar1=-1.0 / 16, scalar2=None, op0=ALU.mult),
                      reads=[Rsa], writes=[Rsa])
                P.pool(lambda e, sav=sav: e.memset(sav[:, 4:8], 0.0), reads=[Rsa], writes=[Rsa])
                for h in range(4):
                    ptt, Rpp = scb[h // 2]
                    P.act(lambda e, h=h, b=b, sav=sav, ptt=ptt: e.activation(out=pe_[b][:, h, :], in_=ptt[:, (h % 2) * 256:(h % 2) * 256 + 256],
                                                                             func=AF.Exp, bias=sav[:, h:h + 1], scale=1.0 / 16,
                                                                             accum_out=sav[:, 4 + h:5 + h]), reads=[Rpp, Rsa], writes=[R_pe[b], Rsa])
                P.dve(lambda e, sav=sav: e.reciprocal(out=sav[:, 4:8], in_=sav[:, 4:8]), reads=[Rsa], writes=[Rsa])
                P.dve(lambda e, b=b, sav=sav: e.tensor_tensor(out=pn[b], in0=pe_[b], in1=sav[:, 4:8].unsqueeze(2).broadcast_to([128, 4, 256]),
                                                              op=ALU.mult), reads=[R_pe[b], Rsa], writes=[R_pn[b]])
                pt, Rp = psum()
                pv = bfv(pt)
                for k2 in range(2):
                    for h in range(4):
                        P.pe(lambda e, k2=k2, h=h, b=b, pv=pv: e.transpose(out=pv[:, (k2 * 4 + h) * 128:(k2 * 4 + h + 1) * 128],
                                                                          in_=pn[b][:, h, k2 * 128:(k2 + 1) * 128], identity=ident[:]),
                             reads=[R_pn[b], R_ident], writes=[Rp])
                for k2 in range(2):
                    P.act(lambda e, s=s, k2=k2, pv=pv: e.activation(out=prT[:, k2, :, s * 128:(s + 1) * 128],
                                                                    in_=pv[:, k2 * 512:(k2 + 1) * 512].rearrange("p (h t) -> p h t", h=4),
                                                                    func=AF.Copy), reads=[Rp], writes=[R_prT])
            for h in range(4):
                for d2 in range(2):
                    pt, Rp = psum()
                    for k2 in range(2):
                        P.pe(lambda e, h=h, d2=d2, k2=k2, pt=pt: e.matmul(pt[:, 0:512], lhsT=vm[:, k2, h * 256 + d2 * 128:h * 256 + (d2 + 1) * 128],
                                                                          rhs=prT[:, k2, h, :], start=(k2 == 0), stop=(k2 == 1)),
                             reads=[R_vm, R_prT], writes=[Rp])
                    P.act(lambda e, h=h, d2=d2, pt=pt: e.activation(out=oT[:, 2 * h + d2, :], in_=pt[:, 0:512], func=AF.Copy),
                          reads=[Rp], writes=[R_oT])
            for j in range(2):
                wt, Rw, _ = wget("O%d" % j)
                for s in range(4):
                    pt, Rp = proj_tm(wt, Rw, s, oT, R_oT)
                    P.dve(lambda e, s=s, j=j, pt=pt: e.tensor_tensor(out=x_tm[:, s, j * 512:(j + 1) * 512], in0=pt[:, 0:512],
                                                                     in1=x_tm[:, s, j * 512:(j + 1) * 512], op=ALU.add),
                          reads=[Rp, R_x[s]], writes=[R_x[s]])
            rms_T(lambda s: x_tm[:, s, :], R_x, 4, hT, R_hT)
            for jc in range(11):
                wt, Rw, _ = wget("GU%d" % jc)
                for jj in range(2):
                    pg, Rpg = proj_fm(wt, Rw, jj, hT, R_hT)
                    pu, Rpu = proj_fm(wt, Rw, 2 + jj, hT, R_hT)
                    sg, Rsg = sgt[jj], R_sgt[jj]
                    P.act(lambda e, pg=pg, sg=sg: e.activation(out=sg, in_=pg[:, 0:512], func=AF.Silu), reads=[Rpg], writes=[Rsg])
                    P.dve(lambda e, pu=pu, sg=sg, jc=jc, jj=jj: e.tensor_tensor(out=actT[:, 2 * jc + jj, :], in0=pu[:, 0:512], in1=sg, op=ALU.mult),
                          reads=[Rpu, Rsg], writes=[R_actT])
            for j in range(2):
                banks = [psum() for _ in range(4)]
                for i in range(3):
                    wt, Rw, _ = wget("D%d%d" % (j, i))
                    nk = 8 if i < 2 else 6
                    for s in range(4):
                        ptt, Rpp = banks[s]
                        for kt in range(nk):
                            P.pe(lambda e, kt=kt, s=s, i=i, nk=nk, ptt=ptt, wt=wt: e.matmul(
                                ptt[:, 0:512], lhsT=actT[:, 8 * i + kt, s * 128:(s + 1) * 128], rhs=wt[:, kt, :],
                                start=(i == 0 and kt == 0), stop=(i == 2 and kt == nk - 1)), reads=[R_actT, Rw], writes=[Rpp])
                for s in range(4):
                    ptt, Rpp = banks[s]
                    P.dve(lambda e, s=s, j=j, ptt=ptt: e.tensor_tensor(out=x_tm[:, s, j * 512:(j + 1) * 512], in0=ptt[:, 0:512],
                                                                       in1=x_tm[:, s, j * 512:(j + 1) * 512], op=ALU.add),
                          reads=[Rpp, R_x[s]], writes=[R_x[s]])
            ssf = stat[:, 32:36]
            Rsf = R_ssf
            P.pool(lambda e: e.memset(ssf, 0.0), writes=[Rsf])
            for s in range(4):
                P.act(lambda e, s=s: e.activation(out=junk[:], in_=x_tm[:, s, :], func=AF.Square, accum_out=ssf[:, s:s + 1]),
                      reads=[R_x[s], Rsf], writes=[R_junk, R_junkh2[0], R_junkh2[1], Rsf])
            rstd_from_ss(ssf, 4, Rsf, 1.0 / D)
            for s in range(4):
                b = 0
                P.dve(lambda e, s=s, b=b: e.scalar_tensor_tensor(out=ost[b][:], in0=x_tm[:, s, :], scalar=ssf[:, s:s + 1], in1=nfw,
                                                                 op0=ALU.mult, op1=ALU.mult), reads=[R_x[s], Rsf, R_nfw], writes=[R_ost[b]])
                out_dmas.append(P.dma("sp", "ost%d" % b, outd[out_row0 + s * 128:out_row0 + (s + 1) * 128, :], ost[b][:],
                                      reads=[R_ost[b]]))
            P.barrier()

        for t in range(NPRE):
            do_tile(xp, t * 512, True, t, 0)
        for t in range(NT):
            do_tile(xm, t * 512, False, 0, t * 512)
        if wseq is not None:
            assert wstate["got"] == len(wseq), (wstate["got"], len(wseq))
            P.emit(final_wait_ops=out_dmas)
    return nc, wrec


def make_par(inp, NPRE, premask):
    f = lambda a: np.asarray(a, dtype=np.float32)
    par = np.zeros((128, PC_MASK + max(NPRE, 1)), np.float32)
    cw = f(inp["conv_w"])[0]
    par[:, PC_CW:PC_CW + 48] = cw.reshape(4, 12, 128).transpose(2, 1, 0).reshape(128, 48)
    par[:, PC_CB:PC_CB + 12] = f(inp["conv_b"])[0].reshape(12, 128).T
    par[:, PC_DTB:PC_DTB + 16] = f(inp["dt_bias"])[0][None, :]
    par[:, PC_ALOG:PC_ALOG + 16] = f(inp["a_log"])[0][None, :]
    par[:, PC_DSK:PC_DSK + 16] = f(inp["d_skip"])[0][None, :]
    hlb = f(inp["hg_lower_bounds"])
    par[:, PC_HLB0:PC_HLB0 + 8] = hlb[0].reshape(8, 128).T
    par[:, PC_HLB1:PC_HLB1 + 8] = hlb[1].reshape(8, 128).T
    par[:, PC_MIX:PC_MIX + 8] = f(inp["norm_mix_w"])[0].reshape(8, 128).T
    par[:, PC_XA:PC_XA + 8] = f(inp["norm_xa_w"])[0].reshape(8, 128).T
    par[:, PC_MEM:PC_MEM + 8] = f(inp["norm_mem_w"])[0].reshape(8, 128).T
    par[:, PC_FFN:PC_FFN + 8] = f(inp["norm_ffn_w"])[0].reshape(8, 128).T
    par[:, PC_WOUT:PC_WOUT + 8] = f(inp["ssd_norm_w"])[0].reshape(8, 128).T
    par[:, PC_WOUT + 8:PC_WOUT + 16] = f(inp["hg_norm_w"])[0][:, None]
    par[:, PC_MASK:PC_MASK + len(premask)] = np.asarray(premask, np.float32)[None, :]
    return par


_NC_CACHE = {}


def run(inp, T, NPRE, nseg):
    x = np.asarray(inp["x"], np.float32)
    mem = np.asarray(inp["mem"], np.float32)
    B, L, _ = x.shape
    assert L == nseg * T
    key = (T, NPRE)
    if key not in _NC_CACHE:
        _NC_CACHE[key] = build(T, NPRE)
    nc = _NC_CACHE[key]
    shared = {
        "w_in": np.ascontiguousarray(inp["w_in"][0], np.float32), "w_out": np.ascontiguousarray(inp["w_out"][0], np.float32),
        "wq": np.ascontiguousarray(inp["xa_wq"][0], np.float32), "wkv": np.ascontiguousarray(inp["xa_wkv"][0], np.float32),
        "wo": np.ascontiguousarray(inp["xa_wo"][0], np.float32), "wg": np.ascontiguousarray(inp["ffn_w_gate"][0], np.float32),
        "wu": np.ascontiguousarray(inp["ffn_w_up"][0], np.float32), "wd": np.ascontiguousarray(inp["ffn_w_down"][0], np.float32),
        "nfw": np.ascontiguousarray(np.broadcast_to(np.asarray(inp["norm_final_w"], np.float32)[None, :], (128, D))),
    }
    in_maps = []
    npre_tok = max(NPRE, 1) * 512
    for b in range(B):
        for sg in range(nseg):
            start = sg * T
            xpre = np.zeros((npre_tok, D), np.float32)
            premask = np.zeros(max(NPRE, 1), np.float32)
            lo = start - NPRE * 512
            for t in range(NPRE):
                p0 = lo + t * 512
                if p0 >= 0:
                    xpre[t * 512:(t + 1) * 512] = x[b, p0:p0 + 512]
                    premask[t] = 1.0
            m = dict(shared)
            m["xm"] = np.ascontiguousarray(x[b, start:start + T])
            m["xp"] = xpre
            m["mem"] = np.ascontiguousarray(mem[b])
            m["par"] = make_par(inp, NPRE, premask)
            in_maps.append(m)
    res = run_bass_kernel_spmd(nc, in_maps, core_ids=list(range(B * nseg)))
    out = np.zeros((B, L, D), np.float32)
    k = 0
    for b in range(B):
        for sg in range(nseg):
            out[b, sg * T:(sg + 1) * T] = res.results[k]["out"]
            k += 1
    return out


def kernel(**inputs):
    return run(inputs, 4096, 24, 4)
```

```python
import contextlib
import threading
import numpy as np
import concourse.bass as bass
import concourse.mybir as mybir
from concourse.bass_utils import run_bass_kernel_spmd

F32 = mybir.dt.float32
BF16 = mybir.dt.bfloat16
AF = mybir.ActivationFunctionType
ALU = mybir.AluOpType
AX = mybir.AxisListType

D = 1024
FF = 2816
EPS = 1e-6
S1, S2, S3, S4, S5, S6 = 1024, 2560, 2576, 3600, 4624, 5648
NWS = 3
import os as _os
_SEQ_DEBUG = bool(_os.environ.get('KSEQ'))


class Res:
    __slots__ = ("name", "last_w", "readers")

    def __init__(self, name=""):
        self.name = name
        self.last_w = None
        self.readers = {}


class Op:
    __slots__ = ("eng", "fn", "idx", "deps", "dma_key", "dma_cnt", "needs_inc", "inc_val")

    def __init__(self, eng, fn, idx, dma_key=None):
        self.eng = eng
        self.fn = fn
        self.idx = idx
        self.deps = {}
        self.dma_key = dma_key
        self.dma_cnt = 0
        self.needs_inc = False
        self.inc_val = 0


COMPUTE = ("pe", "act", "dve", "pool")


class Prog:
    def __init__(self, nc):
        self.nc = nc
        self.ops = []
        self.dma_counts = {}
        self.last_real = {}
        self.after_add = None

    def add(self, eng, fn, reads=(), writes=(), dma_key=None):
        op = Op(eng, fn, len(self.ops), dma_key)
        if dma_key is not None:
            c = self.dma_counts.get(dma_key, 0) + 1
            self.dma_counts[dma_key] = c
            op.dma_cnt = c
        deps = op.deps
        for r in reads:
            if r.last_w is not None:
                deps.setdefault(r.last_w, set()).add("raw")
        for w in writes:
            lw = w.last_w
            if lw is not None:
                if not (dma_key is not None and lw.dma_key == dma_key):
                    deps.setdefault(lw, set()).add("waw")
            for rd in w.readers.values():
                deps.setdefault(rd, set()).add("war")
        k = ("dma", op.idx) if dma_key is not None else eng
        for r in reads:
            r.readers[k] = op
        for w in writes:
            w.last_w = op
            w.readers = {}
        self.ops.append(op)
        if dma_key is None:
            self.last_real[eng] = op
        if self.after_add is not None:
            self.after_add()
        return op

    def pe(self, fn, reads=(), writes=()):
        return self.add("pe", fn, reads, writes)

    def act(self, fn, reads=(), writes=()):
        return self.add("act", fn, reads, writes)

    def dve(self, fn, reads=(), writes=()):
        return self.add("dve", fn, reads, writes)

    def pool(self, fn, reads=(), writes=()):
        return self.add("pool", fn, reads, writes)

    def dma(self, queue, key, out, in_, reads=(), writes=()):
        return self.add(queue, lambda e: e.dma_start(out=out, in_=in_), reads, writes, dma_key=key)

    def barrier(self, extra=()):
        lasts = dict(self.last_real)
        for e in COMPUTE:
            op = Op(e, None, len(self.ops))
            for e2, lo in lasts.items():
                if e2 != e:
                    op.deps[lo] = {"raw"}
            for x in extra:
                op.deps[x] = {"raw"}
            self.ops.append(op)

    def emit(self, final_wait_ops=()):
        nc = self.nc
        fin = Op("sp", None, len(self.ops))
        for o in final_wait_ops:
            fin.deps[o] = {"raw"}
        ops = self.ops + [fin]
        for op in ops:
            real = {}
            for d, kinds in op.deps.items():
                if d.dma_key is None and d.eng == op.eng and op.dma_key is None:
                    if op.eng == "pe" or kinds == {"war"}:
                        continue
                real[d] = kinds
            op.deps = real
            for d in real:
                if d.dma_key is None:
                    d.needs_inc = True
        cnt = {e: 0 for e in COMPUTE + ("sp",)}
        for op in ops:
            if op.dma_key is None and op.needs_inc:
                cnt[op.eng] += 1
                op.inc_val = cnt[op.eng]
        dma_keys = sorted(self.dma_counts.keys())
        with contextlib.ExitStack() as st:
            esem = {e: st.enter_context(nc.semaphore("s_" + e)) for e in cnt}
            dsem = {k: st.enter_context(nc.semaphore("d_%d" % i)) for i, k in enumerate(dma_keys)}
            block = st.enter_context(nc.Block())
            engs = {"pe": block.tensor, "act": block.scalar, "dve": block.vector,
                    "pool": block.gpsimd, "sp": block.sync}
            for ename, deco in engs.items():
                my = [o for o in ops if o.eng == ename]
                if not my:
                    continue

                def body(e, my=my, ename=ename):
                    waited = {}
                    for op in my:
                        need = {}
                        for d in op.deps:
                            if d.dma_key is not None:
                                s, v = ("d", d.dma_key), 16 * d.dma_cnt
                            else:
                                s, v = ("e", d.eng), d.inc_val
                            if need.get(s, 0) < v:
                                need[s] = v
                        for s, v in need.items():
                            if waited.get(s, 0) >= v:
                                continue
                            waited[s] = v
                            e.wait_ge(dsem[s[1]] if s[0] == "d" else esem[s[1]], v)
                        if op.fn is None:
                            continue
                        ins = op.fn(e)
                        if op.dma_key is not None:
                            ins.then_inc(dsem[op.dma_key], 16)
                        elif op.needs_inc:
                            ins.then_inc(esem[ename], 1)

                deco(body)


def chunk_catalog():
    cat = []
    for j in range(2):
        cat.append(("Z%d" % j, "w_in", 0, 8, [(0, 512, 512 * j)], "mix"))
    for j in range(3):
        cat.append(("X%d" % j, "w_in", 0, 8, [(0, 512, S1 + 512 * j)], "mix"))
    for a in range(4):
        cat.append(("H%d" % a, "w_in", 0, 8, [(0, 256, S3 + 256 * a), (256, 256, S4 + 256 * a)], "mix"))
    for j in range(2):
        cat.append(("V%d" % j, "w_in", 0, 8, [(0, 512, S5 + 512 * j)], "mix"))
    for j in range(2):
        cat.append(("G%d" % j, "w_in", 0, 8, [(0, 512, S6 + 512 * j)], "mix"))
    for j in range(2):
        for i in range(2):
            cat.append(("OUT%d%d" % (j, i), "w_out", 8 * i, 8, [(0, 512, 512 * j)], "wout%d" % i))
    for j in range(2):
        cat.append(("Q%d" % j, "wq", 0, 8, [(0, 512, 512 * j)], "xa"))
    for j in range(4):
        cat.append(("KV%d" % j, "wkv", 0, 8, [(0, 512, 512 * j)], "mem"))
    for j in range(2):
        cat.append(("O%d" % j, "wo", 0, 8, [(0, 512, 512 * j)], None))
    for j in range(11):
        cat.append(("GU%d" % j, "wgu", 0, 8, [(0, 256, 256 * j), (256, 256, 256 * j)], "ffn"))
    for j in range(2):
        for i in range(3):
            cat.append(("D%d%d" % (j, i), "wd", 8 * i, 8 if i < 2 else 6, [(0, 512, 512 * j)], None))
    return cat


PC_CW, PC_CB, PC_DTB, PC_ALOG, PC_DSK, PC_HLB0, PC_HLB1 = 0, 48, 60, 76, 92, 108, 116
PC_MIX, PC_XA, PC_MEM, PC_FFN, PC_WOUT, PC_MASK = 124, 132, 140, 148, 156, 172


def build(T, NPRE):
    _, rec = _build(T, NPRE, None)
    nc, _ = _build(T, NPRE, rec)
    return nc


def _build(T, NPRE, wseq_in):
    NT = T // 512
    NPAR = PC_MASK + max(NPRE, 1)
    nc = bass.Bass("TRN2", target_bir_lowering=False)
    xm = nc.dram_tensor("xm", [T, D], F32, kind="ExternalInput").ap()
    xp = nc.dram_tensor("xp", [max(NPRE, 1) * 512, D], F32, kind="ExternalInput").ap()
    memd = nc.dram_tensor("mem", [256, D], F32, kind="ExternalInput").ap()
    pard = nc.dram_tensor("par", [128, NPAR], F32, kind="ExternalInput").ap()
    nfwd = nc.dram_tensor("nfw", [128, D], F32, kind="ExternalInput").ap()
    wd_ = {
        "w_in": nc.dram_tensor("w_in", [D, 6672], F32, kind="ExternalInput").ap(),
        "w_out": nc.dram_tensor("w_out", [2048, D], F32, kind="ExternalInput").ap(),
        "wq": nc.dram_tensor("wq", [D, D], F32, kind="ExternalInput").ap(),
        "wkv": nc.dram_tensor("wkv", [D, 2048], F32, kind="ExternalInput").ap(),
        "wo": nc.dram_tensor("wo", [D, D], F32, kind="ExternalInput").ap(),
        "wg": nc.dram_tensor("wg", [D, FF], F32, kind="ExternalInput").ap(),
        "wu": nc.dram_tensor("wu", [D, FF], F32, kind="ExternalInput").ap(),
        "wd": nc.dram_tensor("wd", [FF, D], F32, kind="ExternalInput").ap(),
    }
    outd = nc.dram_tensor("out", [T, D], F32, kind="ExternalOutput").ap()
    cat = chunk_catalog()
    cid = {c[0]: i for i, c in enumerate(cat)}
    wsc = nc.dram_tensor("wsc", [len(cat), 128, 4096], BF16, kind="Internal").ap()
    R_wsc = [Res("wsc%d" % i) for i in range(len(cat))]

    P = Prog(nc)
    with contextlib.ExitStack() as st:
        def sb(name, shape, dt):
            return st.enter_context(nc.sbuf_tensor("sb_" + name, shape, dt))

        par = sb("par", [128, NPAR], F32); R_par = Res()
        ident = sb("ident", [128, 128], BF16); R_ident = Res()
        U = sb("U", [128, 128], F32); R_U = Res()
        ones = sb("ones", [128, 512], F32); R_ones = Res()
        cst = sb("cst", [128, 64], F32); R_cst = Res()
        wdt = sb("wdt", [128, 8, 16], BF16); R_wdt = Res()
        x_tm = sb("x_tm", [128, 4, D], F32); R_x = [Res() for _ in range(4)]
        hT = sb("hT", [128, 8, 512], BF16); R_hT = Res()
        hn = [sb("hn%d" % i, [128, D], BF16) for i in range(2)]; R_hn = [Res(), Res()]
        junk = sb("junk", [128, D], BF16); R_junk = Res()
        wbuf = [sb("wbuf%d" % i, [128, 8, 512], BF16) for i in range(NWS)]; R_wbuf = [Res() for _ in range(NWS)]
        Ssd = sb("Ssd", [128, D], F32); R_Ssd = Res()
        Ssdb = sb("Ssdb", [128, D], BF16); R_Ssdb = Res()
        Shg = sb("Shg", [128, 8, 128], F32); R_Shg = [Res() for _ in range(8)]
        Shgb = sb("Shgb", [128, 8, 128], BF16); R_Shgb = [Res() for _ in range(8)]
        halo = sb("halo", [128, 12, 3], F32); R_halo = Res()
        mixedT = sb("mixedT", [128, 16, 512], BF16); R_mixT = [Res() for _ in range(4)]; R_mixTh2 = [[Res() for _ in range(4)] for _ in range(2)]; R_junkh2 = [Res(), Res()]
        kmT = sb("kmT", [128, 8, 256], BF16); R_kmT = Res()
        vm = sb("vm", [128, 2, D], BF16); R_vm = Res()
        ost = [sb("ost0", [128, D], F32)]; R_ost = [Res()]
        stat = sb("stat", [128, 128], F32)
        ARENA = 27720
        arena = sb("arena", [128, ARENA], F32)
        psb = [st.enter_context(nc.psum_tensor("ps%d" % i, [128, 512], F32)) for i in range(8)]
        R_ps = [Res() for _ in range(8)]
        pctr = [0]

        tl = threading.local()
        rec_state = {"yield": None, "flags": set()}

        def rec_set(name):
            rec_state["flags"].add(name)

        def rec_wait(name):
            while name not in rec_state["flags"]:
                rec_state["yield"]()

        def psum():
            pool = getattr(tl, "pool", None)
            if pool is None:
                i = pctr[0] % 8
                pctr[0] += 1
            else:
                i = pool["banks"][pool["ctr"] % len(pool["banks"])]
                pool["ctr"] += 1
            return psb[i], R_ps[i]

        def psum_ded():
            pool = getattr(tl, "pool", None)
            if pool is None or pool.get("ded") is None:
                return psum()
            return psb[pool["ded"]], R_ps[pool["ded"]]

        def run_interleaved(funcs, pools):
            n = len(funcs)
            st_ = {"turn": 0, "alive": [True] * n, "err": None}
            cv = threading.Condition()

            def advance(k):
                for d in range(1, n + 1):
                    j = (k + d) % n
                    if st_["alive"][j]:
                        st_["turn"] = j
                        return
                st_["turn"] = -1

            def yield_turn():
                k = getattr(tl, "sid", None)
                if k is None:
                    return
                with cv:
                    advance(k)
                    cv.notify_all()
                    while st_["turn"] != k:
                        cv.wait()

            def runner(k):
                tl.pool = pools[k]
                tl.sid = k
                with cv:
                    while st_["turn"] != k:
                        cv.wait()
                try:
                    funcs[k]()
                except BaseException as ex:
                    st_["err"] = ex
                finally:
                    with cv:
                        st_["alive"][k] = False
                        advance(k)
                        cv.notify_all()

            P.after_add = None if _os.environ.get('KCOARSE') else yield_turn
            rec_state['yield'] = yield_turn
            rec_state['flags'] = set()
            ths = [threading.Thread(target=runner, args=(k,)) for k in range(n)]
            for t in ths:
                t.start()
            for t in ths:
                t.join()
            P.after_add = None
            if st_["err"] is not None:
                raise st_["err"]

        def bfv(pt):
            return pt[:, 0:512].bitcast(BF16)

        class Arena:
            def __init__(self):
                self.off = 0

            def f32(self, n):
                a = arena[:, self.off:self.off + n]
                self.off += n
                assert self.off <= ARENA, self.off
                return a

            def bf(self, n):
                n32 = (n + 1) // 2
                a = arena[:, self.off:self.off + n32].bitcast(BF16)
                self.off += n32
                assert self.off <= ARENA, self.off
                return a

        d_par = P.dma("sp", "par", par[:], pard, writes=[R_par])
        P.pool(lambda e: e.memset(ones[:], 1.0), writes=[R_ones])
        P.pool(lambda e: e.memset(U[:], 1.0), writes=[R_U])
        P.pool(lambda e: e.affine_select(out=U[:], in_=U[:], pattern=[[1, 128]], compare_op=ALU.is_ge,
                                         fill=0.0, base=0, channel_multiplier=-1), reads=[R_U], writes=[R_U])
        idf = arena[:, 0:128]
        R_idf = Res()
        P.pool(lambda e: e.memset(idf, 0.0), writes=[R_idf])
        P.pool(lambda e: e.affine_select(out=idf, in_=ones[:, 0:128], pattern=[[1, 128]], compare_op=ALU.is_equal,
                                         fill=0.0, base=0, channel_multiplier=-1), reads=[R_ones, R_idf], writes=[R_idf])
        P.dve(lambda e: e.tensor_copy(out=ident[:], in_=idf), reads=[R_idf], writes=[R_ident])
        P.pool(lambda e: e.memset(Ssd[:], 0.0), writes=[R_Ssd])
        P.pool(lambda e: e.memset(Ssdb[:], 0.0), writes=[R_Ssdb])
        P.pool(lambda e: e.memset(Shg[:], 0.0), writes=R_Shg)
        P.pool(lambda e: e.memset(Shgb[:], 0.0), writes=R_Shgb)
        P.pool(lambda e: e.memset(halo[:], 0.0), writes=[R_halo])
        P.pool(lambda e: e.memset(cst[:, 32:40], 1.0), writes=[R_cst])
        P.dve(lambda e: e.tensor_tensor(out=cst[:, 40:48], in0=par[:, PC_HLB0:PC_HLB0 + 8],
                                        in1=par[:, PC_HLB1:PC_HLB1 + 8], op=ALU.subtract), reads=[R_par, R_cst], writes=[R_cst])
        P.act(lambda e: e.activation(out=cst[:, 0:8], in_=cst[:, 40:48], func=AF.Sigmoid), reads=[R_cst], writes=[R_cst])
        P.act(lambda e: e.activation(out=cst[:, 8:16], in_=cst[:, 40:48], func=AF.Sigmoid, scale=-1.0), reads=[R_cst], writes=[R_cst])
        P.act(lambda e: e.activation(out=cst[:, 48:64], in_=par[:, PC_ALOG:PC_ALOG + 16], func=AF.Exp), reads=[R_par, R_cst], writes=[R_cst])
        P.dve(lambda e: e.tensor_scalar(out=cst[:, 16:32], in0=cst[:, 48:64], scalar1=-1.0, scalar2=None, op0=ALU.mult),
              reads=[R_cst], writes=[R_cst])
        lb, oml, aneg, onesb = cst[:, 0:8], cst[:, 8:16], cst[:, 16:32], cst[:, 32:40]
        nhalf = sb("nhalf", [128, 8], F32); R_nhalf = Res()
        P.pool(lambda e: e.memset(nhalf[:], -0.5), writes=[R_nhalf])
        hcst = sb("hcst", [128, 40], F32); R_hcst = Res(); R_hm = Res()
        P.dve(lambda e: e.tensor_scalar(out=hcst[:, 0:8], in0=oml, scalar1=0.5, scalar2=None, op0=ALU.mult), reads=[R_cst], writes=[R_hcst])
        P.dve(lambda e: e.tensor_tensor(out=hcst[:, 8:16], in0=hcst[:, 0:8], in1=lb, op=ALU.add), reads=[R_cst, R_hcst], writes=[R_hcst])
        P.dve(lambda e: e.tensor_scalar(out=hcst[:, 16:24], in0=oml, scalar1=-0.5, scalar2=None, op0=ALU.mult), reads=[R_cst, R_hcst], writes=[R_hcst])

        scale_ap = {"mix": par[:, PC_MIX:PC_MIX + 8], "xa": par[:, PC_XA:PC_XA + 8], "mem": par[:, PC_MEM:PC_MEM + 8],
                    "ffn": par[:, PC_FFN:PC_FFN + 8], "wout0": par[:, PC_WOUT:PC_WOUT + 8],
                    "wout1": par[:, PC_WOUT + 8:PC_WOUT + 16], None: onesb}
        ar = Arena(); ar.off = 128
        NSTG = 3
        stg = [ar.f32(4096).rearrange("p (k n) -> p k n", k=8) for _ in range(NSTG)]
        stgb = [ar.bf(4096).rearrange("p (k n) -> p k n", k=8) for _ in range(NSTG)]
        wdt32 = ar.f32(128).rearrange("p (k n) -> p k n", k=8)
        R_stg = [Res() for _ in range(NSTG)]; R_stgb = [Res() for _ in range(NSTG)]; R_wdt32 = Res()
        pro_dmas = []

        def wsrc(key, kt0, nkt, c0, cw):
            return wd_[key].rearrange("(kt p) n -> p kt n", p=128)[:, kt0:kt0 + nkt, c0:c0 + cw]

        def pro_load(ci):
            name, key, kt0, nkt, pieces, sk = cat[ci]
            sl = ci % NSTG
            for pi, (dc, cw, sc) in enumerate(pieces):
                k2 = key
                if key == "wgu":
                    k2 = "wg" if pi == 0 else "wu"
                P.dma("sp", "stg%d" % sl, stg[sl][:, 0:nkt, dc:dc + cw], wsrc(k2, kt0, nkt, sc, cw), writes=[R_stg[sl]])

        def pro_cast_store(ci):
            name, key, kt0, nkt, pieces, sk = cat[ci]
            sl = ci % NSTG
            sap = scale_ap[sk]
            f = (lambda e, sl=sl, nkt=nkt, sap=sap: e.tensor_tensor(
                out=stgb[sl][:, 0:nkt, :], in0=stg[sl][:, 0:nkt, :],
                in1=sap[:, 0:nkt].unsqueeze(2).broadcast_to([128, nkt, 512]), op=ALU.mult))
            (P.pool if ci % 3 == 2 else P.dve)(f, reads=[R_stg[sl], R_par, R_cst], writes=[R_stgb[sl]])
            pro_dmas.append(P.dma("sp", "wscw%d" % sl, wsc[ci].rearrange("p (k n) -> p k n", k=8)[:, 0:nkt, :],
                                  stgb[sl][:, 0:nkt, :], reads=[R_stgb[sl]], writes=[R_wsc[ci]]))

        pro_load(0)
        pro_load(1)
        for ci in range(len(cat)):
            if ci + 2 < len(cat):
                pro_load(ci + 2)
            pro_cast_store(ci)
        P.dma("sp", "wdt32", wdt32, wsrc("w_in", 0, 8, S2, 16), writes=[R_wdt32])
        P.dve(lambda e: e.tensor_tensor(out=wdt[:], in0=wdt32, in1=par[:, PC_MIX:PC_MIX + 8].unsqueeze(2).broadcast_to([128, 8, 16]),
                                        op=ALU.mult), reads=[R_wdt32, R_par], writes=[R_wdt])

        wrec = []
        wseq = wseq_in
        wstate = {"issued": 0, "got": 0}

        def wissue():
            i = wstate["issued"]
            names = wseq if wseq is not None else wrec
            c = cid[names[i]]
            sl = i % NWS
            P.dma("sp", "wb%d" % sl, wbuf[sl][:], wsc[c].rearrange("p (k n) -> p k n", k=8),
                  reads=[R_wsc[c]], writes=[R_wbuf[sl]])
            slot_content[sl] = i
            wstate["issued"] += 1

        slot_content = {}
        occ_done = [0] * NWS
        ref_left = {}

        def can_issue(k):
            return occ_done[k % NWS] == k // NWS

        def wdone(i):
            assert slot_content.get(i % NWS) == i, ("evicted before release", i, slot_content)
            ref_left[i] -= 1
            if ref_left[i] == 0:
                occ_done[i % NWS] += 1

        def wdone_cur():
            cur = getattr(tl, "cur", None)
            if cur is not None:
                wdone(cur)
                tl.cur = None

        def wget(name, auto=True, nref=1):
            if auto:
                wdone_cur()
            i = wstate["got"]
            wstate["got"] += 1
            wrec.append(name)
            ref_left[i] = nref
            if wseq is not None:
                assert wseq[i] == name, (i, wseq[i], name)
            while wstate["issued"] <= i:
                if can_issue(wstate["issued"]):
                    sid = getattr(tl, "sid", None)
                    tl.sid = None
                    try:
                        wissue()
                    finally:
                        tl.sid = sid
                else:
                    rec_state["yield"]()
            if wseq is not None:
                sid = getattr(tl, "sid", None)
                tl.sid = None
                try:
                    while wstate["issued"] < min(len(wseq), i + NWS) and can_issue(wstate["issued"]):
                        wissue()
                finally:
                    tl.sid = sid
            if auto:
                tl.cur = i
            assert slot_content.get(i % NWS) == i, ("not resident at obtain", i, slot_content)
            return wbuf[i % NWS], R_wbuf[i % NWS], i

        def rstd_from_ss(ssv, n, Rs, inv_n):
            P.pool(lambda e: e.tensor_scalar(out=ssv, in0=ssv, scalar1=inv_n, scalar2=EPS, op0=ALU.mult, op1=ALU.add),
                   reads=[Rs], writes=[Rs])
            P.pool(lambda e: e.tensor_tensor(out=ssv, in0=ssv, in1=nhalf[:, 0:n], op=ALU.pow), reads=[Rs, R_nhalf], writes=[Rs])

        rms_ctr = [0]
        R_rms = [Res(), Res()]
        R_ssf = Res()
        R_scp = [Res(), Res()]
        R_prekh = [Res(), Res()]
        R_prekhtm = [Res(), Res()]

        def rms_T(src, Rsrc, nsub, dstT, R_dst):
            k = rms_ctr[0] % 2
            rms_ctr[0] += 1
            ss = stat[:, 8 * k:8 * k + nsub]
            Rss = R_rms[k]
            P.pool(lambda e: e.memset(ss, 0.0), writes=[Rss])
            for s in range(nsub):
                P.act(lambda e, s=s: e.activation(out=junk[:], in_=src(s), func=AF.Square, accum_out=ss[:, s:s + 1]),
                      reads=[Rsrc[s], Rss], writes=[R_junk, R_junkh2[0], R_junkh2[1], Rss])
            rstd_from_ss(ss, nsub, Rss, 1.0 / D)
            for s in range(nsub):
                b = s % 2
                P.dve(lambda e, s=s, b=b: e.tensor_scalar(out=hn[b][:], in0=src(s), scalar1=ss[:, s:s + 1], scalar2=None,
                                                          op0=ALU.mult), reads=[Rsrc[s], Rss], writes=[R_hn[b]])
                pt, Rp = psum()
                pv = bfv(pt)
                for kt in range(8):
                    P.pe(lambda e, kt=kt, b=b, pv=pv: e.transpose(out=pv[:, kt * 128:(kt + 1) * 128],
                                                                  in_=hn[b][:, kt * 128:(kt + 1) * 128], identity=ident[:]),
                         reads=[R_hn[b], R_ident], writes=[Rp])
                P.act(lambda e, s=s, pv=pv: e.activation(out=dstT[:, :, s * 128:(s + 1) * 128],
                                                         in_=pv.rearrange("p (k t) -> p k t", k=8), func=AF.Copy),
                      reads=[Rp], writes=[R_dst])

        def proj_fm(wt, Rw, j, xT, RxT, ncols=512):
            pt, Rp = psum()
            for kt in range(8):
                P.pe(lambda e, kt=kt, pt=pt: e.matmul(pt[:, 0:ncols], lhsT=wt[:, kt, j * 128:(j + 1) * 128], rhs=xT[:, kt, 0:ncols],
                                                      start=(kt == 0), stop=(kt == 7)), reads=[Rw, RxT], writes=[Rp])
            return pt, Rp

        def proj_tm(wt, Rw, s, xT, RxT, ncols=512):
            pt, Rp = psum()
            for kt in range(8):
                P.pe(lambda e, kt=kt, pt=pt: e.matmul(pt[:, 0:ncols], lhsT=xT[:, kt, s * 128:(s + 1) * 128], rhs=wt[:, kt, 0:ncols],
                                                      start=(kt == 0), stop=(kt == 7)), reads=[Rw, RxT], writes=[Rp])
            return pt, Rp

        mem_t = ar.f32(2 * D).rearrange("p (s d) -> p s d", s=2); R_mem = [Res(), Res()]
        mT = ar.bf(8 * 256).rearrange("p (k t) -> p k t", k=8); R_mT = Res()
        for s in range(2):
            P.dma("sp", "mem%d" % s, mem_t[:, s, :], memd[s * 128:(s + 1) * 128, :], writes=[R_mem[s]])
        rms_T(lambda s: mem_t[:, s, :], R_mem, 2, mT, R_mT)
        for jc in range(2):
            wt, Rw, _ = wget("KV%d" % jc)
            for j in range(4):
                pt, Rp = proj_fm(wt, Rw, j, mT, R_mT, ncols=256)
                P.act(lambda e, pt=pt, jc=jc, j=j: e.activation(out=kmT[:, 4 * jc + j, :], in_=pt[:, 0:256], func=AF.Copy),
                      reads=[Rp], writes=[R_kmT])
        for jc in range(2):
            wt, Rw, _ = wget("KV%d" % (2 + jc))
            for s in range(2):
                pt, Rp = proj_tm(wt, Rw, s, mT, R_mT)
                P.act(lambda e, pt=pt, jc=jc, s=s: e.activation(out=vm[:, s, jc * 512:(jc + 1) * 512], in_=pt[:, 0:512], func=AF.Copy),
                      reads=[Rp], writes=[R_vm])
        wdone_cur()
        P.barrier(extra=pro_dmas[-3:])

        out_dmas = []
        tile_ctr = [0]
        pending_barrier = [False]
        mres_store = []

        def mk_mres():
            idx = [0]

            def mres():
                i = idx[0]
                idx[0] += 1
                if i >= len(mres_store):
                    mres_store.append(Res())
                return mres_store[i]
            return mres

        def do_tile(xsrc_d, row0, is_pre, pre_idx, out_row0):
            ti = tile_ctr[0]
            tile_ctr[0] += 1
            A = Arena()
            mres = mk_mres()
            raw = A.f32(4 * 515).rearrange("p (j t) -> p j t", j=4); R_raw = mres()
            cacc = [A.f32(512) for _ in range(2)]; R_cacc = [mres(), mres()]
            xsT = A.bf(8 * 512).rearrange("p (k t) -> p k t", k=8); R_xsT = mres()
            BT = A.bf(2 * 512).rearrange("p (k t) -> p k t", k=2); R_BT = mres()
            CT = A.bf(2 * 512).rearrange("p (k t) -> p k t", k=2); R_CT = mres()
            xs_tm = A.bf(4 * D).rearrange("p (s d) -> p s d", s=4); R_xs = [mres() for _ in range(4)]
            B_tm = A.bf(4 * 256).rearrange("p (s d) -> p s d", s=4); R_Btm = mres()
            zs = A.bf(4 * D).rearrange("p (s d) -> p s d", s=4); R_zs = [mres() for _ in range(4)]
            vt = A.bf(4 * D).rearrange("p (s d) -> p s d", s=4); R_vt = [mres() for _ in range(4)]
            gs = A.bf(4 * D).rearrange("p (s d) -> p s d", s=4); R_gs = [mres() for _ in range(4)]
            dtr = A.f32(64).rearrange("p (s h) -> p s h", s=4); R_dtr = mres()
            dtA = A.f32(64).rearrange("p (s h) -> p s h", s=4); R_dtA = mres()
            acs = [A.f32(96) for _ in range(2)]; R_acs = [mres(), mres()]
            Lseg = [A.f32(512) for _ in range(2)]; R_Lseg = [mres(), mres()]
            MT = A.bf(16 * 128).rearrange("p (h l) -> p h l", h=16); R_MT = mres()
            cbm = A.f32(256).rearrange("p (g l) -> p g l", g=2); R_cbm = mres()
            xdt = A.bf(D); R_xdt = mres()
            xdtd = A.bf(D); R_xdtd = mres()
            t1 = A.f32(D); R_t1 = mres()
            t3 = A.f32(D); R_t3 = mres()
            yn = A.bf(D); R_yn = mres()
            qf = A.f32(1024).rearrange("p (i t) -> p i t", i=2); R_qf = mres()
            gl = A.f32(1024).rearrange("p (i t) -> p i t", i=2); R_gl = mres()
            kf = A.f32(1024).rearrange("p (i t) -> p i t", i=2); R_kf = mres()
            bt = [A.f32(513) for _ in range(2)]; R_bt = [mres(), mres()]
            etmp = [A.f32(128) for _ in range(8)]; R_et = [mres() for _ in range(8)]
            qt_ = [A.bf(128) for _ in range(2)]; R_qt = [mres(), mres()]
            KA = [A.bf(128) for _ in range(2)]; R_KA = [mres(), mres()]
            KB = [A.bf(128) for _ in range(2)]; R_KB = [mres(), mres()]
            KC = [A.bf(128) for _ in range(2)]; R_KC = [mres(), mres()]
            QC = [A.bf(64) for _ in range(2)]; R_QC = [mres(), mres()]
            qh = [A.bf(128) for _ in range(2)]; R_qh = [mres(), mres()]
            kh = [A.bf(128) for _ in range(2)]; R_kh = [mres(), mres()]
            khtm = [A.bf(128) for _ in range(2)]; R_khtm = [mres(), mres()]
            attm = [A.bf(128) for _ in range(2)]; R_attm = [mres(), mres()]
            otmp = A.f32(256); R_otmp = mres()
            og = A.bf(256); R_og = mres()
            sst = A.f32(32); R_sst = mres(); R_ssth2 = [mres(), mres()]
            R_qf2 = [mres(), mres()]; R_gl2 = [mres(), mres()]; R_kf2 = [mres(), mres()]; R_otmp2 = [mres(), mres()]; R_og2 = [mres(), mres()]

            qfb = qf.rearrange("p i t -> p (i t)").bitcast(BF16)
            pre_kh = [qfb[:, 0:512], qfb[:, 512:1024]]
            pre_khtm = [qfb[:, 1024:1536], qfb[:, 1536:2048]]

            for s in range(4):
                P.dma("sp", "x%d" % s, x_tm[:, s, :], xsrc_d[row0 + s * 128:row0 + (s + 1) * 128, :], writes=[R_x[s]])
            rms_T(lambda s: x_tm[:, s, :], R_x, 4, hT, R_hT)
            if pending_barrier[0]:
                P.barrier()
                pending_barrier[0] = False
            if not is_pre:
                for i in range(2):
                    P.pool(lambda e, i=i: e.memset(KA[i], 0.0), writes=[R_KA[i]])
                    P.pool(lambda e, i=i: e.memset(KB[i], 0.0), writes=[R_KB[i]])
                    P.pool(lambda e, i=i: e.memset(KC[i], 0.0), writes=[R_KC[i]])

            def tm_chunk(nm, jc, dst, Rdst, func):
                wt, Rw, _ = wget("%s%d" % (nm, jc))
                for s in range(4):
                    pt, Rp = proj_tm(wt, Rw, s, hT, R_hT)
                    P.act(lambda e, pt=pt, s=s, jc=jc: e.activation(out=dst[:, s, jc * 512:(jc + 1) * 512], in_=pt[:, 0:512], func=func),
                          reads=[Rp], writes=[Rdst[s]])
            if is_pre:
                tm_list = [("V", 0, vt, R_vt, AF.Copy), ("V", 1, vt, R_vt, AF.Copy)]
            else:
                tm_list = [("Z", 0, zs, R_zs, AF.Silu), ("Z", 1, zs, R_zs, AF.Silu), ("V", 0, vt, R_vt, AF.Copy),
                           ("V", 1, vt, R_vt, AF.Copy), ("G", 0, gs, R_gs, AF.Silu), ("G", 1, gs, R_gs, AF.Silu)]
            def secA():
                for c3 in range(3):
                    wt, Rw, _ = wget("X%d" % c3)
                    P.pool(lambda e, c3=c3: e.tensor_copy(out=raw[:, :, 0:3], in_=halo[:, 4 * c3:4 * c3 + 4, :]),
                           reads=[R_halo], writes=[R_raw])
                    for j in range(4):
                        pt, Rp = proj_fm(wt, Rw, j, hT, R_hT)
                        P.act(lambda e, pt=pt, j=j: e.activation(out=raw[:, j, 3:515], in_=pt[:, 0:512], func=AF.Copy),
                              reads=[Rp], writes=[R_raw])
                    P.pool(lambda e, c3=c3: e.tensor_copy(out=halo[:, 4 * c3:4 * c3 + 4, :], in_=raw[:, :, 512:515]),
                           reads=[R_raw], writes=[R_halo])
                    for j in range(4):
                        ct = 4 * c3 + j
                        ca, Rca = cacc[j % 2], R_cacc[j % 2]
                        P.dve(lambda e, j=j, ct=ct, ca=ca: e.tensor_scalar(
                            out=ca, in0=raw[:, j, 0:512], scalar1=par[:, PC_CW + 4 * ct:PC_CW + 4 * ct + 1],
                            scalar2=par[:, PC_CB + ct:PC_CB + ct + 1], op0=ALU.mult, op1=ALU.add), reads=[R_raw, R_par], writes=[Rca])
                        for k in range(1, 4):
                            P.dve(lambda e, j=j, ct=ct, k=k, ca=ca: e.scalar_tensor_tensor(
                                out=ca, in0=raw[:, j, k:k + 512], scalar=par[:, PC_CW + 4 * ct + k:PC_CW + 4 * ct + k + 1],
                                in1=ca, op0=ALU.mult, op1=ALU.add), reads=[R_raw, R_par, Rca], writes=[Rca])
                        if ct < 8:
                            dst, Rd = xsT[:, ct, :], R_xsT
                        elif ct < 10:
                            dst, Rd = BT[:, ct - 8, :], R_BT
                        else:
                            dst, Rd = CT[:, ct - 10, :], R_CT
                        P.act(lambda e, ca=ca, dst=dst: e.activation(out=dst, in_=ca, func=AF.Silu), reads=[Rca], writes=[Rd])
                for s in range(4):
                    pt, Rp = psum()
                    pv = bfv(pt)
                    for kt in range(8):
                        P.pe(lambda e, kt=kt, s=s, pv=pv: e.transpose(out=pv[:, kt * 128:(kt + 1) * 128],
                                                                      in_=xsT[:, kt, s * 128:(s + 1) * 128], identity=ident[:]),
                             reads=[R_xsT, R_ident], writes=[Rp])
                    P.act(lambda e, s=s, pv=pv: e.activation(out=xs_tm[:, s, :], in_=pv, func=AF.Copy), reads=[Rp], writes=[R_xs[s]])
                pt, Rp = psum()
                pv = bfv(pt)
                for s in range(4):
                    for g in range(2):
                        P.pe(lambda e, s=s, g=g, pv=pv: e.transpose(out=pv[:, s * 256 + g * 128:s * 256 + (g + 1) * 128],
                                                                    in_=BT[:, g, s * 128:(s + 1) * 128], identity=ident[:]),
                             reads=[R_BT, R_ident], writes=[Rp])
                P.act(lambda e, pv=pv: e.activation(out=B_tm[:], in_=pv.rearrange("p (s d) -> p s d", s=4), func=AF.Copy),
                      reads=[Rp], writes=[R_Btm])

                pt, Rp = psum()
                for s in range(4):
                    for kt in range(8):
                        P.pe(lambda e, s=s, kt=kt, pt=pt: e.matmul(pt[:, s * 16:(s + 1) * 16], lhsT=hT[:, kt, s * 128:(s + 1) * 128],
                                                                   rhs=wdt[:, kt, :], start=(kt == 0), stop=(kt == 7)),
                             reads=[R_hT, R_wdt], writes=[Rp])
                P.dve(lambda e, pt=pt: e.tensor_tensor(out=dtr[:], in0=pt[:, 0:64].rearrange("p (s h) -> p s h", s=4),
                                                       in1=par[:, PC_DTB:PC_DTB + 16].unsqueeze(1).broadcast_to([128, 4, 16]), op=ALU.add),
                      reads=[Rp, R_par], writes=[R_dtr])
                P.act(lambda e: e.activation(out=dtr[:], in_=dtr[:], func=AF.Exp), reads=[R_dtr], writes=[R_dtr])
                P.act(lambda e: e.activation(out=dtr[:], in_=dtr[:], func=AF.Ln, bias=1.0), reads=[R_dtr], writes=[R_dtr])
                if is_pre:
                    P.dve(lambda e: e.tensor_scalar(out=dtr[:], in0=dtr[:], scalar1=par[:, PC_MASK + pre_idx:PC_MASK + pre_idx + 1],
                                                    scalar2=None, op0=ALU.mult), reads=[R_dtr, R_par], writes=[R_dtr])
                P.dve(lambda e: e.tensor_tensor(out=dtA[:], in0=dtr[:], in1=aneg.unsqueeze(1).broadcast_to([128, 4, 16]), op=ALU.mult),
                      reads=[R_dtr, R_cst], writes=[R_dtA])


            def secB():
                while tm_list:
                    tm_chunk(*tm_list.pop(0))

            def sec_ssd():
                if is_pre:
                    ac = acs[0]; Rac = R_acs[0]
                    pa, Rpa = psum()
                    for j in range(4):
                        P.pe(lambda e, j=j, pa=pa: e.matmul(pa[:, j * 16:(j + 1) * 16], lhsT=U[:], rhs=dtA[:, j, :], start=True, stop=True),
                             reads=[R_U, R_dtA], writes=[Rpa])
                        P.pe(lambda e, j=j, pa=pa: e.matmul(pa[:, 64 + j * 16:64 + (j + 1) * 16], lhsT=ones[:, 0:128], rhs=dtA[:, j, :],
                                                            start=True, stop=True), reads=[R_ones, R_dtA], writes=[Rpa])
                    suf = acs[1]; Rsuf = R_acs[1]
                    P.dve(lambda e, pa=pa: e.tensor_copy(out=suf[:, 0:64], in_=pa[:, 64:128]), reads=[Rpa], writes=[Rsuf])
                    for j in (2, 1, 0):
                        P.dve(lambda e, j=j: e.tensor_tensor(out=suf[:, j * 16:(j + 1) * 16], in0=suf[:, j * 16:(j + 1) * 16],
                                                             in1=suf[:, (j + 1) * 16:(j + 2) * 16], op=ALU.add), reads=[Rsuf], writes=[Rsuf])
                    P.dve(lambda e, pa=pa: e.tensor_tensor(out=ac[:, 0:64], in0=suf[:, 0:64], in1=pa[:, 0:64], op=ALU.subtract),
                          reads=[Rpa, Rsuf], writes=[Rac])
                    P.act(lambda e: e.activation(out=ac[:, 0:64], in_=ac[:, 0:64], func=AF.Exp), reads=[Rac], writes=[Rac])
                    P.act(lambda e: e.activation(out=ac[:, 80:96], in_=suf[:, 0:16], func=AF.Exp), reads=[Rsuf, Rac], writes=[Rac])
                    P.dve(lambda e: e.tensor_tensor(out=ac[:, 0:64], in0=ac[:, 0:64], in1=dtr[:].rearrange("p s h -> p (s h)"), op=ALU.mult),
                          reads=[Rac, R_dtr], writes=[Rac])
                    for j in range(4):
                        P.dve(lambda e, j=j: e.tensor_tensor(out=zs[:, j, :].rearrange("p (h d) -> p h d", h=16),
                                                             in0=xs_tm[:, j, :].rearrange("p (h d) -> p h d", h=16),
                                                             in1=ac[:, j * 16:(j + 1) * 16].unsqueeze(2).broadcast_to([128, 16, 64]), op=ALU.mult),
                              reads=[R_xs[j], Rac], writes=[R_zs[j]])
                    pss = [psum(), psum()]
                    for g in range(2):
                        ptt, Rpp = pss[g]
                        for j in range(4):
                            P.pe(lambda e, g=g, j=j, ptt=ptt: e.matmul(ptt[:, 0:512], lhsT=B_tm[:, j, g * 128:(g + 1) * 128],
                                                                       rhs=zs[:, j, g * 512:(g + 1) * 512], start=(j == 0), stop=(j == 3)),
                                 reads=[R_Btm, R_zs[j]], writes=[Rpp])
                    P.dve(lambda e: e.tensor_tensor(out=Ssd.rearrange("p (h d) -> p h d", h=16),
                                                    in0=Ssd.rearrange("p (h d) -> p h d", h=16),
                                                    in1=ac[:, 80:96].unsqueeze(2).broadcast_to([128, 16, 64]), op=ALU.mult),
                          reads=[R_Ssd, Rac], writes=[R_Ssd])
                    for g in range(2):
                        P.dve(lambda e, g=g, ptt=pss[g][0]: e.tensor_tensor(out=Ssd[:, g * 512:(g + 1) * 512], in0=ptt[:, 0:512],
                                                                            in1=Ssd[:, g * 512:(g + 1) * 512], op=ALU.add),
                              reads=[pss[g][1], R_Ssd], writes=[R_Ssd])
                    P.act(lambda e: e.activation(out=Ssdb[:], in_=Ssd[:], func=AF.Copy), reads=[R_Ssd], writes=[R_Ssdb])
                for c in (range(0) if is_pre else range(4)):
                    ac, Rac = acs[c % 2], R_acs[c % 2]
                    pt, Rp = psum()
                    P.pe(lambda e, c=c, pt=pt: e.matmul(pt[:, 0:16], lhsT=U[:], rhs=dtA[:, c, :], start=True, stop=True),
                         reads=[R_U, R_dtA], writes=[Rp])
                    P.pe(lambda e, c=c, pt=pt: e.matmul(pt[:, 16:32], lhsT=ones[:, 0:128], rhs=dtA[:, c, :], start=True, stop=True),
                         reads=[R_ones, R_dtA], writes=[Rp])
                    P.dve(lambda e, pt=pt, ac=ac: e.tensor_copy(out=ac[:, 0:32], in_=pt[:, 0:32]), reads=[Rp], writes=[Rac])
                    P.dve(lambda e, ac=ac: e.tensor_tensor(out=ac[:, 48:64], in0=ac[:, 16:32], in1=ac[:, 0:16], op=ALU.subtract),
                          reads=[Rac], writes=[Rac])
                    P.act(lambda e, ac=ac: e.activation(out=ac[:, 32:48], in_=ac[:, 0:16], func=AF.Exp), reads=[Rac], writes=[Rac])
                    P.act(lambda e, ac=ac: e.activation(out=ac[:, 48:64], in_=ac[:, 48:64], func=AF.Exp), reads=[Rac], writes=[Rac])
                    P.act(lambda e, ac=ac: e.activation(out=ac[:, 64:80], in_=ac[:, 16:32], func=AF.Exp), reads=[Rac], writes=[Rac])
                    P.dve(lambda e, c=c: e.tensor_tensor(out=xdt.rearrange("p (h d) -> p h d", h=16),
                                                         in0=xs_tm[:, c, :].rearrange("p (h d) -> p h d", h=16),
                                                         in1=dtr[:, c, :].unsqueeze(2).broadcast_to([128, 16, 64]), op=ALU.mult),
                          reads=[R_xs[c], R_dtr], writes=[R_xdt])
                    if not is_pre:
                        pt, Rp = psum()
                        for g in range(2):
                            P.pe(lambda e, c=c, g=g, pt=pt: e.matmul(pt[:, g * 128:(g + 1) * 128], lhsT=BT[:, g, c * 128:(c + 1) * 128],
                                                                     rhs=CT[:, g, c * 128:(c + 1) * 128], start=True, stop=True),
                                 reads=[R_BT, R_CT], writes=[Rp])
                        P.dve(lambda e, pt=pt: e.tensor_tensor(out=cbm[:], in0=pt[:, 0:256].rearrange("p (g l) -> p g l", g=2),
                                                               in1=U[:].unsqueeze(1).broadcast_to([128, 2, 128]), op=ALU.mult),
                              reads=[Rp, R_U], writes=[R_cbm])
                        for hb in range(4):
                            Ls, RLs = Lseg[hb % 2], R_Lseg[hb % 2]
                            pt, Rp = psum()
                            for i in range(4):
                                h = hb * 4 + i
                                P.pe(lambda e, c=c, h=h, i=i, pt=pt: e.matmul(pt[:, i * 128:(i + 1) * 128],
                                                                              lhsT=dtA[:, c, h:h + 1].broadcast_to([128, 128]), rhs=U[:],
                                                                              start=True, stop=True), reads=[R_dtA, R_U], writes=[Rp])
                            P.dve(lambda e, pt=pt, hb=hb, ac=ac, Ls=Ls: e.tensor_tensor(
                                out=Ls.rearrange("p (h l) -> p h l", h=4), in0=pt[:, 0:512].rearrange("p (h l) -> p h l", h=4),
                                in1=ac[:, 4 * hb:4 * hb + 4].unsqueeze(2).broadcast_to([128, 4, 128]), op=ALU.subtract),
                                reads=[Rp, Rac], writes=[RLs])
                            P.dve(lambda e, Ls=Ls: e.tensor_scalar(out=Ls, in0=Ls, scalar1=0.0, scalar2=None, op0=ALU.min),
                                  reads=[RLs], writes=[RLs])
                            P.act(lambda e, Ls=Ls: e.activation(out=Ls, in_=Ls, func=AF.Exp), reads=[RLs], writes=[RLs])
                            g = hb // 2
                            P.pool(lambda e, hb=hb, g=g, Ls=Ls: e.tensor_tensor(
                                out=MT[:, 4 * hb:4 * hb + 4, :], in0=Ls.rearrange("p (h l) -> p h l", h=4),
                                in1=cbm[:, g, :].unsqueeze(1).broadcast_to([128, 4, 128]), op=ALU.mult),
                                reads=[RLs, R_cbm], writes=[R_MT])
                        py = [psum(), psum()]
                        for h in range(16):
                            ptt, Rpp = py[h // 8]
                            P.pe(lambda e, h=h, ptt=ptt: e.matmul(ptt[:, (h % 8) * 64:(h % 8 + 1) * 64], lhsT=MT[:, h, :],
                                                                  rhs=xdt[:, h * 64:(h + 1) * 64], start=True, stop=True),
                                 reads=[R_MT, R_xdt], writes=[Rpp])
                        po = [psum(), psum()]
                        for g in range(2):
                            ptt, Rpp = po[g]
                            P.pe(lambda e, g=g, c=c, ptt=ptt: e.matmul(ptt[:, 0:512], lhsT=CT[:, g, c * 128:(c + 1) * 128],
                                                                       rhs=Ssdb[:, g * 512:(g + 1) * 512], start=True, stop=True),
                                 reads=[R_CT, R_Ssdb], writes=[Rpp])
                        for g in range(2):
                            P.dve(lambda e, g=g, ac=ac, ptt=po[g][0]: e.tensor_tensor(
                                out=t1[:, g * 512:(g + 1) * 512].rearrange("p (h d) -> p h d", h=8),
                                in0=ptt[:, 0:512].rearrange("p (h d) -> p h d", h=8),
                                in1=ac[:, 32 + 8 * g:40 + 8 * g].unsqueeze(2).broadcast_to([128, 8, 64]), op=ALU.mult),
                                reads=[po[g][1], Rac], writes=[R_t1])
                        for g in range(2):
                            P.dve(lambda e, g=g, ptt=py[g][0]: e.tensor_tensor(out=t1[:, g * 512:(g + 1) * 512], in0=ptt[:, 0:512],
                                                                               in1=t1[:, g * 512:(g + 1) * 512], op=ALU.add),
                                  reads=[py[g][1], R_t1], writes=[R_t1])
                        P.pool(lambda e, c=c: e.tensor_tensor(out=t3.rearrange("p (h d) -> p h d", h=16),
                                                              in0=xs_tm[:, c, :].rearrange("p (h d) -> p h d", h=16),
                                                              in1=par[:, PC_DSK:PC_DSK + 16].unsqueeze(2).broadcast_to([128, 16, 64]), op=ALU.mult),
                               reads=[R_xs[c], R_par], writes=[R_t3])
                        P.pool(lambda e: e.tensor_tensor(out=t1, in0=t1, in1=t3, op=ALU.add), reads=[R_t1, R_t3], writes=[R_t1])
                        P.dve(lambda e, c=c: e.tensor_tensor(out=t3, in0=t1, in1=zs[:, c, :], op=ALU.mult),
                              reads=[R_t1, R_zs[c], R_t3], writes=[R_t3])
                        P.pool(lambda e: e.memset(sst[:, 0:2], 0.0), writes=[R_sst])
                        for g in range(2):
                            P.act(lambda e, g=g: e.activation(out=junk[:, 0:512], in_=t3[:, g * 512:(g + 1) * 512], func=AF.Square,
                                                              accum_out=sst[:, g:g + 1]), reads=[R_t3, R_sst], writes=[R_junk, R_sst])
                        rstd_from_ss(sst[:, 0:2], 2, R_sst, 1.0 / 512)
                        for g in range(2):
                            P.dve(lambda e, g=g: e.tensor_scalar(out=yn[:, g * 512:(g + 1) * 512], in0=t3[:, g * 512:(g + 1) * 512],
                                                                 scalar1=sst[:, g:g + 1], scalar2=None, op0=ALU.mult),
                                  reads=[R_t3, R_sst], writes=[R_yn])
                        pt, Rp = psum()
                        pv = bfv(pt)
                        for kt in range(8):
                            P.pe(lambda e, kt=kt, pv=pv: e.transpose(out=pv[:, kt * 128:(kt + 1) * 128], in_=yn[:, kt * 128:(kt + 1) * 128],
                                                                     identity=ident[:]), reads=[R_yn, R_ident], writes=[Rp])
                        P.act(lambda e, c=c, pv=pv: e.activation(out=mixedT[:, 0:8, c * 128:(c + 1) * 128],
                                                                 in_=pv.rearrange("p (k t) -> p k t", k=8), func=AF.Copy),
                              reads=[Rp], writes=[R_mixT[c]])
                    P.dve(lambda e, ac=ac: e.tensor_tensor(out=xdtd.rearrange("p (h d) -> p h d", h=16),
                                                           in0=xdt.rearrange("p (h d) -> p h d", h=16),
                                                           in1=ac[:, 48:64].unsqueeze(2).broadcast_to([128, 16, 64]), op=ALU.mult),
                          reads=[R_xdt, Rac], writes=[R_xdtd])
                    pss = [psum(), psum()]
                    for g in range(2):
                        ptt, Rpp = pss[g]
                        P.pe(lambda e, g=g, c=c, ptt=ptt: e.matmul(ptt[:, 0:512], lhsT=B_tm[:, c, g * 128:(g + 1) * 128],
                                                                   rhs=xdtd[:, g * 512:(g + 1) * 512], start=True, stop=True),
                             reads=[R_Btm, R_xdtd], writes=[Rpp])
                    P.dve(lambda e, ac=ac: e.tensor_tensor(out=Ssd.rearrange("p (h d) -> p h d", h=16),
                                                           in0=Ssd.rearrange("p (h d) -> p h d", h=16),
                                                           in1=ac[:, 64:80].unsqueeze(2).broadcast_to([128, 16, 64]), op=ALU.mult),
                          reads=[R_Ssd, Rac], writes=[R_Ssd])
                    for g in range(2):
                        P.dve(lambda e, g=g, ptt=pss[g][0]: e.tensor_tensor(out=Ssd[:, g * 512:(g + 1) * 512], in0=ptt[:, 0:512],
                                                                            in1=Ssd[:, g * 512:(g + 1) * 512], op=ALU.add),
                              reads=[pss[g][1], R_Ssd], writes=[R_Ssd])
                    P.act(lambda e: e.activation(out=Ssdb[:], in_=Ssd[:], func=AF.Copy), reads=[R_Ssd], writes=[R_Ssdb])

            if is_pre:
                mcol = par[:, PC_MASK + pre_idx:PC_MASK + pre_idx + 1]
                P.dve(lambda e: e.tensor_scalar(out=hcst[:, 24:32], in0=hcst[:, 0:8], scalar1=mcol, scalar2=None, op0=ALU.mult),
                      reads=[R_hcst, R_par, R_hm], writes=[R_hm])
                P.dve(lambda e: e.tensor_scalar(out=hcst[:, 32:40], in0=hcst[:, 16:24], scalar1=mcol, scalar2=None, op0=ALU.mult),
                      reads=[R_hcst, R_par, R_hm], writes=[R_hm])
            hw = {}

            def getH(a):
                if a not in hw:
                    hw[a] = None
                    hw[a] = wget("H%d" % a, auto=False, nref=2)
                while hw[a] is None:
                    rec_state["yield"]()
                return hw[a]

            def sec_hg_head(i):
                b_ = bt[i]
                Rb = R_bt[i]
                scb = stat[:, 64 + 24 * i:64 + 24 * i + 24]
                sc = stat[:, 40 + 8 * i:40 + 8 * i + 8]
                Rsc = R_scp[i]
                e0, e1, e2, e3 = etmp[4 * i:4 * i + 4]
                Re0, Re1, Re2, Re3 = R_et[4 * i:4 * i + 4]
                for a in range(4):
                    h = 2 * a + i
                    if a > 0:
                        wdone(hw[a - 1][2])
                    wt, Rw, _ = getH(a)
                    if not is_pre:
                        pt, Rp = proj_fm(wt, Rw, i, hT, R_hT)
                        P.act(lambda e, pt=pt: e.activation(out=qf[:, i, :], in_=pt[:, 0:512], func=AF.Silu), reads=[Rp], writes=[R_qf2[i]])
                    pt, Rp = proj_fm(wt, Rw, 2 + i, hT, R_hT)
                    P.act(lambda e, pt=pt: e.activation(out=kf[:, i, :], in_=pt[:, 0:512], func=AF.Tanh, scale=0.5), reads=[Rp], writes=[R_kf2[i]])
                    P.dve(lambda e, h=h: e.tensor_scalar(out=gl[:, i, :], in0=kf[:, i, :], scalar1=hcst[:, h:h + 1], scalar2=hcst[:, 8 + h:9 + h],
                                                         op0=ALU.mult, op1=ALU.add), reads=[R_kf2[i], R_hcst], writes=[R_gl2[i]])
                    P.act(lambda e: e.activation(out=gl[:, i, :], in_=gl[:, i, :], func=AF.Ln), reads=[R_gl2[i]], writes=[R_gl2[i]])
                    if is_pre:
                        P.dve(lambda e, h=h: e.tensor_scalar(out=kf[:, i, :], in0=kf[:, i, :], scalar1=hcst[:, 32 + h:33 + h], scalar2=hcst[:, 24 + h:25 + h],
                                                             op0=ALU.mult, op1=ALU.add), reads=[R_kf2[i], R_hm], writes=[R_kf2[i]])
                    else:
                        P.dve(lambda e, h=h: e.tensor_scalar(out=kf[:, i, :], in0=kf[:, i, :], scalar1=hcst[:, 16 + h:17 + h], scalar2=hcst[:, h:h + 1],
                                                             op0=ALU.mult, op1=ALU.add), reads=[R_kf2[i], R_hcst], writes=[R_kf2[i]])
                    P.pool(lambda e: e.memset(b_[:, 0:1], 0.0), writes=[Rb])
                    P.dve(lambda e: e.tensor_tensor_scan(out=b_[:, 1:513], data0=ones[:, 0:512], data1=gl[:, i, :], initial=0.0,
                                                         op0=ALU.mult, op1=ALU.add), reads=[R_ones, R_gl2[i], Rb], writes=[Rb])
                    if not is_pre:
                        v0 = b_[:, 0:512].rearrange("p (c t) -> p c t", c=4)
                        v1 = b_[:, 1:513].rearrange("p (c t) -> p c t", c=4)
                        o3 = lambda k: scb[:, 4 * k:4 * k + 4].unsqueeze(2)
                        P.pool(lambda e: e.tensor_scalar(out=o3(0), in0=v0[:, :, 32:33], scalar1=-1.0, scalar2=None, op0=ALU.mult), reads=[Rb, Rsc], writes=[Rsc])
                        P.pool(lambda e: e.tensor_scalar(out=o3(1), in0=v0[:, :, 96:97], scalar1=-1.0, scalar2=None, op0=ALU.mult), reads=[Rb, Rsc], writes=[Rsc])
                        P.pool(lambda e: e.tensor_scalar(out=o3(2), in0=v0[:, :, 0:1], scalar1=-1.0, scalar2=None, op0=ALU.mult), reads=[Rb, Rsc], writes=[Rsc])
                        P.pool(lambda e: e.tensor_tensor(out=o3(3), in0=v1[:, :, 127:128], in1=v0[:, :, 0:1], op=ALU.subtract), reads=[Rb, Rsc], writes=[Rsc])
                        P.pool(lambda e: e.tensor_tensor(out=o3(4), in0=v0[:, :, 64:65], in1=v0[:, :, 32:33], op=ALU.subtract), reads=[Rb, Rsc], writes=[Rsc])
                        P.pool(lambda e: e.tensor_tensor(out=o3(5), in0=v0[:, :, 96:97], in1=v0[:, :, 64:65], op=ALU.subtract), reads=[Rb, Rsc], writes=[Rsc])
                        P.act(lambda e: e.activation(out=scb[:, 12:24], in_=scb[:, 12:24], func=AF.Exp), reads=[Rsc], writes=[Rsc])
                    if is_pre:
                        khf = pre_kh[i]; khtmf = pre_khtm[i]
                        P.act(lambda e: e.activation(out=gl[:, i, :], in_=b_[:, 1:513], func=AF.Exp, bias=b_[:, 512:513], scale=-1.0),
                              reads=[Rb, R_gl2[i]], writes=[R_gl2[i]])
                        P.dve(lambda e, khf=khf: e.tensor_tensor(out=khf, in0=kf[:, i, :], in1=gl[:, i, :], op=ALU.mult),
                              reads=[R_kf2[i], R_gl2[i]], writes=[R_prekh[i]])
                        P.act(lambda e: e.activation(out=sc[:, 3:4], in_=b_[:, 512:513], func=AF.Exp), reads=[Rb, Rsc], writes=[Rsc])
                        pt, Rp = psum()
                        pv = bfv(pt)
                        for j in range(4):
                            P.pe(lambda e, j=j, pv=pv, khf=khf: e.transpose(out=pv[:, j * 128:(j + 1) * 128], in_=khf[:, j * 128:(j + 1) * 128],
                                                                            identity=ident[:]), reads=[R_prekh[i], R_ident], writes=[Rp])
                        P.act(lambda e, pv=pv, khtmf=khtmf: e.activation(out=khtmf, in_=pv[:, 0:512], func=AF.Copy), reads=[Rp], writes=[R_prekhtm[i]])
                        pt2, Rp2 = psum()
                        for j in range(4):
                            P.pe(lambda e, j=j, h=h, pt2=pt2, khtmf=khtmf: e.matmul(pt2[:, 0:128], lhsT=khtmf[:, j * 128:(j + 1) * 128],
                                                                                    rhs=vt[:, j, h * 128:(h + 1) * 128], start=(j == 0), stop=(j == 3)),
                                 reads=[R_prekhtm[i], R_vt[j]], writes=[Rp2])
                        P.dve(lambda e, h=h, pt2=pt2: e.scalar_tensor_tensor(out=Shg[:, h, :], in0=Shg[:, h, :], scalar=sc[:, 3:4],
                                                                            in1=pt2[:, 0:128], op0=ALU.mult, op1=ALU.add),
                              reads=[R_Shg[h], Rsc, Rp2], writes=[R_Shg[h]])
                        P.act(lambda e, h=h: e.activation(out=Shgb[:, h, :], in_=Shg[:, h, :], func=AF.Copy), reads=[R_Shg[h]], writes=[R_Shgb[h]])
                        continue
                    for c in range(4):
                        pso = psum_ded()
                        P.pool(lambda e: e.memset(sst[:, 8 + i:9 + i], 0.0), writes=[R_ssth2[i]])
                        c0 = c * 128
                        bseg = b_[:, c0 + 1:c0 + 129]
                        blast = b_[:, c0 + 128:c0 + 129]
                        bprev = b_[:, c0:c0 + 1]
                        b31 = b_[:, c0 + 32:c0 + 33]
                        b63 = b_[:, c0 + 64:c0 + 65]
                        b95 = b_[:, c0 + 96:c0 + 97]
                        kfc = kf[:, i, c0:c0 + 128]
                        qfc = qf[:, i, c0:c0 + 128]
                        P.act(lambda e, bseg=bseg, c=c: e.activation(out=e0[:, 0:64], in_=bseg[:, 0:64], func=AF.Exp, bias=scb[:, c:c + 1], scale=1.0),
                              reads=[Rb, Rsc], writes=[Re0])
                        P.act(lambda e, bseg=bseg, c=c: e.activation(out=e0[:, 64:128], in_=bseg[:, 64:128], func=AF.Exp, bias=scb[:, 4 + c:5 + c], scale=1.0),
                              reads=[Rb, Rsc], writes=[Re0])
                        P.dve(lambda e, qfc=qfc, c=c: e.tensor_tensor(out=qt_[i], in0=qfc, in1=e0, op=ALU.mult),
                              reads=[R_qf2[i], Re0], writes=[R_qt[i]])
                        P.dve(lambda e, c=c: e.tensor_scalar(out=QC[i], in0=qt_[i][:, 64:128], scalar1=scb[:, 20 + c:21 + c], scalar2=None, op0=ALU.mult),
                              reads=[R_qt[i], Rsc], writes=[R_QC[i]])
                        P.act(lambda e, bseg=bseg, b31=b31, c=c: e.activation(out=e1[:, 0:64], in_=bseg[:, 0:64], func=AF.Exp, bias=b31, scale=-1.0),
                              reads=[Rb], writes=[Re1])
                        P.act(lambda e, bseg=bseg, b95=b95, c=c: e.activation(out=e1[:, 64:128], in_=bseg[:, 64:128], func=AF.Exp, bias=b95, scale=-1.0),
                              reads=[Rb], writes=[Re1])
                        P.dve(lambda e, kfc=kfc, c=c: e.tensor_tensor(out=KA[i][:, 0:64], in0=kfc[:, 0:64], in1=e1[:, 0:64], op=ALU.mult),
                              reads=[R_kf2[i], Re1], writes=[R_KA[i]])
                        P.dve(lambda e, kfc=kfc, c=c: e.tensor_tensor(out=KB[i][:, 64:128], in0=kfc[:, 64:128], in1=e1[:, 64:128], op=ALU.mult),
                              reads=[R_kf2[i], Re1], writes=[R_KB[i]])
                        P.dve(lambda e, c=c: e.tensor_scalar(out=KC[i][:, 0:64], in0=KA[i][:, 0:64], scalar1=scb[:, 16 + c:17 + c], scalar2=None, op0=ALU.mult),
                              reads=[R_KA[i], Rsc], writes=[R_KC[i]])
                        P.act(lambda e, bseg=bseg, c=c: e.activation(out=e2, in_=bseg, func=AF.Exp, bias=scb[:, 8 + c:9 + c], scale=1.0),
                              reads=[Rb, Rsc], writes=[Re2])
                        P.dve(lambda e, qfc=qfc, c=c: e.tensor_tensor(out=qh[i], in0=qfc, in1=e2, op=ALU.mult),
                              reads=[R_qf2[i], Re2], writes=[R_qh[i]])
                        P.act(lambda e, bseg=bseg, blast=blast, c=c: e.activation(out=e3, in_=bseg, func=AF.Exp, bias=blast, scale=-1.0),
                              reads=[Rb], writes=[Re3])
                        P.dve(lambda e, kfc=kfc, c=c: e.tensor_tensor(out=kh[i], in0=kfc, in1=e3, op=ALU.mult),
                              reads=[R_kf2[i], Re3], writes=[R_kh[i]])
                        vch = vt[:, c, h * 128:(h + 1) * 128]
                        pt, Rp = psum()
                        P.pe(lambda e, pt=pt, c=c: e.matmul(pt[:, 0:64], lhsT=KA[i], rhs=qt_[i][:, 0:64], start=True, stop=True),
                             reads=[R_KA[i], R_qt[i]], writes=[Rp])
                        P.pe(lambda e, pt=pt, c=c: e.matmul(pt[:, 64:128], lhsT=KB[i], rhs=qt_[i][:, 64:128], start=True, stop=False),
                             reads=[R_KB[i], R_qt[i]], writes=[Rp])
                        P.pe(lambda e, pt=pt, c=c: e.matmul(pt[:, 64:128], lhsT=KC[i], rhs=QC[i], start=False, stop=True),
                             reads=[R_KC[i], R_QC[i]], writes=[Rp])
                        P.dve(lambda e, pt=pt, c=c: e.tensor_tensor(out=attm[i], in0=pt[:, 0:128], in1=U[:], op=ALU.mult),
                              reads=[Rp, R_U], writes=[R_attm[i]])
                        P.pe(lambda e, vch=vch, ptt=pso[0], c=c: e.matmul(ptt[:, 0:128], lhsT=attm[i], rhs=vch, start=True, stop=False),
                             reads=[R_attm[i], R_vt[c]], writes=[pso[1]])
                        P.pe(lambda e, h=h, ptt=pso[0], c=c: e.matmul(ptt[:, 0:128], lhsT=qh[i], rhs=Shgb[:, h, :], start=False, stop=True),
                             reads=[R_qh[i], R_Shgb[h]], writes=[pso[1]])
                        pt, Rp = psum()
                        pv = bfv(pt)
                        P.pe(lambda e, pv=pv, c=c: e.transpose(out=pv[:, 0:128], in_=kh[i], identity=ident[:]),
                             reads=[R_kh[i], R_ident], writes=[Rp])
                        P.act(lambda e, pv=pv, c=c: e.activation(out=khtm[i], in_=pv[:, 0:128], func=AF.Copy), reads=[Rp], writes=[R_khtm[i]])
                        pt2, Rp2 = psum()
                        P.pe(lambda e, vch=vch, pt2=pt2, c=c: e.matmul(pt2[:, 0:128], lhsT=khtm[i], rhs=vch, start=True, stop=True),
                             reads=[R_khtm[i], R_vt[c]], writes=[Rp2])
                        P.dve(lambda e, h=h, pt2=pt2, c=c: e.scalar_tensor_tensor(out=Shg[:, h, :], in0=Shg[:, h, :], scalar=scb[:, 12 + c:13 + c],
                                                                            in1=pt2[:, 0:128], op0=ALU.mult, op1=ALU.add),
                              reads=[R_Shg[h], Rsc, Rp2], writes=[R_Shg[h]])
                        P.act(lambda e, h=h, c=c: e.activation(out=Shgb[:, h, :], in_=Shg[:, h, :], func=AF.Copy), reads=[R_Shg[h]], writes=[R_Shgb[h]])
                        ptt, Rpp = pso
                        P.act(lambda e, ptt=ptt: e.activation(out=junk[:, 512 + 128 * i:640 + 128 * i], in_=ptt[:, 0:128], func=AF.Square,
                                                              accum_out=sst[:, 8 + i:9 + i]), reads=[Rpp, R_ssth2[i]], writes=[R_junkh2[i], R_ssth2[i]])
                        rstd_from_ss(sst[:, 8 + i:9 + i], 1, R_ssth2[i], 1.0 / 128)
                        ot = otmp[:, 128 * i:128 * i + 128]
                        ogi = og[:, 128 * i:128 * i + 128]
                        P.dve(lambda e, ptt=ptt, ot=ot: e.tensor_scalar(out=ot, in0=ptt[:, 0:128], scalar1=sst[:, 8 + i:9 + i], scalar2=None, op0=ALU.mult),
                              reads=[Rpp, R_ssth2[i]], writes=[R_otmp2[i]])
                        P.dve(lambda e, h=h, c=c, ot=ot, ogi=ogi: e.tensor_tensor(out=ogi, in0=ot, in1=gs[:, c, 128 * h:128 * h + 128], op=ALU.mult),
                              reads=[R_otmp2[i], R_gs[c]], writes=[R_og2[i]])
                        pt, Rp = psum()
                        pv = bfv(pt)
                        P.pe(lambda e, pv=pv, ogi=ogi: e.transpose(out=pv[:, 0:128], in_=ogi, identity=ident[:]), reads=[R_og2[i], R_ident], writes=[Rp])
                        P.act(lambda e, h=h, c=c, pv=pv: e.activation(out=mixedT[:, 8 + h, c * 128:(c + 1) * 128], in_=pv[:, 0:128], func=AF.Copy),
                              reads=[Rp], writes=[R_mixTh2[i][c]])
            def stream1():
                secA()
                wdone_cur()
                rec_wait("B")
                sec_ssd()

            def stream2():
                secB()
                rec_set("B")
                wdone_cur()
                tl.pool = {'banks': [5], 'ctr': 0, 'ded': 4}
                sec_hg_head(0)
                wdone(hw[3][2])

            def stream3():
                rec_wait("B")
                sec_hg_head(1)
                wdone(hw[3][2])

            wdone_cur()
            if _SEQ_DEBUG:
                tl.pool = {'banks': [0, 1, 2, 3], 'ctr': 0, 'ded': None}
                secA(); wdone_cur()
                tl.pool = {'banks': [4, 5, 6, 7], 'ctr': 0, 'ded': None}
                secB(); wdone_cur()
                tl.pool = {'banks': [0, 1, 2, 3], 'ctr': 0, 'ded': None}
                sec_ssd()
                tl.pool = {'banks': [5], 'ctr': 0, 'ded': 4}
                sec_hg_head(0)
                tl.pool = {'banks': [7], 'ctr': 0, 'ded': 6}
                sec_hg_head(1)
                wdone(hw[3][2]); wdone(hw[3][2])
                tl.pool = None
            else:
              run_interleaved([stream1, stream2, stream3],
                            [{'banks': [0, 1, 2, 3], 'ctr': 0, 'ded': None}, {'banks': [4, 5, 6, 7], 'ctr': 0, 'ded': None},
                             {'banks': [7], 'ctr': 0, 'ded': 6}])
            if is_pre:
                P.pool(lambda e: e.memset(qf[:, 0, 0:1], 0.0), reads=R_prekh + R_prekhtm, writes=R_qf2 + R_prekh + R_prekhtm)
                return
            P.barrier()

            A = Arena()
            qT = A.bf(8 * 512).rearrange("p (k t) -> p k t", k=8); R_qT = Res()
            pe_ = [A.f32(1024).rearrange("p (h k) -> p h k", h=4) for _ in range(2)]; R_pe = [Res(), Res()]
            pn = [A.bf(1024).rearrange("p (h k) -> p h k", h=4) for _ in range(2)]; R_pn = [Res(), Res()]
            prT = A.bf(2 * 4 * 512).rearrange("p (k h t) -> p k h t", k=2, h=4); R_prT = Res()
            oT = A.bf(8 * 512).rearrange("p (k t) -> p k t", k=8); R_oT = Res()
            actT = A.bf(22 * 512).rearrange("p (k t) -> p k t", k=22); R_actT = Res()
            sgt = [A.f32(512) for _ in range(2)]; R_sgt = [Res(), Res()]
            nfw = A.f32(D); R_nfw = Res()
            sa = A.f32(32); R_sa = [Res(), Res()]
            nfw_op = P.dma("sp", "nfw", nfw, nfwd, writes=[R_nfw])
            for lo in P.last_real.values():
                nfw_op.deps[lo] = {"raw"}

            for j in range(2):
                banks = [psum() for _ in range(4)]
                for i in range(2):
                    wt, Rw, _ = wget("OUT%d%d" % (j, i))
                    for s in range(4):
                        ptt, Rpp = banks[s]
                        for kt in range(8):
                            P.pe(lambda e, kt=kt, s=s, i=i, ptt=ptt, wt=wt: e.matmul(
                                ptt[:, 0:512], lhsT=mixedT[:, 8 * i + kt, s * 128:(s + 1) * 128], rhs=wt[:, kt, :],
                                start=(i == 0 and kt == 0), stop=(i == 1 and kt == 7)), reads=[R_mixT[s], R_mixTh2[0][s], R_mixTh2[1][s], Rw], writes=[Rpp])
                for s in range(4):
                    ptt, Rpp = banks[s]
                    P.dve(lambda e, s=s, j=j, ptt=ptt: e.tensor_tensor(out=x_tm[:, s, j * 512:(j + 1) * 512], in0=ptt[:, 0:512],
                                                                       in1=x_tm[:, s, j * 512:(j + 1) * 512], op=ALU.add),
                          reads=[Rpp, R_x[s]], writes=[R_x[s]])
            rms_T(lambda s: x_tm[:, s, :], R_x, 4, hT, R_hT)
            for jc in range(2):
                wt, Rw, _ = wget("Q%d" % jc)
                for j in range(4):
                    pt, Rp = proj_fm(wt, Rw, j, hT, R_hT)
                    P.act(lambda e, pt=pt, jc=jc, j=j: e.activation(out=qT[:, 4 * jc + j, :], in_=pt[:, 0:512], func=AF.Copy),
                          reads=[Rp], writes=[R_qT])
            for s in range(4):
                b = s % 2
                scb = [psum(), psum()]
                for h in range(4):
                    ptt, Rpp = scb[h // 2]
                    for d2 in range(2):
                        P.pe(lambda e, h=h, d2=d2, s=s, ptt=ptt: e.matmul(ptt[:, (h % 2) * 256:(h % 2) * 256 + 256],
                                                                          lhsT=qT[:, 2 * h + d2, s * 128:(s + 1) * 128], rhs=kmT[:, 2 * h + d2, :],
                                                                          start=(d2 == 0), stop=(d2 == 1)), reads=[R_qT, R_kmT], writes=[Rpp])
                sav = sa[:, 16 * b:16 * b + 16]
                Rsa = R_sa[b]
                for hb in range(2):
                    P.dve(lambda e, hb=hb, sav=sav, ptt=scb[hb][0]: e.tensor_reduce(out=sav[:, 2 * hb:2 * hb + 2],
                                                                                   in_=ptt[:, 0:512].rearrange("p (h k) -> p h k", h=2),
                                                                                   axis=AX.X, op=ALU.max), reads=[scb[hb][1], Rsa], writes=[Rsa])
                P.dve(lambda e, sav=sav: e.tensor_scalar(out=sav[:, 0:4], in0=sav[:, 0:4], scalar1=-1.0 / 16, scalar2=None, op0=ALU.mult),
                      reads=[Rsa], writes=[Rsa])
                P.pool(lambda e, sav=sav: e.memset(sav[:, 4:8], 0.0), reads=[Rsa], writes=[Rsa])
                for h in range(4):
                    ptt, Rpp = scb[h // 2]
                    P.act(lambda e, h=h, b=b, sav=sav, ptt=ptt: e.activation(out=pe_[b][:, h, :], in_=ptt[:, (h % 2) * 256:(h % 2) * 256 + 256],
                                                                             func=AF.Exp, bias=sav[:, h:h + 1], scale=1.0 / 16,
                                                                             accum_out=sav[:, 4 + h:5 + h]), reads=[Rpp, Rsa], writes=[R_pe[b], Rsa])
                P.dve(lambda e, sav=sav: e.reciprocal(out=sav[:, 4:8], in_=sav[:, 4:8]), reads=[Rsa], writes=[Rsa])
                P.dve(lambda e, b=b, sav=sav: e.tensor_tensor(out=pn[b], in0=pe_[b], in1=sav[:, 4:8].unsqueeze(2).broadcast_to([128, 4, 256]),
                                                              op=ALU.mult), reads=[R_pe[b], Rsa], writes=[R_pn[b]])
                pt, Rp = psum()
                pv = bfv(pt)
                for k2 in range(2):
                    for h in range(4):
                        P.pe(lambda e, k2=k2, h=h, b=b, pv=pv: e.transpose(out=pv[:, (k2 * 4 + h) * 128:(k2 * 4 + h + 1) * 128],
                                                                          in_=pn[b][:, h, k2 * 128:(k2 + 1) * 128], identity=ident[:]),
                             reads=[R_pn[b], R_ident], writes=[Rp])
                for k2 in range(2):
                    P.act(lambda e, s=s, k2=k2, pv=pv: e.activation(out=prT[:, k2, :, s * 128:(s + 1) * 128],
                                                                    in_=pv[:, k2 * 512:(k2 + 1) * 512].rearrange("p (h t) -> p h t", h=4),
                                                                    func=AF.Copy), reads=[Rp], writes=[R_prT])
            for h in range(4):
                for d2 in range(2):
                    pt, Rp = psum()
                    for k2 in range(2):
                        P.pe(lambda e, h=h, d2=d2, k2=k2, pt=pt: e.matmul(pt[:, 0:512], lhsT=vm[:, k2, h * 256 + d2 * 128:h * 256 + (d2 + 1) * 128],
                                                                          rhs=prT[:, k2, h, :], start=(k2 == 0), stop=(k2 == 1)),
                             reads=[R_vm, R_prT], writes=[Rp])
                    P.act(lambda e, h=h, d2=d2, pt=pt: e.activation(out=oT[:, 2 * h + d2, :], in_=pt[:, 0:512], func=AF.Copy),
                          reads=[Rp], writes=[R_oT])
            for j in range(2):
                wt, Rw, _ = wget("O%d" % j)
                for s in range(4):
                    pt, Rp = proj_tm(wt, Rw, s, oT, R_oT)
                    P.dve(lambda e, s=s, j=j, pt=pt: e.tensor_tensor(out=x_tm[:, s, j * 512:(j + 1) * 512], in0=pt[:, 0:512],
                                                                     in1=x_tm[:, s, j * 512:(j + 1) * 512], op=ALU.add),
                          reads=[Rp, R_x[s]], writes=[R_x[s]])
            rms_T(lambda s: x_tm[:, s, :], R_x, 4, hT, R_hT)
            for jc in range(11):
                wt, Rw, _ = wget("GU%d" % jc)
                for jj in range(2):
                    pg, Rpg = proj_fm(wt, Rw, jj, hT, R_hT)
                    pu, Rpu = proj_fm(wt, Rw, 2 + jj, hT, R_hT)
                    sg, Rsg = sgt[jj], R_sgt[jj]
                    P.act(lambda e, pg=pg, sg=sg: e.activation(out=sg, in_=pg[:, 0:512], func=AF.Silu), reads=[Rpg], writes=[Rsg])
                    P.dve(lambda e, pu=pu, sg=sg, jc=jc, jj=jj: e.tensor_tensor(out=actT[:, 2 * jc + jj, :], in0=pu[:, 0:512], in1=sg, op=ALU.mult),
                          reads=[Rpu, Rsg], writes=[R_actT])
            for j in range(2):
                banks = [psum() for _ in range(4)]
                for i in range(3):
                    wt, Rw, _ = wget("D%d%d" % (j, i))
                    nk = 8 if i < 2 else 6
                    for s in range(4):
                        ptt, Rpp = banks[s]
                        for kt in range(nk):
                            P.pe(lambda e, kt=kt, s=s, i=i, nk=nk, ptt=ptt, wt=wt: e.matmul(
                                ptt[:, 0:512], lhsT=actT[:, 8 * i + kt, s * 128:(s + 1) * 128], rhs=wt[:, kt, :],
                                start=(i == 0 and kt == 0), stop=(i == 2 and kt == nk - 1)), reads=[R_actT, Rw], writes=[Rpp])
                for s in range(4):
                    ptt, Rpp = banks[s]
                    P.dve(lambda e, s=s, j=j, ptt=ptt: e.tensor_tensor(out=x_tm[:, s, j * 512:(j + 1) * 512], in0=ptt[:, 0:512],
                                                                       in1=x_tm[:, s, j * 512:(j + 1) * 512], op=ALU.add),
                          reads=[Rpp, R_x[s]], writes=[R_x[s]])
            ssf = stat[:, 32:36]
            Rsf = R_ssf
            P.pool(lambda e: e.memset(ssf, 0.0), writes=[Rsf])
            for s in range(4):
                P.act(lambda e, s=s: e.activation(out=junk[:], in_=x_tm[:, s, :], func=AF.Square, accum_out=ssf[:, s:s + 1]),
                      reads=[R_x[s], Rsf], writes=[R_junk, R_junkh2[0], R_junkh2[1], Rsf])
            rstd_from_ss(ssf, 4, Rsf, 1.0 / D)
            for s in range(4):
                b = 0
                P.dve(lambda e, s=s, b=b: e.scalar_tensor_tensor(out=ost[b][:], in0=x_tm[:, s, :], scalar=ssf[:, s:s + 1], in1=nfw,
                                                                 op0=ALU.mult, op1=ALU.mult), reads=[R_x[s], Rsf, R_nfw], writes=[R_ost[b]])
                out_dmas.append(P.dma("sp", "ost%d" % b, outd[out_row0 + s * 128:out_row0 + (s + 1) * 128, :], ost[b][:],
                                      reads=[R_ost[b]]))
            pending_barrier[0] = True

        for t in range(NPRE):
            do_tile(xp, t * 512, True, t, 0)
        for t in range(NT):
            do_tile(xm, t * 512, False, 0, t * 512)
        if wseq is not None:
            assert wstate["got"] == len(wseq), (wstate["got"], len(wseq))
            P.emit(final_wait_ops=out_dmas)
    return nc, wrec


def make_par(inp, NPRE, premask):
    f = lambda a: np.asarray(a, dtype=np.float32)
    par = np.zeros((128, PC_MASK + max(NPRE, 1)), np.float32)
    cw = f(inp["conv_w"])[0]
    par[:, PC_CW:PC_CW + 48] = cw.reshape(4, 12, 128).transpose(2, 1, 0).reshape(128, 48)
    par[:, PC_CB:PC_CB + 12] = f(inp["conv_b"])[0].reshape(12, 128).T
    par[:, PC_DTB:PC_DTB + 16] = f(inp["dt_bias"])[0][None, :]
    par[:, PC_ALOG:PC_ALOG + 16] = f(inp["a_log"])[0][None, :]
    par[:, PC_DSK:PC_DSK + 16] = f(inp["d_skip"])[0][None, :]
    hlb = f(inp["hg_lower_bounds"])
    par[:, PC_HLB0:PC_HLB0 + 8] = hlb[0].reshape(8, 128).T
    par[:, PC_HLB1:PC_HLB1 + 8] = hlb[1].reshape(8, 128).T
    par[:, PC_MIX:PC_MIX + 8] = f(inp["norm_mix_w"])[0].reshape(8, 128).T
    par[:, PC_XA:PC_XA + 8] = f(inp["norm_xa_w"])[0].reshape(8, 128).T
    par[:, PC_MEM:PC_MEM + 8] = f(inp["norm_mem_w"])[0].reshape(8, 128).T
    par[:, PC_FFN:PC_FFN + 8] = f(inp["norm_ffn_w"])[0].reshape(8, 128).T
    par[:, PC_WOUT:PC_WOUT + 8] = f(inp["ssd_norm_w"])[0].reshape(8, 128).T
    par[:, PC_WOUT + 8:PC_WOUT + 16] = f(inp["hg_norm_w"])[0][:, None]
    par[:, PC_MASK:PC_MASK + len(premask)] = np.asarray(premask, np.float32)[None, :]
    return par


_NC_CACHE = {}


def run(inp, T, NPRE, nseg):
    x = np.asarray(inp["x"], np.float32)
    mem = np.asarray(inp["mem"], np.float32)
    B, L, _ = x.shape
    assert L == nseg * T
    key = (T, NPRE)
    if key not in _NC_CACHE:
        _NC_CACHE[key] = build(T, NPRE)
    nc = _NC_CACHE[key]
    shared = {
        "w_in": np.ascontiguousarray(inp["w_in"][0], np.float32), "w_out": np.ascontiguousarray(inp["w_out"][0], np.float32),
        "wq": np.ascontiguousarray(inp["xa_wq"][0], np.float32), "wkv": np.ascontiguousarray(inp["xa_wkv"][0], np.float32),
        "wo": np.ascontiguousarray(inp["xa_wo"][0], np.float32), "wg": np.ascontiguousarray(inp["ffn_w_gate"][0], np.float32),
        "wu": np.ascontiguousarray(inp["ffn_w_up"][0], np.float32), "wd": np.ascontiguousarray(inp["ffn_w_down"][0], np.float32),
        "nfw": np.ascontiguousarray(np.broadcast_to(np.asarray(inp["norm_final_w"], np.float32)[None, :], (128, D))),
    }
    in_maps = []
    npre_tok = max(NPRE, 1) * 512
    for b in range(B):
        for sg in range(nseg):
            start = sg * T
            xpre = np.zeros((npre_tok, D), np.float32)
            premask = np.zeros(max(NPRE, 1), np.float32)
            lo = start - NPRE * 512
            for t in range(NPRE):
                p0 = lo + t * 512
                if p0 >= 0:
                    xpre[t * 512:(t + 1) * 512] = x[b, p0:p0 + 512]
                    premask[t] = 1.0
            m = dict(shared)
            m["xm"] = np.ascontiguousarray(x[b, start:start + T])
            m["xp"] = xpre
            m["mem"] = np.ascontiguousarray(mem[b])
            m["par"] = make_par(inp, NPRE, premask)
            in_maps.append(m)
    res = run_bass_kernel_spmd(nc, in_maps, core_ids=list(range(B * nseg)))
    out = np.zeros((B, L, D), np.float32)
    k = 0
    for b in range(B):
        for sg in range(nseg):
            out[b, sg * T:(sg + 1) * T] = res.results[k]["out"]
            k += 1
    return out


def kernel(**inputs):
    return run(inputs, 4096, 24, 4)
```
